# Optimizing a Trainium2 kernel written in Bass

```python
import jax
import jax.numpy as jnp
from jax import lax
import numpy as np

D_MODEL = 1024
BATCH = 2
SEQ = 16384
DEPTH = 4

GRID_W = 64
CTX_LEN = 256
CHUNK = 128
WINDOW = 128
HEAD_DIM = 64
N_GROUPS = 4
GROUP_WIDTH = D_MODEL // N_GROUPS
MIX_WIDTH = N_GROUPS * GROUP_WIDTH
N_HEADS = GROUP_WIDTH // HEAD_DIM
N_KV_HEADS = N_HEADS // 2
KV_WIDTH = N_KV_HEADS * HEAD_DIM
IN_SPLITS = (2 * GROUP_WIDTH, GROUP_WIDTH,
             GROUP_WIDTH, GROUP_WIDTH, GROUP_WIDTH, GROUP_WIDTH,
             GROUP_WIDTH, KV_WIDTH, KV_WIDTH, GROUP_WIDTH,
             GROUP_WIDTH, KV_WIDTH, KV_WIDTH, GROUP_WIDTH)
IN_WIDTH = 11 * GROUP_WIDTH + 4 * KV_WIDTH
ROPE_BASE = 10000.0
RMS_EPS = 1e-6
ATTN_SCALE = HEAD_DIM ** -0.5
NEG_INF = -1e30

kernel_name = 'hybrid_parallel_group_flow_block'


def rms_norm(x, gain):
    xf = x.astype(jnp.float32)
    y = xf * lax.rsqrt(jnp.mean(xf * xf, axis=-1, keepdims=True) + RMS_EPS)
    return (y * gain.astype(jnp.float32)).astype(x.dtype)


def modulate(x, gain, mod):
    shift, scale, gate = jnp.split(mod, 3, axis=-1)
    return rms_norm(x, gain) * (1.0 + scale) + shift, gate


def split_columns(z):
    idx = np.cumsum(IN_SPLITS)[:-1].tolist()
    return jnp.split(z, idx, axis=-1)


def to_heads(t):
    b, l, w = t.shape
    return t.reshape(b, l, w // HEAD_DIM, HEAD_DIM).transpose(0, 2, 1, 3)


def from_heads(t):
    b, n, l, d = t.shape
    return t.transpose(0, 2, 1, 3).reshape(b, l, n * d)


def axial_rope_tables(rows, dtype):
    row = jnp.broadcast_to(jnp.arange(rows, dtype=jnp.float32)[:, None], (rows, GRID_W)).reshape(-1)
    col = jnp.broadcast_to(jnp.arange(GRID_W, dtype=jnp.float32)[None, :], (rows, GRID_W)).reshape(-1)
    half = HEAD_DIM // 2
    inv_freq = 1.0 / (ROPE_BASE ** (jnp.arange(0, half, 2, dtype=jnp.float32) / half))
    ang_r = row[:, None] * inv_freq[None, :]
    ang_c = col[:, None] * inv_freq[None, :]
    ang = jnp.concatenate([ang_r, ang_r, ang_c, ang_c], axis=-1)
    return jnp.cos(ang).astype(dtype), jnp.sin(ang).astype(dtype)


def apply_axial_rope(x, cos, sin):
    def rot_half(u):
        u1, u2 = jnp.split(u, 2, axis=-1)
        return jnp.concatenate([-u2, u1], axis=-1)
    xr, xc = jnp.split(x, 2, axis=-1)
    return x * cos + jnp.concatenate([rot_half(xr), rot_half(xc)], axis=-1) * sin


def chunk_mlp_branch(uv, gate, mix, bias):
    u, v = jnp.split(jax.nn.gelu(uv), 2, axis=-1)
    b, l, _ = v.shape
    n = l // CHUNK
    vb = v.reshape(b, n, CHUNK, N_HEADS, HEAD_DIM)
    sv = jnp.einsum('hij,bnjhd->bnihd', mix, vb) + bias.T[None, None, :, :, None]
    return u * sv.reshape(b, l, GROUP_WIDTH) * jax.nn.silu(gate)


def retention_scan(q, k, v, log_gamma, state0, include_diag):
    b, h, l, dh = q.shape
    n = l // CHUNK
    qc = q.reshape(b, h, n, CHUNK, dh)
    kc = k.reshape(b, h, n, CHUNK, dh)
    vc = v.reshape(b, h, n, CHUNK, dh)
    lg = log_gamma[:, None]
    pos = jnp.arange(CHUNK, dtype=jnp.float32)
    diff = pos[:, None] - pos[None, :]
    keep = diff >= 0 if include_diag else diff > 0
    decay = jnp.where(keep, jnp.exp(lg[:, :, None] * jnp.where(keep, diff, 0.0)), 0.0).astype(q.dtype)
    q_decay = jnp.exp(lg * (pos + 1.0)).astype(q.dtype)
    k_decay = jnp.exp(lg * (CHUNK - 1.0 - pos)).astype(q.dtype)
    chunk_decay = jnp.exp(lg[:, 0] * CHUNK).astype(q.dtype)[None, :, None, None]
    scores = jnp.einsum('bhnid,bhnjd->bhnij', qc, kc) * decay[None, :, None]
    intra = jnp.einsum('bhnij,bhnjd->bhnid', scores, vc)
    kv = jnp.einsum('bhnjd,bhnje->nbhde', kc * k_decay[None, :, None, :, None], vc)

    def step(state, kv_c):
        return chunk_decay * state + kv_c, state

    final, prev = lax.scan(step, state0, kv)
    inter = jnp.einsum('bhnid,nbhde->bhnie', qc, prev) * q_decay[None, :, None, :, None]
    return (intra + inter).reshape(b, h, l, dh), final


def retention_bidir(q, k, v, lg_f, lg_b, init_f, init_b):
    y_f, st_f = retention_scan(q, k, v, lg_f, init_f, True)
    flip = lambda a: jnp.flip(a, axis=2)
    y_b, st_b = retention_scan(flip(q), flip(k), flip(v), lg_b, init_b, False)
    return y_f + flip(y_b), st_f, st_b


def blocked_attention(q, k, v):
    b, h, l, dh = q.shape
    hkv = k.shape[1]
    n = l // CHUNK
    qb = jnp.moveaxis(q.reshape(b, hkv, h // hkv, n, CHUNK, dh), 3, 0)

    def attend(q_blk):
        s = jnp.einsum('bgrid,bgjd->bgrij', q_blk, k, preferred_element_type=jnp.float32) * ATTN_SCALE
        p = jax.nn.softmax(s, axis=-1).astype(v.dtype)
        return jnp.einsum('bgrij,bgjd->bgrid', p, v)

    o = lax.map(attend, qb)
    return jnp.moveaxis(o, 0, 3).reshape(b, h, l, dh)


def window_attention(q, k, v, k_ctx, v_ctx, sink):
    b, h, l, dh = q.shape
    hkv = k.shape[1]
    rep = h // hkv
    n = l // CHUNK
    pad = ((0, 0), (0, 0), (WINDOW, WINDOW), (0, 0))
    kb = jnp.pad(k, pad).reshape(b, hkv, n + 2, CHUNK, dh)
    vb = jnp.pad(v, pad).reshape(b, hkv, n + 2, CHUNK, dh)
    kwin = jnp.concatenate([kb[:, :, :-2], kb[:, :, 1:-1], kb[:, :, 2:]], axis=3)
    vwin = jnp.concatenate([vb[:, :, :-2], vb[:, :, 1:-1], vb[:, :, 2:]], axis=3)
    qb = q.reshape(b, hkv, rep, n, CHUNK, dh)
    s_loc = jnp.einsum('bgrnid,bgnjd->bgrnij', qb, kwin, preferred_element_type=jnp.float32) * ATTN_SCALE
    qpos = jnp.arange(n)[:, None] * CHUNK + jnp.arange(CHUNK)[None, :]
    kpos = (jnp.arange(n)[:, None] - 1) * CHUNK + jnp.arange(3 * CHUNK)[None, :]
    valid = ((jnp.abs(qpos[:, :, None] - kpos[:, None, :]) <= WINDOW)
             & (kpos >= 0)[:, None, :] & (kpos < l)[:, None, :])
    s_loc = jnp.where(valid, s_loc, NEG_INF)
    s_ctx = jnp.einsum('bgrnid,bgjd->bgrnij', qb, k_ctx, preferred_element_type=jnp.float32) * ATTN_SCALE
    s_sink = jnp.broadcast_to(sink.astype(jnp.float32).reshape(hkv, rep)[None, :, :, None, None, None],
                              s_loc.shape[:-1] + (1,))
    p = jax.nn.softmax(jnp.concatenate([s_loc, s_ctx, s_sink], axis=-1), axis=-1).astype(v.dtype)
    nw = 3 * CHUNK
    lc = k_ctx.shape[2]
    o = (jnp.einsum('bgrnij,bgnjd->bgrnid', p[..., :nw], vwin)
         + jnp.einsum('bgrnij,bgjd->bgrnid', p[..., nw:nw + lc], v_ctx))
    return o.reshape(b, h, l, dh)


def context_sink_attention(q, k, v, sink):
    b, h, lc, dh = q.shape
    hkv = k.shape[1]
    rep = h // hkv
    qg = q.reshape(b, hkv, rep, lc, dh)
    s = jnp.einsum('bgrid,bgjd->bgrij', qg, k, preferred_element_type=jnp.float32) * ATTN_SCALE
    s_sink = jnp.broadcast_to(sink.astype(jnp.float32).reshape(hkv, rep)[None, :, :, None, None], s.shape[:-1] + (1,))
    p = jax.nn.softmax(jnp.concatenate([s, s_sink], axis=-1), axis=-1)[..., :lc].astype(v.dtype)
    return jnp.einsum('bgrij,bgjd->bgrid', p, v).reshape(b, h, lc, dh)


def setup_inputs(seed: int = 0) -> dict:
    key = jax.random.key(seed)
    ks = jax.random.split(key, 20)
    f32 = jnp.float32
    nrm = lambda k, shape, s: jax.random.normal(k, shape, f32) * s
    base_rate = jnp.asarray(np.log(-np.log(1.0 - 2.0 ** (-5.0 - np.arange(N_HEADS)))), f32)
    return {
        'x': nrm(ks[0], (BATCH, SEQ, D_MODEL), 1.0),
        'c': nrm(ks[1], (BATCH, D_MODEL), 1.0),
        'ctx': nrm(ks[2], (BATCH, CTX_LEN, D_MODEL), 1.0),
        'c_ctx': nrm(ks[3], (D_MODEL,), 1.0),
        'norm_gain': 1.0 + nrm(ks[4], (DEPTH, D_MODEL), 0.02),
        'w_mod': nrm(ks[5], (DEPTH, D_MODEL, 3 * D_MODEL), 0.5 * D_MODEL ** -0.5),
        'b_mod': nrm(ks[6], (DEPTH, 3 * D_MODEL), 0.01),
        'w_in': nrm(ks[7], (DEPTH, D_MODEL, IN_WIDTH), D_MODEL ** -0.5),
        'w_out': nrm(ks[8], (DEPTH, MIX_WIDTH, D_MODEL), MIX_WIDTH ** -0.5),
        'mlp_mix': nrm(ks[9], (DEPTH, N_HEADS, CHUNK, CHUNK), CHUNK ** -0.5),
        'mlp_bias': 1.0 + nrm(ks[10], (DEPTH, N_HEADS, CHUNK), 0.1),
        'ret_decay_fwd': base_rate + nrm(ks[11], (DEPTH, N_HEADS), 0.05),
        'ret_decay_bwd': base_rate + nrm(ks[12], (DEPTH, N_HEADS), 0.05),
        'ret_norm': 1.0 + nrm(ks[13], (DEPTH, N_HEADS, HEAD_DIM), 0.02),
        'attn_q_norm': 1.0 + nrm(ks[14], (DEPTH, HEAD_DIM), 0.02),
        'attn_k_norm': 1.0 + nrm(ks[15], (DEPTH, HEAD_DIM), 0.02),
        'swa_q_norm': 1.0 + nrm(ks[16], (DEPTH, HEAD_DIM), 0.02),
        'swa_k_norm': 1.0 + nrm(ks[17], (DEPTH, HEAD_DIM), 0.02),
        'swa_sink': nrm(ks[18], (DEPTH, N_HEADS), 0.5),
    }


def reference(x, c, ctx, c_ctx, norm_gain, w_mod, b_mod, w_in, w_out, mlp_mix, mlp_bias,
              ret_decay_fwd, ret_decay_bwd, ret_norm, attn_q_norm, attn_k_norm,
              swa_q_norm, swa_k_norm, swa_sink):
    b, l, _ = x.shape
    rows = l // GRID_W
    cos, sin = axial_rope_tables(rows, x.dtype)
    cond_lat = jax.nn.silu(c)[:, None, :]
    cond_ctx = jax.nn.silu(c_ctx)[None, None, :]
    k_scale = HEAD_DIM ** -0.5
    xc = ctx
    for i in range(DEPTH):
        with_ctx = i < DEPTH - 1
        h, gate_lat = modulate(x, norm_gain[i], cond_lat @ w_mod[i] + b_mod[i])
        hc, gate_ctx = modulate(xc, norm_gain[i], cond_ctx @ w_mod[i] + b_mod[i])
        (a_uv, a_g, r_q, r_k, r_v, r_g, g_q, g_k, g_v, g_g,
         s_q, s_k, s_v, s_g) = split_columns(h @ w_in[i])
        (a_uv_c, a_g_c, r_q_c, r_k_c, r_v_c, r_g_c, g_q_c, g_k_c, g_v_c, g_g_c,
         s_q_c, s_k_c, s_v_c, s_g_c) = split_columns(hc @ w_in[i])

        mlp_l = chunk_mlp_branch(a_uv, a_g, mlp_mix[i], mlp_bias[i])

        lg_f = -jnp.exp(ret_decay_fwd[i].astype(jnp.float32))
        lg_b = -jnp.exp(ret_decay_bwd[i].astype(jnp.float32))
        zero = jnp.zeros((b, N_HEADS, HEAD_DIM, HEAD_DIM), x.dtype)
        ry_c, st_f, st_b = retention_bidir(to_heads(r_q_c), to_heads(r_k_c) * k_scale, to_heads(r_v_c),
                                           lg_f, lg_b, zero, zero)
        ry_l, _, _ = retention_bidir(to_heads(r_q), to_heads(r_k) * k_scale, to_heads(r_v),
                                     lg_f, lg_b, st_f, st_b)
        ret_gain = ret_norm[i][:, None, :]
        ret_l = from_heads(rms_norm(ry_l, ret_gain)) * jax.nn.silu(r_g)

        gk_c = rms_norm(to_heads(g_k_c), attn_k_norm[i])
        gv_c = to_heads(g_v_c)
        gq = apply_axial_rope(rms_norm(to_heads(g_q), attn_q_norm[i]), cos, sin)
        gk = apply_axial_rope(rms_norm(to_heads(g_k), attn_k_norm[i]), cos, sin)
        ga_l = from_heads(blocked_attention(gq, jnp.concatenate([gk_c, gk], axis=2),
                                            jnp.concatenate([gv_c, to_heads(g_v)], axis=2))) * jax.nn.silu(g_g)

        sk_c = rms_norm(to_heads(s_k_c), swa_k_norm[i])
        sv_c = to_heads(s_v_c)
        sq = apply_axial_rope(rms_norm(to_heads(s_q), swa_q_norm[i]), cos, sin)
        sk = apply_axial_rope(rms_norm(to_heads(s_k), swa_k_norm[i]), cos, sin)
        swa_l = from_heads(window_attention(sq, sk, to_heads(s_v), sk_c, sv_c, swa_sink[i])) * jax.nn.silu(s_g)

        if with_ctx:
            mlp_c = chunk_mlp_branch(a_uv_c, a_g_c, mlp_mix[i], mlp_bias[i])
            ret_c = from_heads(rms_norm(ry_c, ret_gain)) * jax.nn.silu(r_g_c)
            ga_c = from_heads(blocked_attention(rms_norm(to_heads(g_q_c), attn_q_norm[i]), gk_c, gv_c)) * jax.nn.silu(g_g_c)
            swa_c = from_heads(context_sink_attention(rms_norm(to_heads(s_q_c), swa_q_norm[i]), sk_c, sv_c,
                                                      swa_sink[i])) * jax.nn.silu(s_g_c)
            xc = xc + gate_ctx * (jnp.concatenate([mlp_c, ret_c, ga_c, swa_c], axis=-1) @ w_out[i])

        x = x + gate_lat * (jnp.concatenate([mlp_l, ret_l, ga_l, swa_l], axis=-1) @ w_out[i])
    return x
```

```python
import math
from contextlib import ExitStack
import numpy as np
import ml_dtypes
import concourse.bass as bass
import concourse.mybir as mybir
from concourse.bass_utils import run_bass_kernel_spmd

F32 = mybir.dt.float32
BF16 = mybir.dt.bfloat16
AF = mybir.ActivationFunctionType
ALU = mybir.AluOpType
AX = mybir.AxisListType

D = 1024
DEPTH = 4
CTXC = 2
R = 4
EPS = 1e-6
SCALE = 0.125
BIGE = 1.0e4


class Buf:
    def __init__(self, name):
        self.name = name
        self.last_w = []
        self.readers = []
        self.sem = None
        self.cnt = 0


class Op:
    __slots__ = ("eng", "fn", "deps", "kind", "sig", "val", "sem")

    def __init__(self, eng, fn, kind):
        self.eng, self.fn, self.kind = eng, fn, kind
        self.deps = []
        self.sig = False
        self.val = 0
        self.sem = None


class Prog:
    def __init__(self, nc, es):
        self.nc, self.es = nc, es
        self.ops = {k: [] for k in ("pe", "act", "dve", "pool", "sp")}
        self.esem = {k: es.enter_context(nc.semaphore("e_" + k)) for k in ("pe", "act", "dve", "pool")}
        self.nsem = 4

    def op(self, eng, fn, reads=(), writes=(), kind="c", par=False):
        o = Op(eng, fn, kind)
        deps = []
        for b in reads:
            deps += b.last_w
        for b in writes:
            deps += b.readers
            if not par:
                deps += b.last_w
        seen = set()
        for d in deps:
            if id(d) in seen or d is o:
                continue
            seen.add(id(d))
            if d.kind == "c" and d.eng == "pe" and eng == "pe" and kind == "c":
                continue
            d.sig = True
            o.deps.append(d)
        for b in reads:
            b.readers.append(o)
        for b in writes:
            if par:
                b.last_w = b.last_w + [o]
            else:
                b.last_w = [o]
            b.readers = []
        if kind in ("d", "cc"):
            b = writes[0]
            if b.sem is None:
                b.sem = self.es.enter_context(self.nc.semaphore("s_" + b.name))
                self.nsem += 1
            b.cnt += 16 if kind == "d" else 1
            o.sem, o.val = b.sem, b.cnt
        self.ops[eng].append(o)
        return o

    def finalize(self):
        for eng in ("pe", "act", "dve", "pool"):
            c = 0
            for o in self.ops[eng]:
                if o.kind == "c" and o.sig:
                    c += 1
                    o.val = c
                    o.sem = self.esem[eng]

    def emit(self, eng, e):
        waited = {}
        for o in self.ops[eng]:
            for d in o.deps:
                k = id(d.sem)
                if waited.get(k, 0) >= d.val:
                    continue
                e.wait_ge(d.sem, d.val)
                waited[k] = d.val
            ins = o.fn(e)
            if o.kind == "d":
                ins.then_inc(o.sem, 16)
            elif o.kind == "cc":
                ins.then_inc(o.sem, 1)
            elif o.sig:
                ins.then_inc(o.sem, 1)

    def final_wait(self, e, bufs):
        for b in bufs:
            e.wait_ge(b.sem, b.cnt)


class Rot:
    def __init__(self, tiles, name, bufs=None):
        self.tiles = tiles
        self.bufs = bufs if bufs is not None else [Buf("%s%d" % (name, i)) for i in range(len(tiles))]
        self.i = -1

    def next(self):
        self.i = (self.i + 1) % len(self.tiles)
        return self.tiles[self.i], self.bufs[self.i]


def build(NCH, depth=DEPTH, QG=2, stop=None, SIDE_RATE=2):
    NT = NCH * 128
    NTT = NT + CTXC * 128
    KB = CTXC + R * NCH
    KTOT = KB * 128
    NG = NCH // QG
    GQ = QG * 128
    PCS = min(NCH, 4)
    GP = min(NCH, 8)
    NGP = NCH // GP
    nc = bass.Bass("TRN2", target_bir_lowering=False)
    es = ExitStack()
    P = Prog(nc, es)

    def din(name, shape, dt=F32):
        return nc.dram_tensor(name, shape, dt, kind="ExternalInput").ap()

    def dint(name, shape, dt):
        return nc.dram_tensor(name, shape, dt, kind="Internal").ap()

    x_in = din("x_in", [NT, D])
    ctx_in = din("ctx_in", [CTXC * 128, D])
    cT_in = din("cT", [128, 16])
    gainT_in = din("gainT", [128, depth * 8])
    w_mod = din("w_mod", [depth, D, 3 * D])
    bmodT_in = din("bmodT", [128, depth * 16])
    bgate_in = din("bgate", [1, depth * D])
    w_in = din("w_in", [depth, D, 3328])
    w_out = din("w_out", [depth, D, D])
    mixT_in = din("mixT", [depth, 128, 512])
    mbias_in = din("mbias", [128, depth * 4])
    rdec_in = din("rdec", [1, depth * 8])
    rnorm_in = din("rnorm", [1, depth * 256])
    qkn_in = din("qkn", [1, depth * 256])
    sink_in = din("sink", [1, depth * 4])
    cos_in = din("cos", [NCH, 128, 64])
    sin_in = din("sin", [NCH, 128, 64])
    etab_in = din("etab", [128, 5])
    hmask_in = din("hmask", [128, 8 * 128], BF16)
    cmask_in = din("cmask", [128, 2 * 128], BF16)
    identb_in = din("identb", [128, 128], BF16)
    identf_in = din("identf", [128, 128])
    rpn_in = din("rpn", [128, 256])
    pos_in = din("pos", [128, 4])
    y_out = nc.dram_tensor("y", [NT, D], F32, kind="ExternalOutput").ap()

    xsA = dint("xsA", [NT, D], F32)
    xsB = dint("xsB", [NT, D], F32)
    yin_d = dint("yin_d", [NTT, 256], F32)
    qfb_d = dint("qfb_d", [NCH + CTXC, 128, 512], BF16)
    sk_d = dint("sk_d", [NCH, 128, 128], BF16)
    sv_d = dint("sv_d", [NCH, 128, 130], BF16)
    gk_x = [dint("gk_x%d" % i, [128, GP * 128], BF16) for i in range(NGP)]
    gk_all = [dint("gk_all%d" % i, [R * 128, GP * 128], BF16) for i in range(NGP)]
    gv_x = [dint("gv_x%d" % i, [128, GP * 130], BF16) for i in range(NGP)]
    gv_all = [dint("gv_all%d" % i, [R * 128, GP * 130], BF16) for i in range(NGP)]
    bnd_x = dint("bnd_x", [128, 516], BF16)
    bnd_all = dint("bnd_all", [R * 128, 516], BF16)
    agg_x = dint("agg_x", [128, 256], F32)
    agg_all = dint("agg_all", [R * 128, 256], F32)
    B_xin, B_xsA, B_xsB, B_y = Buf("xin"), Buf("xsA"), Buf("xsB"), Buf("y")
    B_yin, B_qfb, B_skd, B_svd = Buf("yin"), Buf("qfb"), Buf("skd"), Buf("svd")
    B_gkx = [Buf("gkx%d" % i) for i in range(NGP)]
    B_gkall = [Buf("gkall%d" % i) for i in range(NGP)]
    B_gvx = [Buf("gvx%d" % i) for i in range(NGP)]
    B_gvall = [Buf("gvall%d" % i) for i in range(NGP)]
    B_bndx, B_bndall, B_aggx, B_aggall = Buf("bndx"), Buf("bndall"), Buf("aggx"), Buf("aggall")
    B_const = Buf("constin")

    def sb(name, shape, dt=F32):
        return es.enter_context(nc.sbuf_tensor(name, shape, dt))

    def ps(name, shape, dt=F32):
        return es.enter_context(nc.psum_tensor(name, shape, dt))

    KT = sb("KTc", [128, CTXC * 128], BF16);     B_KT = Buf("KT")
    V = sb("Vc", [128, CTXC, 130], BF16);        B_V = Buf("V")
    ksl = Rot([sb("ksl%d" % i, [128, PCS * 128], BF16) for i in range(2)], "ksl")
    vsl = Rot([sb("vsl%d" % i, [128, PCS, 130], BF16) for i in range(2)], "vsl")
    skTc = sb("skTc", [128, CTXC * 128], BF16);  B_skTc = Buf("skTc")
    sVc = sb("sVc", [128, CTXC, 130], BF16);     B_sVc = Buf("sVc")
    ST = sb("ST", [128, NCH + 2, 256], BF16)
    B_ST = [Buf("ST%d" % i) for i in range(NCH + 2)]
    STc = sb("STc", [128, CTXC + 2, 256], BF16)
    B_STc = [Buf("STc%d" % i) for i in range(CTXC + 2)]
    WA = sb("WA", [128, 8, 1280], BF16);         B_WA = Buf("WA")
    W2 = sb("W2s", [128, 8, 2048], BF16);         B_W2 = Buf("W2")
    mixT = sb("mixTs", [128, 512], BF16);        B_mixT = Buf("mixT")
    xc = sb("xc", [128, CTXC, D], F32)
    B_xc = [Buf("xc%d" % i) for i in range(CTXC)]
    identb = sb("identb_s", [128, 128], BF16)
    identf = sb("identf_s", [128, 128], F32)
    rpn = sb("rpn_s", [128, 256], F32)
    pos = sb("pos_s", [128, 4], F32)
    etab = sb("etab_s", [128, 5], F32)
    hmask = sb("hmask_s", [128, 8 * 128], BF16)
    cmask = sb("cmask_s", [128, 256], BF16)
    cT = sb("cT_s", [128, 16], F32)
    gainT = sb("gainT_s", [128, depth * 8], F32)
    bmodT = sb("bmodT_s", [128, depth * 16], F32)
    bgate = sb("bgate_s", [1, D], F32);  B_bg = Buf("bg")
    grow = sb("grow", [1, 512], F32);    B_grow = Buf("grow")
    mbias = sb("mbias_s", [128, depth * 4], F32)
    rdec = sb("rdec_s", [128, depth * 8], F32)
    rdsel = sb("rdsel_s", [128, depth * 4], F32)
    rnorm = sb("rnorm_s", [128, 256], F32);  B_rq = Buf("rqn")
    qkn = sb("qkn_s", [128, 256], F32)
    sink = sb("sink_s", [128, depth * 4], F32)
    ones1 = sb("ones1", [1, 128], F32)
    B_cb = Buf("constsb")
    Gm = sb("Gm", [128, 2, 8], F32)
    Sm = sb("Sm", [128, 2, 8], F32)
    gateB = sb("gateB", [128, 2, D], F32)
    B_mod = Buf("mod")
    lg = sb("lg", [128, 8], F32)
    lgsel = sb("lgsel", [128, 4], F32)
    kd = sb("kd", [128, 8], F32)
    qd = sb("qd", [128, 8], F32)
    DecT = sb("DecT", [128, 512], F32)
    Dt = sb("Dt", [128, 256], F32)
    Pw = sb("Pw", [128, 256], F32)
    Aagg = sb("Aagg", [128, 256], F32)
    Actx = sb("Actx", [128, 256], F32)
    coef = sb("coef", [128, 5, 4], F32)
    esink = sb("esink", [128, 4], F32)
    B_lay = Buf("laysmall")
    B_Aagg, B_Actx, B_Pw = Buf("Aagg"), Buf("Actx"), Buf("Pw")
    aggs = sb("aggs", [128, R, 256], F32);    B_aggs = Buf("aggs")
    sintmp = sb("sintmp", [128, 256], F32);   B_sintmp = Buf("sintmp")
    xt = Rot([sb("xt%d" % i, [128, D], F32) for i in range(2)], "xt")
    xr = xt
    st4 = Rot([sb("st4_%d" % i, [128, 16], F32) for i in range(4)], "st4")
    xn = Rot([sb("xn%d" % i, [128, D], BF16) for i in range(2)], "xn")
    hT = Rot([sb("hT%d" % i, [128, 8, 128], BF16) for i in range(2)], "hT")
    cs = Rot([sb("cs%d" % i, [128, 2, 64], F32) for i in range(2)], "cs")
    rqkv = Rot([sb("rqkv%d" % i, [128, 768], BF16) for i in range(1)], "rqkv")
    fb = Rot([sb("fb%d" % i, [128, 2, 4, 2, 64], BF16) for i in range(1)], "fb")
    fbT = Rot([sb("fbT%d" % i, [128, 2, 4, 128], BF16) for i in range(1)], "fbT")
    rT = Rot([sb("rT%d" % i, [128, 3, 2, 128], BF16) for i in range(1)], "rT")
    pint = Rot([sb("pint%d" % i, [128, 512], BF16) for i in range(1)], "pint")
    yint = Rot([sb("yint%d" % i, [128, 256], F32) for i in range(1)], "yint")
    kraw = Rot([sb("kraw%d" % i, [128, 512], F32) for i in range(2)], "kraw")
    t1 = Rot([sb("t1_%d" % i, [128, 512], F32) for i in range(1)], "t1")
    t2 = Rot([sb("t2_%d" % i, [128, 512], F32) for i in range(1)], "t2")
    knb = Rot([sb("knb%d" % i, [128, 512], BF16) for i in range(2)], "knb")
    kTs = Rot([sb("kTs%d" % i, [128, 256], BF16) for i in range(2)], "kTs")
    vsb = Rot([sb("vsb%d" % i, [128, 260], BF16) for i in range(2)], "vsb")
    NSL = 2 * QG
    gates = Rot([sb("gates%d" % i, [128, D], BF16) for i in range(NSL)], "gates")
    mixo = Rot([sb("mixo%d" % i, [128, D], BF16) for i in range(NSL)], "mixo")
    uvg = Rot([sb("uvg%d" % i, [128, 512], BF16) for i in range(1)], "uvg")
    gqT = Rot([sb("gqT%d" % i, [128, 2, 2, GQ], BF16) for i in range(2)], "gqT")
    sqT = Rot([sb("sqT%d" % i, [128, 2, 2, 128], BF16) for i in range(2)], "sqT")
    pT = Rot([sb("pT%d" % i, [128, 512], BF16) for i in range(3)], "pT")
    oT = Rot([sb("oT%d" % i, [65, 512], F32) for i in range(2)], "oT")
    rden = Rot([sb("rden%d" % i, [128, 4], F32) for i in range(4)], "rden")
    mixTt = Rot([sb("mixTt%d" % i, [128, 8, 128], BF16) for i in range(1)], "mixTt")
    otmp = Rot([sb("otmp%d" % i, [128, D], F32) for i in range(1)], "otmp")
    wmst = Rot([otmp.tiles[0][:].rearrange("p (k c) -> p k c", k=8)], "wmst")
    wmst.bufs = otmp.bufs
    yinl = Rot([sb("yinl%d" % i, [128, 256], F32) for i in range(1)], "yinl")
    qfbl = Rot([sb("qfbl%d" % i, [128, 512], BF16) for i in range(1)], "qfbl")
    ysum = Rot([sb("ysum%d" % i, [128, 256], F32) for i in range(2)], "ysum")
    wk = Rot([sb("wk%d" % i, [128, 6, 128], BF16) for i in range(1)], "wk")
    wv = Rot([sb("wv%d" % i, [128, 6, 130], BF16) for i in range(1)], "wv")
    wp = Rot([sb("wp%d" % i, [128, 256], BF16) for i in range(3)], "wp")
    woT = Rot([sb("woT%d" % i, [65, 256], F32) for i in range(1)], "woT")
    zps = Rot([ps("zps%d" % i, [128, 512]) for i in range(2)], "zps")
    tps = Rot([ps("tps%d" % i, [128, 1024], BF16) for i in range(1)], "tps")
    sps = Rot([ps("sps%d" % i, [128, 512]) for i in range(3)], "sps")
    ops_ = Rot([ps("ops%d" % i, [128, 512]) for i in range(2)], "ops")
    zps1 = Rot([zps.tiles[0]], "zps1", bufs=[zps.bufs[0]])
    swacc = Rot([zps.tiles[1]], "swacc", bufs=[zps.bufs[1]])
    zpsP1 = Rot(zps.tiles + ops_.tiles, "zpsP1", bufs=zps.bufs + ops_.bufs)
    PS = {"z": zpsP1, "acc": ops_}

    def dma(q, out, in_, reads, writes, par=False, **kw):
        return P.op(q, lambda e: e.dma_start(out=out, in_=in_, **kw), reads, writes, kind="d", par=par)

    def mm(out, lhsT, rhs, start, stop, reads, writes):
        return P.op("pe", lambda e: e.matmul(out, lhsT=lhsT, rhs=rhs, start=start, stop=stop), reads, writes)

    def tr(out, in_, ident, reads, writes):
        return P.op("pe", lambda e: e.transpose(out, in_, ident), reads, writes)

    def act(out, in_, func, reads, writes, **kw):
        return P.op("act", lambda e: e.activation(out=out, in_=in_, func=func, **kw), reads, writes)

    def tt(eng, out, in0, in1, op, reads, writes):
        return P.op(eng, lambda e: e.tensor_tensor(out=out, in0=in0, in1=in1, op=op), reads, writes)

    def ts(eng, out, in0, s1, s2, op0, op1, reads, writes):
        if op1 is None:
            return P.op(eng, lambda e: e.tensor_scalar(out=out, in0=in0, scalar1=s1, scalar2=None, op0=op0), reads, writes)
        return P.op(eng, lambda e: e.tensor_scalar(out=out, in0=in0, scalar1=s1, scalar2=s2, op0=op0, op1=op1), reads, writes)

    def stt(out, in0, scalar, in1, op0, op1, reads, writes):
        return P.op("dve", lambda e: e.scalar_tensor_tensor(out=out, in0=in0, scalar=scalar, in1=in1, op0=op0, op1=op1), reads, writes)

    def cp(eng, out, in_, reads, writes):
        if eng == "act":
            return P.op("act", lambda e: e.copy(out=out, in_=in_), reads, writes)
        return P.op(eng, lambda e: e.tensor_copy(out=out, in_=in_), reads, writes)

    def rsqrt_mean_g(out, ssum, n, reads, writes):
        ts("dve", out, ssum, 1.0 / n, EPS, ALU.mult, ALU.add, reads, writes)
        yield
        act(out, out, AF.Ln, writes, writes)
        yield
        act(out, out, AF.Exp, writes, writes, scale=-0.5)
        yield

    def rsqrt_mean(out, ssum, n, reads, writes):
        ts("dve", out, ssum, 1.0 / n, EPS, ALU.mult, ALU.add, reads, writes)
        act(out, out, AF.Ln, writes, writes)
        act(out, out, AF.Exp, writes, writes, scale=-0.5)

    def bc(ap, n):
        return ap.partition_broadcast(n)

    for dst, src in ((identb, identb_in), (identf, identf_in), (rpn, rpn_in), (pos, pos_in), (etab, etab_in),
                     (hmask, hmask_in), (cmask, cmask_in), (cT, cT_in), (gainT, gainT_in), (bmodT, bmodT_in),
                     (mbias, mbias_in)):
        dma("sp", dst[:], src, [B_const], [B_cb], par=True)
    dma("sp", rdec[:], rdec_in.partition_broadcast(128), [B_const], [B_cb], par=True)
    for l in range(depth):
        dma("sp", rdsel[0:64, l * 4:(l + 1) * 4], rdec_in[:, l * 8:l * 8 + 4].partition_broadcast(64), [B_const], [B_cb], par=True)
        dma("sp", rdsel[64:128, l * 4:(l + 1) * 4], rdec_in[:, l * 8 + 4:l * 8 + 8].partition_broadcast(64), [B_const], [B_cb], par=True)
    dma("sp", sink[:], sink_in.partition_broadcast(128), [B_const], [B_cb], par=True)
    for c in range(CTXC):
        dma("sp", xc[:, c, :], ctx_in[c * 128:(c + 1) * 128, :], [B_const], [B_xc[c]])
    B_c2 = Buf("const2")
    P.op("dve", lambda e: e.memset(ones1[:], 1.0), [], [B_c2])
    for rot_ in (gqT, sqT, rT):
        for i in range(len(rot_.tiles)):
            P.op("pool", lambda e, t_=rot_.tiles[i]: e.memset(t_[:], 0.0), [], [rot_.bufs[i]])
    P.op("dve", lambda e: e.memset(V[:], 1.0), [], [B_V])
    P.op("dve", lambda e: e.memset(sVc[:], 1.0), [], [B_sVc])
    for i in range(2):
        P.op("pool", lambda e, i=i: e.memset(vsb.tiles[i][:], 1.0), [], [vsb.bufs[i]])
    act(cT[:], cT[:], AF.Silu, [B_cb], [B_cb])

    def load_w1(l):
        srcs = [(768, 1536, 0), (2048, 2176, 768), (2816, 2944, 896), (2176, 2304, 1024), (2944, 3072, 1152)]
        for k in range(8):
            for (a, b_, d0) in srcs:
                dma("pool", WA[:, k, d0:d0 + (b_ - a)], w_in[l, k * 128:(k + 1) * 128, a:b_], [B_const], [B_WA], par=True)

    def load_w2(l):
        srcs = [(0, 768, 0), (1536, 1792, 768), (2304, 2560, 1024), (3072, 3328, 1280), (1792, 2048, 1536), (2560, 2816, 1792)]
        for k in range(8):
            for (a, b_, d0) in srcs:
                dma("pool", W2[:, k, d0:d0 + (b_ - a)], w_in[l, k * 128:(k + 1) * 128, a:b_], [B_const], [B_W2], par=True)
        dma("pool", mixT[:], mixT_in[l], [B_const], [B_mixT])

    def load_wo(l):
        for k in range(8):
            dma("pool", WA[:, k, 0:1024], w_out[l, k * 128:(k + 1) * 128, :], [B_const], [B_WA], par=True)

    def layer_consts(l):
        rd = [B_cb, B_c2]
        w = [B_lay]
        dma("sp", rnorm[:], rnorm_in[:, l * 256:(l + 1) * 256].partition_broadcast(128), [B_const], [B_rq])
        dma("sp", qkn[:], qkn_in[:, l * 256:(l + 1) * 256].partition_broadcast(128), [B_const], [B_rq], par=True)
        act(lg[:], rdec[:, l * 8:(l + 1) * 8], AF.Exp, rd, w)
        ts("dve", lg[:], lg[:], -1.0, None, ALU.mult, None, w, w)
        act(lgsel[:], rdsel[:, l * 4:(l + 1) * 4], AF.Exp, rd, w)
        ts("dve", lgsel[:], lgsel[:], -1.0, None, ALU.mult, None, w, w)
        for dr in range(2):
            ts("dve", kd[:, dr * 4:(dr + 1) * 4], lg[:, dr * 4:(dr + 1) * 4], pos[:, dr:dr + 1], None, ALU.mult, None, rd + w, w)
            ts("dve", qd[:, dr * 4:(dr + 1) * 4], lg[:, dr * 4:(dr + 1) * 4], pos[:, 2 + dr:3 + dr], None, ALU.mult, None, rd + w, w)
        act(kd[:], kd[:], AF.Exp, w, w)
        ts("dve", kd[:], kd[:], SCALE, None, ALU.mult, None, w, w)
        act(qd[:], qd[:], AF.Exp, w, w)
        for h in range(4):
            ts("dve", DecT[:, h * 128:(h + 1) * 128], rpn[:, 0:128], lg[:, h:h + 1], None, ALU.mult, None, rd + w, w)
            stt(DecT[:, h * 128:(h + 1) * 128], rpn[:, 128:256], lg[:, 4 + h:5 + h], DecT[:, h * 128:(h + 1) * 128],
                ALU.mult, ALU.add, rd + w, w)
        act(DecT[:], DecT[:], AF.Exp, w, w)
        ts("dve", DecT[:], DecT[:], SCALE, None, ALU.mult, None, w, w)
        ts("dve", esink[:], lgsel[:], 128.0, None, ALU.mult, None, w, w)
        act(esink[:], esink[:], AF.Exp, w, w)
        cp("dve", Dt[:].rearrange("p (h e) -> p h e", h=4), esink[:].unsqueeze(2).to_broadcast([128, 4, 64]), w, w)
        for s_ in range(5):
            ts("dve", coef[:, s_, :], lgsel[:], etab[:, s_:s_ + 1], 128.0 * NCH, ALU.mult, ALU.mult, rd + w, w)
        act(coef[:], coef[:], AF.Exp, w, w)
        act(esink[:], sink[:, l * 4:(l + 1) * 4], AF.Exp, rd + w, w)
        P.op("pool", lambda e: e.memset(Aagg[:], 0.0), [], [B_Aagg])
        P.op("pool", lambda e: e.memset(Actx[:], 0.0), [], [B_Actx])
        P.op("pool", lambda e: e.memset(Pw[:], 1.0), [], [B_Pw])
        P.op("pool", lambda e: e.memset(STc[0:64, 1, :], 0.0), [], [B_STc[1]], par=True)
        P.op("pool", lambda e: e.memset(STc[64:128, 2, :], 0.0), [], [B_STc[2]], par=True)

    def mod_compute(l):
        dma("sp", bgate[:], bgate_in[:, l * D:(l + 1) * D], [B_const], [B_bg])
        mp, mb = zps.next()
        for half in range(-1, 2):
            if half < 0:
                for c in range(16):
                    wt, wb = wmst.next()
                    dma("sp", wt[:], w_mod[l, :, c * 128:(c + 1) * 128].rearrange("(k p) c -> p k c", p=128), [B_const], [wb])
                    for k in range(8):
                        mm(mp[:, c * 2:c * 2 + 2], wt[:, k, :], cT[:, 2 * k:2 * k + 2], k == 0, k == 7, [wb, B_cb], [mb])
                continue
            g0, gb0 = sps.next()
            g1, gb1 = sps.next()
            gps = ((g0, gb0), (g1, gb1))
            for q in range(4):
                col = 2048 + half * 512 + q * 128
                wt, wb = wmst.next()
                dma("sp", wt[:], w_mod[l, :, col:col + 128].rearrange("(k p) c -> p k c", p=128), [B_const], [wb])
                for j in range(2):
                    for k in range(8):
                        mm(gps[j][0][0:1, q * 128:(q + 1) * 128], cT[:, 2 * k + j:2 * k + j + 1], wt[:, k, :], k == 0, k == 7, [wb, B_cb], [gps[j][1]])
            for j in range(2):
                tt("dve", grow[:], gps[j][0][0:1, 0:512], bgate[0:1, half * 512:(half + 1) * 512], ALU.add, [gps[j][1], B_bg], [B_grow])
                zp, zb = ops_.next()
                mm(zp[:, 0:512], ones1[0:1, :], grow[0:1, :], True, True, [B_c2, B_grow], [zb])
                cp("dve", gateB[:, j, half * 512:(half + 1) * 512], zp[:, 0:512], [zb], [B_mod], )
        mpv = mp[:, 0:32].rearrange("p (c j) -> p j c", j=2)
        for j in range(2):
            tt("dve", Sm[:, j, :], mpv[:, j, 0:8], bmodT[:, l * 16:l * 16 + 8], ALU.add, [mb, B_cb], [B_mod])
            tt("dve", Gm[:, j, :], mpv[:, j, 8:16], bmodT[:, l * 16 + 8:l * 16 + 16], ALU.add, [mb, B_cb], [B_mod])
            stt(Gm[:, j, :], Gm[:, j, :], 1.0, gainT[:, l * 8:(l + 1) * 8], ALU.add, ALU.mult, [B_mod, B_cb], [B_mod])

    def norm_tile(l, xap, xbuf, j):
        s4, s4b = st4.next()
        xnt, xnb = xn.next()
        P.op("dve", lambda e: e.scalar_tensor_tensor(out=xnt[:], in0=xap, scalar=1.0, in1=xap, op0=ALU.mult, op1=ALU.mult,
                                                      accum_out=s4[:, 0:1]), [xbuf], [xnb, s4b])
        rsqrt_mean(s4[:, 0:1], s4[:, 0:1], D, [s4b], [s4b])
        ts("dve", xnt[:], xap, s4[:, 0:1], None, ALU.mult, None, [xbuf, s4b], [xnb])
        tp, tb = tps.next()
        for k in range(8):
            tr(tp[:, k * 128:(k + 1) * 128], xnt[:, k * 128:(k + 1) * 128], identb[:], [xnb, B_cb], [tb])
        ht, hb = hT.next()
        for k in range(8):
            ts("dve", ht[:, k, :], tp[:, k * 128:(k + 1) * 128], Gm[:, j, k:k + 1], Sm[:, j, k:k + 1], ALU.mult, ALU.add,
               [tb, B_mod], [hb])
        return ht, hb

    def norm_tile_g(l, xap, xbuf, j):
        s4, s4b = st4.next()
        xnt, xnb = xn.next()
        P.op("dve", lambda e: e.scalar_tensor_tensor(out=xnt[:], in0=xap, scalar=1.0, in1=xap, op0=ALU.mult, op1=ALU.mult,
                                                      accum_out=s4[:, 0:1]), [xbuf], [xnb, s4b])
        yield
        yield from rsqrt_mean_g(s4[:, 0:1], s4[:, 0:1], D, [s4b], [s4b])
        ts("dve", xnt[:], xap, s4[:, 0:1], None, ALU.mult, None, [xbuf, s4b], [xnb])
        yield
        tp, tb = tps.next()
        for k in range(8):
            tr(tp[:, k * 128:(k + 1) * 128], xnt[:, k * 128:(k + 1) * 128], identb[:], [xnb, B_cb], [tb])
        yield
        yield
        ht, hb = hT.next()
        for k in range(8):
            ts("dve", ht[:, k, :], tp[:, k * 128:(k + 1) * 128], Gm[:, j, k:k + 1], Sm[:, j, k:k + 1], ALU.mult, ALU.add,
               [tb, B_mod], [hb])
        yield
        yield
        return ht, hb

    def inproj(ht, hb, W, WB, c0, ncols):
        zp, zb = PS["z"].next()
        for k in range(8):
            mm(zp[:, 0:ncols], ht[:, k, :], W[:, k, c0:c0 + ncols], k == 0, k == 7, [hb, WB], [zb])
        return zp, zb

    def qk_norm_rope(src, srcb, nh, gain_ap, csb, rope):
        n = nh * 64
        Bt, Bb = t1.next()
        Ct, Cb = t2.next()
        s4, s4b = st4.next()
        v3 = lambda ap: ap[:, 0:n].rearrange("p (h d) -> p h d", d=64)
        tt("dve", Bt[:, 0:n], src[:, 0:n], src[:, 0:n], ALU.mult, [srcb], [Bb])
        yield
        P.op("dve", lambda e: e.tensor_reduce(out=s4[:, 0:nh], in_=v3(Bt), axis=AX.X, op=ALU.add), [Bb], [s4b])
        yield
        yield from rsqrt_mean_g(s4[:, 0:nh], s4[:, 0:nh], 64, [s4b], [s4b])
        tt("dve", v3(Ct), v3(src), s4[:, 0:nh].unsqueeze(2).to_broadcast([128, nh, 64]), ALU.mult, [srcb, s4b], [Cb])
        tt("dve", Ct[:, 0:n].rearrange("p (a h d) -> p a h d", a=2, d=64), Ct[:, 0:n].rearrange("p (a h d) -> p a h d", a=2, d=64),
           gain_ap.rearrange("p (a d) -> p a d", a=2).unsqueeze(2).to_broadcast([128, 2, nh // 2, 64]), ALU.mult, [Cb, B_rq], [Cb])
        yield
        ot, ob = knb.next()
        if not rope:
            cp("dve", ot[:, 0:n], Ct[:, 0:n], [Cb], [ob])
            return ot, ob
        cst, csbuf = csb
        v4 = lambda ap: ap[:, 0:n].rearrange("p (h a b c) -> p h a b c", a=2, b=2, c=16)
        cosb = cst[:, 0, :].unsqueeze(1).to_broadcast([128, nh, 64])
        sin4 = cst[:, 1, :].rearrange("p (a b c) -> p a b c", a=2, b=2)
        tt("dve", v3(Bt), v3(Ct), cosb, ALU.mult, [Cb, csbuf], [Bb])
        yield
        for b0 in range(2):
            tt("pool", v4(src)[:, :, :, b0, :], v4(Ct)[:, :, :, 1 - b0, :],
               sin4[:, :, b0, :].unsqueeze(1).to_broadcast([128, nh, 2, 16]), ALU.mult, [Cb, csbuf], [srcb], )
        yield
        yield
        tt("dve", ot[:, 0:n], Bt[:, 0:n], src[:, 0:n], ALU.add, [Bb, srcb], [ob])
        yield
        return ot, ob

    def ret_state(is_ctx):
        return (STc, B_STc, Actx, B_Actx) if is_ctx else (ST, B_ST, Aagg, B_Aagg)

    def p1_tile(l, t, xsrc, B_xsrc):
        is_ctx = t < CTXC
        c = t if is_ctx else t - CTXC
        j = 1 if is_ctx else 0
        if is_ctx:
            xap, xbuf = xc[:, c, :], B_xc[c]
            csb = None
        else:
            xt_, xbuf = xt.next()
            dma("sp", xt_[:], xsrc[c * 128:(c + 1) * 128, :], [B_xsrc], [xbuf])
            xap = xt_[:]
            cst, csbuf = cs.next()
            dma("sp", cst[:, 0, :], cos_in[c], [B_const], [csbuf])
            dma("sp", cst[:, 1, :], sin_in[c], [B_const], [csbuf], par=True)
            csb = (cst, csbuf)
        ht, hb = norm_tile(l, xap, xbuf, j)
        rq, rqb = rqkv.next()
        kr, krb = kraw.next()
        vs, vsbuf = vsb.next()
        zp, zb = inproj(ht, hb, WA, B_WA, 0, 512)
        cp("act", rq[:, 0:512], zp[:, 0:512], [zb], [rqb])
        zp, zb = inproj(ht, hb, WA, B_WA, 512, 512)
        cp("dve", rq[:, 512:768], zp[:, 0:256], [zb], [rqb], )
        cp("dve", kr[:, 0:256], zp[:, 256:512], [zb], [krb])
        zp, zb = inproj(ht, hb, WA, B_WA, 1024, 256)
        cp("dve", vs[:, 1:129], zp[:, 0:128], [zb], [vsbuf])
        cp("dve", vs[:, 131:259], zp[:, 128:256], [zb], [vsbuf])
        fbt, fbb = fb.next()
        for w_, dec in ((0, qd), (1, kd)):
            for dr in range(2):
                tt("dve", fbt[:, w_, :, dr, :], rq[:, w_ * 256:(w_ + 1) * 256].rearrange("p (h d) -> p h d", h=4),
                   dec[:, dr * 4:(dr + 1) * 4].unsqueeze(2).to_broadcast([128, 4, 64]), ALU.mult, [rqb, B_lay], [fbb], )
        tp, tb = tps.next()
        for w_ in range(2):
            for h in range(4):
                tr(tp[:, (w_ * 4 + h) * 128:(w_ * 4 + h + 1) * 128], fbt[:, w_, h, :, :].rearrange("p a d -> p (a d)"), identb[:],
                   [fbb, B_cb], [tb])
        fT, fTb = fbT.next()
        cp("dve", fT[:].rearrange("p a h t -> p (a h t)"), tp[:, 0:1024], [tb], [fTb])
        dma("pool", qfb_d[t], fT[:, 0, :, :].rearrange("p h t -> p (h t)"), [fTb], [B_qfb], par=True)
        tp, tb = tps.next()
        for w_ in range(2):
            for pr in range(2):
                tr(tp[:, (w_ * 2 + pr) * 128:(w_ * 2 + pr + 1) * 128], rq[:, w_ * 256 + pr * 128:w_ * 256 + (pr + 1) * 128], identb[:],
                   [rqb, B_cb], [tb])
        rt, rtb = rT.next()
        cp("act", rt[0:64, 0, :, :], tp[0:64, 0:256].rearrange("p (b t) -> p b t", b=2), [tb], [rtb], )
        cp("act", rt[64:128, 1, :, :], tp[64:128, 0:256].rearrange("p (b t) -> p b t", b=2), [tb], [rtb], )
        cp("act", rt[:, 2, :, :], tp[:, 256:512].rearrange("p (b t) -> p b t", b=2), [tb], [rtb], )
        sp0, sb0 = sps.next()
        sp1, sb1 = sps.next()
        for pr in range(2):
            mm(sp0[:, pr * 128:(pr + 1) * 128], rt[:, 2, pr, :], rt[:, 0, pr, :], True, True, [rtb], [sb0])
            mm(sp1[:, pr * 128:(pr + 1) * 128], rt[:, 2, pr, :], rt[:, 1, pr, :], True, True, [rtb], [sb1])
        pi, pib = pint.next()
        piv = pi[:].rearrange("p (a b i) -> p a b i", a=2, b=2)
        dcv = DecT[:].rearrange("p (a b i) -> p a b i", a=2, b=2)
        for hh, (spx, sbx) in enumerate(((sp0, sb0), (sp1, sb1))):
            tt("dve", piv[:, :, hh, :], spx[:, 0:256].rearrange("p (a i) -> p a i", a=2), dcv[:, :, hh, :], ALU.mult,
               [sbx, B_lay], [pib], )
        mp, mb = PS["z"].next()
        for h in range(4):
            mm(mp[:, h * 64:(h + 1) * 64], pi[:, h * 128:(h + 1) * 128], rq[:, 512 + h * 64:512 + (h + 1) * 64], True, True, [pib, rqb], [mb])
        for h in range(4):
            mm(mp[:, 256 + h * 64:256 + (h + 1) * 64], fbt[:, 1, h, :, :].rearrange("p a d -> p (a d)"),
               rq[:, 512 + h * 64:512 + (h + 1) * 64], True, True, [fbb, rqb], [mb])
        yt, ytb = yint.next()
        cp("dve", yt[:], mp[:, 0:256], [mb], [ytb])
        row0 = t * 128
        dma("pool", yin_d[row0:row0 + 128, :], yt[:], [ytb], [B_yin], par=True)
        Sx, BS, Ax, BA = ret_state(is_ctx)
        cp("dve", Sx[0:64, c + 2, :], mp[0:64, 256:512], [mb], [BS[c + 2]], )
        cp("dve", Sx[64:128, c, :], mp[64:128, 256:512], [mb], [BS[c]], )
        tt("pool", Ax[0:64, :], Ax[0:64, :], Dt[0:64, :], ALU.mult, [BA, B_lay], [BA])
        tt("pool", Ax[0:64, :], Ax[0:64, :], Sx[0:64, c + 2, :], ALU.add, [BA, BS[c + 2]], [BA])
        if is_ctx:
            if c == 0:
                tt("pool", Ax[64:128, :], Ax[64:128, :], Sx[64:128, c, :], ALU.add, [BA, BS[c]], [BA])
            else:
                tt("pool", sintmp[64:128, :], Dt[64:128, :], Sx[64:128, c, :], ALU.mult, [B_lay, BS[c]], [B_sintmp])
                tt("pool", Ax[64:128, :], Ax[64:128, :], sintmp[64:128, :], ALU.add, [BA, B_sintmp], [BA])
        else:
            tt("pool", sintmp[64:128, :], Pw[64:128, :], Sx[64:128, c, :], ALU.mult, [B_Pw, BS[c]], [B_sintmp])
            tt("pool", Ax[64:128, :], Ax[64:128, :], sintmp[64:128, :], ALU.add, [BA, B_sintmp], [BA])
            tt("pool", Pw[64:128, :], Pw[64:128, :], Dt[64:128, :], ALU.mult, [B_Pw, B_lay], [B_Pw])
        gk_gain = qkn[:, 128:256]
        kn, knbuf = run(qk_norm_rope(kr, krb, 4, gk_gain, csb, not is_ctx))
        tp, tb = tps.next()
        for a in range(2):
            tr(tp[:, a * 128:(a + 1) * 128], kn[:, a * 128:(a + 1) * 128], identb[:], [knbuf, B_cb], [tb])
        if is_ctx:
            cp("act", KT[:, c * 128:(c + 1) * 128], tp[:, 0:128], [tb], [B_KT], )
            cp("act", skTc[:, c * 128:(c + 1) * 128], tp[:, 128:256], [tb], [B_skTc], )
            cp("dve", V[:, c, 1:129], vs[:, 1:129], [vsbuf], [B_V])
            cp("dve", sVc[:, c, 1:129], vs[:, 131:259], [vsbuf], [B_sVc])
        else:
            kt_, ktb = kTs.next()
            cp("act", kt_[:], tp[:, 0:256], [tb], [ktb])
            dma("pool", gk_x[c // GP][:, (c % GP) * 128:(c % GP + 1) * 128], kt_[:, 0:128], [ktb], [B_gkx[c // GP]], par=True)
            dma("pool", sk_d[c], kt_[:, 128:256], [ktb], [B_skd], par=True)
            dma("pool", gv_x[c // GP][:, (c % GP) * 130:(c % GP + 1) * 130], vs[:, 0:130], [vsbuf], [B_gvx[c // GP]], par=True)
            dma("pool", sv_d[c], vs[:, 130:260], [vsbuf], [B_svd], par=True)
            if c == 0:
                dma("pool", bnd_x[:, 0:128], kt_[:, 128:256], [ktb], [B_bndx], par=True)
                dma("pool", bnd_x[:, 256:386], vs[:, 130:260], [vsbuf], [B_bndx], par=True)
            if c == NCH - 1:
                dma("pool", bnd_x[:, 128:256], kt_[:, 128:256], [ktb], [B_bndx], par=True)
                dma("pool", bnd_x[:, 386:516], vs[:, 130:260], [vsbuf], [B_bndx], par=True)

    def exchange(l):
        dma("pool", agg_x, Aagg[:], [B_Aagg], [B_aggx])
        groups = [[0, 1, 2, 3], [4, 5, 6, 7]]
        ccl = [(bnd_x, B_bndx, bnd_all, B_bndall), (agg_x, B_aggx, agg_all, B_aggall)]
        for i in range(NGP):
            ccl.append((gk_x[i], B_gkx[i], gk_all[i], B_gkall[i]))
            ccl.append((gv_x[i], B_gvx[i], gv_all[i], B_gvall[i]))
        for (src, bs, dst, bd) in ccl:
            P.op("pool", lambda e, src=src, dst=dst: e.collective_compute("AllGather", ALU.bypass, replica_groups=groups,
                                                                          ins=[src], outs=[dst]), [bs], [bd], kind="cc")

    def exchange_b(l):
        dma("sp", aggs[:], agg_all.rearrange("(r p) c -> p r c", p=128), [B_aggall], [B_aggs])
        B_s2 = Buf("s2")
        v3 = lambda ap: ap.rearrange("p (h e) -> p h e", h=4)
        cb = lambda s_: coef[:, s_, :].unsqueeze(2).to_broadcast([128, 4, 64])
        tt("pool", v3(sintmp[:]), v3(Actx[:]), cb(4), ALU.mult, [B_Actx, B_lay], [B_sintmp])
        for r in range(R):
            tt("pool", v3(aggs[:, r, :]), v3(aggs[:, r, :]), cb(r), ALU.mult, [B_aggs, B_lay], [B_aggs])
            tt("pool", sintmp[:], sintmp[:], aggs[:, r, :], ALU.add, [B_sintmp, B_aggs], [B_sintmp])
        cp("pool", ST[0:64, 1, :], sintmp[0:64, :], [B_sintmp], [B_ST[1]], )
        cp("pool", ST[64:128, NCH, :], sintmp[64:128, :], [B_sintmp], [B_ST[NCH]], )
        for i in range(1, NCH):
            cf = i
            tt("dve", sintmp[0:64, :], sintmp[0:64, :], Dt[0:64, :], ALU.mult, [B_sintmp, B_lay], [B_sintmp])
            tt("dve", sintmp[0:64, :], sintmp[0:64, :], ST[0:64, cf + 1, :], ALU.add, [B_sintmp, B_ST[cf + 1]], [B_sintmp])
            cp("dve", ST[0:64, cf + 1, :], sintmp[0:64, :], [B_sintmp], [B_ST[cf + 1]])
            cbk = NCH - 1 - i
            tt("dve", sintmp[64:128, :], sintmp[64:128, :], Dt[64:128, :], ALU.mult, [B_sintmp, B_lay], [B_sintmp])
            tt("dve", sintmp[64:128, :], sintmp[64:128, :], ST[64:128, cbk + 1, :], ALU.add, [B_sintmp, B_ST[cbk + 1]], [B_sintmp])
            cp("dve", ST[64:128, cbk + 1, :], sintmp[64:128, :], [B_sintmp], [B_ST[cbk + 1]])

    def small_attn(qT_ap, qTb, blocks, sink_l, mo, mob, gt, gtb, colbase):
        for g in range(2):
            op_, opb = PS["acc"].next()
            pend = []

            def pv(item):
                bi, vap, w_, wb_, bufs = item
                mm(op_[0:65, 0:256], vap[:, g * 65:(g + 1) * 65], w_[:], bi == 0, bi == len(blocks) - 1, [wb_] + bufs, [opb])

            for bi, (kap, vap, mask, bufs) in enumerate(blocks):
                sp_, spb = sps.next()
                mm(sp_[:, 0:256], kap, qT_ap[:, g, :, :].rearrange("p r t -> p (r t)"), True, True,
                   [qTb] + bufs, [spb])
                yield
                w_, wb_ = wp.next()
                act(w_[:], sp_[:, 0:256], AF.Exp, [spb], [wb_], scale=SCALE)
                yield
                if mask is not None:
                    tt("pool", w_[:].rearrange("p (r t) -> p r t", r=2), w_[:].rearrange("p (r t) -> p r t", r=2),
                       mask.unsqueeze(1).to_broadcast([128, 2, 128]), ALU.mult, [wb_, B_cb], [wb_])
                    yield
                pend.append((bi, vap, w_, wb_, bufs))
                if len(pend) > 1:
                    pv(pend.pop(0))
            for item in pend:
                pv(item)
            yield
            wo, wob = woT.next()
            cp("dve", wo[:], op_[0:65, 0:256], [opb], [wob])
            yield
            for r in range(2):
                h = g * 2 + r
                zp, zb = PS["z"].next()
                tr(zp[:, 0:65], wo[:, r * 128:(r + 1) * 128], identf[0:65, 0:65], [wob, B_cb], [zb])
                yield
                finish_head(zp, zb, g, h, sink_l, mo, mob, gt, gtb, colbase)
                yield

    def finish_head(zp, zb, g, h, sink_l, mo, mob, gt, gtb, colbase):
        rd_, rdb = rden.next()
        dcol = 0 if g == 0 else 64
        o0 = 1 if g == 0 else 0
        if sink_l:
            ts("dve", rd_[:, 0:1], zp[:, dcol:dcol + 1], esink[:, h:h + 1], None, ALU.add, None, [zb, B_lay], [rdb])
            P.op("dve", lambda e: e.reciprocal(out=rd_[:, 0:1], in_=rd_[:, 0:1]), [rdb], [rdb])
        else:
            P.op("dve", lambda e: e.reciprocal(out=rd_[:, 0:1], in_=zp[:, dcol:dcol + 1]), [zb], [rdb])
        stt(mo[:, colbase + h * 64:colbase + (h + 1) * 64], zp[:, o0:o0 + 64], rd_[:, 0:1], gt[:, colbase + h * 64:colbase + (h + 1) * 64],
            ALU.mult, ALU.mult, [zb, rdb, gtb], [mob])

    def silu_evac(dst, dstb, zp, zb):
        Ct, Cb = t2.next()
        act(Ct[:, 0:512], zp[:, 0:512], AF.Exp, [zb], [Cb], scale=-1.0)
        yield
        ts("dve", Ct[:, 0:512], Ct[:, 0:512], 1.0, None, ALU.add, None, [Cb], [Cb])
        yield
        P.op("dve", lambda e: e.reciprocal(out=Ct[:, 0:512], in_=Ct[:, 0:512]), [Cb], [Cb])
        yield
        yield
        tt("dve", dst, zp[:, 0:512], Ct[:, 0:512], ALU.mult, [zb, Cb], [dstb])

    def p2_front(l, t, xsrc, B_xsrc):
        is_ctx = t < CTXC
        c = t if is_ctx else t - CTXC
        j = 1 if is_ctx else 0
        if is_ctx:
            xap, xbuf = xc[:, c, :], B_xc[c]
            csb = None
        else:
            xt_, xbuf = xt.next()
            dma("sp", xt_[:], xsrc[c * 128:(c + 1) * 128, :], [B_xsrc], [xbuf])
            xap = xt_[:]
            cst, csbuf = cs.next()
            dma("sp", cst[:, 0, :], cos_in[c], [B_const], [csbuf])
            dma("sp", cst[:, 1, :], sin_in[c], [B_const], [csbuf], par=True)
            csb = (cst, csbuf)
        yl, ylb = yinl.next()
        dma("sp", yl[:], yin_d[t * 128:(t + 1) * 128, :], [B_yin], [ylb])
        ql, qlb = qfbl.next()
        dma("sp", ql[:], qfb_d[t], [B_qfb], [qlb])
        yield
        yield
        ht, hb = yield from norm_tile_g(l, xap, xbuf, j)
        ug, ugb = uvg.next()
        gt, gtb = gates.next()
        mo, mob = mixo.next()
        qr, qrb = kraw.next()
        zp, zb = inproj(ht, hb, W2, B_W2, 0, 512)
        yield
        yield
        act(ug[:], zp[:, 0:512], AF.Gelu, [zb], [ugb])
        yield
        zp, zb = inproj(ht, hb, W2, B_W2, 512, 512)
        yield
        yield
        yield from silu_evac(gt[:, 0:512], gtb, zp, zb)
        yield
        zp, zb = inproj(ht, hb, W2, B_W2, 1024, 512)
        yield
        yield
        yield from silu_evac(gt[:, 512:1024], gtb, zp, zb)
        yield
        zp, zb = inproj(ht, hb, W2, B_W2, 1536, 512)
        yield
        yield
        for blk in range(2):
            cp("dve", qr[:, blk * 256:(blk + 1) * 256].rearrange("p (r g d) -> p r g d", r=2, g=2),
               zp[:, blk * 256:(blk + 1) * 256].rearrange("p (g r d) -> p r g d", r=2, g=2), [zb], [qrb], )
        yield
        mp, mb = PS["z"].next()
        for h in range(4):
            mm(mp[:, h * 64:(h + 1) * 64], mixT[:, h * 128:(h + 1) * 128], ug[:, 256 + h * 64:256 + (h + 1) * 64], True, True, [B_mixT, ugb], [mb])
        Sx, BS, _, _ = ret_state(is_ctx)
        for h in range(4):
            mm(mp[:, 256 + h * 64:256 + (h + 1) * 64], ql[:, h * 128:(h + 1) * 128], Sx[:, c + 1, h * 64:(h + 1) * 64], True, True, [qlb, BS[c + 1]], [mb])
        yield
        yield
        ys, ysb = ysum.next()
        v3 = lambda ap: ap.rearrange("p (h d) -> p h d", h=4)
        tt("dve", v3(ys[:]), v3(mp[:, 0:256]), mbias[:, l * 4:(l + 1) * 4].unsqueeze(2).to_broadcast([128, 4, 64]), ALU.add, [mb, B_cb], [ysb])
        tt("dve", ys[:], ys[:], ug[:, 0:256], ALU.mult, [ysb, ugb], [ysb])
        yield
        tt("dve", mo[:, 0:256], ys[:], gt[:, 0:256], ALU.mult, [ysb, gtb], [mob])
        ys, ysb = ysum.next()
        tt("dve", ys[:], mp[:, 256:512], yl[:], ALU.add, [mb, ylb], [ysb])
        yield
        Bt, Bb = t1.next()
        s4, s4b = st4.next()
        tt("dve", Bt[:, 0:256], ys[:], ys[:], ALU.mult, [ysb], [Bb])
        yield
        P.op("dve", lambda e: e.tensor_reduce(out=s4[:, 0:4], in_=v3(Bt[:, 0:256]), axis=AX.X, op=ALU.add), [Bb], [s4b])
        yield
        yield from rsqrt_mean_g(s4[:, 0:4], s4[:, 0:4], 64, [s4b], [s4b])
        tt("dve", v3(ys[:]), v3(ys[:]), s4[:, 0:4].unsqueeze(2).to_broadcast([128, 4, 64]), ALU.mult, [ysb, s4b], [ysb])
        tt("dve", ys[:], ys[:], rnorm[:, 0:256], ALU.mult, [ysb, B_rq], [ysb])
        yield
        tt("dve", mo[:, 256:512], ys[:], gt[:, 256:512], ALU.mult, [ysb, gtb], [mob])
        yield
        qn, qnb = yield from qk_norm_rope(qr, qrb, 8, qkn[:, 0:128], csb, not is_ctx)
        yield
        tp, tb = tps.next()
        for a in range(4):
            tr(tp[:, a * 128:(a + 1) * 128], qn[:, a * 128:(a + 1) * 128], identb[:], [qnb, B_cb], [tb])
        yield
        yield
        return dict(t=t, c=c, is_ctx=is_ctx, gt=gt, gtb=gtb, mo=mo, mob=mob, tp=tp, tb=tb)

    def out_proj(l, st, xsrc, B_xsrc, xdst, B_xdst):
        t, c, is_ctx, mo, mob = st["t"], st["c"], st["is_ctx"], st["mo"], st["mob"]
        j = 1 if is_ctx else 0
        if is_ctx:
            xap, xbuf = xc[:, c, :], B_xc[c]
        else:
            xr_, xbuf = xr.next()
            dma("sp", xr_[:], xsrc[c * 128:(c + 1) * 128, :], [B_xsrc], [xbuf])
            xap = xr_[:]
        tp, tb = tps.next()
        for k in range(8):
            tr(tp[:, k * 128:(k + 1) * 128], mo[:, k * 128:(k + 1) * 128], identb[:], [mob, B_cb], [tb])
        yield
        yield
        mt, mtb = mixTt.next()
        cp("dve", mt[:].rearrange("p k t -> p (k t)"), tp[:, 0:1024], [tb], [mtb])
        yield
        yield
        ot, otb = otmp.next()
        for half in range(2):
            zp, zb = PS["z"].next()
            for k in range(8):
                mm(zp[:, 0:512], mt[:, k, :], WA[:, k, half * 512:(half + 1) * 512], k == 0, k == 7, [mtb, B_WA], [zb])
            yield
            yield
            yield
            tt("dve", ot[:, half * 512:(half + 1) * 512], zp[:, 0:512], gateB[:, j, half * 512:(half + 1) * 512], ALU.mult, [zb, B_mod], [otb], )
            yield
        if is_ctx:
            tt("pool", xc[:, c, :], xc[:, c, :], ot[:], ALU.add, [xbuf, otb], [xbuf])
        else:
            tt("pool", ot[:], ot[:], xap, ALU.add, [otb, xbuf], [otb])
            yield
            yield
            dma("pool", xdst[c * 128:(c + 1) * 128, :], ot[:], [otb], [B_xdst], par=True)
        yield

    def q_evac(st, q_, qb, lo, cols=None):
        for gg in range(2):
            dst = q_[gg * 64:(gg + 1) * 64, gg, :, :] if cols is None else q_[gg * 64:(gg + 1) * 64, gg, :, cols[0]:cols[1]]
            cp("dve", dst, st["tp"][gg * 64:(gg + 1) * 64, lo:lo + 256].rearrange("p (r t) -> p r t", r=2), [st["tb"]], [qb], )

    def p2_ctx(l):
        for c in range(CTXC):
            st = yield from p2_front(l, c, None, None)
            qT_, qTb = sqT.next()
            qT2, qT2b = sqT.next()
            q_evac(st, qT_, qTb, 0)
            q_evac(st, qT2, qT2b, 256)
            yield
            gblocks = [(KT[:, cc * 128:(cc + 1) * 128], V[:, cc, :], None, [B_KT, B_V]) for cc in range(CTXC)]
            yield from small_attn(qT_, qTb, gblocks, False, st["mo"], st["mob"], st["gt"], st["gtb"], 512)
            sblocks = [(skTc[:, cc * 128:(cc + 1) * 128], sVc[:, cc, :], None, [B_skTc, B_sVc]) for cc in range(CTXC)]
            yield from small_attn(qT2, qT2b, sblocks, True, st["mo"], st["mob"], st["gt"], st["gtb"], 768)
            yield from out_proj(l, st, None, None, None, None)

    def swa_tile(l, st, sq_, sqb):
        c = st["c"]
        wk_, wkb = wk.next()
        wv_, wvb = wv.next()
        blocks = []
        bb = [wkb, wvb]
        if c == 0:
            dma("sp", wk_[:, 0:2, :], sk_d[0:2].rearrange("c p k -> p c k"), [B_skd], [wkb])
            dma("sp", wv_[:, 0:2, :], sv_d[0:2].rearrange("c p k -> p c k"), [B_svd], [wvb])
            dma("sp", wk_[:, 2:6, :], bnd_all[:, 128:256].rearrange("(r p) k -> p r k", p=128), [B_bndall], [wkb], par=True)
            dma("sp", wv_[:, 2:6, :], bnd_all[:, 386:516].rearrange("(r p) k -> p r k", p=128), [B_bndall], [wvb], par=True)
            blocks.append((wk_[:, 0, :], wv_[:, 0, :], None, bb))
            blocks.append((wk_[:, 1, :], wv_[:, 1, :], cmask[:, 128:256], bb))
            for r in range(R):
                blocks.append((wk_[:, 2 + r, :], wv_[:, 2 + r, :], hmask[:, r * 128:(r + 1) * 128], bb))
        elif c == NCH - 1:
            dma("sp", wk_[:, 0:2, :], sk_d[c - 1:c + 1].rearrange("c p k -> p c k"), [B_skd], [wkb])
            dma("sp", wv_[:, 0:2, :], sv_d[c - 1:c + 1].rearrange("c p k -> p c k"), [B_svd], [wvb])
            dma("sp", wk_[:, 2:6, :], bnd_all[:, 0:128].rearrange("(r p) k -> p r k", p=128), [B_bndall], [wkb], par=True)
            dma("sp", wv_[:, 2:6, :], bnd_all[:, 256:386].rearrange("(r p) k -> p r k", p=128), [B_bndall], [wvb], par=True)
            blocks.append((wk_[:, 0, :], wv_[:, 0, :], cmask[:, 0:128], bb))
            blocks.append((wk_[:, 1, :], wv_[:, 1, :], None, bb))
            for r in range(R):
                blocks.append((wk_[:, 2 + r, :], wv_[:, 2 + r, :], hmask[:, (4 + r) * 128:(5 + r) * 128], bb))
        else:
            dma("sp", wk_[:, 0:3, :], sk_d[c - 1:c + 2].rearrange("c p k -> p c k"), [B_skd], [wkb])
            dma("sp", wv_[:, 0:3, :], sv_d[c - 1:c + 2].rearrange("c p k -> p c k"), [B_svd], [wvb])
            blocks.append((wk_[:, 0, :], wv_[:, 0, :], cmask[:, 0:128], bb))
            blocks.append((wk_[:, 1, :], wv_[:, 1, :], None, bb))
            blocks.append((wk_[:, 2, :], wv_[:, 2, :], cmask[:, 128:256], bb))
        for cc in range(CTXC):
            blocks.append((skTc[:, cc * 128:(cc + 1) * 128], sVc[:, cc, :], None, [B_skTc, B_sVc]))
        yield
        yield from small_attn(sq_, sqb, blocks, True, st["mo"], st["mob"], st["gt"], st["gtb"], 768)

    def front_group(l, gi, xsrc, B_xsrc, G):
        gq_, gqb = gqT.next()
        G["gq"] = (gq_, gqb)
        G["sts"] = []
        for ti in range(QG):
            st = yield from p2_front(l, CTXC + gi * QG + ti, xsrc, B_xsrc)
            sq_, sqb = sqT.next()
            q_evac(st, gq_, gqb, 0, (ti * 128, (ti + 1) * 128))
            q_evac(st, sq_, sqb, 256)
            yield
            yield from swa_tile(l, st, sq_, sqb)
            G["sts"].append(st)

    def sweep_group(G):
        gq_, gqb = G["gq"]
        accs = [ops_.next() for _ in range(2)]
        pend = []
        NW = 2 * GQ
        pieces = [(None, None)] + [(r, c0) for r in range(R) for c0 in range(0, NCH, PCS)]
        nblk_total = CTXC + R * NCH
        seen = 0

        def pv(item):
            first, last, g0, vap, vb, p0, pb0 = item
            mm(accs[g0][0][0:65, 0:NW], vap[:, g0 * 65:(g0 + 1) * 65], p0[:, 0:NW], first, last, [pb0] + vb, [accs[g0][1]])

        for (r, c0) in pieces:
            if r is None:
                nb = CTXC
                kget = lambda i: KT[:, i * 128:(i + 1) * 128]
                vget = lambda i: V[:, i, :]
                kbufs, vbufs = [B_KT], [B_V]
            else:
                nb = PCS
                kt_, ktb_ = ksl.next()
                vt_, vtb_ = vsl.next()
                gp_, of_ = c0 // GP, c0 % GP
                dma("sp", kt_[:], gk_all[gp_][r * 128:(r + 1) * 128, of_ * 128:(of_ + PCS) * 128], [B_gkall[gp_]], [ktb_])
                dma("sp", vt_[:], gv_all[gp_][r * 128:(r + 1) * 128, of_ * 130:(of_ + PCS) * 130].rearrange("p (c d) -> p c d", d=130), [B_gvall[gp_]], [vtb_])
                kget = lambda i, kt_=kt_: kt_[:, i * 128:(i + 1) * 128]
                vget = lambda i, vt_=vt_: vt_[:, i, :]
                kbufs, vbufs = [ktb_], [vtb_]
            for i in range(nb):
                first, last = seen == 0, seen == nblk_total - 1
                seen += 1
                for g in range(2):
                    sp_, spb = sps.next()
                    mm(sp_[:, 0:NW], kget(i), gq_[:, g, :, :].rearrange("p r t -> p (r t)"), True, True,
                       kbufs + [gqb], [spb])
                    p_, pb = pT.next()
                    act(p_[:, 0:NW], sp_[:, 0:NW], AF.Exp, [spb], [pb], scale=SCALE)
                    pend.append((first, last, g, vget(i), vbufs, p_, pb))
                    if len(pend) > 2:
                        pv(pend.pop(0))
                yield
        for item in pend:
            pv(item)
        G["oT"] = []
        for g in range(2):
            o_, ob_ = oT.next()
            cp("dve", o_[:, 0:NW], accs[g][0][0:65, 0:NW], [accs[g][1]], [ob_])
            G["oT"].append((o_, ob_))

    def tail_group(l, G, xsrc, B_xsrc, xdst, B_xdst):
        sts = G["sts"]
        for g in range(2):
            o_, ob_ = G["oT"][g]
            for r in range(2):
                h = 2 * g + r
                for ti in range(QG):
                    zp, zb = PS["z"].next()
                    tr(zp[:, 0:65], o_[:, r * GQ + ti * 128:r * GQ + (ti + 1) * 128], identf[0:65, 0:65], [ob_, B_cb], [zb])
                    yield
                    yield
                    finish_head(zp, zb, g, h, False, sts[ti]["mo"], sts[ti]["mob"], sts[ti]["gt"], sts[ti]["gtb"], 512)
                    yield
        for st in sts:
            yield from out_proj(l, st, xsrc, B_xsrc, xdst, B_xdst)

    def run(gen):
        try:
            while True:
                next(gen)
        except StopIteration as e:
            return e.value

    def gchain(*gens):
        for g_ in gens:
            yield from g_

    def pass2(l, xsrc, B_xsrc, xdst, B_xdst):
        PS["z"], PS["acc"] = zps1, swacc
        if l < depth - 1:
            run(p2_ctx(l))
        exchange_b(l)
        Gs = [dict() for _ in range(NG)]
        run(front_group(l, 0, xsrc, B_xsrc, Gs[0]))
        for k in range(NG):
            sides = []
            if k > 0:
                sides.append(tail_group(l, Gs[k - 1], xsrc, B_xsrc, xdst, B_xdst))
            if k + 1 < NG:
                sides.append(front_group(l, k + 1, xsrc, B_xsrc, Gs[k + 1]))
            side = gchain(*sides)
            alive = True
            for _ in sweep_group(Gs[k]):
                for _r in range(SIDE_RATE):
                    if alive:
                        try:
                            next(side)
                        except StopIteration:
                            alive = False
            if alive:
                run(side)
        run(tail_group(l, Gs[NG - 1], xsrc, B_xsrc, xdst, B_xdst))
        PS["z"], PS["acc"] = zpsP1, ops_

    chain = [(x_in, B_xin)]
    inter = [(xsA, B_xsA), (xsB, B_xsB)]
    for l in range(depth):
        chain.append((y_out, B_y) if l == depth - 1 else inter[l % 2])
    for l in range(depth):
        xsrc, B_xsrc = chain[l]
        xdst, B_xdst = chain[l + 1]
        load_w1(l)
        layer_consts(l)
        if stop == "consts":
            break
        mod_compute(l)
        if stop == "mod":
            break
        load_w2(l)
        if stop == "w2":
            break
        for t in range(CTXC + NCH):
            p1_tile(l, t, xsrc, B_xsrc)
        if stop == "p1":
            break
        exchange(l)
        if stop == "exch":
            break
        load_wo(l)
        pass2(l, xsrc, B_xsrc, xdst, B_xdst)

    P.finalize()
    with nc.Block() as block:
        @block.sync
        def _(e):
            P.emit("sp", e)

        @block.scalar
        def _(e):
            P.emit("act", e)

        @block.vector
        def _(e):
            P.emit("dve", e)

        @block.tensor
        def _(e):
            P.emit("pe", e)

        @block.gpsimd
        def _(e):
            P.emit("pool", e)
            if B_y.sem is not None:
                P.final_wait(e, [B_y])
            else:
                dummy = P.es.enter_context(nc.semaphore("dummy"))
                e.dma_start(out=y_out[0:128, :], in_=xc[:, 0, :]).then_inc(dummy, 16)
                e.wait_ge(dummy, 16)
    es.close()
    return nc


def host_inputs(inputs, NCH, depth=DEPTH, n_cores=8):
    f = np.float32
    x = np.asarray(inputs["x"], f)
    NT = NCH * 128
    bf = ml_dtypes.bfloat16
    c = np.asarray(inputs["c"], f)
    ctx = np.asarray(inputs["ctx"], f)
    c_ctx = np.asarray(inputs["c_ctx"], f)
    ng = np.asarray(inputs["norm_gain"], f)
    b_mod = np.asarray(inputs["b_mod"], f)
    common = {
        "gainT": np.ascontiguousarray(ng.reshape(depth, 8, 128).transpose(2, 0, 1).reshape(128, depth * 8)),
        "w_mod": np.ascontiguousarray(np.asarray(inputs["w_mod"], f)),
        "bmodT": np.ascontiguousarray(b_mod[:, 0:2048].reshape(depth, 16, 128).transpose(2, 0, 1).reshape(128, depth * 16)),
        "bgate": np.ascontiguousarray(b_mod[:, 2048:3072].reshape(1, depth * D)),
        "w_in": np.ascontiguousarray(np.asarray(inputs["w_in"], f)),
        "w_out": np.ascontiguousarray(np.asarray(inputs["w_out"], f)),
        "mixT": np.ascontiguousarray(np.asarray(inputs["mlp_mix"], f).transpose(0, 3, 1, 2).reshape(depth, 128, 512)),
        "mbias": np.ascontiguousarray(np.asarray(inputs["mlp_bias"], f).transpose(2, 0, 1).reshape(128, depth * 4)),
        "rdec": np.ascontiguousarray(np.stack([np.asarray(inputs["ret_decay_fwd"], f), np.asarray(inputs["ret_decay_bwd"], f)], 1).reshape(1, depth * 8)),
        "rnorm": np.ascontiguousarray(np.asarray(inputs["ret_norm"], f).reshape(1, depth * 256)),
        "qkn": np.ascontiguousarray(np.stack([np.asarray(inputs[k], f) for k in ("attn_q_norm", "swa_q_norm", "attn_k_norm", "swa_k_norm")], 1).reshape(1, depth * 256)),
        "sink": np.ascontiguousarray(np.asarray(inputs["swa_sink"], f).reshape(1, depth * 4)),
    }
    jj = np.arange(128, dtype=f)[:, None]
    ii = np.arange(128, dtype=f)[None, :]
    common["identb"] = np.eye(128, dtype=f).astype(bf)
    common["identf"] = np.eye(128, dtype=f)
    common["rpn"] = np.concatenate([np.maximum(ii - jj, 0), np.maximum(jj - ii, 0)], 1).astype(f)
    p = np.arange(128, dtype=f)
    common["pos"] = np.stack([127 - p, p, p + 1, 128 - p], 1).astype(f)
    mprev = (jj >= ii).astype(f)
    mnext = (jj <= ii).astype(f)
    common["cmask"] = np.concatenate([mprev, mnext], 1).astype(bf)
    half = 32
    inv_freq = (1.0 / (10000.0 ** (np.arange(0, half, 2, dtype=f) / f(half)))).astype(f)
    sgn = np.concatenate([-np.ones(16, f), np.ones(16, f), -np.ones(16, f), np.ones(16, f)])
    maps = []
    for core in range(n_cores):
        b, seg = core // R, core % R
        m = dict(common)
        m["x_in"] = np.ascontiguousarray(x[b, seg * NT:(seg + 1) * NT, :])
        m["ctx_in"] = np.ascontiguousarray(ctx[b])
        cT = np.zeros((128, 16), f)
        cT[:, 0::2] = c[b].reshape(8, 128).T
        cT[:, 1::2] = c_ctx.reshape(8, 128).T
        m["cT"] = cT
        tpos = seg * NT + np.arange(NT)
        row = (tpos // 64).astype(f)
        col = (tpos % 64).astype(f)
        ang_r = row[:, None] * inv_freq[None, :]
        ang_c = col[:, None] * inv_freq[None, :]
        ang = np.concatenate([ang_r, ang_r, ang_c, ang_c], -1).astype(f)
        m["cos"] = np.cos(ang).astype(f).reshape(NCH, 128, 64)
        m["sin"] = (np.sin(ang).astype(f) * sgn[None, :]).reshape(NCH, 128, 64)
        et = np.full((128, 5), BIGE, f)
        for r in range(R):
            if r < seg:
                et[0:64, r] = seg - 1 - r
            if r > seg:
                et[64:128, r] = r - seg - 1
        et[0:64, 4] = seg
        et[64:128, 4] = R - 1 - seg
        m["etab"] = et
        hm = np.zeros((128, 8, 128), f)
        if seg - 1 >= 0:
            hm[:, seg - 1, :] = mprev
        if seg + 1 < R:
            hm[:, 4 + seg + 1, :] = mnext
        m["hmask"] = hm.reshape(128, 1024).astype(bf)
        maps.append(m)
    return maps


_NC_CACHE = {}


def kernel(**inputs):
    x = np.asarray(inputs["x"])
    B, L, _ = x.shape
    NCH = L // R // 128
    depth = np.asarray(inputs["w_in"]).shape[0]
    key = (NCH, depth)
    if key not in _NC_CACHE:
        _NC_CACHE[key] = build(NCH, depth)
    nc = _NC_CACHE[key]
    maps = host_inputs(inputs, NCH, depth)
    res = run_bass_kernel_spmd(nc, maps, core_ids=list(range(8)))
    NT = NCH * 128
    out = np.zeros((B, L, D), np.float32)
    for core in range(8):
        b, seg = core // R, core % R
        out[b, seg * NT:(seg + 1) * NT, :] = res.results[core]["y"]
    return out
```

```python
import math
from contextlib import ExitStack
import numpy as np
import ml_dtypes
import concourse.bass as bass
import concourse.mybir as mybir
from concourse.bass_utils import run_bass_kernel_spmd

F32 = mybir.dt.float32
BF16 = mybir.dt.bfloat16
AF = mybir.ActivationFunctionType
ALU = mybir.AluOpType
AX = mybir.AxisListType

D = 1024
DEPTH = 4
CTXC = 2
R = 4
EPS = 1e-6
SCALE = 0.125
BIGE = 1.0e4


class Buf:
    def __init__(self, name):
        self.name = name
        self.last_w = []
        self.readers = []
        self.sem = None
        self.cnt = 0


class Op:
    __slots__ = ("eng", "fn", "deps", "kind", "sig", "val", "sem")

    def __init__(self, eng, fn, kind):
        self.eng, self.fn, self.kind = eng, fn, kind
        self.deps = []
        self.sig = False
        self.val = 0
        self.sem = None


class Prog:
    def __init__(self, nc, es):
        self.nc, self.es = nc, es
        self.ops = {k: [] for k in ("pe", "act", "dve", "pool", "sp")}
        self.esem = {k: es.enter_context(nc.semaphore("e_" + k)) for k in ("pe", "act", "dve", "pool")}
        self.nsem = 4

    def op(self, eng, fn, reads=(), writes=(), kind="c", par=False):
        o = Op(eng, fn, kind)
        deps = []
        for b in reads:
            deps += b.last_w
        for b in writes:
            deps += b.readers
            if not par:
                deps += b.last_w
        seen = set()
        for d in deps:
            if id(d) in seen or d is o:
                continue
            seen.add(id(d))
            if d.kind == "c" and d.eng == "pe" and eng == "pe" and kind == "c":
                continue
            d.sig = True
            o.deps.append(d)
        for b in reads:
            b.readers.append(o)
        for b in writes:
            if par:
                b.last_w = b.last_w + [o]
            else:
                b.last_w = [o]
            b.readers = []
        if kind in ("d", "cc"):
            b = writes[0]
            if b.sem is None:
                b.sem = self.es.enter_context(self.nc.semaphore("s_" + b.name))
                self.nsem += 1
            b.cnt += 16 if kind == "d" else 1
            o.sem, o.val = b.sem, b.cnt
        self.ops[eng].append(o)
        return o

    def finalize(self):
        for eng in ("pe", "act", "dve", "pool"):
            c = 0
            for o in self.ops[eng]:
                if o.kind == "c" and o.sig:
                    c += 1
                    o.val = c
                    o.sem = self.esem[eng]

    def emit(self, eng, e):
        waited = {}
        for o in self.ops[eng]:
            for d in o.deps:
                k = id(d.sem)
                if waited.get(k, 0) >= d.val:
                    continue
                e.wait_ge(d.sem, d.val)
                waited[k] = d.val
            ins = o.fn(e)
            if o.kind == "d":
                ins.then_inc(o.sem, 16)
            elif o.kind == "cc":
                ins.then_inc(o.sem, 1)
            elif o.sig:
                ins.then_inc(o.sem, 1)

    def final_wait(self, e, bufs):
        for b in bufs:
            e.wait_ge(b.sem, b.cnt)


class Rot:
    def __init__(self, tiles, name, bufs=None):
        self.tiles = tiles
        self.bufs = bufs if bufs is not None else [Buf("%s%d" % (name, i)) for i in range(len(tiles))]
        self.i = -1

    def next(self):
        self.i = (self.i + 1) % len(self.tiles)
        return self.tiles[self.i], self.bufs[self.i]


def build(NCH, depth=DEPTH, QG=2, stop=None, SIDE_RATE=2, P1LAG=9):
    NT = NCH * 128
    NTT = NT + CTXC * 128
    KB = CTXC + R * NCH
    KTOT = KB * 128
    NG = NCH // QG
    GQ = QG * 128
    PCS = min(NCH, 4)
    GP = min(NCH, 8)
    NGP = NCH // GP
    nc = bass.Bass("TRN2", target_bir_lowering=False)
    es = ExitStack()
    P = Prog(nc, es)

    def din(name, shape, dt=F32):
        return nc.dram_tensor(name, shape, dt, kind="ExternalInput").ap()

    def dint(name, shape, dt):
        return nc.dram_tensor(name, shape, dt, kind="Internal").ap()

    x_in = din("x_in", [NT, D])
    ctx_in = din("ctx_in", [CTXC * 128, D])
    cT_in = din("cT", [128, 16])
    gainT_in = din("gainT", [128, depth * 8])
    w_mod = din("w_mod", [depth, D, 3 * D])
    bmodT_in = din("bmodT", [128, depth * 16])
    bgate_in = din("bgate", [1, depth * D])
    w_in = din("w_in", [depth, D, 3328])
    w_out = din("w_out", [depth, D, D])
    mixT_in = din("mixT", [depth, 128, 512])
    mbias_in = din("mbias", [128, depth * 4])
    rdec_in = din("rdec", [1, depth * 8])
    rnorm_in = din("rnorm", [1, depth * 256])
    qkn_in = din("qkn", [1, depth * 256])
    sink_in = din("sink", [1, depth * 4])
    cos_in = din("cos", [NCH, 128, 64])
    sin_in = din("sin", [NCH, 128, 64])
    etab_in = din("etab", [128, 5])
    hmask_in = din("hmask", [128, 8 * 128], BF16)
    cmask_in = din("cmask", [128, 2 * 128], BF16)
    identb_in = din("identb", [128, 128], BF16)
    identf_in = din("identf", [128, 128])
    rpn_in = din("rpn", [128, 256])
    pos_in = din("pos", [128, 4])
    y_out = nc.dram_tensor("y", [NT, D], F32, kind="ExternalOutput").ap()

    xsA = dint("xsA", [NT, D], F32)
    xsB = dint("xsB", [NT, D], F32)
    yin_d = dint("yin_d", [NTT, 256], F32)
    qfb_d = dint("qfb_d", [NCH + CTXC, 128, 512], BF16)
    sk_d = dint("sk_d", [NCH, 128, 128], BF16)
    sv_d = dint("sv_d", [NCH, 128, 130], BF16)
    gk_x = [dint("gk_x%d" % i, [128, GP * 128], BF16) for i in range(NGP)]
    gk_all = [dint("gk_all%d" % i, [R * 128, GP * 128], BF16) for i in range(NGP)]
    gv_x = [dint("gv_x%d" % i, [128, GP * 130], BF16) for i in range(NGP)]
    gv_all = [dint("gv_all%d" % i, [R * 128, GP * 130], BF16) for i in range(NGP)]
    bnd_x = dint("bnd_x", [128, 516], BF16)
    bnd_all = dint("bnd_all", [R * 128, 516], BF16)
    agg_x = dint("agg_x", [128, 256], F32)
    agg_all = dint("agg_all", [R * 128, 256], F32)
    B_xin, B_xsA, B_xsB, B_y = Buf("xin"), Buf("xsA"), Buf("xsB"), Buf("y")
    B_yin, B_qfb, B_skd, B_svd = Buf("yin"), Buf("qfb"), Buf("skd"), Buf("svd")
    B_gkx = [Buf("gkx%d" % i) for i in range(NGP)]
    B_gkall = [Buf("gkall%d" % i) for i in range(NGP)]
    B_gvx = [Buf("gvx%d" % i) for i in range(NGP)]
    B_gvall = [Buf("gvall%d" % i) for i in range(NGP)]
    B_bndx, B_bndall, B_aggx, B_aggall = Buf("bndx"), Buf("bndall"), Buf("aggx"), Buf("aggall")
    B_const = Buf("constin")

    def sb(name, shape, dt=F32):
        return es.enter_context(nc.sbuf_tensor(name, shape, dt))

    def ps(name, shape, dt=F32):
        return es.enter_context(nc.psum_tensor(name, shape, dt))

    KT = sb("KTc", [128, CTXC * 128], BF16);     B_KT = Buf("KT")
    V = sb("Vc", [128, CTXC, 130], BF16);        B_V = Buf("V")
    ksl = Rot([sb("ksl%d" % i, [128, PCS * 128], BF16) for i in range(2)], "ksl")
    vsl = Rot([sb("vsl%d" % i, [128, PCS, 130], BF16) for i in range(2)], "vsl")
    skTc = sb("skTc", [128, CTXC * 128], BF16);  B_skTc = Buf("skTc")
    sVc = sb("sVc", [128, CTXC, 130], BF16);     B_sVc = Buf("sVc")
    ST = sb("ST", [128, NCH + 2, 256], BF16)
    B_ST = [Buf("ST%d" % i) for i in range(NCH + 2)]
    STc = sb("STc", [128, CTXC + 2, 256], BF16)
    B_STc = [Buf("STc%d" % i) for i in range(CTXC + 2)]
    WA = sb("WA", [128, 8, 1280], BF16);         B_WA = Buf("WA")
    W2 = sb("W2s", [128, 8, 2048], BF16);         B_W2 = Buf("W2")
    mixT = sb("mixTs", [128, 512], BF16);        B_mixT = Buf("mixT")
    xc = sb("xc", [128, CTXC, D], F32)
    B_xc = [Buf("xc%d" % i) for i in range(CTXC)]
    identb = sb("identb_s", [128, 128], BF16)
    identf = sb("identf_s", [128, 128], F32)
    rpn = sb("rpn_s", [128, 256], F32)
    pos = sb("pos_s", [128, 4], F32)
    etab = sb("etab_s", [128, 5], F32)
    hmask = sb("hmask_s", [128, 8 * 128], BF16)
    cmask = sb("cmask_s", [128, 256], BF16)
    cT = sb("cT_s", [128, 16], F32)
    gainT = sb("gainT_s", [128, depth * 8], F32)
    bmodT = sb("bmodT_s", [128, depth * 16], F32)
    bgate = sb("bgate_s", [1, D], F32);  B_bg = Buf("bg")
    grow = sb("grow", [1, 512], F32);    B_grow = Buf("grow")
    mbias = sb("mbias_s", [128, depth * 4], F32)
    rdec = sb("rdec_s", [128, depth * 8], F32)
    rdsel = sb("rdsel_s", [128, depth * 4], F32)
    rnorm = sb("rnorm_s", [128, 256], F32);  B_rq = Buf("rqn")
    qkn = sb("qkn_s", [128, 256], F32)
    sink = sb("sink_s", [128, depth * 4], F32)
    ones1 = sb("ones1", [1, 128], F32)
    B_cb = Buf("constsb")
    Gm = sb("Gm", [128, 2, 8], F32)
    Sm = sb("Sm", [128, 2, 8], F32)
    gateB = sb("gateB", [128, 2, D], F32)
    B_mod = Buf("mod")
    lg = sb("lg", [128, 8], F32)
    lgsel = sb("lgsel", [128, 4], F32)
    kd = sb("kd", [128, 8], F32)
    qd = sb("qd", [128, 8], F32)
    DecT = sb("DecT", [128, 512], F32)
    Dt = sb("Dt", [128, 256], F32)
    Pw = sb("Pw", [128, 256], F32)
    Aagg = sb("Aagg", [128, 256], F32)
    Actx = sb("Actx", [128, 256], F32)
    coef = sb("coef", [128, 5, 4], F32)
    esink = sb("esink", [128, 4], F32)
    B_lay = Buf("laysmall")
    B_Aagg, B_Actx, B_Pw = Buf("Aagg"), Buf("Actx"), Buf("Pw")
    aggs = sb("aggs", [128, R, 256], F32);    B_aggs = Buf("aggs")
    sintmp = sb("sintmp", [128, 256], F32);   B_sintmp = Buf("sintmp")
    xt = Rot([sb("xt%d" % i, [128, D], F32) for i in range(2)], "xt")
    xr = xt
    st4 = Rot([sb("st4_%d" % i, [128, 16], F32) for i in range(4)], "st4")
    xn = Rot([sb("xn%d" % i, [128, D], BF16) for i in range(2)], "xn")
    hT = Rot([sb("hT%d" % i, [128, 8, 128], BF16) for i in range(2)], "hT")
    cs = Rot([sb("cs%d" % i, [128, 2, 64], F32) for i in range(2)], "cs")
    rqkv = Rot([sb("rqkv%d" % i, [128, 768], BF16) for i in range(2)], "rqkv")
    fb = Rot([sb("fb%d" % i, [128, 2, 4, 2, 64], BF16) for i in range(2)], "fb")
    fbT = Rot([sb("fbT%d" % i, [128, 2, 4, 128], BF16) for i in range(1)], "fbT")
    rT = Rot([sb("rT%d" % i, [128, 3, 2, 128], BF16) for i in range(1)], "rT")
    pint = Rot([sb("pint%d" % i, [128, 512], BF16) for i in range(1)], "pint")
    yint = Rot([sb("yint%d" % i, [128, 256], F32) for i in range(1)], "yint")
    kraw = Rot([sb("kraw%d" % i, [128, 512], F32) for i in range(2)], "kraw")
    t1 = Rot([sb("t1_%d" % i, [128, 512], F32) for i in range(1)], "t1")
    t2 = Rot([sb("t2_%d" % i, [128, 512], F32) for i in range(1)], "t2")
    knb = Rot([sb("knb%d" % i, [128, 512], BF16) for i in range(2)], "knb")
    kTs = Rot([sb("kTs%d" % i, [128, 256], BF16) for i in range(2)], "kTs")
    vsb = Rot([sb("vsb%d" % i, [128, 260], BF16) for i in range(2)], "vsb")
    NSL = 2 * QG
    gates = Rot([sb("gates%d" % i, [128, D], BF16) for i in range(NSL)], "gates")
    mixo = Rot([sb("mixo%d" % i, [128, D], BF16) for i in range(NSL)], "mixo")
    uvg = Rot([sb("uvg%d" % i, [128, 512], BF16) for i in range(1)], "uvg")
    gqT = Rot([sb("gqT%d" % i, [128, 2, 2, GQ], BF16) for i in range(2)], "gqT")
    sqT = Rot([sb("sqT%d" % i, [128, 2, 2, 128], BF16) for i in range(2)], "sqT")
    pT = Rot([sb("pT%d" % i, [128, 512], BF16) for i in range(3)], "pT")
    oT = Rot([sb("oT%d" % i, [65, 512], F32) for i in range(2)], "oT")
    rden = Rot([sb("rden%d" % i, [128, 4], F32) for i in range(4)], "rden")
    mixTt = Rot([sb("mixTt%d" % i, [128, 8, 128], BF16) for i in range(1)], "mixTt")
    otmp = Rot([sb("otmp%d" % i, [128, D], F32) for i in range(1)], "otmp")
    wmst = Rot([otmp.tiles[0][:].rearrange("p (k c) -> p k c", k=8)], "wmst")
    wmst.bufs = otmp.bufs
    yinl = Rot([sb("yinl%d" % i, [128, 256], F32) for i in range(1)], "yinl")
    qfbl = Rot([sb("qfbl%d" % i, [128, 512], BF16) for i in range(1)], "qfbl")
    ysum = Rot([sb("ysum%d" % i, [128, 256], F32) for i in range(2)], "ysum")
    wk = Rot([sb("wk%d" % i, [128, 6, 128], BF16) for i in range(1)], "wk")
    wv = Rot([sb("wv%d" % i, [128, 6, 130], BF16) for i in range(1)], "wv")
    wp = Rot([sb("wp%d" % i, [128, 256], BF16) for i in range(3)], "wp")
    woT = Rot([sb("woT%d" % i, [65, 256], F32) for i in range(1)], "woT")
    zps = Rot([ps("zps%d" % i, [128, 512]) for i in range(2)], "zps")
    tps = Rot([ps("tps%d" % i, [128, 1024], BF16) for i in range(1)], "tps")
    sps = Rot([ps("sps%d" % i, [128, 512]) for i in range(3)], "sps")
    ops_ = Rot([ps("ops%d" % i, [128, 512]) for i in range(2)], "ops")
    zps1 = Rot([zps.tiles[0]], "zps1", bufs=[zps.bufs[0]])
    swacc = Rot([zps.tiles[1]], "swacc", bufs=[zps.bufs[1]])
    zpsP1 = Rot(zps.tiles + ops_.tiles, "zpsP1", bufs=zps.bufs + ops_.bufs)
    PS = {"z": zpsP1, "acc": ops_}

    def dma(q, out, in_, reads, writes, par=False, **kw):
        return P.op(q, lambda e: e.dma_start(out=out, in_=in_, **kw), reads, writes, kind="d", par=par)

    def mm(out, lhsT, rhs, start, stop, reads, writes):
        return P.op("pe", lambda e: e.matmul(out, lhsT=lhsT, rhs=rhs, start=start, stop=stop), reads, writes)

    def tr(out, in_, ident, reads, writes):
        return P.op("pe", lambda e: e.transpose(out, in_, ident), reads, writes)

    def act(out, in_, func, reads, writes, **kw):
        return P.op("act", lambda e: e.activation(out=out, in_=in_, func=func, **kw), reads, writes)

    def tt(eng, out, in0, in1, op, reads, writes):
        return P.op(eng, lambda e: e.tensor_tensor(out=out, in0=in0, in1=in1, op=op), reads, writes)

    def ts(eng, out, in0, s1, s2, op0, op1, reads, writes):
        if op1 is None:
            return P.op(eng, lambda e: e.tensor_scalar(out=out, in0=in0, scalar1=s1, scalar2=None, op0=op0), reads, writes)
        return P.op(eng, lambda e: e.tensor_scalar(out=out, in0=in0, scalar1=s1, scalar2=s2, op0=op0, op1=op1), reads, writes)

    def stt(out, in0, scalar, in1, op0, op1, reads, writes):
        return P.op("dve", lambda e: e.scalar_tensor_tensor(out=out, in0=in0, scalar=scalar, in1=in1, op0=op0, op1=op1), reads, writes)

    def cp(eng, out, in_, reads, writes):
        if eng == "act":
            return P.op("act", lambda e: e.copy(out=out, in_=in_), reads, writes)
        return P.op(eng, lambda e: e.tensor_copy(out=out, in_=in_), reads, writes)

    def rsqrt_mean_g(out, ssum, n, reads, writes):
        ts("dve", out, ssum, 1.0 / n, EPS, ALU.mult, ALU.add, reads, writes)
        yield
        act(out, out, AF.Ln, writes, writes)
        yield
        act(out, out, AF.Exp, writes, writes, scale=-0.5)
        yield

    def rsqrt_mean(out, ssum, n, reads, writes):
        ts("dve", out, ssum, 1.0 / n, EPS, ALU.mult, ALU.add, reads, writes)
        act(out, out, AF.Ln, writes, writes)
        act(out, out, AF.Exp, writes, writes, scale=-0.5)

    def bc(ap, n):
        return ap.partition_broadcast(n)

    for dst, src in ((identb, identb_in), (identf, identf_in), (rpn, rpn_in), (pos, pos_in), (etab, etab_in),
                     (hmask, hmask_in), (cmask, cmask_in), (cT, cT_in), (gainT, gainT_in), (bmodT, bmodT_in),
                     (mbias, mbias_in)):
        dma("sp", dst[:], src, [B_const], [B_cb], par=True)
    dma("sp", rdec[:], rdec_in.partition_broadcast(128), [B_const], [B_cb], par=True)
    for l in range(depth):
        dma("sp", rdsel[0:64, l * 4:(l + 1) * 4], rdec_in[:, l * 8:l * 8 + 4].partition_broadcast(64), [B_const], [B_cb], par=True)
        dma("sp", rdsel[64:128, l * 4:(l + 1) * 4], rdec_in[:, l * 8 + 4:l * 8 + 8].partition_broadcast(64), [B_const], [B_cb], par=True)
    dma("sp", sink[:], sink_in.partition_broadcast(128), [B_const], [B_cb], par=True)
    for c in range(CTXC):
        dma("sp", xc[:, c, :], ctx_in[c * 128:(c + 1) * 128, :], [B_const], [B_xc[c]])
    B_c2 = Buf("const2")
    P.op("dve", lambda e: e.memset(ones1[:], 1.0), [], [B_c2])
    for rot_ in (gqT, sqT, rT):
        for i in range(len(rot_.tiles)):
            P.op("pool", lambda e, t_=rot_.tiles[i]: e.memset(t_[:], 0.0), [], [rot_.bufs[i]])
    P.op("dve", lambda e: e.memset(V[:], 1.0), [], [B_V])
    P.op("dve", lambda e: e.memset(sVc[:], 1.0), [], [B_sVc])
    for i in range(2):
        P.op("pool", lambda e, i=i: e.memset(vsb.tiles[i][:], 1.0), [], [vsb.bufs[i]])
    act(cT[:], cT[:], AF.Silu, [B_cb], [B_cb])

    def load_w1(l):
        srcs = [(768, 1536, 0), (2048, 2176, 768), (2816, 2944, 896), (2176, 2304, 1024), (2944, 3072, 1152)]
        for k in range(8):
            for (a, b_, d0) in srcs:
                dma("pool", WA[:, k, d0:d0 + (b_ - a)], w_in[l, k * 128:(k + 1) * 128, a:b_], [B_const], [B_WA], par=True)

    def load_w2(l):
        srcs = [(0, 768, 0), (1536, 1792, 768), (2304, 2560, 1024), (3072, 3328, 1280), (1792, 2048, 1536), (2560, 2816, 1792)]
        for k in range(8):
            for (a, b_, d0) in srcs:
                dma("pool", W2[:, k, d0:d0 + (b_ - a)], w_in[l, k * 128:(k + 1) * 128, a:b_], [B_const], [B_W2], par=True)
        dma("pool", mixT[:], mixT_in[l], [B_const], [B_mixT])

    def load_wo(l):
        for k in range(8):
            dma("pool", WA[:, k, 0:1024], w_out[l, k * 128:(k + 1) * 128, :], [B_const], [B_WA], par=True)

    def layer_consts(l):
        rd = [B_cb, B_c2]
        w = [B_lay]
        dma("sp", rnorm[:], rnorm_in[:, l * 256:(l + 1) * 256].partition_broadcast(128), [B_const], [B_rq])
        dma("sp", qkn[:], qkn_in[:, l * 256:(l + 1) * 256].partition_broadcast(128), [B_const], [B_rq], par=True)
        act(lg[:], rdec[:, l * 8:(l + 1) * 8], AF.Exp, rd, w)
        ts("dve", lg[:], lg[:], -1.0, None, ALU.mult, None, w, w)
        act(lgsel[:], rdsel[:, l * 4:(l + 1) * 4], AF.Exp, rd, w)
        ts("dve", lgsel[:], lgsel[:], -1.0, None, ALU.mult, None, w, w)
        for dr in range(2):
            ts("dve", kd[:, dr * 4:(dr + 1) * 4], lg[:, dr * 4:(dr + 1) * 4], pos[:, dr:dr + 1], None, ALU.mult, None, rd + w, w)
            ts("dve", qd[:, dr * 4:(dr + 1) * 4], lg[:, dr * 4:(dr + 1) * 4], pos[:, 2 + dr:3 + dr], None, ALU.mult, None, rd + w, w)
        act(kd[:], kd[:], AF.Exp, w, w)
        ts("dve", kd[:], kd[:], SCALE, None, ALU.mult, None, w, w)
        act(qd[:], qd[:], AF.Exp, w, w)
        for h in range(4):
            ts("dve", DecT[:, h * 128:(h + 1) * 128], rpn[:, 0:128], lg[:, h:h + 1], None, ALU.mult, None, rd + w, w)
            stt(DecT[:, h * 128:(h + 1) * 128], rpn[:, 128:256], lg[:, 4 + h:5 + h], DecT[:, h * 128:(h + 1) * 128],
                ALU.mult, ALU.add, rd + w, w)
        act(DecT[:], DecT[:], AF.Exp, w, w)
        ts("dve", DecT[:], DecT[:], SCALE, None, ALU.mult, None, w, w)
        ts("dve", esink[:], lgsel[:], 128.0, None, ALU.mult, None, w, w)
        act(esink[:], esink[:], AF.Exp, w, w)
        cp("dve", Dt[:].rearrange("p (h e) -> p h e", h=4), esink[:].unsqueeze(2).to_broadcast([128, 4, 64]), w, w)
        for s_ in range(5):
            ts("dve", coef[:, s_, :], lgsel[:], etab[:, s_:s_ + 1], 128.0 * NCH, ALU.mult, ALU.mult, rd + w, w)
        act(coef[:], coef[:], AF.Exp, w, w)
        act(esink[:], sink[:, l * 4:(l + 1) * 4], AF.Exp, rd + w, w)
        P.op("pool", lambda e: e.memset(Aagg[:], 0.0), [], [B_Aagg])
        P.op("pool", lambda e: e.memset(Actx[:], 0.0), [], [B_Actx])
        P.op("pool", lambda e: e.memset(Pw[:], 1.0), [], [B_Pw])
        P.op("pool", lambda e: e.memset(STc[0:64, 1, :], 0.0), [], [B_STc[1]], par=True)
        P.op("pool", lambda e: e.memset(STc[64:128, 2, :], 0.0), [], [B_STc[2]], par=True)

    def mod_compute(l):
        dma("sp", bgate[:], bgate_in[:, l * D:(l + 1) * D], [B_const], [B_bg])
        mp, mb = zps.next()
        for half in range(-1, 2):
            if half < 0:
                for c in range(16):
                    wt, wb = wmst.next()
                    dma("sp", wt[:], w_mod[l, :, c * 128:(c + 1) * 128].rearrange("(k p) c -> p k c", p=128), [B_const], [wb])
                    for k in range(8):
                        mm(mp[:, c * 2:c * 2 + 2], wt[:, k, :], cT[:, 2 * k:2 * k + 2], k == 0, k == 7, [wb, B_cb], [mb])
                continue
            g0, gb0 = sps.next()
            g1, gb1 = sps.next()
            gps = ((g0, gb0), (g1, gb1))
            for q in range(4):
                col = 2048 + half * 512 + q * 128
                wt, wb = wmst.next()
                dma("sp", wt[:], w_mod[l, :, col:col + 128].rearrange("(k p) c -> p k c", p=128), [B_const], [wb])
                for j in range(2):
                    for k in range(8):
                        mm(gps[j][0][0:1, q * 128:(q + 1) * 128], cT[:, 2 * k + j:2 * k + j + 1], wt[:, k, :], k == 0, k == 7, [wb, B_cb], [gps[j][1]])
            for j in range(2):
                tt("dve", grow[:], gps[j][0][0:1, 0:512], bgate[0:1, half * 512:(half + 1) * 512], ALU.add, [gps[j][1], B_bg], [B_grow])
                zp, zb = ops_.next()
                mm(zp[:, 0:512], ones1[0:1, :], grow[0:1, :], True, True, [B_c2, B_grow], [zb])
                cp("dve", gateB[:, j, half * 512:(half + 1) * 512], zp[:, 0:512], [zb], [B_mod], )
        mpv = mp[:, 0:32].rearrange("p (c j) -> p j c", j=2)
        for j in range(2):
            tt("dve", Sm[:, j, :], mpv[:, j, 0:8], bmodT[:, l * 16:l * 16 + 8], ALU.add, [mb, B_cb], [B_mod])
            tt("dve", Gm[:, j, :], mpv[:, j, 8:16], bmodT[:, l * 16 + 8:l * 16 + 16], ALU.add, [mb, B_cb], [B_mod])
            stt(Gm[:, j, :], Gm[:, j, :], 1.0, gainT[:, l * 8:(l + 1) * 8], ALU.add, ALU.mult, [B_mod, B_cb], [B_mod])

    def norm_tile(l, xap, xbuf, j):
        s4, s4b = st4.next()
        xnt, xnb = xn.next()
        P.op("dve", lambda e: e.scalar_tensor_tensor(out=xnt[:], in0=xap, scalar=1.0, in1=xap, op0=ALU.mult, op1=ALU.mult,
                                                      accum_out=s4[:, 0:1]), [xbuf], [xnb, s4b])
        rsqrt_mean(s4[:, 0:1], s4[:, 0:1], D, [s4b], [s4b])
        ts("dve", xnt[:], xap, s4[:, 0:1], None, ALU.mult, None, [xbuf, s4b], [xnb])
        tp, tb = tps.next()
        for k in range(8):
            tr(tp[:, k * 128:(k + 1) * 128], xnt[:, k * 128:(k + 1) * 128], identb[:], [xnb, B_cb], [tb])
        ht, hb = hT.next()
        for k in range(8):
            ts("dve", ht[:, k, :], tp[:, k * 128:(k + 1) * 128], Gm[:, j, k:k + 1], Sm[:, j, k:k + 1], ALU.mult, ALU.add,
               [tb, B_mod], [hb])
        return ht, hb

    def norm_tile_g(l, xap, xbuf, j):
        s4, s4b = st4.next()
        xnt, xnb = xn.next()
        P.op("dve", lambda e: e.scalar_tensor_tensor(out=xnt[:], in0=xap, scalar=1.0, in1=xap, op0=ALU.mult, op1=ALU.mult,
                                                      accum_out=s4[:, 0:1]), [xbuf], [xnb, s4b])
        yield
        yield from rsqrt_mean_g(s4[:, 0:1], s4[:, 0:1], D, [s4b], [s4b])
        ts("dve", xnt[:], xap, s4[:, 0:1], None, ALU.mult, None, [xbuf, s4b], [xnb])
        yield
        tp, tb = tps.next()
        for k in range(8):
            tr(tp[:, k * 128:(k + 1) * 128], xnt[:, k * 128:(k + 1) * 128], identb[:], [xnb, B_cb], [tb])
        yield
        yield
        ht, hb = hT.next()
        for k in range(8):
            ts("dve", ht[:, k, :], tp[:, k * 128:(k + 1) * 128], Gm[:, j, k:k + 1], Sm[:, j, k:k + 1], ALU.mult, ALU.add,
               [tb, B_mod], [hb])
        yield
        yield
        return ht, hb

    def inproj(ht, hb, W, WB, c0, ncols):
        zp, zb = PS["z"].next()
        for k in range(8):
            mm(zp[:, 0:ncols], ht[:, k, :], W[:, k, c0:c0 + ncols], k == 0, k == 7, [hb, WB], [zb])
        return zp, zb

    def qk_norm_rope(src, srcb, nh, gain_ap, csb, rope):
        n = nh * 64
        Bt, Bb = t1.next()
        Ct, Cb = t2.next()
        s4, s4b = st4.next()
        v3 = lambda ap: ap[:, 0:n].rearrange("p (h d) -> p h d", d=64)
        tt("dve", Bt[:, 0:n], src[:, 0:n], src[:, 0:n], ALU.mult, [srcb], [Bb])
        yield
        P.op("dve", lambda e: e.tensor_reduce(out=s4[:, 0:nh], in_=v3(Bt), axis=AX.X, op=ALU.add), [Bb], [s4b])
        yield
        yield from rsqrt_mean_g(s4[:, 0:nh], s4[:, 0:nh], 64, [s4b], [s4b])
        tt("dve", v3(Ct), v3(src), s4[:, 0:nh].unsqueeze(2).to_broadcast([128, nh, 64]), ALU.mult, [srcb, s4b], [Cb])
        tt("dve", Ct[:, 0:n].rearrange("p (a h d) -> p a h d", a=2, d=64), Ct[:, 0:n].rearrange("p (a h d) -> p a h d", a=2, d=64),
           gain_ap.rearrange("p (a d) -> p a d", a=2).unsqueeze(2).to_broadcast([128, 2, nh // 2, 64]), ALU.mult, [Cb, B_rq], [Cb])
        yield
        ot, ob = knb.next()
        if not rope:
            cp("dve", ot[:, 0:n], Ct[:, 0:n], [Cb], [ob])
            return ot, ob
        cst, csbuf = csb
        v4 = lambda ap: ap[:, 0:n].rearrange("p (h a b c) -> p h a b c", a=2, b=2, c=16)
        cosb = cst[:, 0, :].unsqueeze(1).to_broadcast([128, nh, 64])
        sin4 = cst[:, 1, :].rearrange("p (a b c) -> p a b c", a=2, b=2)
        tt("dve", v3(Bt), v3(Ct), cosb, ALU.mult, [Cb, csbuf], [Bb])
        yield
        for b0 in range(2):
            tt("pool", v4(src)[:, :, :, b0, :], v4(Ct)[:, :, :, 1 - b0, :],
               sin4[:, :, b0, :].unsqueeze(1).to_broadcast([128, nh, 2, 16]), ALU.mult, [Cb, csbuf], [srcb], )
        yield
        yield
        tt("dve", ot[:, 0:n], Bt[:, 0:n], src[:, 0:n], ALU.add, [Bb, srcb], [ob])
        yield
        return ot, ob

    def ret_state(is_ctx):
        return (STc, B_STc, Actx, B_Actx) if is_ctx else (ST, B_ST, Aagg, B_Aagg)

    def p1_tile(l, t, xsrc, B_xsrc):
        is_ctx = t < CTXC
        c = t if is_ctx else t - CTXC
        j = 1 if is_ctx else 0
        if is_ctx:
            xap, xbuf = xc[:, c, :], B_xc[c]
            csb = None
        else:
            xt_, xbuf = xt.next()
            dma("sp", xt_[:], xsrc[c * 128:(c + 1) * 128, :], [B_xsrc], [xbuf])
            xap = xt_[:]
            cst, csbuf = cs.next()
            dma("sp", cst[:, 0, :], cos_in[c], [B_const], [csbuf])
            dma("sp", cst[:, 1, :], sin_in[c], [B_const], [csbuf], par=True)
            csb = (cst, csbuf)
        yield
        ht, hb = norm_tile(l, xap, xbuf, j)
        yield
        rq, rqb = rqkv.next()
        kr, krb = kraw.next()
        vs, vsbuf = vsb.next()
        zp, zb = inproj(ht, hb, WA, B_WA, 0, 512)
        cp("act", rq[:, 0:512], zp[:, 0:512], [zb], [rqb])
        yield
        zp, zb = inproj(ht, hb, WA, B_WA, 512, 512)
        cp("dve", rq[:, 512:768], zp[:, 0:256], [zb], [rqb], )
        cp("dve", kr[:, 0:256], zp[:, 256:512], [zb], [krb])
        yield
        zp, zb = inproj(ht, hb, WA, B_WA, 1024, 256)
        cp("dve", vs[:, 1:129], zp[:, 0:128], [zb], [vsbuf])
        cp("dve", vs[:, 131:259], zp[:, 128:256], [zb], [vsbuf])
        yield
        fbt, fbb = fb.next()
        for w_, dec in ((0, qd), (1, kd)):
            for dr in range(2):
                tt("dve", fbt[:, w_, :, dr, :], rq[:, w_ * 256:(w_ + 1) * 256].rearrange("p (h d) -> p h d", h=4),
                   dec[:, dr * 4:(dr + 1) * 4].unsqueeze(2).to_broadcast([128, 4, 64]), ALU.mult, [rqb, B_lay], [fbb], )
        yield
        tp, tb = tps.next()
        for w_ in range(2):
            for h in range(4):
                tr(tp[:, (w_ * 4 + h) * 128:(w_ * 4 + h + 1) * 128], fbt[:, w_, h, :, :].rearrange("p a d -> p (a d)"), identb[:],
                   [fbb, B_cb], [tb])
        fT, fTb = fbT.next()
        cp("dve", fT[:].rearrange("p a h t -> p (a h t)"), tp[:, 0:1024], [tb], [fTb])
        dma("pool", qfb_d[t], fT[:, 0, :, :].rearrange("p h t -> p (h t)"), [fTb], [B_qfb], par=True)
        yield
        tp, tb = tps.next()
        for w_ in range(2):
            for pr in range(2):
                tr(tp[:, (w_ * 2 + pr) * 128:(w_ * 2 + pr + 1) * 128], rq[:, w_ * 256 + pr * 128:w_ * 256 + (pr + 1) * 128], identb[:],
                   [rqb, B_cb], [tb])
        rt, rtb = rT.next()
        cp("act", rt[0:64, 0, :, :], tp[0:64, 0:256].rearrange("p (b t) -> p b t", b=2), [tb], [rtb], )
        cp("act", rt[64:128, 1, :, :], tp[64:128, 0:256].rearrange("p (b t) -> p b t", b=2), [tb], [rtb], )
        cp("act", rt[:, 2, :, :], tp[:, 256:512].rearrange("p (b t) -> p b t", b=2), [tb], [rtb], )
        sp0, sb0 = sps.next()
        sp1, sb1 = sps.next()
        for pr in range(2):
            mm(sp0[:, pr * 128:(pr + 1) * 128], rt[:, 2, pr, :], rt[:, 0, pr, :], True, True, [rtb], [sb0])
            mm(sp1[:, pr * 128:(pr + 1) * 128], rt[:, 2, pr, :], rt[:, 1, pr, :], True, True, [rtb], [sb1])
        pi, pib = pint.next()
        piv = pi[:].rearrange("p (a b i) -> p a b i", a=2, b=2)
        dcv = DecT[:].rearrange("p (a b i) -> p a b i", a=2, b=2)
        for hh, (spx, sbx) in enumerate(((sp0, sb0), (sp1, sb1))):
            tt("dve", piv[:, :, hh, :], spx[:, 0:256].rearrange("p (a i) -> p a i", a=2), dcv[:, :, hh, :], ALU.mult,
               [sbx, B_lay], [pib], )
        mp, mb = PS["z"].next()
        for h in range(4):
            mm(mp[:, h * 64:(h + 1) * 64], pi[:, h * 128:(h + 1) * 128], rq[:, 512 + h * 64:512 + (h + 1) * 64], True, True, [pib, rqb], [mb])
        for h in range(4):
            mm(mp[:, 256 + h * 64:256 + (h + 1) * 64], fbt[:, 1, h, :, :].rearrange("p a d -> p (a d)"),
               rq[:, 512 + h * 64:512 + (h + 1) * 64], True, True, [fbb, rqb], [mb])
        yield
        yt, ytb = yint.next()
        cp("dve", yt[:], mp[:, 0:256], [mb], [ytb])
        row0 = t * 128
        dma("pool", yin_d[row0:row0 + 128, :], yt[:], [ytb], [B_yin], par=True)
        Sx, BS, Ax, BA = ret_state(is_ctx)
        cp("dve", Sx[0:64, c + 2, :], mp[0:64, 256:512], [mb], [BS[c + 2]], )
        cp("dve", Sx[64:128, c, :], mp[64:128, 256:512], [mb], [BS[c]], )
        yield
        tt("pool", Ax[0:64, :], Ax[0:64, :], Dt[0:64, :], ALU.mult, [BA, B_lay], [BA])
        tt("pool", Ax[0:64, :], Ax[0:64, :], Sx[0:64, c + 2, :], ALU.add, [BA, BS[c + 2]], [BA])
        if is_ctx:
            if c == 0:
                tt("pool", Ax[64:128, :], Ax[64:128, :], Sx[64:128, c, :], ALU.add, [BA, BS[c]], [BA])
            else:
                tt("pool", sintmp[64:128, :], Dt[64:128, :], Sx[64:128, c, :], ALU.mult, [B_lay, BS[c]], [B_sintmp])
                tt("pool", Ax[64:128, :], Ax[64:128, :], sintmp[64:128, :], ALU.add, [BA, B_sintmp], [BA])
        else:
            tt("pool", sintmp[64:128, :], Pw[64:128, :], Sx[64:128, c, :], ALU.mult, [B_Pw, BS[c]], [B_sintmp])
            tt("pool", Ax[64:128, :], Ax[64:128, :], sintmp[64:128, :], ALU.add, [BA, B_sintmp], [BA])
            tt("pool", Pw[64:128, :], Pw[64:128, :], Dt[64:128, :], ALU.mult, [B_Pw, B_lay], [B_Pw])
        yield
        gk_gain = qkn[:, 128:256]
        kn, knbuf = yield from qk_norm_rope(kr, krb, 4, gk_gain, csb, not is_ctx)
        yield
        tp, tb = tps.next()
        for a in range(2):
            tr(tp[:, a * 128:(a + 1) * 128], kn[:, a * 128:(a + 1) * 128], identb[:], [knbuf, B_cb], [tb])
        if is_ctx:
            cp("act", KT[:, c * 128:(c + 1) * 128], tp[:, 0:128], [tb], [B_KT], )
            cp("act", skTc[:, c * 128:(c + 1) * 128], tp[:, 128:256], [tb], [B_skTc], )
            cp("dve", V[:, c, 1:129], vs[:, 1:129], [vsbuf], [B_V])
            cp("dve", sVc[:, c, 1:129], vs[:, 131:259], [vsbuf], [B_sVc])
        else:
            kt_, ktb = kTs.next()
            cp("act", kt_[:], tp[:, 0:256], [tb], [ktb])
            dma("pool", gk_x[c // GP][:, (c % GP) * 128:(c % GP + 1) * 128], kt_[:, 0:128], [ktb], [B_gkx[c // GP]], par=True)
            dma("pool", sk_d[c], kt_[:, 128:256], [ktb], [B_skd], par=True)
            dma("pool", gv_x[c // GP][:, (c % GP) * 130:(c % GP + 1) * 130], vs[:, 0:130], [vsbuf], [B_gvx[c // GP]], par=True)
            dma("pool", sv_d[c], vs[:, 130:260], [vsbuf], [B_svd], par=True)
            if c == 0:
                dma("pool", bnd_x[:, 0:128], kt_[:, 128:256], [ktb], [B_bndx], par=True)
                dma("pool", bnd_x[:, 256:386], vs[:, 130:260], [vsbuf], [B_bndx], par=True)
            if c == NCH - 1:
                dma("pool", bnd_x[:, 128:256], kt_[:, 128:256], [ktb], [B_bndx], par=True)
                dma("pool", bnd_x[:, 386:516], vs[:, 130:260], [vsbuf], [B_bndx], par=True)

    def exchange(l):
        dma("pool", agg_x, Aagg[:], [B_Aagg], [B_aggx])
        groups = [[0, 1, 2, 3], [4, 5, 6, 7]]
        ccl = [(bnd_x, B_bndx, bnd_all, B_bndall), (agg_x, B_aggx, agg_all, B_aggall)]
        for i in range(NGP):
            ccl.append((gk_x[i], B_gkx[i], gk_all[i], B_gkall[i]))
            ccl.append((gv_x[i], B_gvx[i], gv_all[i], B_gvall[i]))
        for (src, bs, dst, bd) in ccl:
            P.op("pool", lambda e, src=src, dst=dst: e.collective_compute("AllGather", ALU.bypass, replica_groups=groups,
                                                                          ins=[src], outs=[dst]), [bs], [bd], kind="cc")

    def exchange_b(l):
        dma("sp", aggs[:], agg_all.rearrange("(r p) c -> p r c", p=128), [B_aggall], [B_aggs])
        B_s2 = Buf("s2")
        v3 = lambda ap: ap.rearrange("p (h e) -> p h e", h=4)
        cb = lambda s_: coef[:, s_, :].unsqueeze(2).to_broadcast([128, 4, 64])
        tt("pool", v3(sintmp[:]), v3(Actx[:]), cb(4), ALU.mult, [B_Actx, B_lay], [B_sintmp])
        for r in range(R):
            tt("pool", v3(aggs[:, r, :]), v3(aggs[:, r, :]), cb(r), ALU.mult, [B_aggs, B_lay], [B_aggs])
            tt("pool", sintmp[:], sintmp[:], aggs[:, r, :], ALU.add, [B_sintmp, B_aggs], [B_sintmp])
        cp("pool", ST[0:64, 1, :], sintmp[0:64, :], [B_sintmp], [B_ST[1]], )
        cp("pool", ST[64:128, NCH, :], sintmp[64:128, :], [B_sintmp], [B_ST[NCH]], )
        for i in range(1, NCH):
            cf = i
            tt("dve", sintmp[0:64, :], sintmp[0:64, :], Dt[0:64, :], ALU.mult, [B_sintmp, B_lay], [B_sintmp])
            tt("dve", sintmp[0:64, :], sintmp[0:64, :], ST[0:64, cf + 1, :], ALU.add, [B_sintmp, B_ST[cf + 1]], [B_sintmp])
            cp("dve", ST[0:64, cf + 1, :], sintmp[0:64, :], [B_sintmp], [B_ST[cf + 1]])
            cbk = NCH - 1 - i
            tt("dve", sintmp[64:128, :], sintmp[64:128, :], Dt[64:128, :], ALU.mult, [B_sintmp, B_lay], [B_sintmp])
            tt("dve", sintmp[64:128, :], sintmp[64:128, :], ST[64:128, cbk + 1, :], ALU.add, [B_sintmp, B_ST[cbk + 1]], [B_sintmp])
            cp("dve", ST[64:128, cbk + 1, :], sintmp[64:128, :], [B_sintmp], [B_ST[cbk + 1]])

    def small_attn(qT_ap, qTb, blocks, sink_l, mo, mob, gt, gtb, colbase):
        for g in range(2):
            op_, opb = PS["acc"].next()
            pend = []

            def pv(item):
                bi, vap, w_, wb_, bufs = item
                mm(op_[0:65, 0:256], vap[:, g * 65:(g + 1) * 65], w_[:], bi == 0, bi == len(blocks) - 1, [wb_] + bufs, [opb])

            for bi, (kap, vap, mask, bufs) in enumerate(blocks):
                sp_, spb = sps.next()
                mm(sp_[:, 0:256], kap, qT_ap[:, g, :, :].rearrange("p r t -> p (r t)"), True, True,
                   [qTb] + bufs, [spb])
                yield
                w_, wb_ = wp.next()
                act(w_[:], sp_[:, 0:256], AF.Exp, [spb], [wb_], scale=SCALE)
                yield
                if mask is not None:
                    tt("pool", w_[:].rearrange("p (r t) -> p r t", r=2), w_[:].rearrange("p (r t) -> p r t", r=2),
                       mask.unsqueeze(1).to_broadcast([128, 2, 128]), ALU.mult, [wb_, B_cb], [wb_])
                    yield
                pend.append((bi, vap, w_, wb_, bufs))
                if len(pend) > 1:
                    pv(pend.pop(0))
            for item in pend:
                pv(item)
            yield
            wo, wob = woT.next()
            cp("dve", wo[:], op_[0:65, 0:256], [opb], [wob])
            yield
            for r in range(2):
                h = g * 2 + r
                zp, zb = PS["z"].next()
                tr(zp[:, 0:65], wo[:, r * 128:(r + 1) * 128], identf[0:65, 0:65], [wob, B_cb], [zb])
                yield
                finish_head(zp, zb, g, h, sink_l, mo, mob, gt, gtb, colbase)
                yield

    def finish_head(zp, zb, g, h, sink_l, mo, mob, gt, gtb, colbase):
        rd_, rdb = rden.next()
        dcol = 0 if g == 0 else 64
        o0 = 1 if g == 0 else 0
        if sink_l:
            ts("dve", rd_[:, 0:1], zp[:, dcol:dcol + 1], esink[:, h:h + 1], None, ALU.add, None, [zb, B_lay], [rdb])
            P.op("dve", lambda e: e.reciprocal(out=rd_[:, 0:1], in_=rd_[:, 0:1]), [rdb], [rdb])
        else:
            P.op("dve", lambda e: e.reciprocal(out=rd_[:, 0:1], in_=zp[:, dcol:dcol + 1]), [zb], [rdb])
        stt(mo[:, colbase + h * 64:colbase + (h + 1) * 64], zp[:, o0:o0 + 64], rd_[:, 0:1], gt[:, colbase + h * 64:colbase + (h + 1) * 64],
            ALU.mult, ALU.mult, [zb, rdb, gtb], [mob])

    def silu_evac(dst, dstb, zp, zb):
        Ct, Cb = t2.next()
        act(Ct[:, 0:512], zp[:, 0:512], AF.Exp, [zb], [Cb], scale=-1.0)
        yield
        ts("dve", Ct[:, 0:512], Ct[:, 0:512], 1.0, None, ALU.add, None, [Cb], [Cb])
        yield
        P.op("dve", lambda e: e.reciprocal(out=Ct[:, 0:512], in_=Ct[:, 0:512]), [Cb], [Cb])
        yield
        yield
        tt("dve", dst, zp[:, 0:512], Ct[:, 0:512], ALU.mult, [zb, Cb], [dstb])

    def p2_front(l, t, xsrc, B_xsrc):
        is_ctx = t < CTXC
        c = t if is_ctx else t - CTXC
        j = 1 if is_ctx else 0
        if is_ctx:
            xap, xbuf = xc[:, c, :], B_xc[c]
            csb = None
        else:
            xt_, xbuf = xt.next()
            dma("sp", xt_[:], xsrc[c * 128:(c + 1) * 128, :], [B_xsrc], [xbuf])
            xap = xt_[:]
            cst, csbuf = cs.next()
            dma("sp", cst[:, 0, :], cos_in[c], [B_const], [csbuf])
            dma("sp", cst[:, 1, :], sin_in[c], [B_const], [csbuf], par=True)
            csb = (cst, csbuf)
        yl, ylb = yinl.next()
        dma("sp", yl[:], yin_d[t * 128:(t + 1) * 128, :], [B_yin], [ylb])
        ql, qlb = qfbl.next()
        dma("sp", ql[:], qfb_d[t], [B_qfb], [qlb])
        yield
        yield
        ht, hb = yield from norm_tile_g(l, xap, xbuf, j)
        ug, ugb = uvg.next()
        gt, gtb = gates.next()
        mo, mob = mixo.next()
        qr, qrb = kraw.next()
        zp, zb = inproj(ht, hb, W2, B_W2, 0, 512)
        yield
        yield
        act(ug[:], zp[:, 0:512], AF.Gelu, [zb], [ugb])
        yield
        zp, zb = inproj(ht, hb, W2, B_W2, 512, 512)
        yield
        yield
        yield from silu_evac(gt[:, 0:512], gtb, zp, zb)
        yield
        zp, zb = inproj(ht, hb, W2, B_W2, 1024, 512)
        yield
        yield
        yield from silu_evac(gt[:, 512:1024], gtb, zp, zb)
        yield
        zp, zb = inproj(ht, hb, W2, B_W2, 1536, 512)
        yield
        yield
        for blk in range(2):
            cp("dve", qr[:, blk * 256:(blk + 1) * 256].rearrange("p (r g d) -> p r g d", r=2, g=2),
               zp[:, blk * 256:(blk + 1) * 256].rearrange("p (g r d) -> p r g d", r=2, g=2), [zb], [qrb], )
        yield
        mp, mb = PS["z"].next()
        for h in range(4):
            mm(mp[:, h * 64:(h + 1) * 64], mixT[:, h * 128:(h + 1) * 128], ug[:, 256 + h * 64:256 + (h + 1) * 64], True, True, [B_mixT, ugb], [mb])
        Sx, BS, _, _ = ret_state(is_ctx)
        for h in range(4):
            mm(mp[:, 256 + h * 64:256 + (h + 1) * 64], ql[:, h * 128:(h + 1) * 128], Sx[:, c + 1, h * 64:(h + 1) * 64], True, True, [qlb, BS[c + 1]], [mb])
        yield
        yield
        ys, ysb = ysum.next()
        v3 = lambda ap: ap.rearrange("p (h d) -> p h d", h=4)
        tt("dve", v3(ys[:]), v3(mp[:, 0:256]), mbias[:, l * 4:(l + 1) * 4].unsqueeze(2).to_broadcast([128, 4, 64]), ALU.add, [mb, B_cb], [ysb])
        tt("dve", ys[:], ys[:], ug[:, 0:256], ALU.mult, [ysb, ugb], [ysb])
        yield
        tt("dve", mo[:, 0:256], ys[:], gt[:, 0:256], ALU.mult, [ysb, gtb], [mob])
        ys, ysb = ysum.next()
        tt("dve", ys[:], mp[:, 256:512], yl[:], ALU.add, [mb, ylb], [ysb])
        yield
        Bt, Bb = t1.next()
        s4, s4b = st4.next()
        tt("dve", Bt[:, 0:256], ys[:], ys[:], ALU.mult, [ysb], [Bb])
        yield
        P.op("dve", lambda e: e.tensor_reduce(out=s4[:, 0:4], in_=v3(Bt[:, 0:256]), axis=AX.X, op=ALU.add), [Bb], [s4b])
        yield
        yield from rsqrt_mean_g(s4[:, 0:4], s4[:, 0:4], 64, [s4b], [s4b])
        tt("dve", v3(ys[:]), v3(ys[:]), s4[:, 0:4].unsqueeze(2).to_broadcast([128, 4, 64]), ALU.mult, [ysb, s4b], [ysb])
        tt("dve", ys[:], ys[:], rnorm[:, 0:256], ALU.mult, [ysb, B_rq], [ysb])
        yield
        tt("dve", mo[:, 256:512], ys[:], gt[:, 256:512], ALU.mult, [ysb, gtb], [mob])
        yield
        qn, qnb = yield from qk_norm_rope(qr, qrb, 8, qkn[:, 0:128], csb, not is_ctx)
        yield
        tp, tb = tps.next()
        for a in range(4):
            tr(tp[:, a * 128:(a + 1) * 128], qn[:, a * 128:(a + 1) * 128], identb[:], [qnb, B_cb], [tb])
        yield
        yield
        return dict(t=t, c=c, is_ctx=is_ctx, gt=gt, gtb=gtb, mo=mo, mob=mob, tp=tp, tb=tb)

    def out_proj(l, st, xsrc, B_xsrc, xdst, B_xdst):
        t, c, is_ctx, mo, mob = st["t"], st["c"], st["is_ctx"], st["mo"], st["mob"]
        j = 1 if is_ctx else 0
        if is_ctx:
            xap, xbuf = xc[:, c, :], B_xc[c]
        else:
            xr_, xbuf = xr.next()
            dma("sp", xr_[:], xsrc[c * 128:(c + 1) * 128, :], [B_xsrc], [xbuf])
            xap = xr_[:]
        tp, tb = tps.next()
        for k in range(8):
            tr(tp[:, k * 128:(k + 1) * 128], mo[:, k * 128:(k + 1) * 128], identb[:], [mob, B_cb], [tb])
        yield
        yield
        mt, mtb = mixTt.next()
        cp("dve", mt[:].rearrange("p k t -> p (k t)"), tp[:, 0:1024], [tb], [mtb])
        yield
        yield
        ot, otb = otmp.next()
        for half in range(2):
            zp, zb = PS["z"].next()
            for k in range(8):
                mm(zp[:, 0:512], mt[:, k, :], WA[:, k, half * 512:(half + 1) * 512], k == 0, k == 7, [mtb, B_WA], [zb])
            yield
            yield
            yield
            tt("dve", ot[:, half * 512:(half + 1) * 512], zp[:, 0:512], gateB[:, j, half * 512:(half + 1) * 512], ALU.mult, [zb, B_mod], [otb], )
            yield
        if is_ctx:
            tt("pool", xc[:, c, :], xc[:, c, :], ot[:], ALU.add, [xbuf, otb], [xbuf])
        else:
            tt("pool", ot[:], ot[:], xap, ALU.add, [otb, xbuf], [otb])
            yield
            yield
            dma("pool", xdst[c * 128:(c + 1) * 128, :], ot[:], [otb], [B_xdst], par=True)
        yield

    def q_evac(st, q_, qb, lo, cols=None):
        for gg in range(2):
            dst = q_[gg * 64:(gg + 1) * 64, gg, :, :] if cols is None else q_[gg * 64:(gg + 1) * 64, gg, :, cols[0]:cols[1]]
            cp("dve", dst, st["tp"][gg * 64:(gg + 1) * 64, lo:lo + 256].rearrange("p (r t) -> p r t", r=2), [st["tb"]], [qb], )

    def p2_ctx(l):
        for c in range(CTXC):
            st = yield from p2_front(l, c, None, None)
            qT_, qTb = sqT.next()
            qT2, qT2b = sqT.next()
            q_evac(st, qT_, qTb, 0)
            q_evac(st, qT2, qT2b, 256)
            yield
            gblocks = [(KT[:, cc * 128:(cc + 1) * 128], V[:, cc, :], None, [B_KT, B_V]) for cc in range(CTXC)]
            yield from small_attn(qT_, qTb, gblocks, False, st["mo"], st["mob"], st["gt"], st["gtb"], 512)
            sblocks = [(skTc[:, cc * 128:(cc + 1) * 128], sVc[:, cc, :], None, [B_skTc, B_sVc]) for cc in range(CTXC)]
            yield from small_attn(qT2, qT2b, sblocks, True, st["mo"], st["mob"], st["gt"], st["gtb"], 768)
            yield from out_proj(l, st, None, None, None, None)

    def swa_tile(l, st, sq_, sqb):
        c = st["c"]
        wk_, wkb = wk.next()
        wv_, wvb = wv.next()
        blocks = []
        bb = [wkb, wvb]
        if c == 0:
            dma("sp", wk_[:, 0:2, :], sk_d[0:2].rearrange("c p k -> p c k"), [B_skd], [wkb])
            dma("sp", wv_[:, 0:2, :], sv_d[0:2].rearrange("c p k -> p c k"), [B_svd], [wvb])
            dma("sp", wk_[:, 2:6, :], bnd_all[:, 128:256].rearrange("(r p) k -> p r k", p=128), [B_bndall], [wkb], par=True)
            dma("sp", wv_[:, 2:6, :], bnd_all[:, 386:516].rearrange("(r p) k -> p r k", p=128), [B_bndall], [wvb], par=True)
            blocks.append((wk_[:, 0, :], wv_[:, 0, :], None, bb))
            blocks.append((wk_[:, 1, :], wv_[:, 1, :], cmask[:, 128:256], bb))
            for r in range(R):
                blocks.append((wk_[:, 2 + r, :], wv_[:, 2 + r, :], hmask[:, r * 128:(r + 1) * 128], bb))
        elif c == NCH - 1:
            dma("sp", wk_[:, 0:2, :], sk_d[c - 1:c + 1].rearrange("c p k -> p c k"), [B_skd], [wkb])
            dma("sp", wv_[:, 0:2, :], sv_d[c - 1:c + 1].rearrange("c p k -> p c k"), [B_svd], [wvb])
            dma("sp", wk_[:, 2:6, :], bnd_all[:, 0:128].rearrange("(r p) k -> p r k", p=128), [B_bndall], [wkb], par=True)
            dma("sp", wv_[:, 2:6, :], bnd_all[:, 256:386].rearrange("(r p) k -> p r k", p=128), [B_bndall], [wvb], par=True)
            blocks.append((wk_[:, 0, :], wv_[:, 0, :], cmask[:, 0:128], bb))
            blocks.append((wk_[:, 1, :], wv_[:, 1, :], None, bb))
            for r in range(R):
                blocks.append((wk_[:, 2 + r, :], wv_[:, 2 + r, :], hmask[:, (4 + r) * 128:(5 + r) * 128], bb))
        else:
            dma("sp", wk_[:, 0:3, :], sk_d[c - 1:c + 2].rearrange("c p k -> p c k"), [B_skd], [wkb])
            dma("sp", wv_[:, 0:3, :], sv_d[c - 1:c + 2].rearrange("c p k -> p c k"), [B_svd], [wvb])
            blocks.append((wk_[:, 0, :], wv_[:, 0, :], cmask[:, 0:128], bb))
            blocks.append((wk_[:, 1, :], wv_[:, 1, :], None, bb))
            blocks.append((wk_[:, 2, :], wv_[:, 2, :], cmask[:, 128:256], bb))
        for cc in range(CTXC):
            blocks.append((skTc[:, cc * 128:(cc + 1) * 128], sVc[:, cc, :], None, [B_skTc, B_sVc]))
        yield
        yield from small_attn(sq_, sqb, blocks, True, st["mo"], st["mob"], st["gt"], st["gtb"], 768)

    def front_group(l, gi, xsrc, B_xsrc, G):
        gq_, gqb = gqT.next()
        G["gq"] = (gq_, gqb)
        G["sts"] = []
        for ti in range(QG):
            st = yield from p2_front(l, CTXC + gi * QG + ti, xsrc, B_xsrc)
            sq_, sqb = sqT.next()
            q_evac(st, gq_, gqb, 0, (ti * 128, (ti + 1) * 128))
            q_evac(st, sq_, sqb, 256)
            yield
            yield from swa_tile(l, st, sq_, sqb)
            G["sts"].append(st)

    def sweep_group(G):
        gq_, gqb = G["gq"]
        accs = [ops_.next() for _ in range(2)]
        pend = []
        NW = 2 * GQ
        pieces = [(None, None)] + [(r, c0) for r in range(R) for c0 in range(0, NCH, PCS)]
        nblk_total = CTXC + R * NCH
        seen = 0

        def pv(item):
            first, last, g0, vap, vb, p0, pb0 = item
            mm(accs[g0][0][0:65, 0:NW], vap[:, g0 * 65:(g0 + 1) * 65], p0[:, 0:NW], first, last, [pb0] + vb, [accs[g0][1]])

        for (r, c0) in pieces:
            if r is None:
                nb = CTXC
                kget = lambda i: KT[:, i * 128:(i + 1) * 128]
                vget = lambda i: V[:, i, :]
                kbufs, vbufs = [B_KT], [B_V]
            else:
                nb = PCS
                kt_, ktb_ = ksl.next()
                vt_, vtb_ = vsl.next()
                gp_, of_ = c0 // GP, c0 % GP
                dma("sp", kt_[:], gk_all[gp_][r * 128:(r + 1) * 128, of_ * 128:(of_ + PCS) * 128], [B_gkall[gp_]], [ktb_])
                dma("sp", vt_[:], gv_all[gp_][r * 128:(r + 1) * 128, of_ * 130:(of_ + PCS) * 130].rearrange("p (c d) -> p c d", d=130), [B_gvall[gp_]], [vtb_])
                kget = lambda i, kt_=kt_: kt_[:, i * 128:(i + 1) * 128]
                vget = lambda i, vt_=vt_: vt_[:, i, :]
                kbufs, vbufs = [ktb_], [vtb_]
            for i in range(nb):
                first, last = seen == 0, seen == nblk_total - 1
                seen += 1
                for g in range(2):
                    sp_, spb = sps.next()
                    mm(sp_[:, 0:NW], kget(i), gq_[:, g, :, :].rearrange("p r t -> p (r t)"), True, True,
                       kbufs + [gqb], [spb])
                    p_, pb = pT.next()
                    act(p_[:, 0:NW], sp_[:, 0:NW], AF.Exp, [spb], [pb], scale=SCALE)
                    pend.append((first, last, g, vget(i), vbufs, p_, pb))
                    if len(pend) > 2:
                        pv(pend.pop(0))
                yield
        for item in pend:
            pv(item)
        G["oT"] = []
        for g in range(2):
            o_, ob_ = oT.next()
            cp("dve", o_[:, 0:NW], accs[g][0][0:65, 0:NW], [accs[g][1]], [ob_])
            G["oT"].append((o_, ob_))

    def tail_group(l, G, xsrc, B_xsrc, xdst, B_xdst):
        sts = G["sts"]
        for g in range(2):
            o_, ob_ = G["oT"][g]
            for r in range(2):
                h = 2 * g + r
                for ti in range(QG):
                    zp, zb = PS["z"].next()
                    tr(zp[:, 0:65], o_[:, r * GQ + ti * 128:r * GQ + (ti + 1) * 128], identf[0:65, 0:65], [ob_, B_cb], [zb])
                    yield
                    yield
                    finish_head(zp, zb, g, h, False, sts[ti]["mo"], sts[ti]["mob"], sts[ti]["gt"], sts[ti]["gtb"], 512)
                    yield
        for st in sts:
            yield from out_proj(l, st, xsrc, B_xsrc, xdst, B_xdst)

    def run(gen):
        try:
            while True:
                next(gen)
        except StopIteration as e:
            return e.value

    def gchain(*gens):
        for g_ in gens:
            yield from g_

    def pass2(l, xsrc, B_xsrc, xdst, B_xdst):
        PS["z"], PS["acc"] = zps1, swacc
        if l < depth - 1:
            run(p2_ctx(l))
        exchange_b(l)
        Gs = [dict() for _ in range(NG)]
        run(front_group(l, 0, xsrc, B_xsrc, Gs[0]))
        for k in range(NG):
            sides = []
            if k > 0:
                sides.append(tail_group(l, Gs[k - 1], xsrc, B_xsrc, xdst, B_xdst))
            if k + 1 < NG:
                sides.append(front_group(l, k + 1, xsrc, B_xsrc, Gs[k + 1]))
            side = gchain(*sides)
            alive = True
            for _ in sweep_group(Gs[k]):
                for _r in range(SIDE_RATE):
                    if alive:
                        try:
                            next(side)
                        except StopIteration:
                            alive = False
            if alive:
                run(side)
        run(tail_group(l, Gs[NG - 1], xsrc, B_xsrc, xdst, B_xdst))
        PS["z"], PS["acc"] = zpsP1, ops_

    chain = [(x_in, B_xin)]
    inter = [(xsA, B_xsA), (xsB, B_xsB)]
    for l in range(depth):
        chain.append((y_out, B_y) if l == depth - 1 else inter[l % 2])
    for l in range(depth):
        xsrc, B_xsrc = chain[l]
        xdst, B_xdst = chain[l + 1]
        load_w1(l)
        layer_consts(l)
        if stop == "consts":
            break
        mod_compute(l)
        if stop == "mod":
            break
        load_w2(l)
        if stop == "w2":
            break
        gens = [p1_tile(l, t, xsrc, B_xsrc) for t in range(CTXC + NCH)]
        active = []
        nxt = 0
        while nxt < len(gens) or active:
            if nxt < len(gens) and (not active or (len(active) < 2 and active[-1][1] >= P1LAG)):
                active.append([gens[nxt], 0])
                nxt += 1
            for a_ in list(active):
                try:
                    next(a_[0])
                    a_[1] += 1
                except StopIteration:
                    active.remove(a_)
        if stop == "p1":
            break
        exchange(l)
        if stop == "exch":
            break
        load_wo(l)
        pass2(l, xsrc, B_xsrc, xdst, B_xdst)

    P.finalize()
    with nc.Block() as block:
        @block.sync
        def _(e):
            P.emit("sp", e)

        @block.scalar
        def _(e):
            P.emit("act", e)

        @block.vector
        def _(e):
            P.emit("dve", e)

        @block.tensor
        def _(e):
            P.emit("pe", e)

        @block.gpsimd
        def _(e):
            P.emit("pool", e)
            if B_y.sem is not None:
                P.final_wait(e, [B_y])
            else:
                dummy = P.es.enter_context(nc.semaphore("dummy"))
                e.dma_start(out=y_out[0:128, :], in_=xc[:, 0, :]).then_inc(dummy, 16)
                e.wait_ge(dummy, 16)
    es.close()
    return nc


def host_inputs(inputs, NCH, depth=DEPTH, n_cores=8):
    f = np.float32
    x = np.asarray(inputs["x"], f)
    NT = NCH * 128
    bf = ml_dtypes.bfloat16
    c = np.asarray(inputs["c"], f)
    ctx = np.asarray(inputs["ctx"], f)
    c_ctx = np.asarray(inputs["c_ctx"], f)
    ng = np.asarray(inputs["norm_gain"], f)
    b_mod = np.asarray(inputs["b_mod"], f)
    common = {
        "gainT": np.ascontiguousarray(ng.reshape(depth, 8, 128).transpose(2, 0, 1).reshape(128, depth * 8)),
        "w_mod": np.ascontiguousarray(np.asarray(inputs["w_mod"], f)),
        "bmodT": np.ascontiguousarray(b_mod[:, 0:2048].reshape(depth, 16, 128).transpose(2, 0, 1).reshape(128, depth * 16)),
        "bgate": np.ascontiguousarray(b_mod[:, 2048:3072].reshape(1, depth * D)),
        "w_in": np.ascontiguousarray(np.asarray(inputs["w_in"], f)),
        "w_out": np.ascontiguousarray(np.asarray(inputs["w_out"], f)),
        "mixT": np.ascontiguousarray(np.asarray(inputs["mlp_mix"], f).transpose(0, 3, 1, 2).reshape(depth, 128, 512)),
        "mbias": np.ascontiguousarray(np.asarray(inputs["mlp_bias"], f).transpose(2, 0, 1).reshape(128, depth * 4)),
        "rdec": np.ascontiguousarray(np.stack([np.asarray(inputs["ret_decay_fwd"], f), np.asarray(inputs["ret_decay_bwd"], f)], 1).reshape(1, depth * 8)),
        "rnorm": np.ascontiguousarray(np.asarray(inputs["ret_norm"], f).reshape(1, depth * 256)),
        "qkn": np.ascontiguousarray(np.stack([np.asarray(inputs[k], f) for k in ("attn_q_norm", "swa_q_norm", "attn_k_norm", "swa_k_norm")], 1).reshape(1, depth * 256)),
        "sink": np.ascontiguousarray(np.asarray(inputs["swa_sink"], f).reshape(1, depth * 4)),
    }
    jj = np.arange(128, dtype=f)[:, None]
    ii = np.arange(128, dtype=f)[None, :]
    common["identb"] = np.eye(128, dtype=f).astype(bf)
    common["identf"] = np.eye(128, dtype=f)
    common["rpn"] = np.concatenate([np.maximum(ii - jj, 0), np.maximum(jj - ii, 0)], 1).astype(f)
    p = np.arange(128, dtype=f)
    common["pos"] = np.stack([127 - p, p, p + 1, 128 - p], 1).astype(f)
    mprev = (jj >= ii).astype(f)
    mnext = (jj <= ii).astype(f)
    common["cmask"] = np.concatenate([mprev, mnext], 1).astype(bf)
    half = 32
    inv_freq = (1.0 / (10000.0 ** (np.arange(0, half, 2, dtype=f) / f(half)))).astype(f)
    sgn = np.concatenate([-np.ones(16, f), np.ones(16, f), -np.ones(16, f), np.ones(16, f)])
    maps = []
    for core in range(n_cores):
        b, seg = core // R, core % R
        m = dict(common)
        m["x_in"] = np.ascontiguousarray(x[b, seg * NT:(seg + 1) * NT, :])
        m["ctx_in"] = np.ascontiguousarray(ctx[b])
        cT = np.zeros((128, 16), f)
        cT[:, 0::2] = c[b].reshape(8, 128).T
        cT[:, 1::2] = c_ctx.reshape(8, 128).T
        m["cT"] = cT
        tpos = seg * NT + np.arange(NT)
        row = (tpos // 64).astype(f)
        col = (tpos % 64).astype(f)
        ang_r = row[:, None] * inv_freq[None, :]
        ang_c = col[:, None] * inv_freq[None, :]
        ang = np.concatenate([ang_r, ang_r, ang_c, ang_c], -1).astype(f)
        m["cos"] = np.cos(ang).astype(f).reshape(NCH, 128, 64)
        m["sin"] = (np.sin(ang).astype(f) * sgn[None, :]).reshape(NCH, 128, 64)
        et = np.full((128, 5), BIGE, f)
        for r in range(R):
            if r < seg:
                et[0:64, r] = seg - 1 - r
            if r > seg:
                et[64:128, r] = r - seg - 1
        et[0:64, 4] = seg
        et[64:128, 4] = R - 1 - seg
        m["etab"] = et
        hm = np.zeros((128, 8, 128), f)
        if seg - 1 >= 0:
            hm[:, seg - 1, :] = mprev
        if seg + 1 < R:
            hm[:, 4 + seg + 1, :] = mnext
        m["hmask"] = hm.reshape(128, 1024).astype(bf)
        maps.append(m)
    return maps


_NC_CACHE = {}


def kernel(**inputs):
    x = np.asarray(inputs["x"])
    B, L, _ = x.shape
    NCH = L // R // 128
    depth = np.asarray(inputs["w_in"]).shape[0]
    key = (NCH, depth)
    if key not in _NC_CACHE:
        _NC_CACHE[key] = build(NCH, depth)
    nc = _NC_CACHE[key]
    maps = host_inputs(inputs, NCH, depth)
    res = run_bass_kernel_spmd(nc, maps, core_ids=list(range(8)))
    NT = NCH * 128
    out = np.zeros((B, L, D), np.float32)
    for core in range(8):
        b, seg = core // R, core % R
        out[b, seg * NT:(seg + 1) * NT, :] = res.results[core]["y"]
    return out
```

```python
import math
from contextlib import ExitStack
import numpy as np
import ml_dtypes
import concourse.bass as bass
import concourse.mybir as mybir
from concourse.bass_utils import run_bass_kernel_spmd

F32 = mybir.dt.float32
BF16 = mybir.dt.bfloat16
AF = mybir.ActivationFunctionType
ALU = mybir.AluOpType
AX = mybir.AxisListType

D = 1024
DEPTH = 4
CTXC = 2
R = 4
EPS = 1e-6
SCALE = 0.125
BIGE = 1.0e4


class Buf:
    def __init__(self, name):
        self.name = name
        self.last_w = []
        self.readers = []
        self.sem = None
        self.cnt = 0


class Op:
    __slots__ = ("eng", "fn", "deps", "kind", "sig", "val", "sem")

    def __init__(self, eng, fn, kind):
        self.eng, self.fn, self.kind = eng, fn, kind
        self.deps = []
        self.sig = False
        self.val = 0
        self.sem = None


class Prog:
    def __init__(self, nc, es):
        self.nc, self.es = nc, es
        self.ops = {k: [] for k in ("pe", "act", "dve", "pool", "sp")}
        self.esem = {k: es.enter_context(nc.semaphore("e_" + k)) for k in ("pe", "act", "dve", "pool")}
        self.nsem = 4

    def op(self, eng, fn, reads=(), writes=(), kind="c", par=False):
        o = Op(eng, fn, kind)
        deps = []
        for b in reads:
            deps += b.last_w
        for b in writes:
            deps += b.readers
            if not par:
                deps += b.last_w
        seen = set()
        for d in deps:
            if id(d) in seen or d is o:
                continue
            seen.add(id(d))
            if d.kind == "c" and d.eng == "pe" and eng == "pe" and kind == "c":
                continue
            d.sig = True
            o.deps.append(d)
        for b in reads:
            b.readers.append(o)
        for b in writes:
            if par:
                b.last_w = b.last_w + [o]
            else:
                b.last_w = [o]
            b.readers = []
        if kind in ("d", "cc"):
            b = writes[0]
            if b.sem is None:
                b.sem = self.es.enter_context(self.nc.semaphore("s_" + b.name))
                self.nsem += 1
            b.cnt += 16 if kind == "d" else 1
            o.sem, o.val = b.sem, b.cnt
        self.ops[eng].append(o)
        return o

    def finalize(self):
        for eng in ("pe", "act", "dve", "pool"):
            c = 0
            for o in self.ops[eng]:
                if o.kind == "c" and o.sig:
                    c += 1
                    o.val = c
                    o.sem = self.esem[eng]

    def emit(self, eng, e):
        waited = {}
        for o in self.ops[eng]:
            need = {}
            for d in o.deps:
                k = id(d.sem)
                if d.val > waited.get(k, 0) and (k not in need or d.val > need[k][1]):
                    need[k] = (d.sem, d.val)
            for k, (sem_, val_) in need.items():
                e.wait_ge(sem_, val_)
                waited[k] = val_
            ins = o.fn(e)
            if o.kind == "d":
                ins.then_inc(o.sem, 16)
            elif o.kind == "cc":
                ins.then_inc(o.sem, 1)
            elif o.sig:
                ins.then_inc(o.sem, 1)

    def final_wait(self, e, bufs):
        for b in bufs:
            e.wait_ge(b.sem, b.cnt)


class Rot:
    def __init__(self, tiles, name, bufs=None):
        self.tiles = tiles
        self.bufs = bufs if bufs is not None else [Buf("%s%d" % (name, i)) for i in range(len(tiles))]
        self.i = -1

    def next(self):
        self.i = (self.i + 1) % len(self.tiles)
        return self.tiles[self.i], self.bufs[self.i]


def build(NCH, depth=DEPTH, QG=2, stop=None, SIDE_RATE=2, P1LAG=9):
    NT = NCH * 128
    NTT = NT + CTXC * 128
    KB = CTXC + R * NCH
    KTOT = KB * 128
    NG = NCH // QG
    GQ = QG * 128
    PCS = min(NCH, 4)
    GP = min(NCH, 8)
    NGP = NCH // GP
    nc = bass.Bass("TRN2", target_bir_lowering=False)
    es = ExitStack()
    P = Prog(nc, es)

    def din(name, shape, dt=F32):
        return nc.dram_tensor(name, shape, dt, kind="ExternalInput").ap()

    def dint(name, shape, dt):
        return nc.dram_tensor(name, shape, dt, kind="Internal").ap()

    x_in = din("x_in", [NT, D])
    ctx_in = din("ctx_in", [CTXC * 128, D])
    cT_in = din("cT", [128, 16])
    gainT_in = din("gainT", [128, depth * 8])
    w_mod = din("w_mod", [depth, D, 3 * D])
    bmodT_in = din("bmodT", [128, depth * 16])
    bgate_in = din("bgate", [1, depth * D])
    w_in = din("w_in", [depth, D, 3328])
    w_out = din("w_out", [depth, D, D])
    mixT_in = din("mixT", [depth, 128, 512])
    mbias_in = din("mbias", [128, depth * 4])
    rdec_in = din("rdec", [1, depth * 8])
    rnorm_in = din("rnorm", [1, depth * 256])
    qkn_in = din("qkn", [1, depth * 256])
    sink_in = din("sink", [1, depth * 4])
    cos_in = din("cos", [NCH, 128, 64])
    sin_in = din("sin", [NCH, 128, 64])
    etab_in = din("etab", [128, 5])
    hmask_in = din("hmask", [128, 8 * 128], BF16)
    cmask_in = din("cmask", [128, 2 * 128], BF16)
    identb_in = din("identb", [128, 128], BF16)
    identf_in = din("identf", [128, 128])
    rpn_in = din("rpn", [128, 256])
    pos_in = din("pos", [128, 4])
    y_out = nc.dram_tensor("y", [NT, D], F32, kind="ExternalOutput").ap()

    xsA = dint("xsA", [NT, D], F32)
    xsB = dint("xsB", [NT, D], F32)
    yin_d = dint("yin_d", [NTT, 256], F32)
    qfb_d = dint("qfb_d", [NCH + CTXC, 128, 512], BF16)
    sk_d = dint("sk_d", [NCH, 128, 128], BF16)
    sv_d = dint("sv_d", [NCH, 128, 130], BF16)
    gk_x = [dint("gk_x%d" % i, [128, GP * 128], BF16) for i in range(NGP)]
    gk_all = [dint("gk_all%d" % i, [R * 128, GP * 128], BF16) for i in range(NGP)]
    gv_x = [dint("gv_x%d" % i, [128, GP * 130], BF16) for i in range(NGP)]
    gv_all = [dint("gv_all%d" % i, [R * 128, GP * 130], BF16) for i in range(NGP)]
    bnd_x = dint("bnd_x", [128, 516], BF16)
    bnd_all = dint("bnd_all", [R * 128, 516], BF16)
    agg_x = dint("agg_x", [128, 256], F32)
    agg_all = dint("agg_all", [R * 128, 256], F32)
    B_xin, B_xsA, B_xsB, B_y = Buf("xin"), Buf("xsA"), Buf("xsB"), Buf("y")
    B_yin, B_qfb, B_skd, B_svd = Buf("yin"), Buf("qfb"), Buf("skd"), Buf("svd")
    B_gkx = [Buf("gkx%d" % i) for i in range(NGP)]
    B_gkall = [Buf("gkall%d" % i) for i in range(NGP)]
    B_gvx = [Buf("gvx%d" % i) for i in range(NGP)]
    B_gvall = [Buf("gvall%d" % i) for i in range(NGP)]
    B_bndx, B_bndall, B_aggx, B_aggall = Buf("bndx"), Buf("bndall"), Buf("aggx"), Buf("aggall")
    B_const = Buf("constin")

    def sb(name, shape, dt=F32):
        return es.enter_context(nc.sbuf_tensor(name, shape, dt))

    def ps(name, shape, dt=F32):
        return es.enter_context(nc.psum_tensor(name, shape, dt))

    KT = sb("KTc", [128, CTXC * 128], BF16);     B_KT = Buf("KT")
    V = sb("Vc", [128, CTXC, 130], BF16);        B_V = Buf("V")
    ksl = Rot([sb("ksl%d" % i, [128, PCS * 128], BF16) for i in range(2)], "ksl")
    vsl = Rot([sb("vsl%d" % i, [128, PCS, 130], BF16) for i in range(2)], "vsl")
    skTc = sb("skTc", [128, CTXC * 128], BF16);  B_skTc = Buf("skTc")
    sVc = sb("sVc", [128, CTXC, 130], BF16);     B_sVc = Buf("sVc")
    ST = sb("ST", [128, NCH + 2, 256], BF16)
    B_ST = [Buf("ST%d" % i) for i in range(NCH + 2)]
    STc = sb("STc", [128, CTXC + 2, 256], BF16)
    B_STc = [Buf("STc%d" % i) for i in range(CTXC + 2)]
    WA = sb("WA", [128, 8, 1280], BF16);         B_WA = Buf("WA")
    W2 = sb("W2s", [128, 8, 2048], BF16);         B_W2 = Buf("W2")
    mixT = sb("mixTs", [128, 512], BF16);        B_mixT = Buf("mixT")
    xc = sb("xc", [128, CTXC, D], F32)
    B_xc = [Buf("xc%d" % i) for i in range(CTXC)]
    identb = sb("identb_s", [128, 128], BF16)
    identf = sb("identf_s", [128, 128], F32)
    rpn = sb("rpn_s", [128, 256], F32)
    pos = sb("pos_s", [128, 4], F32)
    etab = sb("etab_s", [128, 5], F32)
    hmask = sb("hmask_s", [128, 8 * 128], BF16)
    cmask = sb("cmask_s", [128, 256], BF16)
    cT = sb("cT_s", [128, 16], F32)
    gainT = sb("gainT_s", [128, depth * 8], F32)
    bmodT = sb("bmodT_s", [128, depth * 16], F32)
    bgate = sb("bgate_s", [1, D], F32);  B_bg = Buf("bg")
    grow = sb("grow", [1, 512], F32);    B_grow = Buf("grow")
    mbias = sb("mbias_s", [128, depth * 4], F32)
    rdec = sb("rdec_s", [128, depth * 8], F32)
    rdsel = sb("rdsel_s", [128, depth * 4], F32)
    rnorm = sb("rnorm_s", [128, 256], F32);  B_rq = Buf("rqn")
    qkn = sb("qkn_s", [128, 256], F32)
    sink = sb("sink_s", [128, depth * 4], F32)
    ones1 = sb("ones1", [1, 128], F32)
    B_cb = Buf("constsb")
    Gm = sb("Gm", [128, 2, 8], F32)
    Sm = sb("Sm", [128, 2, 8], F32)
    gateB = sb("gateB", [128, 2, D], F32)
    B_mod = Buf("mod")
    lg = sb("lg", [128, 8], F32)
    lgsel = sb("lgsel", [128, 4], F32)
    kd = sb("kd", [128, 8], F32)
    qd = sb("qd", [128, 8], F32)
    DecT = sb("DecT", [128, 512], F32)
    Dt = sb("Dt", [128, 256], F32)
    Pw = sb("Pw", [128, 256], F32)
    Aagg = sb("Aagg", [128, 256], F32)
    Actx = sb("Actx", [128, 256], F32)
    coef = sb("coef", [128, 5, 4], F32)
    esink = sb("esink", [128, 4], F32)
    B_lay = Buf("laysmall")
    B_Aagg, B_Actx, B_Pw = Buf("Aagg"), Buf("Actx"), Buf("Pw")
    aggs = sb("aggs", [128, R, 256], F32);    B_aggs = Buf("aggs")
    sintmp = sb("sintmp", [128, 256], F32);   B_sintmp = Buf("sintmp")
    xt = Rot([sb("xt%d" % i, [128, D], F32) for i in range(2)], "xt")
    xr = xt
    st4 = Rot([sb("st4_%d" % i, [128, 16], F32) for i in range(4)], "st4")
    xn = Rot([sb("xn%d" % i, [128, D], BF16) for i in range(2)], "xn")
    hT = Rot([sb("hT%d" % i, [128, 8, 128], BF16) for i in range(2)], "hT")
    cs = Rot([sb("cs%d" % i, [128, 2, 64], F32) for i in range(2)], "cs")
    rqkv = Rot([sb("rqkv%d" % i, [128, 768], BF16) for i in range(2)], "rqkv")
    fb = Rot([sb("fb%d" % i, [128, 2, 4, 2, 64], BF16) for i in range(2)], "fb")
    fbT = Rot([sb("fbT%d" % i, [128, 2, 4, 128], BF16) for i in range(1)], "fbT")
    rT = Rot([sb("rT%d" % i, [128, 3, 2, 128], BF16) for i in range(1)], "rT")
    pint = Rot([sb("pint%d" % i, [128, 512], BF16) for i in range(1)], "pint")
    yint = Rot([sb("yint%d" % i, [128, 256], F32) for i in range(1)], "yint")
    kraw = Rot([sb("kraw%d" % i, [128, 512], F32) for i in range(2)], "kraw")
    t1 = Rot([sb("t1_%d" % i, [128, 512], F32) for i in range(1)], "t1")
    t2 = Rot([sb("t2_%d" % i, [128, 512], F32) for i in range(1)], "t2")
    knb = Rot([sb("knb%d" % i, [128, 512], BF16) for i in range(2)], "knb")
    kTs = Rot([sb("kTs%d" % i, [128, 256], BF16) for i in range(2)], "kTs")
    vsb = Rot([sb("vsb%d" % i, [128, 260], BF16) for i in range(2)], "vsb")
    NSL = 2 * QG
    gates = Rot([sb("gates%d" % i, [128, D], BF16) for i in range(NSL)], "gates")
    mixo = Rot([sb("mixo%d" % i, [128, D], BF16) for i in range(NSL)], "mixo")
    uvg = Rot([sb("uvg%d" % i, [128, 512], BF16) for i in range(1)], "uvg")
    gqT = Rot([sb("gqT%d" % i, [128, 2, 2, GQ], BF16) for i in range(2)], "gqT")
    sqT = Rot([sb("sqT%d" % i, [128, 2, 2, 128], BF16) for i in range(2)], "sqT")
    pT = Rot([sb("pT%d" % i, [128, 512], BF16) for i in range(3)], "pT")
    oT = Rot([sb("oT%d" % i, [65, 512], F32) for i in range(2)], "oT")
    rden = Rot([sb("rden%d" % i, [128, 4], F32) for i in range(4)], "rden")
    mixTt = Rot([sb("mixTt%d" % i, [128, 8, 128], BF16) for i in range(1)], "mixTt")
    otmp = Rot([sb("otmp%d" % i, [128, D], F32) for i in range(1)], "otmp")
    wmst = Rot([otmp.tiles[0][:].rearrange("p (k c) -> p k c", k=8)], "wmst")
    wmst.bufs = otmp.bufs
    yinl = Rot([sb("yinl%d" % i, [128, 256], F32) for i in range(1)], "yinl")
    qfbl = Rot([sb("qfbl%d" % i, [128, 512], BF16) for i in range(1)], "qfbl")
    ysum = Rot([sb("ysum%d" % i, [128, 256], F32) for i in range(2)], "ysum")
    wk = Rot([sb("wk%d" % i, [128, 6, 128], BF16) for i in range(1)], "wk")
    wv = Rot([sb("wv%d" % i, [128, 6, 130], BF16) for i in range(1)], "wv")
    wp = Rot([sb("wp%d" % i, [128, 256], BF16) for i in range(3)], "wp")
    woT = Rot([sb("woT%d" % i, [65, 256], F32) for i in range(1)], "woT")
    zps = Rot([ps("zps%d" % i, [128, 512]) for i in range(2)], "zps")
    tps = Rot([ps("tps%d" % i, [128, 1024], BF16) for i in range(1)], "tps")
    sps = Rot([ps("sps%d" % i, [128, 512]) for i in range(3)], "sps")
    ops_ = Rot([ps("ops%d" % i, [128, 512]) for i in range(2)], "ops")
    zps1 = Rot([zps.tiles[0]], "zps1", bufs=[zps.bufs[0]])
    swacc = Rot([zps.tiles[1]], "swacc", bufs=[zps.bufs[1]])
    zpsP1 = Rot(zps.tiles + ops_.tiles, "zpsP1", bufs=zps.bufs + ops_.bufs)
    PS = {"z": zpsP1, "acc": ops_}

    def dma(q, out, in_, reads, writes, par=False, **kw):
        return P.op(q, lambda e: e.dma_start(out=out, in_=in_, **kw), reads, writes, kind="d", par=par)

    def mm(out, lhsT, rhs, start, stop, reads, writes):
        return P.op("pe", lambda e: e.matmul(out, lhsT=lhsT, rhs=rhs, start=start, stop=stop), reads, writes)

    def tr(out, in_, ident, reads, writes):
        return P.op("pe", lambda e: e.transpose(out, in_, ident), reads, writes)

    def act(out, in_, func, reads, writes, **kw):
        return P.op("act", lambda e: e.activation(out=out, in_=in_, func=func, **kw), reads, writes)

    def tt(eng, out, in0, in1, op, reads, writes):
        return P.op(eng, lambda e: e.tensor_tensor(out=out, in0=in0, in1=in1, op=op), reads, writes)

    def ts(eng, out, in0, s1, s2, op0, op1, reads, writes):
        if op1 is None:
            return P.op(eng, lambda e: e.tensor_scalar(out=out, in0=in0, scalar1=s1, scalar2=None, op0=op0), reads, writes)
        return P.op(eng, lambda e: e.tensor_scalar(out=out, in0=in0, scalar1=s1, scalar2=s2, op0=op0, op1=op1), reads, writes)

    def stt(out, in0, scalar, in1, op0, op1, reads, writes):
        return P.op("dve", lambda e: e.scalar_tensor_tensor(out=out, in0=in0, scalar=scalar, in1=in1, op0=op0, op1=op1), reads, writes)

    def cp(eng, out, in_, reads, writes):
        if eng == "act":
            return P.op("act", lambda e: e.copy(out=out, in_=in_), reads, writes)
        return P.op(eng, lambda e: e.tensor_copy(out=out, in_=in_), reads, writes)

    def rsqrt_mean_g(out, ssum, n, reads, writes):
        ts("dve", out, ssum, 1.0 / n, EPS, ALU.mult, ALU.add, reads, writes)
        yield
        act(out, out, AF.Ln, writes, writes)
        yield
        act(out, out, AF.Exp, writes, writes, scale=-0.5)
        yield

    def rsqrt_mean(out, ssum, n, reads, writes):
        ts("dve", out, ssum, 1.0 / n, EPS, ALU.mult, ALU.add, reads, writes)
        act(out, out, AF.Ln, writes, writes)
        act(out, out, AF.Exp, writes, writes, scale=-0.5)

    def bc(ap, n):
        return ap.partition_broadcast(n)

    for dst, src in ((identb, identb_in), (identf, identf_in), (rpn, rpn_in), (pos, pos_in), (etab, etab_in),
                     (hmask, hmask_in), (cmask, cmask_in), (cT, cT_in), (gainT, gainT_in), (bmodT, bmodT_in),
                     (mbias, mbias_in)):
        dma("sp", dst[:], src, [B_const], [B_cb], par=True)
    dma("sp", rdec[:], rdec_in.partition_broadcast(128), [B_const], [B_cb], par=True)
    for l in range(depth):
        dma("sp", rdsel[0:64, l * 4:(l + 1) * 4], rdec_in[:, l * 8:l * 8 + 4].partition_broadcast(64), [B_const], [B_cb], par=True)
        dma("sp", rdsel[64:128, l * 4:(l + 1) * 4], rdec_in[:, l * 8 + 4:l * 8 + 8].partition_broadcast(64), [B_const], [B_cb], par=True)
    dma("sp", sink[:], sink_in.partition_broadcast(128), [B_const], [B_cb], par=True)
    for c in range(CTXC):
        dma("sp", xc[:, c, :], ctx_in[c * 128:(c + 1) * 128, :], [B_const], [B_xc[c]])
    B_c2 = Buf("const2")
    P.op("dve", lambda e: e.memset(ones1[:], 1.0), [], [B_c2])
    for rot_ in (gqT, sqT, rT):
        for i in range(len(rot_.tiles)):
            P.op("pool", lambda e, t_=rot_.tiles[i]: e.memset(t_[:], 0.0), [], [rot_.bufs[i]])
    P.op("dve", lambda e: e.memset(V[:], 1.0), [], [B_V])
    P.op("dve", lambda e: e.memset(sVc[:], 1.0), [], [B_sVc])
    for i in range(2):
        P.op("pool", lambda e, i=i: e.memset(vsb.tiles[i][:], 1.0), [], [vsb.bufs[i]])
    act(cT[:], cT[:], AF.Silu, [B_cb], [B_cb])

    def load_w1(l):
        srcs = [(768, 1536, 0), (2048, 2176, 768), (2816, 2944, 896), (2176, 2304, 1024), (2944, 3072, 1152)]
        for k in range(8):
            for (a, b_, d0) in srcs:
                dma("pool", WA[:, k, d0:d0 + (b_ - a)], w_in[l, k * 128:(k + 1) * 128, a:b_], [B_const], [B_WA], par=True)

    def load_w2(l):
        srcs = [(0, 768, 0), (1536, 1792, 768), (2304, 2560, 1024), (3072, 3328, 1280), (1792, 2048, 1536), (2560, 2816, 1792)]
        for k in range(8):
            for (a, b_, d0) in srcs:
                dma("pool", W2[:, k, d0:d0 + (b_ - a)], w_in[l, k * 128:(k + 1) * 128, a:b_], [B_const], [B_W2], par=True)
        dma("pool", mixT[:], mixT_in[l], [B_const], [B_mixT])

    def load_wo(l):
        for k in range(8):
            dma("pool", WA[:, k, 0:1024], w_out[l, k * 128:(k + 1) * 128, :], [B_const], [B_WA], par=True)

    def layer_consts(l):
        rd = [B_cb, B_c2]
        w = [B_lay]
        dma("sp", rnorm[:], rnorm_in[:, l * 256:(l + 1) * 256].partition_broadcast(128), [B_const], [B_rq])
        dma("sp", qkn[:], qkn_in[:, l * 256:(l + 1) * 256].partition_broadcast(128), [B_const], [B_rq], par=True)
        act(lg[:], rdec[:, l * 8:(l + 1) * 8], AF.Exp, rd, w)
        ts("dve", lg[:], lg[:], -1.0, None, ALU.mult, None, w, w)
        act(lgsel[:], rdsel[:, l * 4:(l + 1) * 4], AF.Exp, rd, w)
        ts("dve", lgsel[:], lgsel[:], -1.0, None, ALU.mult, None, w, w)
        for dr in range(2):
            ts("dve", kd[:, dr * 4:(dr + 1) * 4], lg[:, dr * 4:(dr + 1) * 4], pos[:, dr:dr + 1], None, ALU.mult, None, rd + w, w)
            ts("dve", qd[:, dr * 4:(dr + 1) * 4], lg[:, dr * 4:(dr + 1) * 4], pos[:, 2 + dr:3 + dr], None, ALU.mult, None, rd + w, w)
        act(kd[:], kd[:], AF.Exp, w, w)
        ts("dve", kd[:], kd[:], SCALE, None, ALU.mult, None, w, w)
        act(qd[:], qd[:], AF.Exp, w, w)
        for h in range(4):
            ts("dve", DecT[:, h * 128:(h + 1) * 128], rpn[:, 0:128], lg[:, h:h + 1], None, ALU.mult, None, rd + w, w)
            stt(DecT[:, h * 128:(h + 1) * 128], rpn[:, 128:256], lg[:, 4 + h:5 + h], DecT[:, h * 128:(h + 1) * 128],
                ALU.mult, ALU.add, rd + w, w)
        act(DecT[:], DecT[:], AF.Exp, w, w)
        ts("dve", DecT[:], DecT[:], SCALE, None, ALU.mult, None, w, w)
        ts("dve", esink[:], lgsel[:], 128.0, None, ALU.mult, None, w, w)
        act(esink[:], esink[:], AF.Exp, w, w)
        cp("dve", Dt[:].rearrange("p (h e) -> p h e", h=4), esink[:].unsqueeze(2).to_broadcast([128, 4, 64]), w, w)
        for s_ in range(5):
            ts("dve", coef[:, s_, :], lgsel[:], etab[:, s_:s_ + 1], 128.0 * NCH, ALU.mult, ALU.mult, rd + w, w)
        act(coef[:], coef[:], AF.Exp, w, w)
        act(esink[:], sink[:, l * 4:(l + 1) * 4], AF.Exp, rd + w, w)
        P.op("pool", lambda e: e.memset(Aagg[:], 0.0), [], [B_Aagg])
        P.op("pool", lambda e: e.memset(Actx[:], 0.0), [], [B_Actx])
        P.op("pool", lambda e: e.memset(Pw[:], 1.0), [], [B_Pw])
        P.op("pool", lambda e: e.memset(STc[0:64, 1, :], 0.0), [], [B_STc[1]], par=True)
        P.op("pool", lambda e: e.memset(STc[64:128, 2, :], 0.0), [], [B_STc[2]], par=True)

    def mod_compute(l):
        dma("sp", bgate[:], bgate_in[:, l * D:(l + 1) * D], [B_const], [B_bg])
        mp, mb = zps.next()
        for half in range(-1, 2):
            if half < 0:
                for c in range(16):
                    wt, wb = wmst.next()
                    dma("sp", wt[:], w_mod[l, :, c * 128:(c + 1) * 128].rearrange("(k p) c -> p k c", p=128), [B_const], [wb])
                    for k in range(8):
                        mm(mp[:, c * 2:c * 2 + 2], wt[:, k, :], cT[:, 2 * k:2 * k + 2], k == 0, k == 7, [wb, B_cb], [mb])
                continue
            g0, gb0 = sps.next()
            g1, gb1 = sps.next()
            gps = ((g0, gb0), (g1, gb1))
            for q in range(4):
                col = 2048 + half * 512 + q * 128
                wt, wb = wmst.next()
                dma("sp", wt[:], w_mod[l, :, col:col + 128].rearrange("(k p) c -> p k c", p=128), [B_const], [wb])
                for j in range(2):
                    for k in range(8):
                        mm(gps[j][0][0:1, q * 128:(q + 1) * 128], cT[:, 2 * k + j:2 * k + j + 1], wt[:, k, :], k == 0, k == 7, [wb, B_cb], [gps[j][1]])
            for j in range(2):
                tt("dve", grow[:], gps[j][0][0:1, 0:512], bgate[0:1, half * 512:(half + 1) * 512], ALU.add, [gps[j][1], B_bg], [B_grow])
                zp, zb = ops_.next()
                mm(zp[:, 0:512], ones1[0:1, :], grow[0:1, :], True, True, [B_c2, B_grow], [zb])
                cp("dve", gateB[:, j, half * 512:(half + 1) * 512], zp[:, 0:512], [zb], [B_mod], )
        mpv = mp[:, 0:32].rearrange("p (c j) -> p j c", j=2)
        for j in range(2):
            tt("dve", Sm[:, j, :], mpv[:, j, 0:8], bmodT[:, l * 16:l * 16 + 8], ALU.add, [mb, B_cb], [B_mod])
            tt("dve", Gm[:, j, :], mpv[:, j, 8:16], bmodT[:, l * 16 + 8:l * 16 + 16], ALU.add, [mb, B_cb], [B_mod])
            stt(Gm[:, j, :], Gm[:, j, :], 1.0, gainT[:, l * 8:(l + 1) * 8], ALU.add, ALU.mult, [B_mod, B_cb], [B_mod])

    def norm_tile(l, xap, xbuf, j):
        s4, s4b = st4.next()
        xnt, xnb = xn.next()
        P.op("dve", lambda e: e.scalar_tensor_tensor(out=xnt[:], in0=xap, scalar=1.0, in1=xap, op0=ALU.mult, op1=ALU.mult,
                                                      accum_out=s4[:, 0:1]), [xbuf], [xnb, s4b])
        rsqrt_mean(s4[:, 0:1], s4[:, 0:1], D, [s4b], [s4b])
        ts("dve", xnt[:], xap, s4[:, 0:1], None, ALU.mult, None, [xbuf, s4b], [xnb])
        tp, tb = tps.next()
        for k in range(8):
            tr(tp[:, k * 128:(k + 1) * 128], xnt[:, k * 128:(k + 1) * 128], identb[:], [xnb, B_cb], [tb])
        ht, hb = hT.next()
        for k in range(8):
            ts("dve", ht[:, k, :], tp[:, k * 128:(k + 1) * 128], Gm[:, j, k:k + 1], Sm[:, j, k:k + 1], ALU.mult, ALU.add,
               [tb, B_mod], [hb])
        return ht, hb

    def norm_tile_g(l, xap, xbuf, j):
        s4, s4b = st4.next()
        xnt, xnb = xn.next()
        P.op("dve", lambda e: e.scalar_tensor_tensor(out=xnt[:], in0=xap, scalar=1.0, in1=xap, op0=ALU.mult, op1=ALU.mult,
                                                      accum_out=s4[:, 0:1]), [xbuf], [xnb, s4b])
        yield
        yield from rsqrt_mean_g(s4[:, 0:1], s4[:, 0:1], D, [s4b], [s4b])
        ts("dve", xnt[:], xap, s4[:, 0:1], None, ALU.mult, None, [xbuf, s4b], [xnb])
        yield
        tp, tb = tps.next()
        for k in range(8):
            tr(tp[:, k * 128:(k + 1) * 128], xnt[:, k * 128:(k + 1) * 128], identb[:], [xnb, B_cb], [tb])
        yield
        yield
        ht, hb = hT.next()
        for k in range(8):
            ts("dve", ht[:, k, :], tp[:, k * 128:(k + 1) * 128], Gm[:, j, k:k + 1], Sm[:, j, k:k + 1], ALU.mult, ALU.add,
               [tb, B_mod], [hb])
        yield
        yield
        return ht, hb

    def inproj(ht, hb, W, WB, c0, ncols):
        zp, zb = PS["z"].next()
        for k in range(8):
            mm(zp[:, 0:ncols], ht[:, k, :], W[:, k, c0:c0 + ncols], k == 0, k == 7, [hb, WB], [zb])
        return zp, zb

    def qk_norm_rope(src, srcb, nh, gain_ap, csb, rope):
        n = nh * 64
        Bt, Bb = t1.next()
        Ct, Cb = t2.next()
        s4, s4b = st4.next()
        v3 = lambda ap: ap[:, 0:n].rearrange("p (h d) -> p h d", d=64)
        tt("dve", Bt[:, 0:n], src[:, 0:n], src[:, 0:n], ALU.mult, [srcb], [Bb])
        yield
        P.op("dve", lambda e: e.tensor_reduce(out=s4[:, 0:nh], in_=v3(Bt), axis=AX.X, op=ALU.add), [Bb], [s4b])
        yield
        yield from rsqrt_mean_g(s4[:, 0:nh], s4[:, 0:nh], 64, [s4b], [s4b])
        tt("dve", v3(Ct), v3(src), s4[:, 0:nh].unsqueeze(2).to_broadcast([128, nh, 64]), ALU.mult, [srcb, s4b], [Cb])
        tt("dve", Ct[:, 0:n].rearrange("p (a h d) -> p a h d", a=2, d=64), Ct[:, 0:n].rearrange("p (a h d) -> p a h d", a=2, d=64),
           gain_ap.rearrange("p (a d) -> p a d", a=2).unsqueeze(2).to_broadcast([128, 2, nh // 2, 64]), ALU.mult, [Cb, B_rq], [Cb])
        yield
        ot, ob = knb.next()
        if not rope:
            cp("dve", ot[:, 0:n], Ct[:, 0:n], [Cb], [ob])
            return ot, ob
        cst, csbuf = csb
        v4 = lambda ap: ap[:, 0:n].rearrange("p (h a b c) -> p h a b c", a=2, b=2, c=16)
        cosb = cst[:, 0, :].unsqueeze(1).to_broadcast([128, nh, 64])
        sin4 = cst[:, 1, :].rearrange("p (a b c) -> p a b c", a=2, b=2)
        tt("dve", v3(Bt), v3(Ct), cosb, ALU.mult, [Cb, csbuf], [Bb])
        yield
        for b0 in range(2):
            tt("pool", v4(src)[:, :, :, b0, :], v4(Ct)[:, :, :, 1 - b0, :],
               sin4[:, :, b0, :].unsqueeze(1).to_broadcast([128, nh, 2, 16]), ALU.mult, [Cb, csbuf], [srcb], )
        yield
        yield
        tt("dve", ot[:, 0:n], Bt[:, 0:n], src[:, 0:n], ALU.add, [Bb, srcb], [ob])
        yield
        return ot, ob

    def ret_state(is_ctx):
        return (STc, B_STc, Actx, B_Actx) if is_ctx else (ST, B_ST, Aagg, B_Aagg)

    def p1_tile(l, t, xsrc, B_xsrc):
        is_ctx = t < CTXC
        c = t if is_ctx else t - CTXC
        j = 1 if is_ctx else 0
        if is_ctx:
            xap, xbuf = xc[:, c, :], B_xc[c]
            csb = None
        else:
            xt_, xbuf = xt.next()
            dma("sp", xt_[:], xsrc[c * 128:(c + 1) * 128, :], [B_xsrc], [xbuf])
            xap = xt_[:]
            cst, csbuf = cs.next()
            dma("sp", cst[:, 0, :], cos_in[c], [B_const], [csbuf])
            dma("sp", cst[:, 1, :], sin_in[c], [B_const], [csbuf], par=True)
            csb = (cst, csbuf)
        yield
        ht, hb = norm_tile(l, xap, xbuf, j)
        yield
        rq, rqb = rqkv.next()
        kr, krb = kraw.next()
        vs, vsbuf = vsb.next()
        zp, zb = inproj(ht, hb, WA, B_WA, 0, 512)
        cp("act", rq[:, 0:512], zp[:, 0:512], [zb], [rqb])
        yield
        zp, zb = inproj(ht, hb, WA, B_WA, 512, 512)
        cp("act", rq[:, 512:768], zp[:, 0:256], [zb], [rqb], )
        cp("act", kr[:, 0:256], zp[:, 256:512], [zb], [krb])
        yield
        zp, zb = inproj(ht, hb, WA, B_WA, 1024, 256)
        cp("dve", vs[:, 1:129], zp[:, 0:128], [zb], [vsbuf])
        cp("dve", vs[:, 131:259], zp[:, 128:256], [zb], [vsbuf])
        yield
        fbt, fbb = fb.next()
        for w_, dec in ((0, qd), (1, kd)):
            for dr in range(2):
                tt("dve", fbt[:, w_, :, dr, :], rq[:, w_ * 256:(w_ + 1) * 256].rearrange("p (h d) -> p h d", h=4),
                   dec[:, dr * 4:(dr + 1) * 4].unsqueeze(2).to_broadcast([128, 4, 64]), ALU.mult, [rqb, B_lay], [fbb], )
        yield
        tp, tb = tps.next()
        for w_ in range(2):
            for h in range(4):
                tr(tp[:, (w_ * 4 + h) * 128:(w_ * 4 + h + 1) * 128], fbt[:, w_, h, :, :].rearrange("p a d -> p (a d)"), identb[:],
                   [fbb, B_cb], [tb])
        fT, fTb = fbT.next()
        cp("act", fT[:].rearrange("p a h t -> p (a h t)"), tp[:, 0:1024], [tb], [fTb])
        dma("pool", qfb_d[t], fT[:, 0, :, :].rearrange("p h t -> p (h t)"), [fTb], [B_qfb], par=True)
        yield
        tp, tb = tps.next()
        for w_ in range(2):
            for pr in range(2):
                tr(tp[:, (w_ * 2 + pr) * 128:(w_ * 2 + pr + 1) * 128], rq[:, w_ * 256 + pr * 128:w_ * 256 + (pr + 1) * 128], identb[:],
                   [rqb, B_cb], [tb])
        rt, rtb = rT.next()
        cp("act", rt[0:64, 0, :, :], tp[0:64, 0:256].rearrange("p (b t) -> p b t", b=2), [tb], [rtb], )
        cp("act", rt[64:128, 1, :, :], tp[64:128, 0:256].rearrange("p (b t) -> p b t", b=2), [tb], [rtb], )
        cp("act", rt[:, 2, :, :], tp[:, 256:512].rearrange("p (b t) -> p b t", b=2), [tb], [rtb], )
        sp0, sb0 = sps.next()
        sp1, sb1 = sps.next()
        for pr in range(2):
            mm(sp0[:, pr * 128:(pr + 1) * 128], rt[:, 2, pr, :], rt[:, 0, pr, :], True, True, [rtb], [sb0])
            mm(sp1[:, pr * 128:(pr + 1) * 128], rt[:, 2, pr, :], rt[:, 1, pr, :], True, True, [rtb], [sb1])
        pi, pib = pint.next()
        piv = pi[:].rearrange("p (a b i) -> p a b i", a=2, b=2)
        dcv = DecT[:].rearrange("p (a b i) -> p a b i", a=2, b=2)
        for hh, (spx, sbx) in enumerate(((sp0, sb0), (sp1, sb1))):
            tt("dve", piv[:, :, hh, :], spx[:, 0:256].rearrange("p (a i) -> p a i", a=2), dcv[:, :, hh, :], ALU.mult,
               [sbx, B_lay], [pib], )
        mp, mb = PS["z"].next()
        for h in range(4):
            mm(mp[:, h * 64:(h + 1) * 64], pi[:, h * 128:(h + 1) * 128], rq[:, 512 + h * 64:512 + (h + 1) * 64], True, True, [pib, rqb], [mb])
        for h in range(4):
            mm(mp[:, 256 + h * 64:256 + (h + 1) * 64], fbt[:, 1, h, :, :].rearrange("p a d -> p (a d)"),
               rq[:, 512 + h * 64:512 + (h + 1) * 64], True, True, [fbb, rqb], [mb])
        yield
        yt, ytb = yint.next()
        cp("act", yt[:], mp[:, 0:256], [mb], [ytb])
        row0 = t * 128
        dma("pool", yin_d[row0:row0 + 128, :], yt[:], [ytb], [B_yin], par=True)
        Sx, BS, Ax, BA = ret_state(is_ctx)
        cp("act", Sx[0:64, c + 2, :], mp[0:64, 256:512], [mb], [BS[c + 2]], )
        cp("act", Sx[64:128, c, :], mp[64:128, 256:512], [mb], [BS[c]], )
        yield
        tt("pool", Ax[0:64, :], Ax[0:64, :], Dt[0:64, :], ALU.mult, [BA, B_lay], [BA])
        tt("pool", Ax[0:64, :], Ax[0:64, :], Sx[0:64, c + 2, :], ALU.add, [BA, BS[c + 2]], [BA])
        if is_ctx:
            if c == 0:
                tt("pool", Ax[64:128, :], Ax[64:128, :], Sx[64:128, c, :], ALU.add, [BA, BS[c]], [BA])
            else:
                tt("pool", sintmp[64:128, :], Dt[64:128, :], Sx[64:128, c, :], ALU.mult, [B_lay, BS[c]], [B_sintmp])
                tt("pool", Ax[64:128, :], Ax[64:128, :], sintmp[64:128, :], ALU.add, [BA, B_sintmp], [BA])
        else:
            tt("pool", sintmp[64:128, :], Pw[64:128, :], Sx[64:128, c, :], ALU.mult, [B_Pw, BS[c]], [B_sintmp])
            tt("pool", Ax[64:128, :], Ax[64:128, :], sintmp[64:128, :], ALU.add, [BA, B_sintmp], [BA])
            tt("pool", Pw[64:128, :], Pw[64:128, :], Dt[64:128, :], ALU.mult, [B_Pw, B_lay], [B_Pw])
        yield
        gk_gain = qkn[:, 128:256]
        kn, knbuf = yield from qk_norm_rope(kr, krb, 4, gk_gain, csb, not is_ctx)
        yield
        tp, tb = tps.next()
        for a in range(2):
            tr(tp[:, a * 128:(a + 1) * 128], kn[:, a * 128:(a + 1) * 128], identb[:], [knbuf, B_cb], [tb])
        if is_ctx:
            cp("act", KT[:, c * 128:(c + 1) * 128], tp[:, 0:128], [tb], [B_KT], )
            cp("act", skTc[:, c * 128:(c + 1) * 128], tp[:, 128:256], [tb], [B_skTc], )
            cp("dve", V[:, c, 1:129], vs[:, 1:129], [vsbuf], [B_V])
            cp("dve", sVc[:, c, 1:129], vs[:, 131:259], [vsbuf], [B_sVc])
        else:
            kt_, ktb = kTs.next()
            cp("act", kt_[:], tp[:, 0:256], [tb], [ktb])
            dma("pool", gk_x[c // GP][:, (c % GP) * 128:(c % GP + 1) * 128], kt_[:, 0:128], [ktb], [B_gkx[c // GP]], par=True)
            dma("pool", sk_d[c], kt_[:, 128:256], [ktb], [B_skd], par=True)
            dma("pool", gv_x[c // GP][:, (c % GP) * 130:(c % GP + 1) * 130], vs[:, 0:130], [vsbuf], [B_gvx[c // GP]], par=True)
            dma("pool", sv_d[c], vs[:, 130:260], [vsbuf], [B_svd], par=True)
            if c == 0:
                dma("pool", bnd_x[:, 0:128], kt_[:, 128:256], [ktb], [B_bndx], par=True)
                dma("pool", bnd_x[:, 256:386], vs[:, 130:260], [vsbuf], [B_bndx], par=True)
            if c == NCH - 1:
                dma("pool", bnd_x[:, 128:256], kt_[:, 128:256], [ktb], [B_bndx], par=True)
                dma("pool", bnd_x[:, 386:516], vs[:, 130:260], [vsbuf], [B_bndx], par=True)

    def allgather(src, bs, dst, bd):
        groups = [[0, 1, 2, 3], [4, 5, 6, 7]]
        P.op("pool", lambda e: e.collective_compute("AllGather", ALU.bypass, replica_groups=groups, ins=[src], outs=[dst]),
             [bs], [bd], kind="cc")

    def gather_piece(i):
        allgather(gk_x[i], B_gkx[i], gk_all[i], B_gkall[i])
        allgather(gv_x[i], B_gvx[i], gv_all[i], B_gvall[i])

    def exchange(l):
        dma("pool", agg_x, Aagg[:], [B_Aagg], [B_aggx])
        allgather(agg_x, B_aggx, agg_all, B_aggall)
        allgather(bnd_x, B_bndx, bnd_all, B_bndall)

    def exchange_b(l):
        dma("sp", aggs[:], agg_all.rearrange("(r p) c -> p r c", p=128), [B_aggall], [B_aggs])
        B_s2 = Buf("s2")
        v3 = lambda ap: ap.rearrange("p (h e) -> p h e", h=4)
        cb = lambda s_: coef[:, s_, :].unsqueeze(2).to_broadcast([128, 4, 64])
        tt("pool", v3(sintmp[:]), v3(Actx[:]), cb(4), ALU.mult, [B_Actx, B_lay], [B_sintmp])
        for r in range(R):
            tt("pool", v3(aggs[:, r, :]), v3(aggs[:, r, :]), cb(r), ALU.mult, [B_aggs, B_lay], [B_aggs])
            tt("pool", sintmp[:], sintmp[:], aggs[:, r, :], ALU.add, [B_sintmp, B_aggs], [B_sintmp])
        cp("pool", ST[0:64, 1, :], sintmp[0:64, :], [B_sintmp], [B_ST[1]], )
        cp("pool", ST[64:128, NCH, :], sintmp[64:128, :], [B_sintmp], [B_ST[NCH]], )
        for i in range(1, NCH):
            cf = i
            tt("dve", sintmp[0:64, :], sintmp[0:64, :], Dt[0:64, :], ALU.mult, [B_sintmp, B_lay], [B_sintmp])
            tt("dve", sintmp[0:64, :], sintmp[0:64, :], ST[0:64, cf + 1, :], ALU.add, [B_sintmp, B_ST[cf + 1]], [B_sintmp])
            cp("dve", ST[0:64, cf + 1, :], sintmp[0:64, :], [B_sintmp], [B_ST[cf + 1]])
            cbk = NCH - 1 - i
            tt("dve", sintmp[64:128, :], sintmp[64:128, :], Dt[64:128, :], ALU.mult, [B_sintmp, B_lay], [B_sintmp])
            tt("dve", sintmp[64:128, :], sintmp[64:128, :], ST[64:128, cbk + 1, :], ALU.add, [B_sintmp, B_ST[cbk + 1]], [B_sintmp])
            cp("dve", ST[64:128, cbk + 1, :], sintmp[64:128, :], [B_sintmp], [B_ST[cbk + 1]])

    def small_attn(qT_ap, qTb, blocks, sink_l, mo, mob, gt, gtb, colbase):
        for g in range(2):
            op_, opb = PS["acc"].next()
            pend = []

            def pv(item):
                bi, vap, w_, wb_, bufs = item
                mm(op_[0:65, 0:256], vap[:, g * 65:(g + 1) * 65], w_[:], bi == 0, bi == len(blocks) - 1, [wb_] + bufs, [opb])

            for bi, (kap, vap, mask, bufs) in enumerate(blocks):
                sp_, spb = sps.next()
                mm(sp_[:, 0:256], kap, qT_ap[:, g, :, :].rearrange("p r t -> p (r t)"), True, True,
                   [qTb] + bufs, [spb])
                yield
                w_, wb_ = wp.next()
                act(w_[:], sp_[:, 0:256], AF.Exp, [spb], [wb_], scale=SCALE)
                yield
                if mask is not None:
                    tt("pool", w_[:].rearrange("p (r t) -> p r t", r=2), w_[:].rearrange("p (r t) -> p r t", r=2),
                       mask.unsqueeze(1).to_broadcast([128, 2, 128]), ALU.mult, [wb_, B_cb], [wb_])
                    yield
                pend.append((bi, vap, w_, wb_, bufs))
                if len(pend) > 1:
                    pv(pend.pop(0))
            for item in pend:
                pv(item)
            yield
            wo, wob = woT.next()
            cp("dve", wo[:], op_[0:65, 0:256], [opb], [wob])
            yield
            for r in range(2):
                h = g * 2 + r
                zp, zb = PS["z"].next()
                tr(zp[:, 0:65], wo[:, r * 128:(r + 1) * 128], identf[0:65, 0:65], [wob, B_cb], [zb])
                yield
                finish_head(zp, zb, g, h, sink_l, mo, mob, gt, gtb, colbase)
                yield

    def finish_head(zp, zb, g, h, sink_l, mo, mob, gt, gtb, colbase):
        rd_, rdb = rden.next()
        dcol = 0 if g == 0 else 64
        o0 = 1 if g == 0 else 0
        if sink_l:
            ts("dve", rd_[:, 0:1], zp[:, dcol:dcol + 1], esink[:, h:h + 1], None, ALU.add, None, [zb, B_lay], [rdb])
            P.op("dve", lambda e: e.reciprocal(out=rd_[:, 0:1], in_=rd_[:, 0:1]), [rdb], [rdb])
        else:
            P.op("dve", lambda e: e.reciprocal(out=rd_[:, 0:1], in_=zp[:, dcol:dcol + 1]), [zb], [rdb])
        stt(mo[:, colbase + h * 64:colbase + (h + 1) * 64], zp[:, o0:o0 + 64], rd_[:, 0:1], gt[:, colbase + h * 64:colbase + (h + 1) * 64],
            ALU.mult, ALU.mult, [zb, rdb, gtb], [mob])

    def silu_evac(dst, dstb, zp, zb):
        Ct, Cb = t2.next()
        act(Ct[:, 0:512], zp[:, 0:512], AF.Exp, [zb], [Cb], scale=-1.0)
        yield
        ts("dve", Ct[:, 0:512], Ct[:, 0:512], 1.0, None, ALU.add, None, [Cb], [Cb])
        yield
        P.op("dve", lambda e: e.reciprocal(out=Ct[:, 0:512], in_=Ct[:, 0:512]), [Cb], [Cb])
        yield
        yield
        tt("dve", dst, zp[:, 0:512], Ct[:, 0:512], ALU.mult, [zb, Cb], [dstb])

    def p2_front(l, t, xsrc, B_xsrc):
        is_ctx = t < CTXC
        c = t if is_ctx else t - CTXC
        j = 1 if is_ctx else 0
        if is_ctx:
            xap, xbuf = xc[:, c, :], B_xc[c]
            csb = None
        else:
            xt_, xbuf = xt.next()
            dma("sp", xt_[:], xsrc[c * 128:(c + 1) * 128, :], [B_xsrc], [xbuf])
            xap = xt_[:]
            cst, csbuf = cs.next()
            dma("sp", cst[:, 0, :], cos_in[c], [B_const], [csbuf])
            dma("sp", cst[:, 1, :], sin_in[c], [B_const], [csbuf], par=True)
            csb = (cst, csbuf)
        yl, ylb = yinl.next()
        dma("sp", yl[:], yin_d[t * 128:(t + 1) * 128, :], [B_yin], [ylb])
        ql, qlb = qfbl.next()
        dma("sp", ql[:], qfb_d[t], [B_qfb], [qlb])
        yield
        yield
        ht, hb = yield from norm_tile_g(l, xap, xbuf, j)
        ug, ugb = uvg.next()
        gt, gtb = gates.next()
        mo, mob = mixo.next()
        qr, qrb = kraw.next()
        zp, zb = inproj(ht, hb, W2, B_W2, 0, 512)
        yield
        yield
        act(ug[:], zp[:, 0:512], AF.Gelu, [zb], [ugb])
        yield
        zp, zb = inproj(ht, hb, W2, B_W2, 512, 512)
        yield
        yield
        yield from silu_evac(gt[:, 0:512], gtb, zp, zb)
        yield
        zp, zb = inproj(ht, hb, W2, B_W2, 1024, 512)
        yield
        yield
        yield from silu_evac(gt[:, 512:1024], gtb, zp, zb)
        yield
        zp, zb = inproj(ht, hb, W2, B_W2, 1536, 512)
        yield
        yield
        for blk in range(2):
            cp("dve", qr[:, blk * 256:(blk + 1) * 256].rearrange("p (r g d) -> p r g d", r=2, g=2),
               zp[:, blk * 256:(blk + 1) * 256].rearrange("p (g r d) -> p r g d", r=2, g=2), [zb], [qrb], )
        yield
        mp, mb = PS["z"].next()
        for h in range(4):
            mm(mp[:, h * 64:(h + 1) * 64], mixT[:, h * 128:(h + 1) * 128], ug[:, 256 + h * 64:256 + (h + 1) * 64], True, True, [B_mixT, ugb], [mb])
        Sx, BS, _, _ = ret_state(is_ctx)
        for h in range(4):
            mm(mp[:, 256 + h * 64:256 + (h + 1) * 64], ql[:, h * 128:(h + 1) * 128], Sx[:, c + 1, h * 64:(h + 1) * 64], True, True, [qlb, BS[c + 1]], [mb])
        yield
        yield
        ys, ysb = ysum.next()
        v3 = lambda ap: ap.rearrange("p (h d) -> p h d", h=4)
        tt("dve", v3(ys[:]), v3(mp[:, 0:256]), mbias[:, l * 4:(l + 1) * 4].unsqueeze(2).to_broadcast([128, 4, 64]), ALU.add, [mb, B_cb], [ysb])
        tt("dve", ys[:], ys[:], ug[:, 0:256], ALU.mult, [ysb, ugb], [ysb])
        yield
        tt("dve", mo[:, 0:256], ys[:], gt[:, 0:256], ALU.mult, [ysb, gtb], [mob])
        ys, ysb = ysum.next()
        tt("dve", ys[:], mp[:, 256:512], yl[:], ALU.add, [mb, ylb], [ysb])
        yield
        Bt, Bb = t1.next()
        s4, s4b = st4.next()
        tt("dve", Bt[:, 0:256], ys[:], ys[:], ALU.mult, [ysb], [Bb])
        yield
        P.op("dve", lambda e: e.tensor_reduce(out=s4[:, 0:4], in_=v3(Bt[:, 0:256]), axis=AX.X, op=ALU.add), [Bb], [s4b])
        yield
        yield from rsqrt_mean_g(s4[:, 0:4], s4[:, 0:4], 64, [s4b], [s4b])
        tt("dve", v3(ys[:]), v3(ys[:]), s4[:, 0:4].unsqueeze(2).to_broadcast([128, 4, 64]), ALU.mult, [ysb, s4b], [ysb])
        tt("dve", ys[:], ys[:], rnorm[:, 0:256], ALU.mult, [ysb, B_rq], [ysb])
        yield
        tt("dve", mo[:, 256:512], ys[:], gt[:, 256:512], ALU.mult, [ysb, gtb], [mob])
        yield
        qn, qnb = yield from qk_norm_rope(qr, qrb, 8, qkn[:, 0:128], csb, not is_ctx)
        yield
        tp, tb = tps.next()
        for a in range(4):
            tr(tp[:, a * 128:(a + 1) * 128], qn[:, a * 128:(a + 1) * 128], identb[:], [qnb, B_cb], [tb])
        yield
        yield
        return dict(t=t, c=c, is_ctx=is_ctx, gt=gt, gtb=gtb, mo=mo, mob=mob, tp=tp, tb=tb)

    def out_proj(l, st, xsrc, B_xsrc, xdst, B_xdst):
        t, c, is_ctx, mo, mob = st["t"], st["c"], st["is_ctx"], st["mo"], st["mob"]
        j = 1 if is_ctx else 0
        if is_ctx:
            xap, xbuf = xc[:, c, :], B_xc[c]
        else:
            xr_, xbuf = xr.next()
            dma("sp", xr_[:], xsrc[c * 128:(c + 1) * 128, :], [B_xsrc], [xbuf])
            xap = xr_[:]
        tp, tb = tps.next()
        for k in range(8):
            tr(tp[:, k * 128:(k + 1) * 128], mo[:, k * 128:(k + 1) * 128], identb[:], [mob, B_cb], [tb])
        yield
        yield
        mt, mtb = mixTt.next()
        cp("dve", mt[:].rearrange("p k t -> p (k t)"), tp[:, 0:1024], [tb], [mtb])
        yield
        yield
        ot, otb = otmp.next()
        for half in range(2):
            zp, zb = PS["z"].next()
            for k in range(8):
                mm(zp[:, 0:512], mt[:, k, :], WA[:, k, half * 512:(half + 1) * 512], k == 0, k == 7, [mtb, B_WA], [zb])
            yield
            yield
            yield
            tt("dve", ot[:, half * 512:(half + 1) * 512], zp[:, 0:512], gateB[:, j, half * 512:(half + 1) * 512], ALU.mult, [zb, B_mod], [otb], )
            yield
        if is_ctx:
            tt("pool", xc[:, c, :], xc[:, c, :], ot[:], ALU.add, [xbuf, otb], [xbuf])
        else:
            tt("pool", ot[:], ot[:], xap, ALU.add, [otb, xbuf], [otb])
            yield
            yield
            dma("pool", xdst[c * 128:(c + 1) * 128, :], ot[:], [otb], [B_xdst], par=True)
        yield

    def q_evac(st, q_, qb, lo, cols=None):
        for gg in range(2):
            dst = q_[gg * 64:(gg + 1) * 64, gg, :, :] if cols is None else q_[gg * 64:(gg + 1) * 64, gg, :, cols[0]:cols[1]]
            cp("dve", dst, st["tp"][gg * 64:(gg + 1) * 64, lo:lo + 256].rearrange("p (r t) -> p r t", r=2), [st["tb"]], [qb], )

    def p2_ctx(l):
        for c in range(CTXC):
            st = yield from p2_front(l, c, None, None)
            qT_, qTb = sqT.next()
            qT2, qT2b = sqT.next()
            q_evac(st, qT_, qTb, 0)
            q_evac(st, qT2, qT2b, 256)
            yield
            gblocks = [(KT[:, cc * 128:(cc + 1) * 128], V[:, cc, :], None, [B_KT, B_V]) for cc in range(CTXC)]
            yield from small_attn(qT_, qTb, gblocks, False, st["mo"], st["mob"], st["gt"], st["gtb"], 512)
            sblocks = [(skTc[:, cc * 128:(cc + 1) * 128], sVc[:, cc, :], None, [B_skTc, B_sVc]) for cc in range(CTXC)]
            yield from small_attn(qT2, qT2b, sblocks, True, st["mo"], st["mob"], st["gt"], st["gtb"], 768)
            yield from out_proj(l, st, None, None, None, None)

    def swa_tile(l, st, sq_, sqb):
        c = st["c"]
        wk_, wkb = wk.next()
        wv_, wvb = wv.next()
        blocks = []
        bb = [wkb, wvb]
        if c == 0:
            dma("sp", wk_[:, 0:2, :], sk_d[0:2].rearrange("c p k -> p c k"), [B_skd], [wkb])
            dma("sp", wv_[:, 0:2, :], sv_d[0:2].rearrange("c p k -> p c k"), [B_svd], [wvb])
            dma("sp", wk_[:, 2:6, :], bnd_all[:, 128:256].rearrange("(r p) k -> p r k", p=128), [B_bndall], [wkb], par=True)
            dma("sp", wv_[:, 2:6, :], bnd_all[:, 386:516].rearrange("(r p) k -> p r k", p=128), [B_bndall], [wvb], par=True)
            blocks.append((wk_[:, 0, :], wv_[:, 0, :], None, bb))
            blocks.append((wk_[:, 1, :], wv_[:, 1, :], cmask[:, 128:256], bb))
            for r in range(R):
                blocks.append((wk_[:, 2 + r, :], wv_[:, 2 + r, :], hmask[:, r * 128:(r + 1) * 128], bb))
        elif c == NCH - 1:
            dma("sp", wk_[:, 0:2, :], sk_d[c - 1:c + 1].rearrange("c p k -> p c k"), [B_skd], [wkb])
            dma("sp", wv_[:, 0:2, :], sv_d[c - 1:c + 1].rearrange("c p k -> p c k"), [B_svd], [wvb])
            dma("sp", wk_[:, 2:6, :], bnd_all[:, 0:128].rearrange("(r p) k -> p r k", p=128), [B_bndall], [wkb], par=True)
            dma("sp", wv_[:, 2:6, :], bnd_all[:, 256:386].rearrange("(r p) k -> p r k", p=128), [B_bndall], [wvb], par=True)
            blocks.append((wk_[:, 0, :], wv_[:, 0, :], cmask[:, 0:128], bb))
            blocks.append((wk_[:, 1, :], wv_[:, 1, :], None, bb))
            for r in range(R):
                blocks.append((wk_[:, 2 + r, :], wv_[:, 2 + r, :], hmask[:, (4 + r) * 128:(5 + r) * 128], bb))
        else:
            dma("sp", wk_[:, 0:3, :], sk_d[c - 1:c + 2].rearrange("c p k -> p c k"), [B_skd], [wkb])
            dma("sp", wv_[:, 0:3, :], sv_d[c - 1:c + 2].rearrange("c p k -> p c k"), [B_svd], [wvb])
            blocks.append((wk_[:, 0, :], wv_[:, 0, :], cmask[:, 0:128], bb))
            blocks.append((wk_[:, 1, :], wv_[:, 1, :], None, bb))
            blocks.append((wk_[:, 2, :], wv_[:, 2, :], cmask[:, 128:256], bb))
        for cc in range(CTXC):
            blocks.append((skTc[:, cc * 128:(cc + 1) * 128], sVc[:, cc, :], None, [B_skTc, B_sVc]))
        yield
        yield from small_attn(sq_, sqb, blocks, True, st["mo"], st["mob"], st["gt"], st["gtb"], 768)

    def front_group(l, gi, xsrc, B_xsrc, G):
        gq_, gqb = gqT.next()
        G["gq"] = (gq_, gqb)
        G["sts"] = []
        for ti in range(QG):
            st = yield from p2_front(l, CTXC + gi * QG + ti, xsrc, B_xsrc)
            sq_, sqb = sqT.next()
            q_evac(st, gq_, gqb, 0, (ti * 128, (ti + 1) * 128))
            q_evac(st, sq_, sqb, 256)
            yield
            yield from swa_tile(l, st, sq_, sqb)
            G["sts"].append(st)

    def sweep_group(G):
        gq_, gqb = G["gq"]
        accs = [ops_.next() for _ in range(2)]
        pend = []
        NW = 2 * GQ
        pieces = [(None, None)] + [(r, c0) for r in range(R) for c0 in range(0, NCH, PCS)]
        nblk_total = CTXC + R * NCH
        seen = 0

        def pv(item):
            first, last, g0, vap, vb, p0, pb0 = item
            mm(accs[g0][0][0:65, 0:NW], vap[:, g0 * 65:(g0 + 1) * 65], p0[:, 0:NW], first, last, [pb0] + vb, [accs[g0][1]])

        for (r, c0) in pieces:
            if r is None:
                nb = CTXC
                kget = lambda i: KT[:, i * 128:(i + 1) * 128]
                vget = lambda i: V[:, i, :]
                kbufs, vbufs = [B_KT], [B_V]
            else:
                nb = PCS
                kt_, ktb_ = ksl.next()
                vt_, vtb_ = vsl.next()
                gp_, of_ = c0 // GP, c0 % GP
                dma("sp", kt_[:], gk_all[gp_][r * 128:(r + 1) * 128, of_ * 128:(of_ + PCS) * 128], [B_gkall[gp_]], [ktb_])
                dma("sp", vt_[:], gv_all[gp_][r * 128:(r + 1) * 128, of_ * 130:(of_ + PCS) * 130].rearrange("p (c d) -> p c d", d=130), [B_gvall[gp_]], [vtb_])
                kget = lambda i, kt_=kt_: kt_[:, i * 128:(i + 1) * 128]
                vget = lambda i, vt_=vt_: vt_[:, i, :]
                kbufs, vbufs = [ktb_], [vtb_]
            for i in range(nb):
                first, last = seen == 0, seen == nblk_total - 1
                seen += 1
                for g in range(2):
                    sp_, spb = sps.next()
                    mm(sp_[:, 0:NW], kget(i), gq_[:, g, :, :].rearrange("p r t -> p (r t)"), True, True,
                       kbufs + [gqb], [spb])
                    p_, pb = pT.next()
                    act(p_[:, 0:NW], sp_[:, 0:NW], AF.Exp, [spb], [pb], scale=SCALE)
                    pend.append((first, last, g, vget(i), vbufs, p_, pb))
                    if len(pend) > 2:
                        pv(pend.pop(0))
                yield
        for item in pend:
            pv(item)
        G["oT"] = []
        for g in range(2):
            o_, ob_ = oT.next()
            cp("dve", o_[:, 0:NW], accs[g][0][0:65, 0:NW], [accs[g][1]], [ob_])
            G["oT"].append((o_, ob_))

    def tail_group(l, G, xsrc, B_xsrc, xdst, B_xdst):
        sts = G["sts"]
        for g in range(2):
            o_, ob_ = G["oT"][g]
            for r in range(2):
                h = 2 * g + r
                for ti in range(QG):
                    zp, zb = PS["z"].next()
                    tr(zp[:, 0:65], o_[:, r * GQ + ti * 128:r * GQ + (ti + 1) * 128], identf[0:65, 0:65], [ob_, B_cb], [zb])
                    yield
                    yield
                    finish_head(zp, zb, g, h, False, sts[ti]["mo"], sts[ti]["mob"], sts[ti]["gt"], sts[ti]["gtb"], 512)
                    yield
        for st in sts:
            yield from out_proj(l, st, xsrc, B_xsrc, xdst, B_xdst)

    def run(gen):
        try:
            while True:
                next(gen)
        except StopIteration as e:
            return e.value

    def gchain(*gens):
        for g_ in gens:
            yield from g_

    def pass2(l, xsrc, B_xsrc, xdst, B_xdst):
        PS["z"], PS["acc"] = zps1, swacc
        if l < depth - 1:
            run(p2_ctx(l))
        exchange_b(l)
        Gs = [dict() for _ in range(NG)]
        run(front_group(l, 0, xsrc, B_xsrc, Gs[0]))
        for k in range(NG):
            sides = []
            if k > 0:
                sides.append(tail_group(l, Gs[k - 1], xsrc, B_xsrc, xdst, B_xdst))
            if k + 1 < NG:
                sides.append(front_group(l, k + 1, xsrc, B_xsrc, Gs[k + 1]))
            side = gchain(*sides)
            alive = True
            for _ in sweep_group(Gs[k]):
                for _r in range(SIDE_RATE):
                    if alive:
                        try:
                            next(side)
                        except StopIteration:
                            alive = False
            if alive:
                run(side)
        run(tail_group(l, Gs[NG - 1], xsrc, B_xsrc, xdst, B_xdst))
        PS["z"], PS["acc"] = zpsP1, ops_

    chain = [(x_in, B_xin)]
    inter = [(xsA, B_xsA), (xsB, B_xsB)]
    for l in range(depth):
        chain.append((y_out, B_y) if l == depth - 1 else inter[l % 2])
    for l in range(depth):
        xsrc, B_xsrc = chain[l]
        xdst, B_xdst = chain[l + 1]
        load_w1(l)
        layer_consts(l)
        if stop == "consts":
            break
        mod_compute(l)
        if stop == "mod":
            break
        load_w2(l)
        if stop == "w2":
            break
        gens = [p1_tile(l, t, xsrc, B_xsrc) for t in range(CTXC + NCH)]
        active = []
        nxt = 0
        while nxt < len(gens) or active:
            if nxt < len(gens) and (not active or (len(active) < 2 and active[-1][1] >= P1LAG)):
                active.append([gens[nxt], 0, nxt - CTXC])
                nxt += 1
            for a_ in list(active):
                try:
                    next(a_[0])
                    a_[1] += 1
                except StopIteration:
                    active.remove(a_)
                    if a_[2] >= 0 and a_[2] % GP == GP - 1:
                        gather_piece(a_[2] // GP)
        if stop == "p1":
            break
        exchange(l)
        if stop == "exch":
            break
        load_wo(l)
        pass2(l, xsrc, B_xsrc, xdst, B_xdst)

    P.finalize()
    with nc.Block() as block:
        @block.sync
        def _(e):
            P.emit("sp", e)

        @block.scalar
        def _(e):
            P.emit("act", e)

        @block.vector
        def _(e):
            P.emit("dve", e)

        @block.tensor
        def _(e):
            P.emit("pe", e)

        @block.gpsimd
        def _(e):
            P.emit("pool", e)
            if B_y.sem is not None:
                P.final_wait(e, [B_y])
            else:
                dummy = P.es.enter_context(nc.semaphore("dummy"))
                e.dma_start(out=y_out[0:128, :], in_=xc[:, 0, :]).then_inc(dummy, 16)
                e.wait_ge(dummy, 16)
    es.close()
    return nc


def host_inputs(inputs, NCH, depth=DEPTH, n_cores=8):
    f = np.float32
    x = np.asarray(inputs["x"], f)
    NT = NCH * 128
    bf = ml_dtypes.bfloat16
    c = np.asarray(inputs["c"], f)
    ctx = np.asarray(inputs["ctx"], f)
    c_ctx = np.asarray(inputs["c_ctx"], f)
    ng = np.asarray(inputs["norm_gain"], f)
    b_mod = np.asarray(inputs["b_mod"], f)
    common = {
        "gainT": np.ascontiguousarray(ng.reshape(depth, 8, 128).transpose(2, 0, 1).reshape(128, depth * 8)),
        "w_mod": np.ascontiguousarray(np.asarray(inputs["w_mod"], f)),
        "bmodT": np.ascontiguousarray(b_mod[:, 0:2048].reshape(depth, 16, 128).transpose(2, 0, 1).reshape(128, depth * 16)),
        "bgate": np.ascontiguousarray(b_mod[:, 2048:3072].reshape(1, depth * D)),
        "w_in": np.ascontiguousarray(np.asarray(inputs["w_in"], f)),
        "w_out": np.ascontiguousarray(np.asarray(inputs["w_out"], f)),
        "mixT": np.ascontiguousarray(np.asarray(inputs["mlp_mix"], f).transpose(0, 3, 1, 2).reshape(depth, 128, 512)),
        "mbias": np.ascontiguousarray(np.asarray(inputs["mlp_bias"], f).transpose(2, 0, 1).reshape(128, depth * 4)),
        "rdec": np.ascontiguousarray(np.stack([np.asarray(inputs["ret_decay_fwd"], f), np.asarray(inputs["ret_decay_bwd"], f)], 1).reshape(1, depth * 8)),
        "rnorm": np.ascontiguousarray(np.asarray(inputs["ret_norm"], f).reshape(1, depth * 256)),
        "qkn": np.ascontiguousarray(np.stack([np.asarray(inputs[k], f) for k in ("attn_q_norm", "swa_q_norm", "attn_k_norm", "swa_k_norm")], 1).reshape(1, depth * 256)),
        "sink": np.ascontiguousarray(np.asarray(inputs["swa_sink"], f).reshape(1, depth * 4)),
    }
    jj = np.arange(128, dtype=f)[:, None]
    ii = np.arange(128, dtype=f)[None, :]
    common["identb"] = np.eye(128, dtype=f).astype(bf)
    common["identf"] = np.eye(128, dtype=f)
    common["rpn"] = np.concatenate([np.maximum(ii - jj, 0), np.maximum(jj - ii, 0)], 1).astype(f)
    p = np.arange(128, dtype=f)
    common["pos"] = np.stack([127 - p, p, p + 1, 128 - p], 1).astype(f)
    mprev = (jj >= ii).astype(f)
    mnext = (jj <= ii).astype(f)
    common["cmask"] = np.concatenate([mprev, mnext], 1).astype(bf)
    half = 32
    inv_freq = (1.0 / (10000.0 ** (np.arange(0, half, 2, dtype=f) / f(half)))).astype(f)
    sgn = np.concatenate([-np.ones(16, f), np.ones(16, f), -np.ones(16, f), np.ones(16, f)])
    maps = []
    for core in range(n_cores):
        b, seg = core // R, core % R
        m = dict(common)
        m["x_in"] = np.ascontiguousarray(x[b, seg * NT:(seg + 1) * NT, :])
        m["ctx_in"] = np.ascontiguousarray(ctx[b])
        cT = np.zeros((128, 16), f)
        cT[:, 0::2] = c[b].reshape(8, 128).T
        cT[:, 1::2] = c_ctx.reshape(8, 128).T
        m["cT"] = cT
        tpos = seg * NT + np.arange(NT)
        row = (tpos // 64).astype(f)
        col = (tpos % 64).astype(f)
        ang_r = row[:, None] * inv_freq[None, :]
        ang_c = col[:, None] * inv_freq[None, :]
        ang = np.concatenate([ang_r, ang_r, ang_c, ang_c], -1).astype(f)
        m["cos"] = np.cos(ang).astype(f).reshape(NCH, 128, 64)
        m["sin"] = (np.sin(ang).astype(f) * sgn[None, :]).reshape(NCH, 128, 64)
        et = np.full((128, 5), BIGE, f)
        for r in range(R):
            if r < seg:
                et[0:64, r] = seg - 1 - r
            if r > seg:
                et[64:128, r] = r - seg - 1
        et[0:64, 4] = seg
        et[64:128, 4] = R - 1 - seg
        m["etab"] = et
        hm = np.zeros((128, 8, 128), f)
        if seg - 1 >= 0:
            hm[:, seg - 1, :] = mprev
        if seg + 1 < R:
            hm[:, 4 + seg + 1, :] = mnext
        m["hmask"] = hm.reshape(128, 1024).astype(bf)
        maps.append(m)
    return maps


_NC_CACHE = {}


def kernel(**inputs):
    x = np.asarray(inputs["x"])
    B, L, _ = x.shape
    NCH = L // R // 128
    depth = np.asarray(inputs["w_in"]).shape[0]
    key = (NCH, depth)
    if key not in _NC_CACHE:
        _NC_CACHE[key] = build(NCH, depth)
    nc = _NC_CACHE[key]
    maps = host_inputs(inputs, NCH, depth)
    res = run_bass_kernel_spmd(nc, maps, core_ids=list(range(8)))
    NT = NCH * 128
    out = np.zeros((B, L, D), np.float32)
    for core in range(8):
        b, seg = core // R, core % R
        out[b, seg * NT:(seg + 1) * NT, :] = res.results[core]["y"]
    return out
```

```python
import math
from contextlib import ExitStack
import numpy as np
import ml_dtypes
import concourse.bass as bass
import concourse.mybir as mybir
from concourse.bass_utils import run_bass_kernel_spmd

F32 = mybir.dt.float32
BF16 = mybir.dt.bfloat16
AF = mybir.ActivationFunctionType
ALU = mybir.AluOpType
AX = mybir.AxisListType

D = 1024
DEPTH = 4
CTXC = 2
R = 4
EPS = 1e-6
SCALE = 0.125
BIGE = 1.0e4


class Buf:
    def __init__(self, name):
        self.name = name
        self.last_w = []
        self.readers = []
        self.sem = None
        self.cnt = 0


class Op:
    __slots__ = ("eng", "fn", "deps", "kind", "sig", "val", "sem")

    def __init__(self, eng, fn, kind):
        self.eng, self.fn, self.kind = eng, fn, kind
        self.deps = []
        self.sig = False
        self.val = 0
        self.sem = None


class Prog:
    def __init__(self, nc, es):
        self.nc, self.es = nc, es
        self.ops = {k: [] for k in ("pe", "act", "dve", "pool", "sp")}
        self.esem = {k: es.enter_context(nc.semaphore("e_" + k)) for k in ("pe", "act", "dve", "pool")}
        self.nsem = 4

    def op(self, eng, fn, reads=(), writes=(), kind="c", par=False):
        o = Op(eng, fn, kind)
        deps = []
        for b in reads:
            deps += b.last_w
        for b in writes:
            deps += b.readers
            if not par:
                deps += b.last_w
        seen = set()
        for d in deps:
            if id(d) in seen or d is o:
                continue
            seen.add(id(d))
            if d.kind == "c" and d.eng == "pe" and eng == "pe" and kind == "c":
                continue
            d.sig = True
            o.deps.append(d)
        for b in reads:
            b.readers.append(o)
        for b in writes:
            if par:
                b.last_w = b.last_w + [o]
            else:
                b.last_w = [o]
            b.readers = []
        if kind in ("d", "cc"):
            b = writes[0]
            if b.sem is None:
                b.sem = self.es.enter_context(self.nc.semaphore("s_" + b.name))
                self.nsem += 1
            b.cnt += 16 if kind == "d" else 1
            o.sem, o.val = b.sem, b.cnt
        self.ops[eng].append(o)
        return o

    def finalize(self):
        for eng in ("pe", "act", "dve", "pool"):
            c = 0
            for o in self.ops[eng]:
                if o.kind == "c" and o.sig:
                    c += 1
                    o.val = c
                    o.sem = self.esem[eng]

    def emit(self, eng, e):
        waited = {}
        for o in self.ops[eng]:
            need = {}
            for d in o.deps:
                k = id(d.sem)
                if d.val > waited.get(k, 0) and (k not in need or d.val > need[k][1]):
                    need[k] = (d.sem, d.val)
            for k, (sem_, val_) in need.items():
                e.wait_ge(sem_, val_)
                waited[k] = val_
            ins = o.fn(e)
            if o.kind == "d":
                ins.then_inc(o.sem, 16)
            elif o.kind == "cc":
                ins.then_inc(o.sem, 1)
            elif o.sig:
                ins.then_inc(o.sem, 1)

    def final_wait(self, e, bufs):
        for b in bufs:
            e.wait_ge(b.sem, b.cnt)


class Rot:
    def __init__(self, tiles, name, bufs=None):
        self.tiles = tiles
        self.bufs = bufs if bufs is not None else [Buf("%s%d" % (name, i)) for i in range(len(tiles))]
        self.i = -1

    def next(self):
        self.i = (self.i + 1) % len(self.tiles)
        return self.tiles[self.i], self.bufs[self.i]


def build(NCH, depth=DEPTH, QG=2, stop=None, SIDE_RATE=2, P1LAG=9):
    NT = NCH * 128
    NTT = NT + CTXC * 128
    KB = CTXC + R * NCH
    KTOT = KB * 128
    NG = NCH // QG
    GQ = QG * 128
    PCS = min(NCH, 4)
    GP = min(NCH, 8)
    NGP = NCH // GP
    nc = bass.Bass("TRN2", target_bir_lowering=False)
    es = ExitStack()
    P = Prog(nc, es)

    def din(name, shape, dt=F32):
        return nc.dram_tensor(name, shape, dt, kind="ExternalInput").ap()

    def dint(name, shape, dt):
        return nc.dram_tensor(name, shape, dt, kind="Internal").ap()

    x_in = din("x_in", [NT, D])
    ctx_in = din("ctx_in", [CTXC * 128, D])
    cT_in = din("cT", [128, 16])
    gainT_in = din("gainT", [128, depth * 8])
    w_mod = din("w_mod", [depth, D, 3 * D])
    bmodT_in = din("bmodT", [128, depth * 16])
    bgate_in = din("bgate", [1, depth * D])
    w_in = din("w_in", [depth, D, 3328])
    w_out = din("w_out", [depth, D, D])
    mixT_in = din("mixT", [depth, 128, 512])
    mbias_in = din("mbias", [128, depth * 4])
    rdec_in = din("rdec", [1, depth * 8])
    rnorm_in = din("rnorm", [1, depth * 256])
    qkn_in = din("qkn", [1, depth * 256])
    sink_in = din("sink", [1, depth * 4])
    cos_in = din("cos", [NCH, 128, 64])
    sin_in = din("sin", [NCH, 128, 64])
    etab_in = din("etab", [128, 5])
    hmask_in = din("hmask", [128, 8 * 128], BF16)
    cmask_in = din("cmask", [128, 2 * 128], BF16)
    identb_in = din("identb", [128, 128], BF16)
    identf_in = din("identf", [128, 128])
    rpn_in = din("rpn", [128, 256])
    pos_in = din("pos", [128, 4])
    y_out = nc.dram_tensor("y", [NT, D], F32, kind="ExternalOutput").ap()

    xsA = dint("xsA", [NT, D], F32)
    xsB = dint("xsB", [NT, D], F32)
    yin_d = dint("yin_d", [NTT, 256], F32)
    qfb_d = dint("qfb_d", [NCH + CTXC, 128, 512], BF16)
    sk_d = dint("sk_d", [NCH, 128, 128], BF16)
    sv_d = dint("sv_d", [NCH, 128, 130], BF16)
    gk_x = [dint("gk_x%d" % i, [128, GP * 128], BF16) for i in range(NGP)]
    gk_all = [dint("gk_all%d" % i, [R * 128, GP * 128], BF16) for i in range(NGP)]
    gv_x = [dint("gv_x%d" % i, [128, GP * 130], BF16) for i in range(NGP)]
    gv_all = [dint("gv_all%d" % i, [R * 128, GP * 130], BF16) for i in range(NGP)]
    bnd_x = dint("bnd_x", [128, 516], BF16)
    bnd_all = dint("bnd_all", [R * 128, 516], BF16)
    agg_x = dint("agg_x", [128, 256], F32)
    agg_all = dint("agg_all", [R * 128, 256], F32)
    B_xin, B_xsA, B_xsB, B_y = Buf("xin"), Buf("xsA"), Buf("xsB"), Buf("y")
    B_yin, B_qfb, B_skd, B_svd = Buf("yin"), Buf("qfb"), Buf("skd"), Buf("svd")
    B_gkx = [Buf("gkx%d" % i) for i in range(NGP)]
    B_gkall = [Buf("gkall%d" % i) for i in range(NGP)]
    B_gvx = [Buf("gvx%d" % i) for i in range(NGP)]
    B_gvall = [Buf("gvall%d" % i) for i in range(NGP)]
    B_bndx, B_bndall, B_aggx, B_aggall = Buf("bndx"), Buf("bndall"), Buf("aggx"), Buf("aggall")
    B_const = Buf("constin")

    def sb(name, shape, dt=F32):
        return es.enter_context(nc.sbuf_tensor(name, shape, dt))

    def ps(name, shape, dt=F32):
        return es.enter_context(nc.psum_tensor(name, shape, dt))

    KT = sb("KTc", [128, CTXC * 128], BF16);     B_KT = Buf("KT")
    V = sb("Vc", [128, CTXC, 130], BF16);        B_V = Buf("V")
    ksl = Rot([sb("ksl%d" % i, [128, PCS * 128], BF16) for i in range(2)], "ksl")
    vsl = Rot([sb("vsl%d" % i, [128, PCS, 130], BF16) for i in range(2)], "vsl")
    skTc = sb("skTc", [128, CTXC * 128], BF16);  B_skTc = Buf("skTc")
    sVc = sb("sVc", [128, CTXC, 130], BF16);     B_sVc = Buf("sVc")
    ST = sb("ST", [128, NCH + 2, 256], BF16)
    B_ST = [Buf("ST%d" % i) for i in range(NCH + 2)]
    STc = sb("STc", [128, CTXC + 2, 256], BF16)
    B_STc = [Buf("STc%d" % i) for i in range(CTXC + 2)]
    WA = sb("WA", [128, 8, 1280], BF16);         B_WA = Buf("WA")
    W2 = sb("W2s", [128, 8, 2048], BF16);         B_W2 = Buf("W2")
    mixT = sb("mixTs", [128, 512], BF16);        B_mixT = Buf("mixT")
    xc = sb("xc", [128, CTXC, D], F32)
    B_xc = [Buf("xc%d" % i) for i in range(CTXC)]
    identb = sb("identb_s", [128, 128], BF16)
    identf = sb("identf_s", [128, 128], F32)
    rpn = sb("rpn_s", [128, 256], F32)
    pos = sb("pos_s", [128, 4], F32)
    etab = sb("etab_s", [128, 5], F32)
    hmask = sb("hmask_s", [128, 8 * 128], BF16)
    cmask = sb("cmask_s", [128, 256], BF16)
    cT = sb("cT_s", [128, 16], F32)
    gainT = sb("gainT_s", [128, depth * 8], F32)
    bmodT = sb("bmodT_s", [128, depth * 16], F32)
    bgate = sb("bgate_s", [1, D], F32);  B_bg = Buf("bg")
    grow = sb("grow", [1, 512], F32);    B_grow = Buf("grow")
    mbias = sb("mbias_s", [128, depth * 4], F32)
    rdec = sb("rdec_s", [128, depth * 8], F32)
    rdsel = sb("rdsel_s", [128, depth * 4], F32)
    rnorm = sb("rnorm_s", [128, 256], F32);  B_rq = Buf("rqn")
    qkn = sb("qkn_s", [128, 256], F32)
    sink = sb("sink_s", [128, depth * 4], F32)
    ones1 = sb("ones1", [1, 128], F32)
    B_cb = Buf("constsb")
    Gm = sb("Gm", [128, 2, 8], F32)
    Sm = sb("Sm", [128, 2, 8], F32)
    gateB = sb("gateB", [128, 2, D], F32)
    B_mod = Buf("mod")
    lg = sb("lg", [128, 8], F32)
    lgsel = sb("lgsel", [128, 4], F32)
    kd = sb("kd", [128, 8], F32)
    qd = sb("qd", [128, 8], F32)
    DecT = sb("DecT", [128, 512], F32)
    Dt = sb("Dt", [128, 256], F32)
    Pw = sb("Pw", [128, 256], F32)
    Aagg = sb("Aagg", [128, 256], F32)
    Actx = sb("Actx", [128, 256], F32)
    coef = sb("coef", [128, 5, 4], F32)
    esink = sb("esink", [128, 4], F32)
    B_lay = Buf("laysmall")
    B_Aagg, B_Actx, B_Pw = Buf("Aagg"), Buf("Actx"), Buf("Pw")
    aggs = sb("aggs", [128, R, 256], F32);    B_aggs = Buf("aggs")
    sintmp = sb("sintmp", [128, 256], F32);   B_sinF = Buf("sinF");  B_sinB = Buf("sinB")
    xt = Rot([sb("xt%d" % i, [128, D], F32) for i in range(2)], "xt")
    xr = xt
    st4 = Rot([sb("st4_%d" % i, [128, 16], F32) for i in range(4)], "st4")
    xn = Rot([sb("xn%d" % i, [128, D], BF16) for i in range(2)], "xn")
    hT = Rot([sb("hT%d" % i, [128, 8, 128], BF16) for i in range(2)], "hT")
    cs = Rot([sb("cs%d" % i, [128, 2, 64], F32) for i in range(2)], "cs")
    rqkv = Rot([sb("rqkv%d" % i, [128, 768], BF16) for i in range(2)], "rqkv")
    fb = Rot([sb("fb%d" % i, [128, 2, 4, 2, 64], BF16) for i in range(2)], "fb")
    fbT = Rot([sb("fbT%d" % i, [128, 2, 4, 128], BF16) for i in range(1)], "fbT")
    rT = Rot([sb("rT%d" % i, [128, 3, 2, 128], BF16) for i in range(1)], "rT")
    pint = Rot([sb("pint%d" % i, [128, 512], BF16) for i in range(1)], "pint")
    yint = Rot([sb("yint%d" % i, [128, 256], F32) for i in range(1)], "yint")
    kraw = Rot([sb("kraw%d" % i, [128, 512], F32) for i in range(2)], "kraw")
    t1 = Rot([sb("t1_%d" % i, [128, 512], F32) for i in range(1)], "t1")
    t2 = Rot([sb("t2_%d" % i, [128, 512], F32) for i in range(1)], "t2")
    knb = Rot([sb("knb%d" % i, [128, 512], BF16) for i in range(2)], "knb")
    kTs = Rot([sb("kTs%d" % i, [128, 256], BF16) for i in range(2)], "kTs")
    vsb = Rot([sb("vsb%d" % i, [128, 260], BF16) for i in range(2)], "vsb")
    NSL = 2 * QG
    gates = Rot([sb("gates%d" % i, [128, D], BF16) for i in range(NSL)], "gates")
    mixo = Rot([sb("mixo%d" % i, [128, D], BF16) for i in range(NSL)], "mixo")
    uvg = Rot([sb("uvg%d" % i, [128, 512], BF16) for i in range(1)], "uvg")
    gqT = Rot([sb("gqT%d" % i, [128, 2, 2, GQ], BF16) for i in range(2)], "gqT")
    sqT = Rot([sb("sqT%d" % i, [128, 2, 2, 128], BF16) for i in range(2)], "sqT")
    pT = Rot([sb("pT%d" % i, [128, 512], BF16) for i in range(3)], "pT")
    oT = Rot([sb("oT%d" % i, [65, 512], F32) for i in range(2)], "oT")
    rden = Rot([sb("rden%d" % i, [128, 4], F32) for i in range(4)], "rden")
    mixTt = Rot([sb("mixTt%d" % i, [128, 8, 128], BF16) for i in range(1)], "mixTt")
    otmp = Rot([sb("otmp%d" % i, [128, D], F32) for i in range(1)], "otmp")
    wmst = Rot([t_[:].rearrange("p (k c) -> p k c", k=8) for t_ in (otmp.tiles[0], xt.tiles[0], xt.tiles[1])], "wmst",
               bufs=[otmp.bufs[0], xt.bufs[0], xt.bufs[1]])
    yinl = Rot([sb("yinl%d" % i, [128, 256], F32) for i in range(1)], "yinl")
    qfbl = Rot([sb("qfbl%d" % i, [128, 512], BF16) for i in range(1)], "qfbl")
    ysum = Rot([sb("ysum%d" % i, [128, 256], F32) for i in range(2)], "ysum")
    wk = Rot([sb("wk%d" % i, [128, 6, 128], BF16) for i in range(1)], "wk")
    wv = Rot([sb("wv%d" % i, [128, 6, 130], BF16) for i in range(1)], "wv")
    wp = Rot([sb("wp%d" % i, [128, 256], BF16) for i in range(3)], "wp")
    woT = Rot([sb("woT%d" % i, [65, 256], F32) for i in range(1)], "woT")
    zps = Rot([ps("zps%d" % i, [128, 512]) for i in range(2)], "zps")
    tps = Rot([ps("tps%d" % i, [128, 1024], BF16) for i in range(1)], "tps")
    sps = Rot([ps("sps%d" % i, [128, 512]) for i in range(3)], "sps")
    ops_ = Rot([ps("ops%d" % i, [128, 512]) for i in range(2)], "ops")
    zps1 = Rot([zps.tiles[0]], "zps1", bufs=[zps.bufs[0]])
    swacc = Rot([zps.tiles[1]], "swacc", bufs=[zps.bufs[1]])
    zpsP1 = Rot(zps.tiles + ops_.tiles, "zpsP1", bufs=zps.bufs + ops_.bufs)
    PS = {"z": zpsP1, "acc": ops_}

    def dma(q, out, in_, reads, writes, par=False, **kw):
        return P.op(q, lambda e: e.dma_start(out=out, in_=in_, **kw), reads, writes, kind="d", par=par)

    def mm(out, lhsT, rhs, start, stop, reads, writes):
        return P.op("pe", lambda e: e.matmul(out, lhsT=lhsT, rhs=rhs, start=start, stop=stop), reads, writes)

    def tr(out, in_, ident, reads, writes):
        return P.op("pe", lambda e: e.transpose(out, in_, ident), reads, writes)

    def act(out, in_, func, reads, writes, **kw):
        return P.op("act", lambda e: e.activation(out=out, in_=in_, func=func, **kw), reads, writes)

    def tt(eng, out, in0, in1, op, reads, writes):
        return P.op(eng, lambda e: e.tensor_tensor(out=out, in0=in0, in1=in1, op=op), reads, writes)

    def ts(eng, out, in0, s1, s2, op0, op1, reads, writes):
        if op1 is None:
            return P.op(eng, lambda e: e.tensor_scalar(out=out, in0=in0, scalar1=s1, scalar2=None, op0=op0), reads, writes)
        return P.op(eng, lambda e: e.tensor_scalar(out=out, in0=in0, scalar1=s1, scalar2=s2, op0=op0, op1=op1), reads, writes)

    def stt(out, in0, scalar, in1, op0, op1, reads, writes):
        return P.op("dve", lambda e: e.scalar_tensor_tensor(out=out, in0=in0, scalar=scalar, in1=in1, op0=op0, op1=op1), reads, writes)

    def cp(eng, out, in_, reads, writes):
        if eng == "act":
            return P.op("act", lambda e: e.copy(out=out, in_=in_), reads, writes)
        return P.op(eng, lambda e: e.tensor_copy(out=out, in_=in_), reads, writes)

    def rsqrt_mean_g(out, ssum, n, reads, writes):
        ts("dve", out, ssum, 1.0 / n, EPS, ALU.mult, ALU.add, reads, writes)
        yield
        act(out, out, AF.Ln, writes, writes)
        yield
        act(out, out, AF.Exp, writes, writes, scale=-0.5)
        yield

    def rsqrt_mean(out, ssum, n, reads, writes):
        ts("dve", out, ssum, 1.0 / n, EPS, ALU.mult, ALU.add, reads, writes)
        act(out, out, AF.Ln, writes, writes)
        act(out, out, AF.Exp, writes, writes, scale=-0.5)

    def bc(ap, n):
        return ap.partition_broadcast(n)

    for dst, src in ((identb, identb_in), (identf, identf_in), (rpn, rpn_in), (pos, pos_in), (etab, etab_in),
                     (hmask, hmask_in), (cmask, cmask_in), (cT, cT_in), (gainT, gainT_in), (bmodT, bmodT_in),
                     (mbias, mbias_in)):
        dma("sp", dst[:], src, [B_const], [B_cb], par=True)
    dma("sp", rdec[:], rdec_in.partition_broadcast(128), [B_const], [B_cb], par=True)
    for l in range(depth):
        dma("sp", rdsel[0:64, l * 4:(l + 1) * 4], rdec_in[:, l * 8:l * 8 + 4].partition_broadcast(64), [B_const], [B_cb], par=True)
        dma("sp", rdsel[64:128, l * 4:(l + 1) * 4], rdec_in[:, l * 8 + 4:l * 8 + 8].partition_broadcast(64), [B_const], [B_cb], par=True)
    dma("sp", sink[:], sink_in.partition_broadcast(128), [B_const], [B_cb], par=True)
    for c in range(CTXC):
        dma("sp", xc[:, c, :], ctx_in[c * 128:(c + 1) * 128, :], [B_const], [B_xc[c]])
    B_c2 = Buf("const2")
    P.op("dve", lambda e: e.memset(ones1[:], 1.0), [], [B_c2])
    for rot_ in (gqT, sqT, rT):
        for i in range(len(rot_.tiles)):
            P.op("pool", lambda e, t_=rot_.tiles[i]: e.memset(t_[:], 0.0), [], [rot_.bufs[i]])
    P.op("dve", lambda e: e.memset(V[:], 1.0), [], [B_V])
    P.op("dve", lambda e: e.memset(sVc[:], 1.0), [], [B_sVc])
    for i in range(2):
        P.op("pool", lambda e, i=i: e.memset(vsb.tiles[i][:], 1.0), [], [vsb.bufs[i]])
    act(cT[:], cT[:], AF.Silu, [B_cb], [B_cb])

    def load_w1(l):
        srcs = [(768, 1536, 0), (2048, 2176, 768), (2816, 2944, 896), (2176, 2304, 1024), (2944, 3072, 1152)]
        for k in range(8):
            for (a, b_, d0) in srcs:
                dma("pool", WA[:, k, d0:d0 + (b_ - a)], w_in[l, k * 128:(k + 1) * 128, a:b_], [B_const], [B_WA], par=True)

    def load_w2(l):
        srcs = [(0, 768, 0), (1536, 1792, 768), (2304, 2560, 1024), (3072, 3328, 1280), (1792, 2048, 1536), (2560, 2816, 1792)]
        for k in range(8):
            for (a, b_, d0) in srcs:
                dma("pool", W2[:, k, d0:d0 + (b_ - a)], w_in[l, k * 128:(k + 1) * 128, a:b_], [B_const], [B_W2], par=True)
        dma("pool", mixT[:], mixT_in[l], [B_const], [B_mixT])

    def load_wo(l):
        for k in range(8):
            dma("pool", WA[:, k, 0:1024], w_out[l, k * 128:(k + 1) * 128, :], [B_const], [B_WA], par=True)

    def layer_consts(l):
        rd = [B_cb, B_c2]
        w = [B_lay]
        dma("sp", rnorm[:], rnorm_in[:, l * 256:(l + 1) * 256].partition_broadcast(128), [B_const], [B_rq])
        dma("sp", qkn[:], qkn_in[:, l * 256:(l + 1) * 256].partition_broadcast(128), [B_const], [B_rq], par=True)
        act(lg[:], rdec[:, l * 8:(l + 1) * 8], AF.Exp, rd, w)
        ts("dve", lg[:], lg[:], -1.0, None, ALU.mult, None, w, w)
        act(lgsel[:], rdsel[:, l * 4:(l + 1) * 4], AF.Exp, rd, w)
        ts("dve", lgsel[:], lgsel[:], -1.0, None, ALU.mult, None, w, w)
        for dr in range(2):
            ts("dve", kd[:, dr * 4:(dr + 1) * 4], lg[:, dr * 4:(dr + 1) * 4], pos[:, dr:dr + 1], None, ALU.mult, None, rd + w, w)
            ts("dve", qd[:, dr * 4:(dr + 1) * 4], lg[:, dr * 4:(dr + 1) * 4], pos[:, 2 + dr:3 + dr], None, ALU.mult, None, rd + w, w)
        act(kd[:], kd[:], AF.Exp, w, w)
        ts("dve", kd[:], kd[:], SCALE, None, ALU.mult, None, w, w)
        act(qd[:], qd[:], AF.Exp, w, w)
        for h in range(4):
            ts("dve", DecT[:, h * 128:(h + 1) * 128], rpn[:, 0:128], lg[:, h:h + 1], None, ALU.mult, None, rd + w, w)
            stt(DecT[:, h * 128:(h + 1) * 128], rpn[:, 128:256], lg[:, 4 + h:5 + h], DecT[:, h * 128:(h + 1) * 128],
                ALU.mult, ALU.add, rd + w, w)
        act(DecT[:], DecT[:], AF.Exp, w, w)
        ts("dve", DecT[:], DecT[:], SCALE, None, ALU.mult, None, w, w)
        ts("dve", esink[:], lgsel[:], 128.0, None, ALU.mult, None, w, w)
        act(esink[:], esink[:], AF.Exp, w, w)
        cp("dve", Dt[:].rearrange("p (h e) -> p h e", h=4), esink[:].unsqueeze(2).to_broadcast([128, 4, 64]), w, w)
        for s_ in range(5):
            ts("dve", coef[:, s_, :], lgsel[:], etab[:, s_:s_ + 1], 128.0 * NCH, ALU.mult, ALU.mult, rd + w, w)
        act(coef[:], coef[:], AF.Exp, w, w)
        act(esink[:], sink[:, l * 4:(l + 1) * 4], AF.Exp, rd + w, w)
        P.op("pool", lambda e: e.memset(Aagg[:], 0.0), [], [B_Aagg])
        P.op("pool", lambda e: e.memset(Actx[:], 0.0), [], [B_Actx])
        P.op("pool", lambda e: e.memset(Pw[:], 1.0), [], [B_Pw])
        P.op("pool", lambda e: e.memset(STc[0:64, 1, :], 0.0), [], [B_STc[1]], par=True)
        P.op("pool", lambda e: e.memset(STc[64:128, 2, :], 0.0), [], [B_STc[2]], par=True)

    def mod_compute(l):
        dma("sp", bgate[:], bgate_in[:, l * D:(l + 1) * D], [B_const], [B_bg])
        mp, mb = zps.next()
        for half in range(-1, 2):
            if half < 0:
                for c in range(16):
                    wt, wb = wmst.next()
                    dma("sp", wt[:], w_mod[l, :, c * 128:(c + 1) * 128].rearrange("(k p) c -> p k c", p=128), [B_const], [wb])
                    for k in range(8):
                        mm(mp[:, c * 2:c * 2 + 2], wt[:, k, :], cT[:, 2 * k:2 * k + 2], k == 0, k == 7, [wb, B_cb], [mb])
                continue
            g0, gb0 = sps.next()
            g1, gb1 = sps.next()
            gps = ((g0, gb0), (g1, gb1))
            for q in range(4):
                col = 2048 + half * 512 + q * 128
                wt, wb = wmst.next()
                dma("sp", wt[:], w_mod[l, :, col:col + 128].rearrange("(k p) c -> p k c", p=128), [B_const], [wb])
                for j in range(2):
                    for k in range(8):
                        mm(gps[j][0][0:1, q * 128:(q + 1) * 128], cT[:, 2 * k + j:2 * k + j + 1], wt[:, k, :], k == 0, k == 7, [wb, B_cb], [gps[j][1]])
            for j in range(2):
                tt("dve", grow[:], gps[j][0][0:1, 0:512], bgate[0:1, half * 512:(half + 1) * 512], ALU.add, [gps[j][1], B_bg], [B_grow])
                zp, zb = ops_.next()
                mm(zp[:, 0:512], ones1[0:1, :], grow[0:1, :], True, True, [B_c2, B_grow], [zb])
                cp("dve", gateB[:, j, half * 512:(half + 1) * 512], zp[:, 0:512], [zb], [B_mod], )
        mpv = mp[:, 0:32].rearrange("p (c j) -> p j c", j=2)
        for j in range(2):
            tt("dve", Sm[:, j, :], mpv[:, j, 0:8], bmodT[:, l * 16:l * 16 + 8], ALU.add, [mb, B_cb], [B_mod])
            tt("dve", Gm[:, j, :], mpv[:, j, 8:16], bmodT[:, l * 16 + 8:l * 16 + 16], ALU.add, [mb, B_cb], [B_mod])
            stt(Gm[:, j, :], Gm[:, j, :], 1.0, gainT[:, l * 8:(l + 1) * 8], ALU.add, ALU.mult, [B_mod, B_cb], [B_mod])

    def norm_tile(l, xap, xbuf, j):
        s4, s4b = st4.next()
        xnt, xnb = xn.next()
        P.op("dve", lambda e: e.scalar_tensor_tensor(out=xnt[:], in0=xap, scalar=1.0, in1=xap, op0=ALU.mult, op1=ALU.mult,
                                                      accum_out=s4[:, 0:1]), [xbuf], [xnb, s4b])
        rsqrt_mean(s4[:, 0:1], s4[:, 0:1], D, [s4b], [s4b])
        ts("dve", xnt[:], xap, s4[:, 0:1], None, ALU.mult, None, [xbuf, s4b], [xnb])
        tp, tb = tps.next()
        for k in range(8):
            tr(tp[:, k * 128:(k + 1) * 128], xnt[:, k * 128:(k + 1) * 128], identb[:], [xnb, B_cb], [tb])
        ht, hb = hT.next()
        for k in range(8):
            ts("dve", ht[:, k, :], tp[:, k * 128:(k + 1) * 128], Gm[:, j, k:k + 1], Sm[:, j, k:k + 1], ALU.mult, ALU.add,
               [tb, B_mod], [hb])
        return ht, hb

    def norm_tile_g(l, xap, xbuf, j):
        s4, s4b = st4.next()
        xnt, xnb = xn.next()
        P.op("dve", lambda e: e.scalar_tensor_tensor(out=xnt[:], in0=xap, scalar=1.0, in1=xap, op0=ALU.mult, op1=ALU.mult,
                                                      accum_out=s4[:, 0:1]), [xbuf], [xnb, s4b])
        yield
        yield from rsqrt_mean_g(s4[:, 0:1], s4[:, 0:1], D, [s4b], [s4b])
        ts("dve", xnt[:], xap, s4[:, 0:1], None, ALU.mult, None, [xbuf, s4b], [xnb])
        yield
        tp, tb = tps.next()
        for k in range(8):
            tr(tp[:, k * 128:(k + 1) * 128], xnt[:, k * 128:(k + 1) * 128], identb[:], [xnb, B_cb], [tb])
        yield
        yield
        ht, hb = hT.next()
        for k in range(8):
            ts("dve", ht[:, k, :], tp[:, k * 128:(k + 1) * 128], Gm[:, j, k:k + 1], Sm[:, j, k:k + 1], ALU.mult, ALU.add,
               [tb, B_mod], [hb])
        yield
        yield
        return ht, hb

    def inproj(ht, hb, W, WB, c0, ncols):
        zp, zb = PS["z"].next()
        for k in range(8):
            mm(zp[:, 0:ncols], ht[:, k, :], W[:, k, c0:c0 + ncols], k == 0, k == 7, [hb, WB], [zb])
        return zp, zb

    def qk_norm_rope(src, srcb, nh, gain_ap, csb, rope):
        n = nh * 64
        Bt, Bb = t1.next()
        Ct, Cb = t2.next()
        s4, s4b = st4.next()
        v3 = lambda ap: ap[:, 0:n].rearrange("p (h d) -> p h d", d=64)
        tt("dve", Bt[:, 0:n], src[:, 0:n], src[:, 0:n], ALU.mult, [srcb], [Bb])
        yield
        P.op("dve", lambda e: e.tensor_reduce(out=s4[:, 0:nh], in_=v3(Bt), axis=AX.X, op=ALU.add), [Bb], [s4b])
        yield
        yield from rsqrt_mean_g(s4[:, 0:nh], s4[:, 0:nh], 64, [s4b], [s4b])
        tt("dve", v3(Ct), v3(src), s4[:, 0:nh].unsqueeze(2).to_broadcast([128, nh, 64]), ALU.mult, [srcb, s4b], [Cb])
        tt("dve", Ct[:, 0:n].rearrange("p (a h d) -> p a h d", a=2, d=64), Ct[:, 0:n].rearrange("p (a h d) -> p a h d", a=2, d=64),
           gain_ap.rearrange("p (a d) -> p a d", a=2).unsqueeze(2).to_broadcast([128, 2, nh // 2, 64]), ALU.mult, [Cb, B_rq], [Cb])
        yield
        ot, ob = knb.next()
        if not rope:
            cp("dve", ot[:, 0:n], Ct[:, 0:n], [Cb], [ob])
            return ot, ob
        cst, csbuf = csb
        v4 = lambda ap: ap[:, 0:n].rearrange("p (h a b c) -> p h a b c", a=2, b=2, c=16)
        cosb = cst[:, 0, :].unsqueeze(1).to_broadcast([128, nh, 64])
        sin4 = cst[:, 1, :].rearrange("p (a b c) -> p a b c", a=2, b=2)
        tt("dve", v3(Bt), v3(Ct), cosb, ALU.mult, [Cb, csbuf], [Bb])
        yield
        for b0 in range(2):
            tt("pool", v4(src)[:, :, :, b0, :], v4(Ct)[:, :, :, 1 - b0, :],
               sin4[:, :, b0, :].unsqueeze(1).to_broadcast([128, nh, 2, 16]), ALU.mult, [Cb, csbuf], [srcb], )
        yield
        yield
        tt("dve", ot[:, 0:n], Bt[:, 0:n], src[:, 0:n], ALU.add, [Bb, srcb], [ob])
        yield
        return ot, ob

    def ret_state(is_ctx):
        return (STc, B_STc, Actx, B_Actx) if is_ctx else (ST, B_ST, Aagg, B_Aagg)

    def p1_tile(l, t, xsrc, B_xsrc):
        is_ctx = t < CTXC
        c = t if is_ctx else t - CTXC
        j = 1 if is_ctx else 0
        if is_ctx:
            xap, xbuf = xc[:, c, :], B_xc[c]
            csb = None
        else:
            xt_, xbuf = xt.next()
            dma("sp", xt_[:], xsrc[c * 128:(c + 1) * 128, :], [B_xsrc], [xbuf])
            xap = xt_[:]
            cst, csbuf = cs.next()
            dma("sp", cst[:, 0, :], cos_in[c], [B_const], [csbuf])
            dma("sp", cst[:, 1, :], sin_in[c], [B_const], [csbuf], par=True)
            csb = (cst, csbuf)
        yield
        ht, hb = norm_tile(l, xap, xbuf, j)
        yield
        rq, rqb = rqkv.next()
        kr, krb = kraw.next()
        vs, vsbuf = vsb.next()
        zp, zb = inproj(ht, hb, WA, B_WA, 0, 512)
        cp("act", rq[:, 0:512], zp[:, 0:512], [zb], [rqb])
        yield
        zp, zb = inproj(ht, hb, WA, B_WA, 512, 512)
        cp("act", rq[:, 512:768], zp[:, 0:256], [zb], [rqb], )
        cp("act", kr[:, 0:256], zp[:, 256:512], [zb], [krb])
        yield
        zp, zb = inproj(ht, hb, WA, B_WA, 1024, 256)
        cp("dve", vs[:, 1:129], zp[:, 0:128], [zb], [vsbuf])
        cp("dve", vs[:, 131:259], zp[:, 128:256], [zb], [vsbuf])
        yield
        fbt, fbb = fb.next()
        for w_, dec in ((0, qd), (1, kd)):
            for dr in range(2):
                tt("dve", fbt[:, w_, :, dr, :], rq[:, w_ * 256:(w_ + 1) * 256].rearrange("p (h d) -> p h d", h=4),
                   dec[:, dr * 4:(dr + 1) * 4].unsqueeze(2).to_broadcast([128, 4, 64]), ALU.mult, [rqb, B_lay], [fbb], )
        yield
        tp, tb = tps.next()
        for w_ in range(2):
            for h in range(4):
                tr(tp[:, (w_ * 4 + h) * 128:(w_ * 4 + h + 1) * 128], fbt[:, w_, h, :, :].rearrange("p a d -> p (a d)"), identb[:],
                   [fbb, B_cb], [tb])
        fT, fTb = fbT.next()
        cp("act", fT[:].rearrange("p a h t -> p (a h t)"), tp[:, 0:1024], [tb], [fTb])
        dma("pool", qfb_d[t], fT[:, 0, :, :].rearrange("p h t -> p (h t)"), [fTb], [B_qfb], par=True)
        yield
        tp, tb = tps.next()
        for w_ in range(2):
            for pr in range(2):
                tr(tp[:, (w_ * 2 + pr) * 128:(w_ * 2 + pr + 1) * 128], rq[:, w_ * 256 + pr * 128:w_ * 256 + (pr + 1) * 128], identb[:],
                   [rqb, B_cb], [tb])
        rt, rtb = rT.next()
        cp("act", rt[0:64, 0, :, :], tp[0:64, 0:256].rearrange("p (b t) -> p b t", b=2), [tb], [rtb], )
        cp("act", rt[64:128, 1, :, :], tp[64:128, 0:256].rearrange("p (b t) -> p b t", b=2), [tb], [rtb], )
        cp("act", rt[:, 2, :, :], tp[:, 256:512].rearrange("p (b t) -> p b t", b=2), [tb], [rtb], )
        sp0, sb0 = sps.next()
        sp1, sb1 = sps.next()
        for pr in range(2):
            mm(sp0[:, pr * 128:(pr + 1) * 128], rt[:, 2, pr, :], rt[:, 0, pr, :], True, True, [rtb], [sb0])
            mm(sp1[:, pr * 128:(pr + 1) * 128], rt[:, 2, pr, :], rt[:, 1, pr, :], True, True, [rtb], [sb1])
        pi, pib = pint.next()
        piv = pi[:].rearrange("p (a b i) -> p a b i", a=2, b=2)
        dcv = DecT[:].rearrange("p (a b i) -> p a b i", a=2, b=2)
        for hh, (spx, sbx) in enumerate(((sp0, sb0), (sp1, sb1))):
            tt("dve", piv[:, :, hh, :], spx[:, 0:256].rearrange("p (a i) -> p a i", a=2), dcv[:, :, hh, :], ALU.mult,
               [sbx, B_lay], [pib], )
        mp, mb = PS["z"].next()
        for h in range(4):
            mm(mp[:, h * 64:(h + 1) * 64], pi[:, h * 128:(h + 1) * 128], rq[:, 512 + h * 64:512 + (h + 1) * 64], True, True, [pib, rqb], [mb])
        for h in range(4):
            mm(mp[:, 256 + h * 64:256 + (h + 1) * 64], fbt[:, 1, h, :, :].rearrange("p a d -> p (a d)"),
               rq[:, 512 + h * 64:512 + (h + 1) * 64], True, True, [fbb, rqb], [mb])
        yield
        yt, ytb = yint.next()
        cp("act", yt[:], mp[:, 0:256], [mb], [ytb])
        row0 = t * 128
        dma("pool", yin_d[row0:row0 + 128, :], yt[:], [ytb], [B_yin], par=True)
        Sx, BS, Ax, BA = ret_state(is_ctx)
        cp("act", Sx[0:64, c + 2, :], mp[0:64, 256:512], [mb], [BS[c + 2]], )
        cp("act", Sx[64:128, c, :], mp[64:128, 256:512], [mb], [BS[c]], )
        yield
        tt("pool", Ax[0:64, :], Ax[0:64, :], Dt[0:64, :], ALU.mult, [BA, B_lay], [BA])
        tt("pool", Ax[0:64, :], Ax[0:64, :], Sx[0:64, c + 2, :], ALU.add, [BA, BS[c + 2]], [BA])
        if is_ctx:
            if c == 0:
                tt("pool", Ax[64:128, :], Ax[64:128, :], Sx[64:128, c, :], ALU.add, [BA, BS[c]], [BA])
            else:
                tt("pool", sintmp[64:128, :], Dt[64:128, :], Sx[64:128, c, :], ALU.mult, [B_lay, BS[c]], [B_sinB])
                tt("pool", Ax[64:128, :], Ax[64:128, :], sintmp[64:128, :], ALU.add, [BA, B_sinB], [BA])
        else:
            tt("pool", sintmp[64:128, :], Pw[64:128, :], Sx[64:128, c, :], ALU.mult, [B_Pw, BS[c]], [B_sinB])
            tt("pool", Ax[64:128, :], Ax[64:128, :], sintmp[64:128, :], ALU.add, [BA, B_sinB], [BA])
            tt("pool", Pw[64:128, :], Pw[64:128, :], Dt[64:128, :], ALU.mult, [B_Pw, B_lay], [B_Pw])
        yield
        gk_gain = qkn[:, 128:256]
        kn, knbuf = yield from qk_norm_rope(kr, krb, 4, gk_gain, csb, not is_ctx)
        yield
        tp, tb = tps.next()
        for a in range(2):
            tr(tp[:, a * 128:(a + 1) * 128], kn[:, a * 128:(a + 1) * 128], identb[:], [knbuf, B_cb], [tb])
        if is_ctx:
            cp("act", KT[:, c * 128:(c + 1) * 128], tp[:, 0:128], [tb], [B_KT], )
            cp("act", skTc[:, c * 128:(c + 1) * 128], tp[:, 128:256], [tb], [B_skTc], )
            cp("dve", V[:, c, 1:129], vs[:, 1:129], [vsbuf], [B_V])
            cp("dve", sVc[:, c, 1:129], vs[:, 131:259], [vsbuf], [B_sVc])
        else:
            kt_, ktb = kTs.next()
            cp("act", kt_[:], tp[:, 0:256], [tb], [ktb])
            dma("pool", gk_x[c // GP][:, (c % GP) * 128:(c % GP + 1) * 128], kt_[:, 0:128], [ktb], [B_gkx[c // GP]], par=True)
            dma("pool", sk_d[c], kt_[:, 128:256], [ktb], [B_skd], par=True)
            dma("pool", gv_x[c // GP][:, (c % GP) * 130:(c % GP + 1) * 130], vs[:, 0:130], [vsbuf], [B_gvx[c // GP]], par=True)
            dma("pool", sv_d[c], vs[:, 130:260], [vsbuf], [B_svd], par=True)
            if c == 0:
                dma("pool", bnd_x[:, 0:128], kt_[:, 128:256], [ktb], [B_bndx], par=True)
                dma("pool", bnd_x[:, 256:386], vs[:, 130:260], [vsbuf], [B_bndx], par=True)
            if c == NCH - 1:
                dma("pool", bnd_x[:, 128:256], kt_[:, 128:256], [ktb], [B_bndx], par=True)
                dma("pool", bnd_x[:, 386:516], vs[:, 130:260], [vsbuf], [B_bndx], par=True)

    def allgather(src, bs, dst, bd):
        groups = [[0, 1, 2, 3], [4, 5, 6, 7]]
        P.op("pool", lambda e: e.collective_compute("AllGather", ALU.bypass, replica_groups=groups, ins=[src], outs=[dst]),
             [bs], [bd], kind="cc")

    def gather_piece(i):
        allgather(gk_x[i], B_gkx[i], gk_all[i], B_gkall[i])
        allgather(gv_x[i], B_gvx[i], gv_all[i], B_gvall[i])

    def exchange(l):
        dma("pool", agg_x, Aagg[:], [B_Aagg], [B_aggx])
        allgather(agg_x, B_aggx, agg_all, B_aggall)
        allgather(bnd_x, B_bndx, bnd_all, B_bndall)

    def exchange_b(l):
        dma("sp", aggs[:], agg_all.rearrange("(r p) c -> p r c", p=128), [B_aggall], [B_aggs])
        B_s2 = Buf("s2")
        v3 = lambda ap: ap.rearrange("p (h e) -> p h e", h=4)
        cb = lambda s_: coef[:, s_, :].unsqueeze(2).to_broadcast([128, 4, 64])
        SB2 = [B_sinF, B_sinB]
        tt("dve", v3(sintmp[:]), v3(Actx[:]), cb(4), ALU.mult, [B_Actx, B_lay], SB2)
        for r in range(R):
            tt("dve", v3(aggs[:, r, :]), v3(aggs[:, r, :]), cb(r), ALU.mult, [B_aggs, B_lay], [B_aggs])
            tt("dve", sintmp[:], sintmp[:], aggs[:, r, :], ALU.add, SB2 + [B_aggs], SB2)
        cp("dve", ST[0:64, 1, :], sintmp[0:64, :], [B_sinF], [B_ST[1]], )
        cp("dve", ST[64:128, NCH, :], sintmp[64:128, :], [B_sinB], [B_ST[NCH]], )
        sF, sB_ = sintmp[0:64, :], sintmp[64:128, :]
        for i in range(1, NCH):
            cf = i
            cbk = NCH - 1 - i
            tt("dve", sF, sF, Dt[0:64, :], ALU.mult, [B_sinF, B_lay], [B_sinF])
            tt("dve", sB_, sB_, Dt[64:128, :], ALU.mult, [B_sinB, B_lay], [B_sinB])
            tt("dve", sF, sF, ST[0:64, cf + 1, :], ALU.add, [B_sinF, B_ST[cf + 1]], [B_sinF])
            tt("dve", sB_, sB_, ST[64:128, cbk + 1, :], ALU.add, [B_sinB, B_ST[cbk + 1]], [B_sinB])
            cp("dve", ST[0:64, cf + 1, :], sF, [B_sinF], [B_ST[cf + 1]], )
            cp("dve", ST[64:128, cbk + 1, :], sB_, [B_sinB], [B_ST[cbk + 1]], )

    def small_attn(qT_ap, qTb, blocks, sink_l, mo, mob, gt, gtb, colbase):
        for g in range(2):
            op_, opb = PS["acc"].next()
            pend = []

            def pv(item):
                bi, vap, w_, wb_, bufs = item
                mm(op_[0:65, 0:256], vap[:, g * 65:(g + 1) * 65], w_[:], bi == 0, bi == len(blocks) - 1, [wb_] + bufs, [opb])

            for bi, (kap, vap, mask, bufs) in enumerate(blocks):
                sp_, spb = sps.next()
                mm(sp_[:, 0:256], kap, qT_ap[:, g, :, :].rearrange("p r t -> p (r t)"), True, True,
                   [qTb] + bufs, [spb])
                yield
                w_, wb_ = wp.next()
                act(w_[:], sp_[:, 0:256], AF.Exp, [spb], [wb_], scale=SCALE)
                yield
                if mask is not None:
                    tt("pool", w_[:].rearrange("p (r t) -> p r t", r=2), w_[:].rearrange("p (r t) -> p r t", r=2),
                       mask.unsqueeze(1).to_broadcast([128, 2, 128]), ALU.mult, [wb_, B_cb], [wb_])
                    yield
                pend.append((bi, vap, w_, wb_, bufs))
                if len(pend) > 1:
                    pv(pend.pop(0))
            for item in pend:
                pv(item)
            yield
            wo, wob = woT.next()
            cp("dve", wo[:], op_[0:65, 0:256], [opb], [wob])
            yield
            for r in range(2):
                h = g * 2 + r
                zp, zb = PS["z"].next()
                tr(zp[:, 0:65], wo[:, r * 128:(r + 1) * 128], identf[0:65, 0:65], [wob, B_cb], [zb])
                yield
                finish_head(zp, zb, g, h, sink_l, mo, mob, gt, gtb, colbase)
                yield

    def finish_head(zp, zb, g, h, sink_l, mo, mob, gt, gtb, colbase):
        rd_, rdb = rden.next()
        dcol = 0 if g == 0 else 64
        o0 = 1 if g == 0 else 0
        if sink_l:
            ts("dve", rd_[:, 0:1], zp[:, dcol:dcol + 1], esink[:, h:h + 1], None, ALU.add, None, [zb, B_lay], [rdb])
            P.op("dve", lambda e: e.reciprocal(out=rd_[:, 0:1], in_=rd_[:, 0:1]), [rdb], [rdb])
        else:
            P.op("dve", lambda e: e.reciprocal(out=rd_[:, 0:1], in_=zp[:, dcol:dcol + 1]), [zb], [rdb])
        stt(mo[:, colbase + h * 64:colbase + (h + 1) * 64], zp[:, o0:o0 + 64], rd_[:, 0:1], gt[:, colbase + h * 64:colbase + (h + 1) * 64],
            ALU.mult, ALU.mult, [zb, rdb, gtb], [mob])

    def silu_evac(dst, dstb, zp, zb):
        Ct, Cb = t2.next()
        act(Ct[:, 0:512], zp[:, 0:512], AF.Exp, [zb], [Cb], scale=-1.0)
        yield
        ts("dve", Ct[:, 0:512], Ct[:, 0:512], 1.0, None, ALU.add, None, [Cb], [Cb])
        yield
        P.op("dve", lambda e: e.reciprocal(out=Ct[:, 0:512], in_=Ct[:, 0:512]), [Cb], [Cb])
        yield
        yield
        tt("dve", dst, zp[:, 0:512], Ct[:, 0:512], ALU.mult, [zb, Cb], [dstb])

    def p2_front(l, t, xsrc, B_xsrc):
        is_ctx = t < CTXC
        c = t if is_ctx else t - CTXC
        j = 1 if is_ctx else 0
        if is_ctx:
            xap, xbuf = xc[:, c, :], B_xc[c]
            csb = None
        else:
            xt_, xbuf = xt.next()
            dma("sp", xt_[:], xsrc[c * 128:(c + 1) * 128, :], [B_xsrc], [xbuf])
            xap = xt_[:]
            cst, csbuf = cs.next()
            dma("sp", cst[:, 0, :], cos_in[c], [B_const], [csbuf])
            dma("sp", cst[:, 1, :], sin_in[c], [B_const], [csbuf], par=True)
            csb = (cst, csbuf)
        yl, ylb = yinl.next()
        dma("sp", yl[:], yin_d[t * 128:(t + 1) * 128, :], [B_yin], [ylb])
        ql, qlb = qfbl.next()
        dma("sp", ql[:], qfb_d[t], [B_qfb], [qlb])
        yield
        yield
        ht, hb = yield from norm_tile_g(l, xap, xbuf, j)
        ug, ugb = uvg.next()
        gt, gtb = gates.next()
        mo, mob = mixo.next()
        qr, qrb = kraw.next()
        zp, zb = inproj(ht, hb, W2, B_W2, 0, 512)
        yield
        yield
        act(ug[:], zp[:, 0:512], AF.Gelu, [zb], [ugb])
        yield
        zp, zb = inproj(ht, hb, W2, B_W2, 512, 512)
        yield
        yield
        yield from silu_evac(gt[:, 0:512], gtb, zp, zb)
        yield
        zp, zb = inproj(ht, hb, W2, B_W2, 1024, 512)
        yield
        yield
        yield from silu_evac(gt[:, 512:1024], gtb, zp, zb)
        yield
        zp, zb = inproj(ht, hb, W2, B_W2, 1536, 512)
        yield
        yield
        for blk in range(2):
            cp("dve", qr[:, blk * 256:(blk + 1) * 256].rearrange("p (r g d) -> p r g d", r=2, g=2),
               zp[:, blk * 256:(blk + 1) * 256].rearrange("p (g r d) -> p r g d", r=2, g=2), [zb], [qrb], )
        yield
        mp, mb = PS["z"].next()
        for h in range(4):
            mm(mp[:, h * 64:(h + 1) * 64], mixT[:, h * 128:(h + 1) * 128], ug[:, 256 + h * 64:256 + (h + 1) * 64], True, True, [B_mixT, ugb], [mb])
        Sx, BS, _, _ = ret_state(is_ctx)
        for h in range(4):
            mm(mp[:, 256 + h * 64:256 + (h + 1) * 64], ql[:, h * 128:(h + 1) * 128], Sx[:, c + 1, h * 64:(h + 1) * 64], True, True, [qlb, BS[c + 1]], [mb])
        yield
        yield
        ys, ysb = ysum.next()
        v3 = lambda ap: ap.rearrange("p (h d) -> p h d", h=4)
        tt("dve", v3(ys[:]), v3(mp[:, 0:256]), mbias[:, l * 4:(l + 1) * 4].unsqueeze(2).to_broadcast([128, 4, 64]), ALU.add, [mb, B_cb], [ysb])
        tt("dve", ys[:], ys[:], ug[:, 0:256], ALU.mult, [ysb, ugb], [ysb])
        yield
        tt("dve", mo[:, 0:256], ys[:], gt[:, 0:256], ALU.mult, [ysb, gtb], [mob])
        ys, ysb = ysum.next()
        tt("dve", ys[:], mp[:, 256:512], yl[:], ALU.add, [mb, ylb], [ysb])
        yield
        Bt, Bb = t1.next()
        s4, s4b = st4.next()
        tt("dve", Bt[:, 0:256], ys[:], ys[:], ALU.mult, [ysb], [Bb])
        yield
        P.op("dve", lambda e: e.tensor_reduce(out=s4[:, 0:4], in_=v3(Bt[:, 0:256]), axis=AX.X, op=ALU.add), [Bb], [s4b])
        yield
        yield from rsqrt_mean_g(s4[:, 0:4], s4[:, 0:4], 64, [s4b], [s4b])
        tt("dve", v3(ys[:]), v3(ys[:]), s4[:, 0:4].unsqueeze(2).to_broadcast([128, 4, 64]), ALU.mult, [ysb, s4b], [ysb])
        tt("dve", ys[:], ys[:], rnorm[:, 0:256], ALU.mult, [ysb, B_rq], [ysb])
        yield
        tt("dve", mo[:, 256:512], ys[:], gt[:, 256:512], ALU.mult, [ysb, gtb], [mob])
        yield
        qn, qnb = yield from qk_norm_rope(qr, qrb, 8, qkn[:, 0:128], csb, not is_ctx)
        yield
        tp, tb = tps.next()
        for a in range(4):
            tr(tp[:, a * 128:(a + 1) * 128], qn[:, a * 128:(a + 1) * 128], identb[:], [qnb, B_cb], [tb])
        yield
        yield
        return dict(t=t, c=c, is_ctx=is_ctx, gt=gt, gtb=gtb, mo=mo, mob=mob, tp=tp, tb=tb)

    def out_proj(l, st, xsrc, B_xsrc, xdst, B_xdst):
        t, c, is_ctx, mo, mob = st["t"], st["c"], st["is_ctx"], st["mo"], st["mob"]
        j = 1 if is_ctx else 0
        if is_ctx:
            xap, xbuf = xc[:, c, :], B_xc[c]
        else:
            xr_, xbuf = xr.next()
            dma("sp", xr_[:], xsrc[c * 128:(c + 1) * 128, :], [B_xsrc], [xbuf])
            xap = xr_[:]
        tp, tb = tps.next()
        for k in range(8):
            tr(tp[:, k * 128:(k + 1) * 128], mo[:, k * 128:(k + 1) * 128], identb[:], [mob, B_cb], [tb])
        yield
        yield
        mt, mtb = mixTt.next()
        cp("dve", mt[:].rearrange("p k t -> p (k t)"), tp[:, 0:1024], [tb], [mtb])
        yield
        yield
        ot, otb = otmp.next()
        for half in range(2):
            zp, zb = PS["z"].next()
            for k in range(8):
                mm(zp[:, 0:512], mt[:, k, :], WA[:, k, half * 512:(half + 1) * 512], k == 0, k == 7, [mtb, B_WA], [zb])
            yield
            yield
            yield
            tt("dve", ot[:, half * 512:(half + 1) * 512], zp[:, 0:512], gateB[:, j, half * 512:(half + 1) * 512], ALU.mult, [zb, B_mod], [otb], )
            yield
        if is_ctx:
            tt("pool", xc[:, c, :], xc[:, c, :], ot[:], ALU.add, [xbuf, otb], [xbuf])
        else:
            tt("pool", ot[:], ot[:], xap, ALU.add, [otb, xbuf], [otb])
            yield
            yield
            dma("pool", xdst[c * 128:(c + 1) * 128, :], ot[:], [otb], [B_xdst], par=True)
        yield

    def q_evac(st, q_, qb, lo, cols=None):
        for gg in range(2):
            dst = q_[gg * 64:(gg + 1) * 64, gg, :, :] if cols is None else q_[gg * 64:(gg + 1) * 64, gg, :, cols[0]:cols[1]]
            cp("dve", dst, st["tp"][gg * 64:(gg + 1) * 64, lo:lo + 256].rearrange("p (r t) -> p r t", r=2), [st["tb"]], [qb], )

    def p2_ctx(l):
        for c in range(CTXC):
            st = yield from p2_front(l, c, None, None)
            qT_, qTb = sqT.next()
            qT2, qT2b = sqT.next()
            q_evac(st, qT_, qTb, 0)
            q_evac(st, qT2, qT2b, 256)
            yield
            gblocks = [(KT[:, cc * 128:(cc + 1) * 128], V[:, cc, :], None, [B_KT, B_V]) for cc in range(CTXC)]
            yield from small_attn(qT_, qTb, gblocks, False, st["mo"], st["mob"], st["gt"], st["gtb"], 512)
            sblocks = [(skTc[:, cc * 128:(cc + 1) * 128], sVc[:, cc, :], None, [B_skTc, B_sVc]) for cc in range(CTXC)]
            yield from small_attn(qT2, qT2b, sblocks, True, st["mo"], st["mob"], st["gt"], st["gtb"], 768)
            yield from out_proj(l, st, None, None, None, None)

    def swa_tile(l, st, sq_, sqb):
        c = st["c"]
        wk_, wkb = wk.next()
        wv_, wvb = wv.next()
        blocks = []
        bb = [wkb, wvb]
        if c == 0:
            dma("sp", wk_[:, 0:2, :], sk_d[0:2].rearrange("c p k -> p c k"), [B_skd], [wkb])
            dma("sp", wv_[:, 0:2, :], sv_d[0:2].rearrange("c p k -> p c k"), [B_svd], [wvb])
            dma("sp", wk_[:, 2:6, :], bnd_all[:, 128:256].rearrange("(r p) k -> p r k", p=128), [B_bndall], [wkb], par=True)
            dma("sp", wv_[:, 2:6, :], bnd_all[:, 386:516].rearrange("(r p) k -> p r k", p=128), [B_bndall], [wvb], par=True)
            blocks.append((wk_[:, 0, :], wv_[:, 0, :], None, bb))
            blocks.append((wk_[:, 1, :], wv_[:, 1, :], cmask[:, 128:256], bb))
            for r in range(R):
                blocks.append((wk_[:, 2 + r, :], wv_[:, 2 + r, :], hmask[:, r * 128:(r + 1) * 128], bb))
        elif c == NCH - 1:
            dma("sp", wk_[:, 0:2, :], sk_d[c - 1:c + 1].rearrange("c p k -> p c k"), [B_skd], [wkb])
            dma("sp", wv_[:, 0:2, :], sv_d[c - 1:c + 1].rearrange("c p k -> p c k"), [B_svd], [wvb])
            dma("sp", wk_[:, 2:6, :], bnd_all[:, 0:128].rearrange("(r p) k -> p r k", p=128), [B_bndall], [wkb], par=True)
            dma("sp", wv_[:, 2:6, :], bnd_all[:, 256:386].rearrange("(r p) k -> p r k", p=128), [B_bndall], [wvb], par=True)
            blocks.append((wk_[:, 0, :], wv_[:, 0, :], cmask[:, 0:128], bb))
            blocks.append((wk_[:, 1, :], wv_[:, 1, :], None, bb))
            for r in range(R):
                blocks.append((wk_[:, 2 + r, :], wv_[:, 2 + r, :], hmask[:, (4 + r) * 128:(5 + r) * 128], bb))
        else:
            dma("sp", wk_[:, 0:3, :], sk_d[c - 1:c + 2].rearrange("c p k -> p c k"), [B_skd], [wkb])
            dma("sp", wv_[:, 0:3, :], sv_d[c - 1:c + 2].rearrange("c p k -> p c k"), [B_svd], [wvb])
            blocks.append((wk_[:, 0, :], wv_[:, 0, :], cmask[:, 0:128], bb))
            blocks.append((wk_[:, 1, :], wv_[:, 1, :], None, bb))
            blocks.append((wk_[:, 2, :], wv_[:, 2, :], cmask[:, 128:256], bb))
        for cc in range(CTXC):
            blocks.append((skTc[:, cc * 128:(cc + 1) * 128], sVc[:, cc, :], None, [B_skTc, B_sVc]))
        yield
        yield from small_attn(sq_, sqb, blocks, True, st["mo"], st["mob"], st["gt"], st["gtb"], 768)

    def front_group(l, gi, xsrc, B_xsrc, G):
        gq_, gqb = gqT.next()
        G["gq"] = (gq_, gqb)
        G["sts"] = []
        for ti in range(QG):
            st = yield from p2_front(l, CTXC + gi * QG + ti, xsrc, B_xsrc)
            sq_, sqb = sqT.next()
            q_evac(st, gq_, gqb, 0, (ti * 128, (ti + 1) * 128))
            q_evac(st, sq_, sqb, 256)
            yield
            yield from swa_tile(l, st, sq_, sqb)
            G["sts"].append(st)

    def sweep_group(G):
        gq_, gqb = G["gq"]
        accs = [ops_.next() for _ in range(2)]
        pend = []
        NW = 2 * GQ
        pieces = [(None, None)] + [(r, c0) for r in range(R) for c0 in range(0, NCH, PCS)]
        nblk_total = CTXC + R * NCH
        seen = 0

        def pv(item):
            first, last, g0, vap, vb, p0, pb0 = item
            mm(accs[g0][0][0:65, 0:NW], vap[:, g0 * 65:(g0 + 1) * 65], p0[:, 0:NW], first, last, [pb0] + vb, [accs[g0][1]])

        for (r, c0) in pieces:
            if r is None:
                nb = CTXC
                kget = lambda i: KT[:, i * 128:(i + 1) * 128]
                vget = lambda i: V[:, i, :]
                kbufs, vbufs = [B_KT], [B_V]
            else:
                nb = PCS
                kt_, ktb_ = ksl.next()
                vt_, vtb_ = vsl.next()
                gp_, of_ = c0 // GP, c0 % GP
                dma("sp", kt_[:], gk_all[gp_][r * 128:(r + 1) * 128, of_ * 128:(of_ + PCS) * 128], [B_gkall[gp_]], [ktb_])
                dma("sp", vt_[:], gv_all[gp_][r * 128:(r + 1) * 128, of_ * 130:(of_ + PCS) * 130].rearrange("p (c d) -> p c d", d=130), [B_gvall[gp_]], [vtb_])
                kget = lambda i, kt_=kt_: kt_[:, i * 128:(i + 1) * 128]
                vget = lambda i, vt_=vt_: vt_[:, i, :]
                kbufs, vbufs = [ktb_], [vtb_]
            for i in range(nb):
                first, last = seen == 0, seen == nblk_total - 1
                seen += 1
                for g in range(2):
                    sp_, spb = sps.next()
                    mm(sp_[:, 0:NW], kget(i), gq_[:, g, :, :].rearrange("p r t -> p (r t)"), True, True,
                       kbufs + [gqb], [spb])
                    p_, pb = pT.next()
                    act(p_[:, 0:NW], sp_[:, 0:NW], AF.Exp, [spb], [pb], scale=SCALE)
                    pend.append((first, last, g, vget(i), vbufs, p_, pb))
                    if len(pend) > 2:
                        pv(pend.pop(0))
                yield
        for item in pend:
            pv(item)
        G["oT"] = []
        for g in range(2):
            o_, ob_ = oT.next()
            cp("dve", o_[:, 0:NW], accs[g][0][0:65, 0:NW], [accs[g][1]], [ob_])
            G["oT"].append((o_, ob_))

    def tail_group(l, G, xsrc, B_xsrc, xdst, B_xdst):
        sts = G["sts"]
        for g in range(2):
            o_, ob_ = G["oT"][g]
            for r in range(2):
                h = 2 * g + r
                for ti in range(QG):
                    zp, zb = PS["z"].next()
                    tr(zp[:, 0:65], o_[:, r * GQ + ti * 128:r * GQ + (ti + 1) * 128], identf[0:65, 0:65], [ob_, B_cb], [zb])
                    yield
                    yield
                    finish_head(zp, zb, g, h, False, sts[ti]["mo"], sts[ti]["mob"], sts[ti]["gt"], sts[ti]["gtb"], 512)
                    yield
        for st in sts:
            yield from out_proj(l, st, xsrc, B_xsrc, xdst, B_xdst)

    def run(gen):
        try:
            while True:
                next(gen)
        except StopIteration as e:
            return e.value

    def gchain(*gens):
        for g_ in gens:
            yield from g_

    def pass2(l, xsrc, B_xsrc, xdst, B_xdst):
        PS["z"], PS["acc"] = zps1, swacc
        if l < depth - 1:
            run(p2_ctx(l))
        exchange_b(l)
        Gs = [dict() for _ in range(NG)]
        run(front_group(l, 0, xsrc, B_xsrc, Gs[0]))
        for k in range(NG):
            sides = []
            if k > 0:
                sides.append(tail_group(l, Gs[k - 1], xsrc, B_xsrc, xdst, B_xdst))
            if k + 1 < NG:
                sides.append(front_group(l, k + 1, xsrc, B_xsrc, Gs[k + 1]))
            side = gchain(*sides)
            alive = True
            for _ in sweep_group(Gs[k]):
                for _r in range(SIDE_RATE):
                    if alive:
                        try:
                            next(side)
                        except StopIteration:
                            alive = False
            if alive:
                run(side)
        run(tail_group(l, Gs[NG - 1], xsrc, B_xsrc, xdst, B_xdst))
        PS["z"], PS["acc"] = zpsP1, ops_

    chain = [(x_in, B_xin)]
    inter = [(xsA, B_xsA), (xsB, B_xsB)]
    for l in range(depth):
        chain.append((y_out, B_y) if l == depth - 1 else inter[l % 2])
    for l in range(depth):
        xsrc, B_xsrc = chain[l]
        xdst, B_xdst = chain[l + 1]
        load_w1(l)
        layer_consts(l)
        if stop == "consts":
            break
        mod_compute(l)
        if stop == "mod":
            break
        load_w2(l)
        if stop == "w2":
            break
        gens = [p1_tile(l, t, xsrc, B_xsrc) for t in range(CTXC + NCH)]
        active = []
        nxt = 0
        while nxt < len(gens) or active:
            if nxt < len(gens) and (not active or (len(active) < 2 and active[-1][1] >= P1LAG)):
                active.append([gens[nxt], 0, nxt - CTXC])
                nxt += 1
            for a_ in list(active):
                try:
                    next(a_[0])
                    a_[1] += 1
                except StopIteration:
                    active.remove(a_)
                    if a_[2] >= 0 and a_[2] % GP == GP - 1:
                        gather_piece(a_[2] // GP)
        if stop == "p1":
            break
        exchange(l)
        if stop == "exch":
            break
        load_wo(l)
        pass2(l, xsrc, B_xsrc, xdst, B_xdst)

    P.finalize()
    with nc.Block() as block:
        @block.sync
        def _(e):
            P.emit("sp", e)

        @block.scalar
        def _(e):
            P.emit("act", e)

        @block.vector
        def _(e):
            P.emit("dve", e)

        @block.tensor
        def _(e):
            P.emit("pe", e)

        @block.gpsimd
        def _(e):
            P.emit("pool", e)
            if B_y.sem is not None:
                P.final_wait(e, [B_y])
            else:
                dummy = P.es.enter_context(nc.semaphore("dummy"))
                e.dma_start(out=y_out[0:128, :], in_=xc[:, 0, :]).then_inc(dummy, 16)
                e.wait_ge(dummy, 16)
    es.close()
    return nc


def host_inputs(inputs, NCH, depth=DEPTH, n_cores=8):
    f = np.float32
    x = np.asarray(inputs["x"], f)
    NT = NCH * 128
    bf = ml_dtypes.bfloat16
    c = np.asarray(inputs["c"], f)
    ctx = np.asarray(inputs["ctx"], f)
    c_ctx = np.asarray(inputs["c_ctx"], f)
    ng = np.asarray(inputs["norm_gain"], f)
    b_mod = np.asarray(inputs["b_mod"], f)
    common = {
        "gainT": np.ascontiguousarray(ng.reshape(depth, 8, 128).transpose(2, 0, 1).reshape(128, depth * 8)),
        "w_mod": np.ascontiguousarray(np.asarray(inputs["w_mod"], f)),
        "bmodT": np.ascontiguousarray(b_mod[:, 0:2048].reshape(depth, 16, 128).transpose(2, 0, 1).reshape(128, depth * 16)),
        "bgate": np.ascontiguousarray(b_mod[:, 2048:3072].reshape(1, depth * D)),
        "w_in": np.ascontiguousarray(np.asarray(inputs["w_in"], f)),
        "w_out": np.ascontiguousarray(np.asarray(inputs["w_out"], f)),
        "mixT": np.ascontiguousarray(np.asarray(inputs["mlp_mix"], f).transpose(0, 3, 1, 2).reshape(depth, 128, 512)),
        "mbias": np.ascontiguousarray(np.asarray(inputs["mlp_bias"], f).transpose(2, 0, 1).reshape(128, depth * 4)),
        "rdec": np.ascontiguousarray(np.stack([np.asarray(inputs["ret_decay_fwd"], f), np.asarray(inputs["ret_decay_bwd"], f)], 1).reshape(1, depth * 8)),
        "rnorm": np.ascontiguousarray(np.asarray(inputs["ret_norm"], f).reshape(1, depth * 256)),
        "qkn": np.ascontiguousarray(np.stack([np.asarray(inputs[k], f) for k in ("attn_q_norm", "swa_q_norm", "attn_k_norm", "swa_k_norm")], 1).reshape(1, depth * 256)),
        "sink": np.ascontiguousarray(np.asarray(inputs["swa_sink"], f).reshape(1, depth * 4)),
    }
    jj = np.arange(128, dtype=f)[:, None]
    ii = np.arange(128, dtype=f)[None, :]
    common["identb"] = np.eye(128, dtype=f).astype(bf)
    common["identf"] = np.eye(128, dtype=f)
    common["rpn"] = np.concatenate([np.maximum(ii - jj, 0), np.maximum(jj - ii, 0)], 1).astype(f)
    p = np.arange(128, dtype=f)
    common["pos"] = np.stack([127 - p, p, p + 1, 128 - p], 1).astype(f)
    mprev = (jj >= ii).astype(f)
    mnext = (jj <= ii).astype(f)
    common["cmask"] = np.concatenate([mprev, mnext], 1).astype(bf)
    half = 32
    inv_freq = (1.0 / (10000.0 ** (np.arange(0, half, 2, dtype=f) / f(half)))).astype(f)
    sgn = np.concatenate([-np.ones(16, f), np.ones(16, f), -np.ones(16, f), np.ones(16, f)])
    maps = []
    for core in range(n_cores):
        b, seg = core // R, core % R
        m = dict(common)
        m["x_in"] = np.ascontiguousarray(x[b, seg * NT:(seg + 1) * NT, :])
        m["ctx_in"] = np.ascontiguousarray(ctx[b])
        cT = np.zeros((128, 16), f)
        cT[:, 0::2] = c[b].reshape(8, 128).T
        cT[:, 1::2] = c_ctx.reshape(8, 128).T
        m["cT"] = cT
        tpos = seg * NT + np.arange(NT)
        row = (tpos // 64).astype(f)
        col = (tpos % 64).astype(f)
        ang_r = row[:, None] * inv_freq[None, :]
        ang_c = col[:, None] * inv_freq[None, :]
        ang = np.concatenate([ang_r, ang_r, ang_c, ang_c], -1).astype(f)
        m["cos"] = np.cos(ang).astype(f).reshape(NCH, 128, 64)
        m["sin"] = (np.sin(ang).astype(f) * sgn[None, :]).reshape(NCH, 128, 64)
        et = np.full((128, 5), BIGE, f)
        for r in range(R):
            if r < seg:
                et[0:64, r] = seg - 1 - r
            if r > seg:
                et[64:128, r] = r - seg - 1
        et[0:64, 4] = seg
        et[64:128, 4] = R - 1 - seg
        m["etab"] = et
        hm = np.zeros((128, 8, 128), f)
        if seg - 1 >= 0:
            hm[:, seg - 1, :] = mprev
        if seg + 1 < R:
            hm[:, 4 + seg + 1, :] = mnext
        m["hmask"] = hm.reshape(128, 1024).astype(bf)
        maps.append(m)
    return maps


_NC_CACHE = {}


def kernel(**inputs):
    x = np.asarray(inputs["x"])
    B, L, _ = x.shape
    NCH = L // R // 128
    depth = np.asarray(inputs["w_in"]).shape[0]
    key = (NCH, depth)
    if key not in _NC_CACHE:
        _NC_CACHE[key] = build(NCH, depth)
    nc = _NC_CACHE[key]
    maps = host_inputs(inputs, NCH, depth)
    res = run_bass_kernel_spmd(nc, maps, core_ids=list(range(8)))
    NT = NCH * 128
    out = np.zeros((B, L, D), np.float32)
    for core in range(8):
        b, seg = core // R, core % R
        out[b, seg * NT:(seg + 1) * NT, :] = res.results[core]["y"]
    return out
```

```python
import math
from contextlib import ExitStack
import numpy as np
import ml_dtypes
import concourse.bass as bass
import concourse.mybir as mybir
from concourse.bass_utils import run_bass_kernel_spmd

F32 = mybir.dt.float32
BF16 = mybir.dt.bfloat16
AF = mybir.ActivationFunctionType
ALU = mybir.AluOpType
AX = mybir.AxisListType

D = 1024
DEPTH = 4
CTXC = 2
R = 4
EPS = 1e-6
SCALE = 0.125
BIGE = 1.0e4


class Buf:
    def __init__(self, name):
        self.name = name
        self.last_w = []
        self.readers = []
        self.sem = None
        self.cnt = 0


class Op:
    __slots__ = ("eng", "fn", "deps", "kind", "sig", "val", "sem", "owner")

    def __init__(self, eng, fn, kind):
        self.eng, self.fn, self.kind = eng, fn, kind
        self.deps = []
        self.sig = False
        self.val = 0
        self.sem = None
        self.owner = None


class Prog:
    def __init__(self, nc, es):
        self.nc, self.es = nc, es
        self.ops = {k: [] for k in ("pe", "act", "dve", "pool", "sp")}
        self.esem = {k: es.enter_context(nc.semaphore("e_" + k)) for k in ("pe", "act", "dve", "pool")}
        self.nsem = 4

    def op(self, eng, fn, reads=(), writes=(), kind="c", par=False):
        o = Op(eng, fn, kind)
        deps = []
        for b in reads:
            deps += b.last_w
        for b in writes:
            deps += b.readers
            if not par:
                deps += b.last_w
        seen = set()
        for d in deps:
            if id(d) in seen or d is o:
                continue
            seen.add(id(d))
            if d.kind == "c" and d.eng == "pe" and eng == "pe" and kind == "c":
                continue
            d.sig = True
            o.deps.append((d, d.owner.cnt if d.owner is not None else None))
        for b in reads:
            b.readers.append(o)
        for b in writes:
            if par:
                b.last_w = b.last_w + [o]
            else:
                b.last_w = [o]
            b.readers = []
        if kind in ("d", "cc"):
            b = writes[0]
            if b.sem is None:
                b.sem = self.es.enter_context(self.nc.semaphore("s_" + b.name))
                self.nsem += 1
            b.cnt += 16 if kind == "d" else 1
            o.sem, o.val, o.owner = b.sem, b.cnt, b
        self.ops[eng].append(o)
        return o

    def finalize(self):
        for eng in ("pe", "act", "dve", "pool"):
            c = 0
            for o in self.ops[eng]:
                if o.kind == "c" and o.sig:
                    c += 1
                    o.val = c
                    o.sem = self.esem[eng]

    def emit(self, eng, e):
        waited = {}
        for o in self.ops[eng]:
            need = {}
            for d, fixed in o.deps:
                k = id(d.sem)
                v = d.val if fixed is None else fixed
                if v > waited.get(k, 0) and (k not in need or v > need[k][1]):
                    need[k] = (d.sem, v)
            for k, (sem_, val_) in need.items():
                e.wait_ge(sem_, val_)
                waited[k] = val_
            ins = o.fn(e)
            if o.kind == "d":
                ins.then_inc(o.sem, 16)
            elif o.kind == "cc":
                ins.then_inc(o.sem, 1)
            elif o.sig:
                ins.then_inc(o.sem, 1)

    def final_wait(self, e, bufs):
        for b in bufs:
            e.wait_ge(b.sem, b.cnt)


class Rot:
    def __init__(self, tiles, name, bufs=None):
        self.tiles = tiles
        self.bufs = bufs if bufs is not None else [Buf("%s%d" % (name, i)) for i in range(len(tiles))]
        self.i = -1

    def next(self):
        self.i = (self.i + 1) % len(self.tiles)
        return self.tiles[self.i], self.bufs[self.i]


def build(NCH, depth=DEPTH, QG=2, stop=None, SIDE_RATE=2, P1LAG=9):
    NT = NCH * 128
    NTT = NT + CTXC * 128
    KB = CTXC + R * NCH
    KTOT = KB * 128
    NG = NCH // QG
    GQ = QG * 128
    PCS = min(NCH, 4)
    GP = min(NCH, 8)
    NGP = NCH // GP
    nc = bass.Bass("TRN2", target_bir_lowering=False)
    es = ExitStack()
    P = Prog(nc, es)

    def din(name, shape, dt=F32):
        return nc.dram_tensor(name, shape, dt, kind="ExternalInput").ap()

    def dint(name, shape, dt):
        return nc.dram_tensor(name, shape, dt, kind="Internal").ap()

    x_in = din("x_in", [NT, D])
    ctx_in = din("ctx_in", [CTXC * 128, D])
    cT_in = din("cT", [128, 16])
    gainT_in = din("gainT", [128, depth * 8])
    w_mod = din("w_mod", [depth, D, 3 * D])
    bmodT_in = din("bmodT", [128, depth * 16])
    bgate_in = din("bgate", [1, depth * D])
    w_in = din("w_in", [depth, D, 3328])
    w_out = din("w_out", [depth, D, D])
    mixT_in = din("mixT", [depth, 128, 512])
    mbias_in = din("mbias", [128, depth * 4])
    rdec_in = din("rdec", [1, depth * 8])
    rnorm_in = din("rnorm", [1, depth * 256])
    qkn_in = din("qkn", [1, depth * 256])
    sink_in = din("sink", [1, depth * 4])
    cos_in = din("cos", [NCH, 128, 64])
    sin_in = din("sin", [NCH, 128, 64])
    etab_in = din("etab", [128, 5])
    hmask_in = din("hmask", [128, 8 * 128], BF16)
    cmask_in = din("cmask", [128, 2 * 128], BF16)
    identb_in = din("identb", [128, 128], BF16)
    identf_in = din("identf", [128, 128])
    rpn_in = din("rpn", [128, 256])
    pos_in = din("pos", [128, 4])
    y_out = nc.dram_tensor("y", [NT, D], F32, kind="ExternalOutput").ap()

    xsA = dint("xsA", [NT, D], F32)
    xsB = dint("xsB", [NT, D], F32)
    yin_d = dint("yin_d", [NTT, 256], F32)
    qfb_d = dint("qfb_d", [NCH + CTXC, 128, 512], BF16)
    sk_d = dint("sk_d", [NCH, 128, 128], BF16)
    sv_d = dint("sv_d", [NCH, 128, 130], BF16)
    gk_x = [dint("gk_x%d" % i, [128, GP * 128], BF16) for i in range(NGP)]
    gk_all = [dint("gk_all%d" % i, [R * 128, GP * 128], BF16) for i in range(NGP)]
    gv_x = [dint("gv_x%d" % i, [128, GP * 130], BF16) for i in range(NGP)]
    gv_all = [dint("gv_all%d" % i, [R * 128, GP * 130], BF16) for i in range(NGP)]
    bnd_x = dint("bnd_x", [128, 516], BF16)
    bnd_all = dint("bnd_all", [R * 128, 516], BF16)
    agg_x = dint("agg_x", [128, 256], F32)
    agg_all = dint("agg_all", [R * 128, 256], F32)
    B_xin, B_xsA, B_xsB, B_y = Buf("xin"), Buf("xsA"), Buf("xsB"), Buf("y")
    B_yin, B_qfb, B_skd, B_svd = Buf("yin"), Buf("qfb"), Buf("skd"), Buf("svd")
    B_gkx = [Buf("gkx%d" % i) for i in range(NGP)]
    B_gkall = [Buf("gkall%d" % i) for i in range(NGP)]
    B_gvx = [Buf("gvx%d" % i) for i in range(NGP)]
    B_gvall = [Buf("gvall%d" % i) for i in range(NGP)]
    B_bndx, B_bndall, B_aggx, B_aggall = Buf("bndx"), Buf("bndall"), Buf("aggx"), Buf("aggall")
    B_const = Buf("constin")

    def sb(name, shape, dt=F32):
        return es.enter_context(nc.sbuf_tensor(name, shape, dt))

    def ps(name, shape, dt=F32):
        return es.enter_context(nc.psum_tensor(name, shape, dt))

    KT = sb("KTc", [128, CTXC * 128], BF16);     B_KT = Buf("KT")
    V = sb("Vc", [128, CTXC, 130], BF16);        B_V = Buf("V")
    ksl = Rot([sb("ksl%d" % i, [128, PCS * 128], BF16) for i in range(2)], "ksl")
    vsl = Rot([sb("vsl%d" % i, [128, PCS, 130], BF16) for i in range(2)], "vsl")
    skTc = sb("skTc", [128, CTXC * 128], BF16);  B_skTc = Buf("skTc")
    sVc = sb("sVc", [128, CTXC, 130], BF16);     B_sVc = Buf("sVc")
    ST = sb("ST", [128, NCH + 2, 256], BF16)
    B_ST = [Buf("ST%d" % i) for i in range(NCH + 2)]
    STc = sb("STc", [128, CTXC + 2, 256], BF16)
    B_STc = [Buf("STc%d" % i) for i in range(CTXC + 2)]
    WA = sb("WA", [128, 8, 1280], BF16);         B_WA = Buf("WA")
    W2 = sb("W2s", [128, 8, 2048], BF16);         B_W2 = Buf("W2")
    mixT = sb("mixTs", [128, 512], BF16);        B_mixT = Buf("mixT")
    xc = sb("xc", [128, CTXC, D], F32)
    B_xc = [Buf("xc%d" % i) for i in range(CTXC)]
    identb = sb("identb_s", [128, 128], BF16)
    identf = sb("identf_s", [128, 128], F32)
    rpn = sb("rpn_s", [128, 256], F32)
    pos = sb("pos_s", [128, 4], F32)
    etab = sb("etab_s", [128, 5], F32)
    hmask = sb("hmask_s", [128, 8 * 128], BF16)
    cmask = sb("cmask_s", [128, 256], BF16)
    cT = sb("cT_s", [128, 16], F32)
    gainT = sb("gainT_s", [128, depth * 8], F32)
    bmodT = sb("bmodT_s", [128, depth * 16], F32)
    bgate = sb("bgate_s", [1, D], F32);  B_bg = Buf("bg")
    grow = sb("grow", [1, 512], F32);    B_grow = Buf("grow")
    mbias = sb("mbias_s", [128, depth * 4], F32)
    rdec = sb("rdec_s", [128, depth * 8], F32)
    rdsel = sb("rdsel_s", [128, depth * 4], F32)
    rnorm = sb("rnorm_s", [128, 256], F32);  B_rq = Buf("rqn")
    qkn = sb("qkn_s", [128, 256], F32)
    sink = sb("sink_s", [128, depth * 4], F32)
    ones1 = sb("ones1", [1, 128], F32)
    B_cb = Buf("constsb")
    Gm = sb("Gm", [128, 2, 8], F32)
    Sm = sb("Sm", [128, 2, 8], F32)
    gateB = sb("gateB", [128, 2, D], F32)
    B_mod = Buf("mod")
    lg = sb("lg", [128, 8], F32)
    lgsel = sb("lgsel", [128, 4], F32)
    kd = sb("kd", [128, 8], F32)
    qd = sb("qd", [128, 8], F32)
    DecT = sb("DecT", [128, 512], F32)
    Dt = sb("Dt", [128, 256], F32)
    Pw = sb("Pw", [128, 256], F32)
    Aagg = sb("Aagg", [128, 256], F32)
    Actx = sb("Actx", [128, 256], F32)
    coef = sb("coef", [128, 5, 4], F32)
    esink = sb("esink", [128, 4], F32)
    B_lay = Buf("laysmall")
    B_Aagg, B_Actx, B_Pw = Buf("Aagg"), Buf("Actx"), Buf("Pw")
    aggs = sb("aggs", [128, R, 256], F32);    B_aggs = Buf("aggs")
    sintmp = sb("sintmp", [128, 256], F32);   B_sinF = Buf("sinF");  B_sinB = Buf("sinB")
    xt = Rot([sb("xt%d" % i, [128, D], F32) for i in range(2)], "xt")
    xr = xt
    st4 = Rot([sb("st4_%d" % i, [128, 16], F32) for i in range(4)], "st4")
    xn = Rot([sb("xn%d" % i, [128, D], BF16) for i in range(2)], "xn")
    hT = Rot([sb("hT%d" % i, [128, 8, 128], BF16) for i in range(2)], "hT")
    cs = Rot([sb("cs%d" % i, [128, 2, 64], F32) for i in range(2)], "cs")
    rqkv = Rot([sb("rqkv%d" % i, [128, 768], BF16) for i in range(2)], "rqkv")
    fb = Rot([sb("fb%d" % i, [128, 2, 4, 2, 64], BF16) for i in range(2)], "fb")
    fbT = Rot([sb("fbT%d" % i, [128, 2, 4, 128], BF16) for i in range(1)], "fbT")
    rT = Rot([sb("rT%d" % i, [128, 3, 2, 128], BF16) for i in range(1)], "rT")
    pint = Rot([sb("pint%d" % i, [128, 512], BF16) for i in range(1)], "pint")
    yint = Rot([sb("yint%d" % i, [128, 256], F32) for i in range(1)], "yint")
    kraw = Rot([sb("kraw%d" % i, [128, 512], F32) for i in range(2)], "kraw")
    t1 = Rot([sb("t1_%d" % i, [128, 512], F32) for i in range(1)], "t1")
    t2 = Rot([sb("t2_%d" % i, [128, 512], F32) for i in range(1)], "t2")
    knb = Rot([sb("knb%d" % i, [128, 512], BF16) for i in range(2)], "knb")
    kTs = Rot([sb("kTs%d" % i, [128, 256], BF16) for i in range(2)], "kTs")
    vsb = Rot([sb("vsb%d" % i, [128, 260], BF16) for i in range(2)], "vsb")
    NSL = 2 * QG
    gates = Rot([sb("gates%d" % i, [128, D], BF16) for i in range(NSL)], "gates")
    mixo = Rot([sb("mixo%d" % i, [128, D], BF16) for i in range(NSL)], "mixo")
    uvg = Rot([sb("uvg%d" % i, [128, 512], BF16) for i in range(1)], "uvg")
    gqT = Rot([sb("gqT%d" % i, [128, 2, 2, GQ], BF16) for i in range(2)], "gqT")
    sqT = Rot([sb("sqT%d" % i, [128, 2, 2, 128], BF16) for i in range(2)], "sqT")
    pT = Rot([sb("pT%d" % i, [128, 512], BF16) for i in range(3)], "pT")
    oT = Rot([sb("oT%d" % i, [65, 512], F32) for i in range(2)], "oT")
    rden = Rot([sb("rden%d" % i, [128, 4], F32) for i in range(4)], "rden")
    mixTt = Rot([sb("mixTt%d" % i, [128, 8, 128], BF16) for i in range(1)], "mixTt")
    otmp = Rot([sb("otmp%d" % i, [128, D], F32) for i in range(1)], "otmp")
    wmst = Rot([t_[:].rearrange("p (k c) -> p k c", k=8) for t_ in (otmp.tiles[0], xt.tiles[0], xt.tiles[1])], "wmst",
               bufs=[otmp.bufs[0], xt.bufs[0], xt.bufs[1]])
    yinl = Rot([sb("yinl%d" % i, [128, 256], F32) for i in range(1)], "yinl")
    qfbl = Rot([sb("qfbl%d" % i, [128, 512], BF16) for i in range(1)], "qfbl")
    ysum = Rot([sb("ysum%d" % i, [128, 256], F32) for i in range(2)], "ysum")
    wk = Rot([sb("wk%d" % i, [128, 6, 128], BF16) for i in range(1)], "wk")
    wv = Rot([sb("wv%d" % i, [128, 6, 130], BF16) for i in range(1)], "wv")
    wp = Rot([sb("wp%d" % i, [128, 256], BF16) for i in range(3)], "wp")
    woT = Rot([sb("woT%d" % i, [65, 256], F32) for i in range(1)], "woT")
    zps = Rot([ps("zps%d" % i, [128, 512]) for i in range(2)], "zps")
    tps = Rot([ps("tps%d" % i, [128, 1024], BF16) for i in range(1)], "tps")
    sps = Rot([ps("sps%d" % i, [128, 512]) for i in range(3)], "sps")
    ops_ = Rot([ps("ops%d" % i, [128, 512]) for i in range(2)], "ops")
    zps1 = Rot([zps.tiles[0]], "zps1", bufs=[zps.bufs[0]])
    swacc = Rot([zps.tiles[1]], "swacc", bufs=[zps.bufs[1]])
    zpsP1 = Rot(zps.tiles + ops_.tiles, "zpsP1", bufs=zps.bufs + ops_.bufs)
    PS = {"z": zpsP1, "acc": ops_}

    def dma(q, out, in_, reads, writes, par=False, **kw):
        return P.op(q, lambda e: e.dma_start(out=out, in_=in_, **kw), reads, writes, kind="d", par=par)

    def mm(out, lhsT, rhs, start, stop, reads, writes):
        return P.op("pe", lambda e: e.matmul(out, lhsT=lhsT, rhs=rhs, start=start, stop=stop), reads, writes)

    def tr(out, in_, ident, reads, writes):
        return P.op("pe", lambda e: e.transpose(out, in_, ident), reads, writes)

    def act(out, in_, func, reads, writes, **kw):
        return P.op("act", lambda e: e.activation(out=out, in_=in_, func=func, **kw), reads, writes)

    def tt(eng, out, in0, in1, op, reads, writes):
        return P.op(eng, lambda e: e.tensor_tensor(out=out, in0=in0, in1=in1, op=op), reads, writes)

    def ts(eng, out, in0, s1, s2, op0, op1, reads, writes):
        if op1 is None:
            return P.op(eng, lambda e: e.tensor_scalar(out=out, in0=in0, scalar1=s1, scalar2=None, op0=op0), reads, writes)
        return P.op(eng, lambda e: e.tensor_scalar(out=out, in0=in0, scalar1=s1, scalar2=s2, op0=op0, op1=op1), reads, writes)

    def stt(out, in0, scalar, in1, op0, op1, reads, writes):
        return P.op("dve", lambda e: e.scalar_tensor_tensor(out=out, in0=in0, scalar=scalar, in1=in1, op0=op0, op1=op1), reads, writes)

    def cp(eng, out, in_, reads, writes):
        if eng == "act":
            return P.op("act", lambda e: e.copy(out=out, in_=in_), reads, writes)
        return P.op(eng, lambda e: e.tensor_copy(out=out, in_=in_), reads, writes)

    def rsqrt_mean_g(out, ssum, n, reads, writes):
        ts("dve", out, ssum, 1.0 / n, EPS, ALU.mult, ALU.add, reads, writes)
        yield
        act(out, out, AF.Ln, writes, writes)
        yield
        act(out, out, AF.Exp, writes, writes, scale=-0.5)
        yield

    def rsqrt_mean(out, ssum, n, reads, writes):
        ts("dve", out, ssum, 1.0 / n, EPS, ALU.mult, ALU.add, reads, writes)
        act(out, out, AF.Ln, writes, writes)
        act(out, out, AF.Exp, writes, writes, scale=-0.5)

    def bc(ap, n):
        return ap.partition_broadcast(n)

    for dst, src in ((identb, identb_in), (identf, identf_in), (rpn, rpn_in), (pos, pos_in), (etab, etab_in),
                     (hmask, hmask_in), (cmask, cmask_in), (cT, cT_in), (gainT, gainT_in), (bmodT, bmodT_in),
                     (mbias, mbias_in)):
        dma("sp", dst[:], src, [B_const], [B_cb], par=True)
    dma("sp", rdec[:], rdec_in.partition_broadcast(128), [B_const], [B_cb], par=True)
    for l in range(depth):
        dma("sp", rdsel[0:64, l * 4:(l + 1) * 4], rdec_in[:, l * 8:l * 8 + 4].partition_broadcast(64), [B_const], [B_cb], par=True)
        dma("sp", rdsel[64:128, l * 4:(l + 1) * 4], rdec_in[:, l * 8 + 4:l * 8 + 8].partition_broadcast(64), [B_const], [B_cb], par=True)
    dma("sp", sink[:], sink_in.partition_broadcast(128), [B_const], [B_cb], par=True)
    for c in range(CTXC):
        dma("sp", xc[:, c, :], ctx_in[c * 128:(c + 1) * 128, :], [B_const], [B_xc[c]])
    B_c2 = Buf("const2")
    P.op("dve", lambda e: e.memset(ones1[:], 1.0), [], [B_c2])
    for rot_ in (gqT, sqT, rT):
        for i in range(len(rot_.tiles)):
            P.op("pool", lambda e, t_=rot_.tiles[i]: e.memset(t_[:], 0.0), [], [rot_.bufs[i]])
    P.op("dve", lambda e: e.memset(V[:], 1.0), [], [B_V])
    P.op("dve", lambda e: e.memset(sVc[:], 1.0), [], [B_sVc])
    for i in range(2):
        P.op("pool", lambda e, i=i: e.memset(vsb.tiles[i][:], 1.0), [], [vsb.bufs[i]])
    act(cT[:], cT[:], AF.Silu, [B_cb], [B_cb])

    def load_w1(l):
        srcs = [(768, 1536, 0), (2048, 2176, 768), (2816, 2944, 896), (2176, 2304, 1024), (2944, 3072, 1152)]
        for k in range(8):
            for (a, b_, d0) in srcs:
                dma("pool", WA[:, k, d0:d0 + (b_ - a)], w_in[l, k * 128:(k + 1) * 128, a:b_], [B_const], [B_WA], par=True)

    def load_w2(l):
        srcs = [(0, 768, 0), (1536, 1792, 768), (2304, 2560, 1024), (3072, 3328, 1280), (1792, 2048, 1536), (2560, 2816, 1792)]
        for k in range(8):
            for (a, b_, d0) in srcs:
                dma("pool", W2[:, k, d0:d0 + (b_ - a)], w_in[l, k * 128:(k + 1) * 128, a:b_], [B_const], [B_W2], par=True)
        dma("pool", mixT[:], mixT_in[l], [B_const], [B_mixT])

    def load_wo(l):
        for k in range(8):
            dma("pool", WA[:, k, 0:1024], w_out[l, k * 128:(k + 1) * 128, :], [B_const], [B_WA], par=True)

    def layer_consts(l):
        rd = [B_cb, B_c2]
        w = [B_lay]
        dma("sp", rnorm[:], rnorm_in[:, l * 256:(l + 1) * 256].partition_broadcast(128), [B_const], [B_rq])
        dma("sp", qkn[:], qkn_in[:, l * 256:(l + 1) * 256].partition_broadcast(128), [B_const], [B_rq], par=True)
        act(lg[:], rdec[:, l * 8:(l + 1) * 8], AF.Exp, rd, w)
        ts("dve", lg[:], lg[:], -1.0, None, ALU.mult, None, w, w)
        act(lgsel[:], rdsel[:, l * 4:(l + 1) * 4], AF.Exp, rd, w)
        ts("dve", lgsel[:], lgsel[:], -1.0, None, ALU.mult, None, w, w)
        for dr in range(2):
            ts("dve", kd[:, dr * 4:(dr + 1) * 4], lg[:, dr * 4:(dr + 1) * 4], pos[:, dr:dr + 1], None, ALU.mult, None, rd + w, w)
            ts("dve", qd[:, dr * 4:(dr + 1) * 4], lg[:, dr * 4:(dr + 1) * 4], pos[:, 2 + dr:3 + dr], None, ALU.mult, None, rd + w, w)
        act(kd[:], kd[:], AF.Exp, w, w)
        ts("dve", kd[:], kd[:], SCALE, None, ALU.mult, None, w, w)
        act(qd[:], qd[:], AF.Exp, w, w)
        for h in range(4):
            ts("dve", DecT[:, h * 128:(h + 1) * 128], rpn[:, 0:128], lg[:, h:h + 1], None, ALU.mult, None, rd + w, w)
            stt(DecT[:, h * 128:(h + 1) * 128], rpn[:, 128:256], lg[:, 4 + h:5 + h], DecT[:, h * 128:(h + 1) * 128],
                ALU.mult, ALU.add, rd + w, w)
        act(DecT[:], DecT[:], AF.Exp, w, w)
        ts("dve", DecT[:], DecT[:], SCALE, None, ALU.mult, None, w, w)
        ts("dve", esink[:], lgsel[:], 128.0, None, ALU.mult, None, w, w)
        act(esink[:], esink[:], AF.Exp, w, w)
        cp("dve", Dt[:].rearrange("p (h e) -> p h e", h=4), esink[:].unsqueeze(2).to_broadcast([128, 4, 64]), w, w)
        for s_ in range(5):
            ts("dve", coef[:, s_, :], lgsel[:], etab[:, s_:s_ + 1], 128.0 * NCH, ALU.mult, ALU.mult, rd + w, w)
        act(coef[:], coef[:], AF.Exp, w, w)
        act(esink[:], sink[:, l * 4:(l + 1) * 4], AF.Exp, rd + w, w)
        P.op("pool", lambda e: e.memset(Aagg[:], 0.0), [], [B_Aagg])
        P.op("pool", lambda e: e.memset(Actx[:], 0.0), [], [B_Actx])
        P.op("pool", lambda e: e.memset(Pw[:], 1.0), [], [B_Pw])
        P.op("pool", lambda e: e.memset(STc[0:64, 1, :], 0.0), [], [B_STc[1]], par=True)
        P.op("pool", lambda e: e.memset(STc[64:128, 2, :], 0.0), [], [B_STc[2]], par=True)

    def mod_compute(l):
        dma("sp", bgate[:], bgate_in[:, l * D:(l + 1) * D], [B_const], [B_bg])
        mp, mb = zps.next()
        for half in range(-1, 2):
            if half < 0:
                for c in range(16):
                    wt, wb = wmst.next()
                    dma("sp", wt[:], w_mod[l, :, c * 128:(c + 1) * 128].rearrange("(k p) c -> p k c", p=128), [B_const], [wb])
                    for k in range(8):
                        mm(mp[:, c * 2:c * 2 + 2], wt[:, k, :], cT[:, 2 * k:2 * k + 2], k == 0, k == 7, [wb, B_cb], [mb])
                continue
            g0, gb0 = sps.next()
            g1, gb1 = sps.next()
            gps = ((g0, gb0), (g1, gb1))
            for q in range(4):
                col = 2048 + half * 512 + q * 128
                wt, wb = wmst.next()
                dma("sp", wt[:], w_mod[l, :, col:col + 128].rearrange("(k p) c -> p k c", p=128), [B_const], [wb])
                for j in range(2):
                    for k in range(8):
                        mm(gps[j][0][0:1, q * 128:(q + 1) * 128], cT[:, 2 * k + j:2 * k + j + 1], wt[:, k, :], k == 0, k == 7, [wb, B_cb], [gps[j][1]])
            for j in range(2):
                tt("dve", grow[:], gps[j][0][0:1, 0:512], bgate[0:1, half * 512:(half + 1) * 512], ALU.add, [gps[j][1], B_bg], [B_grow])
                zp, zb = ops_.next()
                mm(zp[:, 0:512], ones1[0:1, :], grow[0:1, :], True, True, [B_c2, B_grow], [zb])
                cp("dve", gateB[:, j, half * 512:(half + 1) * 512], zp[:, 0:512], [zb], [B_mod], )
        mpv = mp[:, 0:32].rearrange("p (c j) -> p j c", j=2)
        for j in range(2):
            tt("dve", Sm[:, j, :], mpv[:, j, 0:8], bmodT[:, l * 16:l * 16 + 8], ALU.add, [mb, B_cb], [B_mod])
            tt("dve", Gm[:, j, :], mpv[:, j, 8:16], bmodT[:, l * 16 + 8:l * 16 + 16], ALU.add, [mb, B_cb], [B_mod])
            stt(Gm[:, j, :], Gm[:, j, :], 1.0, gainT[:, l * 8:(l + 1) * 8], ALU.add, ALU.mult, [B_mod, B_cb], [B_mod])

    def norm_tile(l, xap, xbuf, j):
        s4, s4b = st4.next()
        xnt, xnb = xn.next()
        P.op("dve", lambda e: e.scalar_tensor_tensor(out=xnt[:], in0=xap, scalar=1.0, in1=xap, op0=ALU.mult, op1=ALU.mult,
                                                      accum_out=s4[:, 0:1]), [xbuf], [xnb, s4b])
        rsqrt_mean(s4[:, 0:1], s4[:, 0:1], D, [s4b], [s4b])
        ts("dve", xnt[:], xap, s4[:, 0:1], None, ALU.mult, None, [xbuf, s4b], [xnb])
        tp, tb = tps.next()
        for k in range(8):
            tr(tp[:, k * 128:(k + 1) * 128], xnt[:, k * 128:(k + 1) * 128], identb[:], [xnb, B_cb], [tb])
        ht, hb = hT.next()
        for k in range(8):
            ts("dve", ht[:, k, :], tp[:, k * 128:(k + 1) * 128], Gm[:, j, k:k + 1], Sm[:, j, k:k + 1], ALU.mult, ALU.add,
               [tb, B_mod], [hb])
        return ht, hb

    def norm_tile_g(l, xap, xbuf, j):
        s4, s4b = st4.next()
        xnt, xnb = xn.next()
        P.op("dve", lambda e: e.scalar_tensor_tensor(out=xnt[:], in0=xap, scalar=1.0, in1=xap, op0=ALU.mult, op1=ALU.mult,
                                                      accum_out=s4[:, 0:1]), [xbuf], [xnb, s4b])
        yield
        yield from rsqrt_mean_g(s4[:, 0:1], s4[:, 0:1], D, [s4b], [s4b])
        ts("dve", xnt[:], xap, s4[:, 0:1], None, ALU.mult, None, [xbuf, s4b], [xnb])
        yield
        tp, tb = tps.next()
        for k in range(8):
            tr(tp[:, k * 128:(k + 1) * 128], xnt[:, k * 128:(k + 1) * 128], identb[:], [xnb, B_cb], [tb])
        yield
        yield
        ht, hb = hT.next()
        for k in range(8):
            ts("dve", ht[:, k, :], tp[:, k * 128:(k + 1) * 128], Gm[:, j, k:k + 1], Sm[:, j, k:k + 1], ALU.mult, ALU.add,
               [tb, B_mod], [hb])
        yield
        yield
        return ht, hb

    def inproj(ht, hb, W, WB, c0, ncols):
        zp, zb = PS["z"].next()
        for k in range(8):
            mm(zp[:, 0:ncols], ht[:, k, :], W[:, k, c0:c0 + ncols], k == 0, k == 7, [hb, WB], [zb])
        return zp, zb

    def qk_norm_rope(src, srcb, nh, gain_ap, csb, rope):
        n = nh * 64
        Bt, Bb = t1.next()
        Ct, Cb = t2.next()
        s4, s4b = st4.next()
        v3 = lambda ap: ap[:, 0:n].rearrange("p (h d) -> p h d", d=64)
        tt("dve", Bt[:, 0:n], src[:, 0:n], src[:, 0:n], ALU.mult, [srcb], [Bb])
        yield
        P.op("dve", lambda e: e.tensor_reduce(out=s4[:, 0:nh], in_=v3(Bt), axis=AX.X, op=ALU.add), [Bb], [s4b])
        yield
        yield from rsqrt_mean_g(s4[:, 0:nh], s4[:, 0:nh], 64, [s4b], [s4b])
        tt("dve", v3(Ct), v3(src), s4[:, 0:nh].unsqueeze(2).to_broadcast([128, nh, 64]), ALU.mult, [srcb, s4b], [Cb])
        tt("dve", Ct[:, 0:n].rearrange("p (a h d) -> p a h d", a=2, d=64), Ct[:, 0:n].rearrange("p (a h d) -> p a h d", a=2, d=64),
           gain_ap.rearrange("p (a d) -> p a d", a=2).unsqueeze(2).to_broadcast([128, 2, nh // 2, 64]), ALU.mult, [Cb, B_rq], [Cb])
        yield
        ot, ob = knb.next()
        if not rope:
            cp("dve", ot[:, 0:n], Ct[:, 0:n], [Cb], [ob])
            return ot, ob
        cst, csbuf = csb
        v4 = lambda ap: ap[:, 0:n].rearrange("p (h a b c) -> p h a b c", a=2, b=2, c=16)
        cosb = cst[:, 0, :].unsqueeze(1).to_broadcast([128, nh, 64])
        sin4 = cst[:, 1, :].rearrange("p (a b c) -> p a b c", a=2, b=2)
        tt("dve", v3(Bt), v3(Ct), cosb, ALU.mult, [Cb, csbuf], [Bb])
        yield
        for b0 in range(2):
            tt("pool", v4(src)[:, :, :, b0, :], v4(Ct)[:, :, :, 1 - b0, :],
               sin4[:, :, b0, :].unsqueeze(1).to_broadcast([128, nh, 2, 16]), ALU.mult, [Cb, csbuf], [srcb], )
        yield
        yield
        tt("dve", ot[:, 0:n], Bt[:, 0:n], src[:, 0:n], ALU.add, [Bb, srcb], [ob])
        yield
        return ot, ob

    def ret_state(is_ctx):
        return (STc, B_STc, Actx, B_Actx) if is_ctx else (ST, B_ST, Aagg, B_Aagg)

    def p1_tile(l, t, xsrc, B_xsrc):
        is_ctx = t < CTXC
        c = t if is_ctx else t - CTXC
        j = 1 if is_ctx else 0
        if is_ctx:
            xap, xbuf = xc[:, c, :], B_xc[c]
            csb = None
        else:
            xt_, xbuf = xt.next()
            dma("sp", xt_[:], xsrc[c * 128:(c + 1) * 128, :], [B_xsrc], [xbuf])
            xap = xt_[:]
            cst, csbuf = cs.next()
            dma("sp", cst[:, 0, :], cos_in[c], [B_const], [csbuf])
            dma("sp", cst[:, 1, :], sin_in[c], [B_const], [csbuf], par=True)
            csb = (cst, csbuf)
        yield
        ht, hb = norm_tile(l, xap, xbuf, j)
        yield
        rq, rqb = rqkv.next()
        kr, krb = kraw.next()
        vs, vsbuf = vsb.next()
        zp, zb = inproj(ht, hb, WA, B_WA, 0, 512)
        cp("act", rq[:, 0:512], zp[:, 0:512], [zb], [rqb])
        yield
        zp, zb = inproj(ht, hb, WA, B_WA, 512, 512)
        cp("act", rq[:, 512:768], zp[:, 0:256], [zb], [rqb], )
        cp("act", kr[:, 0:256], zp[:, 256:512], [zb], [krb])
        yield
        zp, zb = inproj(ht, hb, WA, B_WA, 1024, 256)
        cp("dve", vs[:, 1:129], zp[:, 0:128], [zb], [vsbuf])
        cp("dve", vs[:, 131:259], zp[:, 128:256], [zb], [vsbuf])
        yield
        fbt, fbb = fb.next()
        for w_, dec in ((0, qd), (1, kd)):
            for dr in range(2):
                tt("dve", fbt[:, w_, :, dr, :], rq[:, w_ * 256:(w_ + 1) * 256].rearrange("p (h d) -> p h d", h=4),
                   dec[:, dr * 4:(dr + 1) * 4].unsqueeze(2).to_broadcast([128, 4, 64]), ALU.mult, [rqb, B_lay], [fbb], )
        yield
        tp, tb = tps.next()
        for w_ in range(2):
            for h in range(4):
                tr(tp[:, (w_ * 4 + h) * 128:(w_ * 4 + h + 1) * 128], fbt[:, w_, h, :, :].rearrange("p a d -> p (a d)"), identb[:],
                   [fbb, B_cb], [tb])
        fT, fTb = fbT.next()
        cp("act", fT[:].rearrange("p a h t -> p (a h t)"), tp[:, 0:1024], [tb], [fTb])
        dma("pool", qfb_d[t], fT[:, 0, :, :].rearrange("p h t -> p (h t)"), [fTb], [B_qfb], par=True)
        yield
        tp, tb = tps.next()
        for w_ in range(2):
            for pr in range(2):
                tr(tp[:, (w_ * 2 + pr) * 128:(w_ * 2 + pr + 1) * 128], rq[:, w_ * 256 + pr * 128:w_ * 256 + (pr + 1) * 128], identb[:],
                   [rqb, B_cb], [tb])
        rt, rtb = rT.next()
        cp("act", rt[0:64, 0, :, :], tp[0:64, 0:256].rearrange("p (b t) -> p b t", b=2), [tb], [rtb], )
        cp("act", rt[64:128, 1, :, :], tp[64:128, 0:256].rearrange("p (b t) -> p b t", b=2), [tb], [rtb], )
        cp("act", rt[:, 2, :, :], tp[:, 256:512].rearrange("p (b t) -> p b t", b=2), [tb], [rtb], )
        sp0, sb0 = sps.next()
        sp1, sb1 = sps.next()
        for pr in range(2):
            mm(sp0[:, pr * 128:(pr + 1) * 128], rt[:, 2, pr, :], rt[:, 0, pr, :], True, True, [rtb], [sb0])
            mm(sp1[:, pr * 128:(pr + 1) * 128], rt[:, 2, pr, :], rt[:, 1, pr, :], True, True, [rtb], [sb1])
        pi, pib = pint.next()
        piv = pi[:].rearrange("p (a b i) -> p a b i", a=2, b=2)
        dcv = DecT[:].rearrange("p (a b i) -> p a b i", a=2, b=2)
        for hh, (spx, sbx) in enumerate(((sp0, sb0), (sp1, sb1))):
            tt("dve", piv[:, :, hh, :], spx[:, 0:256].rearrange("p (a i) -> p a i", a=2), dcv[:, :, hh, :], ALU.mult,
               [sbx, B_lay], [pib], )
        mp, mb = PS["z"].next()
        for h in range(4):
            mm(mp[:, h * 64:(h + 1) * 64], pi[:, h * 128:(h + 1) * 128], rq[:, 512 + h * 64:512 + (h + 1) * 64], True, True, [pib, rqb], [mb])
        for h in range(4):
            mm(mp[:, 256 + h * 64:256 + (h + 1) * 64], fbt[:, 1, h, :, :].rearrange("p a d -> p (a d)"),
               rq[:, 512 + h * 64:512 + (h + 1) * 64], True, True, [fbb, rqb], [mb])
        yield
        yt, ytb = yint.next()
        cp("act", yt[:], mp[:, 0:256], [mb], [ytb])
        row0 = t * 128
        dma("pool", yin_d[row0:row0 + 128, :], yt[:], [ytb], [B_yin], par=True)
        Sx, BS, Ax, BA = ret_state(is_ctx)
        cp("act", Sx[0:64, c + 2, :], mp[0:64, 256:512], [mb], [BS[c + 2]], )
        cp("act", Sx[64:128, c, :], mp[64:128, 256:512], [mb], [BS[c]], )
        yield
        tt("pool", Ax[0:64, :], Ax[0:64, :], Dt[0:64, :], ALU.mult, [BA, B_lay], [BA])
        tt("pool", Ax[0:64, :], Ax[0:64, :], Sx[0:64, c + 2, :], ALU.add, [BA, BS[c + 2]], [BA])
        if is_ctx:
            if c == 0:
                tt("pool", Ax[64:128, :], Ax[64:128, :], Sx[64:128, c, :], ALU.add, [BA, BS[c]], [BA])
            else:
                tt("pool", sintmp[64:128, :], Dt[64:128, :], Sx[64:128, c, :], ALU.mult, [B_lay, BS[c]], [B_sinB])
                tt("pool", Ax[64:128, :], Ax[64:128, :], sintmp[64:128, :], ALU.add, [BA, B_sinB], [BA])
        else:
            tt("pool", sintmp[64:128, :], Pw[64:128, :], Sx[64:128, c, :], ALU.mult, [B_Pw, BS[c]], [B_sinB])
            tt("pool", Ax[64:128, :], Ax[64:128, :], sintmp[64:128, :], ALU.add, [BA, B_sinB], [BA])
            tt("pool", Pw[64:128, :], Pw[64:128, :], Dt[64:128, :], ALU.mult, [B_Pw, B_lay], [B_Pw])
        yield
        gk_gain = qkn[:, 128:256]
        kn, knbuf = yield from qk_norm_rope(kr, krb, 4, gk_gain, csb, not is_ctx)
        yield
        tp, tb = tps.next()
        for a in range(2):
            tr(tp[:, a * 128:(a + 1) * 128], kn[:, a * 128:(a + 1) * 128], identb[:], [knbuf, B_cb], [tb])
        if is_ctx:
            cp("act", KT[:, c * 128:(c + 1) * 128], tp[:, 0:128], [tb], [B_KT], )
            cp("act", skTc[:, c * 128:(c + 1) * 128], tp[:, 128:256], [tb], [B_skTc], )
            cp("dve", V[:, c, 1:129], vs[:, 1:129], [vsbuf], [B_V])
            cp("dve", sVc[:, c, 1:129], vs[:, 131:259], [vsbuf], [B_sVc])
        else:
            kt_, ktb = kTs.next()
            cp("act", kt_[:], tp[:, 0:256], [tb], [ktb])
            dma("pool", gk_x[c // GP][:, (c % GP) * 128:(c % GP + 1) * 128], kt_[:, 0:128], [ktb], [B_gkx[c // GP]], par=True)
            dma("pool", sk_d[c], kt_[:, 128:256], [ktb], [B_skd], par=True)
            dma("pool", gv_x[c // GP][:, (c % GP) * 130:(c % GP + 1) * 130], vs[:, 0:130], [vsbuf], [B_gvx[c // GP]], par=True)
            dma("pool", sv_d[c], vs[:, 130:260], [vsbuf], [B_svd], par=True)
            if c == 0:
                dma("pool", bnd_x[:, 0:128], kt_[:, 128:256], [ktb], [B_bndx], par=True)
                dma("pool", bnd_x[:, 256:386], vs[:, 130:260], [vsbuf], [B_bndx], par=True)
            if c == NCH - 1:
                dma("pool", bnd_x[:, 128:256], kt_[:, 128:256], [ktb], [B_bndx], par=True)
                dma("pool", bnd_x[:, 386:516], vs[:, 130:260], [vsbuf], [B_bndx], par=True)

    def allgather(src, bs, dst, bd):
        groups = [[0, 1, 2, 3], [4, 5, 6, 7]]
        P.op("pool", lambda e: e.collective_compute("AllGather", ALU.bypass, replica_groups=groups, ins=[src], outs=[dst]),
             [bs], [bd], kind="cc")

    def gather_piece(i):
        allgather(gk_x[i], B_gkx[i], gk_all[i], B_gkall[i])
        allgather(gv_x[i], B_gvx[i], gv_all[i], B_gvall[i])

    def exchange(l):
        dma("pool", agg_x, Aagg[:], [B_Aagg], [B_aggx])
        allgather(agg_x, B_aggx, agg_all, B_aggall)
        allgather(bnd_x, B_bndx, bnd_all, B_bndall)

    def exchange_b(l):
        dma("sp", aggs[:], agg_all.rearrange("(r p) c -> p r c", p=128), [B_aggall], [B_aggs])
        B_s2 = Buf("s2")
        v3 = lambda ap: ap.rearrange("p (h e) -> p h e", h=4)
        cb = lambda s_: coef[:, s_, :].unsqueeze(2).to_broadcast([128, 4, 64])
        SB2 = [B_sinF, B_sinB]
        tt("dve", v3(sintmp[:]), v3(Actx[:]), cb(4), ALU.mult, [B_Actx, B_lay], SB2)
        for r in range(R):
            tt("dve", v3(aggs[:, r, :]), v3(aggs[:, r, :]), cb(r), ALU.mult, [B_aggs, B_lay], [B_aggs])
            tt("dve", sintmp[:], sintmp[:], aggs[:, r, :], ALU.add, SB2 + [B_aggs], SB2)
        cp("dve", ST[0:64, 1, :], sintmp[0:64, :], [B_sinF], [B_ST[1]], )
        cp("dve", ST[64:128, NCH, :], sintmp[64:128, :], [B_sinB], [B_ST[NCH]], )
        sF, sB_ = sintmp[0:64, :], sintmp[64:128, :]
        for i in range(1, NCH):
            cf = i
            cbk = NCH - 1 - i
            tt("dve", sF, sF, Dt[0:64, :], ALU.mult, [B_sinF, B_lay], [B_sinF])
            tt("dve", sB_, sB_, Dt[64:128, :], ALU.mult, [B_sinB, B_lay], [B_sinB])
            tt("dve", sF, sF, ST[0:64, cf + 1, :], ALU.add, [B_sinF, B_ST[cf + 1]], [B_sinF])
            tt("dve", sB_, sB_, ST[64:128, cbk + 1, :], ALU.add, [B_sinB, B_ST[cbk + 1]], [B_sinB])
            cp("dve", ST[0:64, cf + 1, :], sF, [B_sinF], [B_ST[cf + 1]], )
            cp("dve", ST[64:128, cbk + 1, :], sB_, [B_sinB], [B_ST[cbk + 1]], )

    def small_attn(qT_ap, qTb, blocks, sink_l, mo, mob, gt, gtb, colbase):
        for g in range(2):
            op_, opb = PS["acc"].next()
            pend = []

            def pv(item):
                bi, vap, w_, wb_, bufs = item
                mm(op_[0:65, 0:256], vap[:, g * 65:(g + 1) * 65], w_[:], bi == 0, bi == len(blocks) - 1, [wb_] + bufs, [opb])

            for bi, (kap, vap, mask, bufs) in enumerate(blocks):
                sp_, spb = sps.next()
                mm(sp_[:, 0:256], kap, qT_ap[:, g, :, :].rearrange("p r t -> p (r t)"), True, True,
                   [qTb] + bufs, [spb])
                yield
                w_, wb_ = wp.next()
                act(w_[:], sp_[:, 0:256], AF.Exp, [spb], [wb_], scale=SCALE)
                yield
                if mask is not None:
                    tt("pool", w_[:].rearrange("p (r t) -> p r t", r=2), w_[:].rearrange("p (r t) -> p r t", r=2),
                       mask.unsqueeze(1).to_broadcast([128, 2, 128]), ALU.mult, [wb_, B_cb], [wb_])
                    yield
                pend.append((bi, vap, w_, wb_, bufs))
                if len(pend) > 1:
                    pv(pend.pop(0))
            for item in pend:
                pv(item)
            yield
            wo, wob = woT.next()
            cp("dve", wo[:], op_[0:65, 0:256], [opb], [wob])
            yield
            for r in range(2):
                h = g * 2 + r
                zp, zb = PS["z"].next()
                tr(zp[:, 0:65], wo[:, r * 128:(r + 1) * 128], identf[0:65, 0:65], [wob, B_cb], [zb])
                yield
                finish_head(zp, zb, g, h, sink_l, mo, mob, gt, gtb, colbase)
                yield

    def finish_head(zp, zb, g, h, sink_l, mo, mob, gt, gtb, colbase):
        rd_, rdb = rden.next()
        dcol = 0 if g == 0 else 64
        o0 = 1 if g == 0 else 0
        if sink_l:
            ts("dve", rd_[:, 0:1], zp[:, dcol:dcol + 1], esink[:, h:h + 1], None, ALU.add, None, [zb, B_lay], [rdb])
            P.op("dve", lambda e: e.reciprocal(out=rd_[:, 0:1], in_=rd_[:, 0:1]), [rdb], [rdb])
        else:
            P.op("dve", lambda e: e.reciprocal(out=rd_[:, 0:1], in_=zp[:, dcol:dcol + 1]), [zb], [rdb])
        stt(mo[:, colbase + h * 64:colbase + (h + 1) * 64], zp[:, o0:o0 + 64], rd_[:, 0:1], gt[:, colbase + h * 64:colbase + (h + 1) * 64],
            ALU.mult, ALU.mult, [zb, rdb, gtb], [mob])

    def silu_evac(dst, dstb, zp, zb):
        Ct, Cb = t2.next()
        act(Ct[:, 0:512], zp[:, 0:512], AF.Exp, [zb], [Cb], scale=-1.0)
        yield
        ts("dve", Ct[:, 0:512], Ct[:, 0:512], 1.0, None, ALU.add, None, [Cb], [Cb])
        yield
        P.op("dve", lambda e: e.reciprocal(out=Ct[:, 0:512], in_=Ct[:, 0:512]), [Cb], [Cb])
        yield
        yield
        tt("dve", dst, zp[:, 0:512], Ct[:, 0:512], ALU.mult, [zb, Cb], [dstb])

    def p2_front(l, t, xsrc, B_xsrc):
        is_ctx = t < CTXC
        c = t if is_ctx else t - CTXC
        j = 1 if is_ctx else 0
        if is_ctx:
            xap, xbuf = xc[:, c, :], B_xc[c]
            csb = None
        else:
            xt_, xbuf = xt.next()
            dma("sp", xt_[:], xsrc[c * 128:(c + 1) * 128, :], [B_xsrc], [xbuf])
            xap = xt_[:]
            cst, csbuf = cs.next()
            dma("sp", cst[:, 0, :], cos_in[c], [B_const], [csbuf])
            dma("sp", cst[:, 1, :], sin_in[c], [B_const], [csbuf], par=True)
            csb = (cst, csbuf)
        yl, ylb = yinl.next()
        dma("sp", yl[:], yin_d[t * 128:(t + 1) * 128, :], [B_yin], [ylb])
        ql, qlb = qfbl.next()
        dma("sp", ql[:], qfb_d[t], [B_qfb], [qlb])
        yield
        yield
        ht, hb = yield from norm_tile_g(l, xap, xbuf, j)
        ug, ugb = uvg.next()
        gt, gtb = gates.next()
        mo, mob = mixo.next()
        qr, qrb = kraw.next()
        zp, zb = inproj(ht, hb, W2, B_W2, 0, 512)
        yield
        yield
        act(ug[:], zp[:, 0:512], AF.Gelu, [zb], [ugb])
        yield
        zp, zb = inproj(ht, hb, W2, B_W2, 512, 512)
        yield
        yield
        yield from silu_evac(gt[:, 0:512], gtb, zp, zb)
        yield
        zp, zb = inproj(ht, hb, W2, B_W2, 1024, 512)
        yield
        yield
        yield from silu_evac(gt[:, 512:1024], gtb, zp, zb)
        yield
        zp, zb = inproj(ht, hb, W2, B_W2, 1536, 512)
        yield
        yield
        for blk in range(2):
            cp("dve", qr[:, blk * 256:(blk + 1) * 256].rearrange("p (r g d) -> p r g d", r=2, g=2),
               zp[:, blk * 256:(blk + 1) * 256].rearrange("p (g r d) -> p r g d", r=2, g=2), [zb], [qrb], )
        yield
        mp, mb = PS["z"].next()
        for h in range(4):
            mm(mp[:, h * 64:(h + 1) * 64], mixT[:, h * 128:(h + 1) * 128], ug[:, 256 + h * 64:256 + (h + 1) * 64], True, True, [B_mixT, ugb], [mb])
        Sx, BS, _, _ = ret_state(is_ctx)
        for h in range(4):
            mm(mp[:, 256 + h * 64:256 + (h + 1) * 64], ql[:, h * 128:(h + 1) * 128], Sx[:, c + 1, h * 64:(h + 1) * 64], True, True, [qlb, BS[c + 1]], [mb])
        yield
        yield
        ys, ysb = ysum.next()
        v3 = lambda ap: ap.rearrange("p (h d) -> p h d", h=4)
        tt("dve", v3(ys[:]), v3(mp[:, 0:256]), mbias[:, l * 4:(l + 1) * 4].unsqueeze(2).to_broadcast([128, 4, 64]), ALU.add, [mb, B_cb], [ysb])
        tt("dve", ys[:], ys[:], ug[:, 0:256], ALU.mult, [ysb, ugb], [ysb])
        yield
        tt("dve", mo[:, 0:256], ys[:], gt[:, 0:256], ALU.mult, [ysb, gtb], [mob])
        ys, ysb = ysum.next()
        tt("dve", ys[:], mp[:, 256:512], yl[:], ALU.add, [mb, ylb], [ysb])
        yield
        Bt, Bb = t1.next()
        s4, s4b = st4.next()
        tt("dve", Bt[:, 0:256], ys[:], ys[:], ALU.mult, [ysb], [Bb])
        yield
        P.op("dve", lambda e: e.tensor_reduce(out=s4[:, 0:4], in_=v3(Bt[:, 0:256]), axis=AX.X, op=ALU.add), [Bb], [s4b])
        yield
        yield from rsqrt_mean_g(s4[:, 0:4], s4[:, 0:4], 64, [s4b], [s4b])
        tt("dve", v3(ys[:]), v3(ys[:]), s4[:, 0:4].unsqueeze(2).to_broadcast([128, 4, 64]), ALU.mult, [ysb, s4b], [ysb])
        tt("dve", ys[:], ys[:], rnorm[:, 0:256], ALU.mult, [ysb, B_rq], [ysb])
        yield
        tt("dve", mo[:, 256:512], ys[:], gt[:, 256:512], ALU.mult, [ysb, gtb], [mob])
        yield
        qn, qnb = yield from qk_norm_rope(qr, qrb, 8, qkn[:, 0:128], csb, not is_ctx)
        yield
        tp, tb = tps.next()
        for a in range(4):
            tr(tp[:, a * 128:(a + 1) * 128], qn[:, a * 128:(a + 1) * 128], identb[:], [qnb, B_cb], [tb])
        yield
        yield
        return dict(t=t, c=c, is_ctx=is_ctx, gt=gt, gtb=gtb, mo=mo, mob=mob, tp=tp, tb=tb)

    def out_proj(l, st, xsrc, B_xsrc, xdst, B_xdst):
        t, c, is_ctx, mo, mob = st["t"], st["c"], st["is_ctx"], st["mo"], st["mob"]
        j = 1 if is_ctx else 0
        if is_ctx:
            xap, xbuf = xc[:, c, :], B_xc[c]
        else:
            xr_, xbuf = xr.next()
            dma("sp", xr_[:], xsrc[c * 128:(c + 1) * 128, :], [B_xsrc], [xbuf])
            xap = xr_[:]
        tp, tb = tps.next()
        for k in range(8):
            tr(tp[:, k * 128:(k + 1) * 128], mo[:, k * 128:(k + 1) * 128], identb[:], [mob, B_cb], [tb])
        yield
        yield
        mt, mtb = mixTt.next()
        cp("dve", mt[:].rearrange("p k t -> p (k t)"), tp[:, 0:1024], [tb], [mtb])
        yield
        yield
        ot, otb = otmp.next()
        for half in range(2):
            zp, zb = PS["z"].next()
            for k in range(8):
                mm(zp[:, 0:512], mt[:, k, :], WA[:, k, half * 512:(half + 1) * 512], k == 0, k == 7, [mtb, B_WA], [zb])
            yield
            yield
            yield
            tt("dve", ot[:, half * 512:(half + 1) * 512], zp[:, 0:512], gateB[:, j, half * 512:(half + 1) * 512], ALU.mult, [zb, B_mod], [otb], )
            yield
        if is_ctx:
            tt("pool", xc[:, c, :], xc[:, c, :], ot[:], ALU.add, [xbuf, otb], [xbuf])
        else:
            tt("pool", ot[:], ot[:], xap, ALU.add, [otb, xbuf], [otb])
            yield
            yield
            dma("pool", xdst[c * 128:(c + 1) * 128, :], ot[:], [otb], [B_xdst], par=True)
        yield

    def q_evac(st, q_, qb, lo, cols=None):
        for gg in range(2):
            dst = q_[gg * 64:(gg + 1) * 64, gg, :, :] if cols is None else q_[gg * 64:(gg + 1) * 64, gg, :, cols[0]:cols[1]]
            cp("dve", dst, st["tp"][gg * 64:(gg + 1) * 64, lo:lo + 256].rearrange("p (r t) -> p r t", r=2), [st["tb"]], [qb], )

    def p2_ctx(l):
        for c in range(CTXC):
            st = yield from p2_front(l, c, None, None)
            qT_, qTb = sqT.next()
            qT2, qT2b = sqT.next()
            q_evac(st, qT_, qTb, 0)
            q_evac(st, qT2, qT2b, 256)
            yield
            gblocks = [(KT[:, cc * 128:(cc + 1) * 128], V[:, cc, :], None, [B_KT, B_V]) for cc in range(CTXC)]
            yield from small_attn(qT_, qTb, gblocks, False, st["mo"], st["mob"], st["gt"], st["gtb"], 512)
            sblocks = [(skTc[:, cc * 128:(cc + 1) * 128], sVc[:, cc, :], None, [B_skTc, B_sVc]) for cc in range(CTXC)]
            yield from small_attn(qT2, qT2b, sblocks, True, st["mo"], st["mob"], st["gt"], st["gtb"], 768)
            yield from out_proj(l, st, None, None, None, None)

    def swa_tile(l, st, sq_, sqb):
        c = st["c"]
        wk_, wkb = wk.next()
        wv_, wvb = wv.next()
        blocks = []
        bb = [wkb, wvb]
        if c == 0:
            dma("sp", wk_[:, 0:2, :], sk_d[0:2].rearrange("c p k -> p c k"), [B_skd], [wkb])
            dma("sp", wv_[:, 0:2, :], sv_d[0:2].rearrange("c p k -> p c k"), [B_svd], [wvb])
            dma("sp", wk_[:, 2:6, :], bnd_all[:, 128:256].rearrange("(r p) k -> p r k", p=128), [B_bndall], [wkb], par=True)
            dma("sp", wv_[:, 2:6, :], bnd_all[:, 386:516].rearrange("(r p) k -> p r k", p=128), [B_bndall], [wvb], par=True)
            blocks.append((wk_[:, 0, :], wv_[:, 0, :], None, bb))
            blocks.append((wk_[:, 1, :], wv_[:, 1, :], cmask[:, 128:256], bb))
            for r in range(R):
                blocks.append((wk_[:, 2 + r, :], wv_[:, 2 + r, :], hmask[:, r * 128:(r + 1) * 128], bb))
        elif c == NCH - 1:
            dma("sp", wk_[:, 0:2, :], sk_d[c - 1:c + 1].rearrange("c p k -> p c k"), [B_skd], [wkb])
            dma("sp", wv_[:, 0:2, :], sv_d[c - 1:c + 1].rearrange("c p k -> p c k"), [B_svd], [wvb])
            dma("sp", wk_[:, 2:6, :], bnd_all[:, 0:128].rearrange("(r p) k -> p r k", p=128), [B_bndall], [wkb], par=True)
            dma("sp", wv_[:, 2:6, :], bnd_all[:, 256:386].rearrange("(r p) k -> p r k", p=128), [B_bndall], [wvb], par=True)
            blocks.append((wk_[:, 0, :], wv_[:, 0, :], cmask[:, 0:128], bb))
            blocks.append((wk_[:, 1, :], wv_[:, 1, :], None, bb))
            for r in range(R):
                blocks.append((wk_[:, 2 + r, :], wv_[:, 2 + r, :], hmask[:, (4 + r) * 128:(5 + r) * 128], bb))
        else:
            dma("sp", wk_[:, 0:3, :], sk_d[c - 1:c + 2].rearrange("c p k -> p c k"), [B_skd], [wkb])
            dma("sp", wv_[:, 0:3, :], sv_d[c - 1:c + 2].rearrange("c p k -> p c k"), [B_svd], [wvb])
            blocks.append((wk_[:, 0, :], wv_[:, 0, :], cmask[:, 0:128], bb))
            blocks.append((wk_[:, 1, :], wv_[:, 1, :], None, bb))
            blocks.append((wk_[:, 2, :], wv_[:, 2, :], cmask[:, 128:256], bb))
        for cc in range(CTXC):
            blocks.append((skTc[:, cc * 128:(cc + 1) * 128], sVc[:, cc, :], None, [B_skTc, B_sVc]))
        yield
        yield from small_attn(sq_, sqb, blocks, True, st["mo"], st["mob"], st["gt"], st["gtb"], 768)

    def front_group(l, gi, xsrc, B_xsrc, G):
        gq_, gqb = gqT.next()
        G["gq"] = (gq_, gqb)
        G["sts"] = []
        for ti in range(QG):
            st = yield from p2_front(l, CTXC + gi * QG + ti, xsrc, B_xsrc)
            sq_, sqb = sqT.next()
            q_evac(st, gq_, gqb, 0, (ti * 128, (ti + 1) * 128))
            q_evac(st, sq_, sqb, 256)
            yield
            yield from swa_tile(l, st, sq_, sqb)
            G["sts"].append(st)

    def sweep_group(G):
        gq_, gqb = G["gq"]
        accs = [ops_.next() for _ in range(2)]
        pend = []
        NW = 2 * GQ
        pieces = [(None, None)] + [(r, c0) for r in range(R) for c0 in range(0, NCH, PCS)]
        nblk_total = CTXC + R * NCH
        seen = 0

        def pv(item):
            first, last, g0, vap, vb, p0, pb0 = item
            mm(accs[g0][0][0:65, 0:NW], vap[:, g0 * 65:(g0 + 1) * 65], p0[:, 0:NW], first, last, [pb0] + vb, [accs[g0][1]])

        for (r, c0) in pieces:
            if r is None:
                nb = CTXC
                kget = lambda i: KT[:, i * 128:(i + 1) * 128]
                vget = lambda i: V[:, i, :]
                kbufs, vbufs = [B_KT], [B_V]
            else:
                nb = PCS
                kt_, ktb_ = ksl.next()
                vt_, vtb_ = vsl.next()
                gp_, of_ = c0 // GP, c0 % GP
                dma("sp", kt_[:], gk_all[gp_][r * 128:(r + 1) * 128, of_ * 128:(of_ + PCS) * 128], [B_gkall[gp_]], [ktb_])
                dma("sp", vt_[:], gv_all[gp_][r * 128:(r + 1) * 128, of_ * 130:(of_ + PCS) * 130].rearrange("p (c d) -> p c d", d=130), [B_gvall[gp_]], [vtb_])
                kget = lambda i, kt_=kt_: kt_[:, i * 128:(i + 1) * 128]
                vget = lambda i, vt_=vt_: vt_[:, i, :]
                kbufs, vbufs = [ktb_], [vtb_]
            for i in range(nb):
                first, last = seen == 0, seen == nblk_total - 1
                seen += 1
                for g in range(2):
                    sp_, spb = sps.next()
                    mm(sp_[:, 0:NW], kget(i), gq_[:, g, :, :].rearrange("p r t -> p (r t)"), True, True,
                       kbufs + [gqb], [spb])
                    p_, pb = pT.next()
                    act(p_[:, 0:NW], sp_[:, 0:NW], AF.Exp, [spb], [pb], scale=SCALE)
                    pend.append((first, last, g, vget(i), vbufs, p_, pb))
                    if len(pend) > 2:
                        pv(pend.pop(0))
                yield
        for item in pend:
            pv(item)
        G["oT"] = []
        for g in range(2):
            o_, ob_ = oT.next()
            cp("dve", o_[:, 0:NW], accs[g][0][0:65, 0:NW], [accs[g][1]], [ob_])
            G["oT"].append((o_, ob_))

    def tail_group(l, G, xsrc, B_xsrc, xdst, B_xdst):
        sts = G["sts"]
        for g in range(2):
            o_, ob_ = G["oT"][g]
            for r in range(2):
                h = 2 * g + r
                for ti in range(QG):
                    zp, zb = PS["z"].next()
                    tr(zp[:, 0:65], o_[:, r * GQ + ti * 128:r * GQ + (ti + 1) * 128], identf[0:65, 0:65], [ob_, B_cb], [zb])
                    yield
                    yield
                    finish_head(zp, zb, g, h, False, sts[ti]["mo"], sts[ti]["mob"], sts[ti]["gt"], sts[ti]["gtb"], 512)
                    yield
        for st in sts:
            yield from out_proj(l, st, xsrc, B_xsrc, xdst, B_xdst)

    def run(gen):
        try:
            while True:
                next(gen)
        except StopIteration as e:
            return e.value

    def gchain(*gens):
        for g_ in gens:
            yield from g_

    def pass2(l, xsrc, B_xsrc, xdst, B_xdst):
        PS["z"], PS["acc"] = zps1, swacc
        if l < depth - 1:
            run(p2_ctx(l))
        exchange_b(l)
        Gs = [dict() for _ in range(NG)]
        run(front_group(l, 0, xsrc, B_xsrc, Gs[0]))
        for k in range(NG):
            sides = []
            if k > 0:
                sides.append(tail_group(l, Gs[k - 1], xsrc, B_xsrc, xdst, B_xdst))
            if k + 1 < NG:
                sides.append(front_group(l, k + 1, xsrc, B_xsrc, Gs[k + 1]))
            side = gchain(*sides)
            alive = True
            for _ in sweep_group(Gs[k]):
                for _r in range(SIDE_RATE):
                    if alive:
                        try:
                            next(side)
                        except StopIteration:
                            alive = False
            if alive:
                run(side)
        run(tail_group(l, Gs[NG - 1], xsrc, B_xsrc, xdst, B_xdst))
        PS["z"], PS["acc"] = zpsP1, ops_

    chain = [(x_in, B_xin)]
    inter = [(xsA, B_xsA), (xsB, B_xsB)]
    for l in range(depth):
        chain.append((y_out, B_y) if l == depth - 1 else inter[l % 2])
    for l in range(depth):
        xsrc, B_xsrc = chain[l]
        xdst, B_xdst = chain[l + 1]
        load_w1(l)
        layer_consts(l)
        if stop == "consts":
            break
        mod_compute(l)
        if stop == "mod":
            break
        load_w2(l)
        if stop == "w2":
            break
        gens = [p1_tile(l, t, xsrc, B_xsrc) for t in range(CTXC + NCH)]
        active = []
        nxt = 0
        while nxt < len(gens) or active:
            if nxt < len(gens) and (not active or (len(active) < 2 and active[-1][1] >= P1LAG)):
                active.append([gens[nxt], 0, nxt - CTXC])
                nxt += 1
            for a_ in list(active):
                try:
                    next(a_[0])
                    a_[1] += 1
                except StopIteration:
                    active.remove(a_)
                    if a_[2] >= 0 and a_[2] % GP == GP - 1:
                        gather_piece(a_[2] // GP)
        if stop == "p1":
            break
        exchange(l)
        if stop == "exch":
            break
        load_wo(l)
        pass2(l, xsrc, B_xsrc, xdst, B_xdst)

    P.finalize()
    with nc.Block() as block:
        @block.sync
        def _(e):
            P.emit("sp", e)

        @block.scalar
        def _(e):
            P.emit("act", e)

        @block.vector
        def _(e):
            P.emit("dve", e)

        @block.tensor
        def _(e):
            P.emit("pe", e)

        @block.gpsimd
        def _(e):
            P.emit("pool", e)
            if B_y.sem is not None:
                P.final_wait(e, [B_y])
            else:
                dummy = P.es.enter_context(nc.semaphore("dummy"))
                e.dma_start(out=y_out[0:128, :], in_=xc[:, 0, :]).then_inc(dummy, 16)
                e.wait_ge(dummy, 16)
    es.close()
    return nc


def host_inputs(inputs, NCH, depth=DEPTH, n_cores=8):
    f = np.float32
    x = np.asarray(inputs["x"], f)
    NT = NCH * 128
    bf = ml_dtypes.bfloat16
    c = np.asarray(inputs["c"], f)
    ctx = np.asarray(inputs["ctx"], f)
    c_ctx = np.asarray(inputs["c_ctx"], f)
    ng = np.asarray(inputs["norm_gain"], f)
    b_mod = np.asarray(inputs["b_mod"], f)
    common = {
        "gainT": np.ascontiguousarray(ng.reshape(depth, 8, 128).transpose(2, 0, 1).reshape(128, depth * 8)),
        "w_mod": np.ascontiguousarray(np.asarray(inputs["w_mod"], f)),
        "bmodT": np.ascontiguousarray(b_mod[:, 0:2048].reshape(depth, 16, 128).transpose(2, 0, 1).reshape(128, depth * 16)),
        "bgate": np.ascontiguousarray(b_mod[:, 2048:3072].reshape(1, depth * D)),
        "w_in": np.ascontiguousarray(np.asarray(inputs["w_in"], f)),
        "w_out": np.ascontiguousarray(np.asarray(inputs["w_out"], f)),
        "mixT": np.ascontiguousarray(np.asarray(inputs["mlp_mix"], f).transpose(0, 3, 1, 2).reshape(depth, 128, 512)),
        "mbias": np.ascontiguousarray(np.asarray(inputs["mlp_bias"], f).transpose(2, 0, 1).reshape(128, depth * 4)),
        "rdec": np.ascontiguousarray(np.stack([np.asarray(inputs["ret_decay_fwd"], f), np.asarray(inputs["ret_decay_bwd"], f)], 1).reshape(1, depth * 8)),
        "rnorm": np.ascontiguousarray(np.asarray(inputs["ret_norm"], f).reshape(1, depth * 256)),
        "qkn": np.ascontiguousarray(np.stack([np.asarray(inputs[k], f) for k in ("attn_q_norm", "swa_q_norm", "attn_k_norm", "swa_k_norm")], 1).reshape(1, depth * 256)),
        "sink": np.ascontiguousarray(np.asarray(inputs["swa_sink"], f).reshape(1, depth * 4)),
    }
    jj = np.arange(128, dtype=f)[:, None]
    ii = np.arange(128, dtype=f)[None, :]
    common["identb"] = np.eye(128, dtype=f).astype(bf)
    common["identf"] = np.eye(128, dtype=f)
    common["rpn"] = np.concatenate([np.maximum(ii - jj, 0), np.maximum(jj - ii, 0)], 1).astype(f)
    p = np.arange(128, dtype=f)
    common["pos"] = np.stack([127 - p, p, p + 1, 128 - p], 1).astype(f)
    mprev = (jj >= ii).astype(f)
    mnext = (jj <= ii).astype(f)
    common["cmask"] = np.concatenate([mprev, mnext], 1).astype(bf)
    half = 32
    inv_freq = (1.0 / (10000.0 ** (np.arange(0, half, 2, dtype=f) / f(half)))).astype(f)
    sgn = np.concatenate([-np.ones(16, f), np.ones(16, f), -np.ones(16, f), np.ones(16, f)])
    maps = []
    for core in range(n_cores):
        b, seg = core // R, core % R
        m = dict(common)
        m["x_in"] = np.ascontiguousarray(x[b, seg * NT:(seg + 1) * NT, :])
        m["ctx_in"] = np.ascontiguousarray(ctx[b])
        cT = np.zeros((128, 16), f)
        cT[:, 0::2] = c[b].reshape(8, 128).T
        cT[:, 1::2] = c_ctx.reshape(8, 128).T
        m["cT"] = cT
        tpos = seg * NT + np.arange(NT)
        row = (tpos // 64).astype(f)
        col = (tpos % 64).astype(f)
        ang_r = row[:, None] * inv_freq[None, :]
        ang_c = col[:, None] * inv_freq[None, :]
        ang = np.concatenate([ang_r, ang_r, ang_c, ang_c], -1).astype(f)
        m["cos"] = np.cos(ang).astype(f).reshape(NCH, 128, 64)
        m["sin"] = (np.sin(ang).astype(f) * sgn[None, :]).reshape(NCH, 128, 64)
        et = np.full((128, 5), BIGE, f)
        for r in range(R):
            if r < seg:
                et[0:64, r] = seg - 1 - r
            if r > seg:
                et[64:128, r] = r - seg - 1
        et[0:64, 4] = seg
        et[64:128, 4] = R - 1 - seg
        m["etab"] = et
        hm = np.zeros((128, 8, 128), f)
        if seg - 1 >= 0:
            hm[:, seg - 1, :] = mprev
        if seg + 1 < R:
            hm[:, 4 + seg + 1, :] = mnext
        m["hmask"] = hm.reshape(128, 1024).astype(bf)
        maps.append(m)
    return maps


_NC_CACHE = {}


def kernel(**inputs):
    x = np.asarray(inputs["x"])
    B, L, _ = x.shape
    NCH = L // R // 128
    depth = np.asarray(inputs["w_in"]).shape[0]
    key = (NCH, depth)
    if key not in _NC_CACHE:
        _NC_CACHE[key] = build(NCH, depth)
    nc = _NC_CACHE[key]
    maps = host_inputs(inputs, NCH, depth)
    res = run_bass_kernel_spmd(nc, maps, core_ids=list(range(8)))
    NT = NCH * 128
    out = np.zeros((B, L, D), np.float32)
    for core in range(8):
        b, seg = core // R, core % R
        out[b, seg * NT:(seg + 1) * NT, :] = res.results[core]["y"]
    return out
```

```python
import math
from contextlib import ExitStack
import numpy as np
import ml_dtypes
import concourse.bass as bass
import concourse.mybir as mybir
from concourse.bass_utils import run_bass_kernel_spmd

F32 = mybir.dt.float32
BF16 = mybir.dt.bfloat16
AF = mybir.ActivationFunctionType
ALU = mybir.AluOpType
AX = mybir.AxisListType

D = 1024
DEPTH = 4
CTXC = 2
R = 4
EPS = 1e-6
SCALE = 0.125
BIGE = 1.0e4


class Buf:
    def __init__(self, name):
        self.name = name
        self.last_w = []
        self.readers = []
        self.sem = None
        self.cnt = 0


class Op:
    __slots__ = ("eng", "fn", "deps", "kind", "sig", "val", "sem", "owner")

    def __init__(self, eng, fn, kind):
        self.eng, self.fn, self.kind = eng, fn, kind
        self.deps = []
        self.sig = False
        self.val = 0
        self.sem = None
        self.owner = None


class Prog:
    def __init__(self, nc, es):
        self.nc, self.es = nc, es
        self.ops = {k: [] for k in ("pe", "act", "dve", "pool", "sp")}
        self.esem = {k: es.enter_context(nc.semaphore("e_" + k)) for k in ("pe", "act", "dve", "pool")}
        self.nsem = 4

    def op(self, eng, fn, reads=(), writes=(), kind="c", par=False):
        o = Op(eng, fn, kind)
        deps = []
        for b in reads:
            deps += b.last_w
        for b in writes:
            deps += b.readers
            if not par:
                deps += b.last_w
        seen = set()
        for d in deps:
            if id(d) in seen or d is o:
                continue
            seen.add(id(d))
            if d.kind == "c" and d.eng == "pe" and eng == "pe" and kind == "c":
                continue
            d.sig = True
            o.deps.append((d, d.owner.cnt if d.owner is not None else None))
        for b in reads:
            b.readers.append(o)
        for b in writes:
            if par:
                b.last_w = b.last_w + [o]
            else:
                b.last_w = [o]
            b.readers = []
        if kind in ("d", "cc"):
            b = writes[0]
            if b.sem is None:
                b.sem = self.es.enter_context(self.nc.semaphore("s_" + b.name))
                self.nsem += 1
            b.cnt += 16 if kind == "d" else 1
            o.sem, o.val, o.owner = b.sem, b.cnt, b
        self.ops[eng].append(o)
        return o

    def finalize(self):
        for eng in ("pe", "act", "dve", "pool"):
            c = 0
            for o in self.ops[eng]:
                if o.kind == "c" and o.sig:
                    c += 1
                    o.val = c
                    o.sem = self.esem[eng]

    def emit(self, eng, e):
        waited = {}
        for o in self.ops[eng]:
            need = {}
            for d, fixed in o.deps:
                k = id(d.sem)
                v = d.val if fixed is None else fixed
                if v > waited.get(k, 0) and (k not in need or v > need[k][1]):
                    need[k] = (d.sem, v)
            for k, (sem_, val_) in need.items():
                e.wait_ge(sem_, val_)
                waited[k] = val_
            ins = o.fn(e)
            if o.kind == "d":
                ins.then_inc(o.sem, 16)
            elif o.kind == "cc":
                ins.then_inc(o.sem, 1)
            elif o.sig:
                ins.then_inc(o.sem, 1)

    def final_wait(self, e, bufs):
        for b in bufs:
            e.wait_ge(b.sem, b.cnt)


class Rot:
    def __init__(self, tiles, name, bufs=None):
        self.tiles = tiles
        self.bufs = bufs if bufs is not None else [Buf("%s%d" % (name, i)) for i in range(len(tiles))]
        self.i = -1

    def next(self):
        self.i = (self.i + 1) % len(self.tiles)
        return self.tiles[self.i], self.bufs[self.i]


def build(NCH, depth=DEPTH, QG=2, stop=None, SIDE_RATE=2, P1LAG=9):
    NT = NCH * 128
    NTT = NT + CTXC * 128
    KB = CTXC + R * NCH
    KTOT = KB * 128
    NG = NCH // QG
    GQ = QG * 128
    PCS = min(NCH, 4)
    GP = min(NCH, 8)
    NGP = NCH // GP
    nc = bass.Bass("TRN2", target_bir_lowering=False)
    es = ExitStack()
    P = Prog(nc, es)

    def din(name, shape, dt=F32):
        return nc.dram_tensor(name, shape, dt, kind="ExternalInput").ap()

    def dint(name, shape, dt):
        return nc.dram_tensor(name, shape, dt, kind="Internal").ap()

    x_in = din("x_in", [NT, D])
    ctx_in = din("ctx_in", [CTXC * 128, D])
    cT_in = din("cT", [128, 16])
    gainT_in = din("gainT", [128, depth * 8])
    w_mod = din("w_mod", [depth, D, 3 * D])
    bmodT_in = din("bmodT", [128, depth * 16])
    bgate_in = din("bgate", [1, depth * D])
    w_in = din("w_in", [depth, D, 3328])
    w_out = din("w_out", [depth, D, D])
    mixT_in = din("mixT", [depth, 128, 512])
    mbias_in = din("mbias", [128, depth * 4])
    rdec_in = din("rdec", [1, depth * 8])
    rnorm_in = din("rnorm", [1, depth * 256])
    qkn_in = din("qkn", [1, depth * 256])
    sink_in = din("sink", [1, depth * 4])
    cos_in = din("cos", [NCH, 128, 64])
    sin_in = din("sin", [NCH, 128, 64])
    etab_in = din("etab", [128, 5])
    hmask_in = din("hmask", [128, 8 * 128], BF16)
    cmask_in = din("cmask", [128, 2 * 128], BF16)
    identb_in = din("identb", [128, 128], BF16)
    identf_in = din("identf", [128, 128])
    rpn_in = din("rpn", [128, 256])
    pos_in = din("pos", [128, 4])
    y_out = nc.dram_tensor("y", [NT, D], F32, kind="ExternalOutput").ap()

    xsA = dint("xsA", [NT, D], F32)
    xsB = dint("xsB", [NT, D], F32)
    yin_d = dint("yin_d", [NTT, 256], F32)
    qfb_d = dint("qfb_d", [NCH + CTXC, 128, 512], BF16)
    sk_d = dint("sk_d", [NCH, 128, 128], BF16)
    sv_d = dint("sv_d", [NCH, 128, 130], BF16)
    gk_x = [dint("gk_x%d" % i, [128, GP * 128], BF16) for i in range(NGP)]
    gk_all = [dint("gk_all%d" % i, [R * 128, GP * 128], BF16) for i in range(NGP)]
    gv_x = [dint("gv_x%d" % i, [128, GP * 130], BF16) for i in range(NGP)]
    gv_all = [dint("gv_all%d" % i, [R * 128, GP * 130], BF16) for i in range(NGP)]
    bnd_x = dint("bnd_x", [128, 516], BF16)
    bnd_all = dint("bnd_all", [R * 128, 516], BF16)
    agg_x = dint("agg_x", [128, 256], F32)
    agg_all = dint("agg_all", [R * 128, 256], F32)
    B_xin, B_xsA, B_xsB, B_y = Buf("xin"), Buf("xsA"), Buf("xsB"), Buf("y")
    B_yin, B_qfb, B_skd, B_svd = Buf("yin"), Buf("qfb"), Buf("skd"), Buf("svd")
    B_gkx = [Buf("gkx%d" % i) for i in range(NGP)]
    B_gkall = [Buf("gkall%d" % i) for i in range(NGP)]
    B_gvx = [Buf("gvx%d" % i) for i in range(NGP)]
    B_gvall = [Buf("gvall%d" % i) for i in range(NGP)]
    B_bndx, B_bndall, B_aggx, B_aggall = Buf("bndx"), Buf("bndall"), Buf("aggx"), Buf("aggall")
    B_const = Buf("constin")

    def sb(name, shape, dt=F32):
        return es.enter_context(nc.sbuf_tensor(name, shape, dt))

    def ps(name, shape, dt=F32):
        return es.enter_context(nc.psum_tensor(name, shape, dt))

    KT = sb("KTc", [128, CTXC * 128], BF16);     B_KT = Buf("KT")
    V = sb("Vc", [128, CTXC, 130], BF16);        B_V = Buf("V")
    ksl = Rot([sb("ksl%d" % i, [128, PCS * 128], BF16) for i in range(2)], "ksl")
    vsl = Rot([sb("vsl%d" % i, [128, PCS, 130], BF16) for i in range(2)], "vsl")
    skTc = sb("skTc", [128, CTXC * 128], BF16);  B_skTc = Buf("skTc")
    sVc = sb("sVc", [128, CTXC, 130], BF16);     B_sVc = Buf("sVc")
    ST = sb("ST", [128, NCH + 2, 256], BF16)
    B_ST = [Buf("ST%d" % i) for i in range(NCH + 2)]
    STc = sb("STc", [128, CTXC + 2, 256], BF16)
    B_STc = [Buf("STc%d" % i) for i in range(CTXC + 2)]
    WA = sb("WA", [128, 8, 1280], BF16);         B_WA = Buf("WA")
    W2 = sb("W2s", [128, 8, 2048], BF16);         B_W2 = Buf("W2")
    mixT = sb("mixTs", [128, 512], BF16);        B_mixT = Buf("mixT")
    xc = sb("xc", [128, CTXC, D], F32)
    B_xc = [Buf("xc%d" % i) for i in range(CTXC)]
    identb = sb("identb_s", [128, 128], BF16)
    identf = sb("identf_s", [128, 128], F32)
    rpn = sb("rpn_s", [128, 256], F32)
    pos = sb("pos_s", [128, 4], F32)
    etab = sb("etab_s", [128, 5], F32)
    hmask = sb("hmask_s", [128, 8 * 128], BF16)
    cmask = sb("cmask_s", [128, 256], BF16)
    cT = sb("cT_s", [128, 16], F32)
    gainT = sb("gainT_s", [128, depth * 8], F32)
    bmodT = sb("bmodT_s", [128, depth * 16], F32)
    bgate = sb("bgate_s", [1, D], F32);  B_bg = Buf("bg")
    grow = sb("grow", [1, 512], F32);    B_grow = Buf("grow")
    mbias = sb("mbias_s", [128, depth * 4], F32)
    rdec = sb("rdec_s", [128, depth * 8], F32)
    rdsel = sb("rdsel_s", [128, depth * 4], F32)
    rnorm = sb("rnorm_s", [128, 256], F32);  B_rq = Buf("rqn")
    qkn = sb("qkn_s", [128, 256], F32)
    sink = sb("sink_s", [128, depth * 4], F32)
    ones1 = sb("ones1", [1, 128], F32)
    B_cb = Buf("constsb")
    Gm = sb("Gm", [128, 2, 8], F32)
    Sm = sb("Sm", [128, 2, 8], F32)
    gateB = sb("gateB", [128, 2, D], F32)
    B_mod = Buf("mod")
    lg = sb("lg", [128, 8], F32)
    lgsel = sb("lgsel", [128, 4], F32)
    kd = sb("kd", [128, 8], F32)
    qd = sb("qd", [128, 8], F32)
    DecT = sb("DecT", [128, 512], F32)
    Dt = sb("Dt", [128, 256], F32)
    Pw = sb("Pw", [128, 256], F32)
    Aagg = sb("Aagg", [128, 256], F32)
    Actx = sb("Actx", [128, 256], F32)
    coef = sb("coef", [128, 5, 4], F32)
    esink = sb("esink", [128, 4], F32)
    B_lay = Buf("laysmall")
    B_Aagg, B_Actx, B_Pw = Buf("Aagg"), Buf("Actx"), Buf("Pw")
    aggs = sb("aggs", [128, R, 256], F32);    B_aggs = Buf("aggs")
    sintmp = sb("sintmp", [128, 256], F32);   B_sinF = Buf("sinF");  B_sinB = Buf("sinB")
    xt = Rot([sb("xt%d" % i, [128, D], F32) for i in range(2)], "xt")
    xr = xt
    st4 = Rot([sb("st4_%d" % i, [128, 16], F32) for i in range(4)], "st4")
    xn = Rot([sb("xn%d" % i, [128, D], BF16) for i in range(2)], "xn")
    hT = Rot([sb("hT%d" % i, [128, 8, 128], BF16) for i in range(2)], "hT")
    cs = Rot([sb("cs%d" % i, [128, 2, 64], F32) for i in range(2)], "cs")
    rqkv = Rot([sb("rqkv%d" % i, [128, 768], BF16) for i in range(2)], "rqkv")
    fb = Rot([sb("fb%d" % i, [128, 2, 4, 2, 64], BF16) for i in range(2)], "fb")
    fbT = Rot([sb("fbT%d" % i, [128, 2, 4, 128], BF16) for i in range(1)], "fbT")
    rT = Rot([sb("rT%d" % i, [128, 3, 2, 128], BF16) for i in range(1)], "rT")
    pint = Rot([sb("pint%d" % i, [128, 512], BF16) for i in range(1)], "pint")
    yint = Rot([sb("yint%d" % i, [128, 256], F32) for i in range(1)], "yint")
    kraw = Rot([sb("kraw%d" % i, [128, 512], F32) for i in range(2)], "kraw")
    t1 = Rot([sb("t1_%d" % i, [128, 512], F32) for i in range(1)], "t1")
    t2 = Rot([sb("t2_%d" % i, [128, 512], F32) for i in range(1)], "t2")
    knb = Rot([sb("knb%d" % i, [128, 512], BF16) for i in range(2)], "knb")
    kTs = Rot([sb("kTs%d" % i, [128, 256], BF16) for i in range(2)], "kTs")
    vsb = Rot([sb("vsb%d" % i, [128, 260], BF16) for i in range(2)], "vsb")
    NSL = 2 * QG
    gates = Rot([sb("gates%d" % i, [128, D], BF16) for i in range(NSL)], "gates")
    mixo = Rot([sb("mixo%d" % i, [128, D], BF16) for i in range(NSL)], "mixo")
    uvg = Rot([sb("uvg%d" % i, [128, 512], BF16) for i in range(1)], "uvg")
    gqT = Rot([sb("gqT%d" % i, [128, 2, 2, GQ], BF16) for i in range(2)], "gqT")
    sqT = Rot([sb("sqT%d" % i, [128, 2, 2, 128], BF16) for i in range(2)], "sqT")
    pT = Rot([sb("pT%d" % i, [128, 512], BF16) for i in range(3)], "pT")
    oT = Rot([sb("oT%d" % i, [65, 512], F32) for i in range(2)], "oT")
    rden = Rot([sb("rden%d" % i, [128, 4], F32) for i in range(4)], "rden")
    mixTt = Rot([sb("mixTt%d" % i, [128, 8, 128], BF16) for i in range(1)], "mixTt")
    otmp = Rot([sb("otmp%d" % i, [128, D], F32) for i in range(1)], "otmp")
    wmst = Rot([t_[:].rearrange("p (k c) -> p k c", k=8) for t_ in (otmp.tiles[0], xt.tiles[0], xt.tiles[1])], "wmst",
               bufs=[otmp.bufs[0], xt.bufs[0], xt.bufs[1]])
    yinl = Rot([sb("yinl%d" % i, [128, 256], F32) for i in range(1)], "yinl")
    qfbl = Rot([sb("qfbl%d" % i, [128, 512], BF16) for i in range(1)], "qfbl")
    ysum = Rot([sb("ysum%d" % i, [128, 256], F32) for i in range(2)], "ysum")
    wk = Rot([sb("wk%d" % i, [128, 6, 128], BF16) for i in range(1)], "wk")
    wv = Rot([sb("wv%d" % i, [128, 6, 130], BF16) for i in range(1)], "wv")
    wp = Rot([sb("wp%d" % i, [128, 256], BF16) for i in range(3)], "wp")
    woT = Rot([sb("woT%d" % i, [65, 256], F32) for i in range(1)], "woT")
    zps = Rot([ps("zps%d" % i, [128, 512]) for i in range(2)], "zps")
    tps = Rot([ps("tps%d" % i, [128, 1024], BF16) for i in range(1)], "tps")
    sps = Rot([ps("sps%d" % i, [128, 512]) for i in range(3)], "sps")
    ops_ = Rot([ps("ops%d" % i, [128, 512]) for i in range(2)], "ops")
    zps1 = Rot([zps.tiles[0]], "zps1", bufs=[zps.bufs[0]])
    swacc = Rot([zps.tiles[1]], "swacc", bufs=[zps.bufs[1]])
    zpsP1 = Rot(zps.tiles + ops_.tiles, "zpsP1", bufs=zps.bufs + ops_.bufs)
    PS = {"z": zpsP1, "acc": ops_}

    def dma(q, out, in_, reads, writes, par=False, **kw):
        return P.op(q, lambda e: e.dma_start(out=out, in_=in_, **kw), reads, writes, kind="d", par=par)

    def mm(out, lhsT, rhs, start, stop, reads, writes):
        return P.op("pe", lambda e: e.matmul(out, lhsT=lhsT, rhs=rhs, start=start, stop=stop), reads, writes)

    def tr(out, in_, ident, reads, writes):
        return P.op("pe", lambda e: e.transpose(out, in_, ident), reads, writes)

    def act(out, in_, func, reads, writes, **kw):
        return P.op("act", lambda e: e.activation(out=out, in_=in_, func=func, **kw), reads, writes)

    def tt(eng, out, in0, in1, op, reads, writes):
        return P.op(eng, lambda e: e.tensor_tensor(out=out, in0=in0, in1=in1, op=op), reads, writes)

    def ts(eng, out, in0, s1, s2, op0, op1, reads, writes):
        if op1 is None:
            return P.op(eng, lambda e: e.tensor_scalar(out=out, in0=in0, scalar1=s1, scalar2=None, op0=op0), reads, writes)
        return P.op(eng, lambda e: e.tensor_scalar(out=out, in0=in0, scalar1=s1, scalar2=s2, op0=op0, op1=op1), reads, writes)

    def stt(out, in0, scalar, in1, op0, op1, reads, writes):
        return P.op("dve", lambda e: e.scalar_tensor_tensor(out=out, in0=in0, scalar=scalar, in1=in1, op0=op0, op1=op1), reads, writes)

    def cp(eng, out, in_, reads, writes):
        if eng == "act":
            return P.op("act", lambda e: e.copy(out=out, in_=in_), reads, writes)
        return P.op(eng, lambda e: e.tensor_copy(out=out, in_=in_), reads, writes)

    def rsqrt_mean_g(out, ssum, n, reads, writes):
        ts("dve", out, ssum, 1.0 / n, EPS, ALU.mult, ALU.add, reads, writes)
        yield
        act(out, out, AF.Ln, writes, writes)
        yield
        act(out, out, AF.Exp, writes, writes, scale=-0.5)
        yield

    def rsqrt_mean(out, ssum, n, reads, writes):
        ts("dve", out, ssum, 1.0 / n, EPS, ALU.mult, ALU.add, reads, writes)
        act(out, out, AF.Ln, writes, writes)
        act(out, out, AF.Exp, writes, writes, scale=-0.5)

    def bc(ap, n):
        return ap.partition_broadcast(n)

    for dst, src in ((identb, identb_in), (identf, identf_in), (rpn, rpn_in), (pos, pos_in), (etab, etab_in),
                     (hmask, hmask_in), (cmask, cmask_in), (cT, cT_in), (gainT, gainT_in), (bmodT, bmodT_in),
                     (mbias, mbias_in)):
        dma("sp", dst[:], src, [B_const], [B_cb], par=True)
    dma("sp", rdec[:], rdec_in.partition_broadcast(128), [B_const], [B_cb], par=True)
    for l in range(depth):
        dma("sp", rdsel[0:64, l * 4:(l + 1) * 4], rdec_in[:, l * 8:l * 8 + 4].partition_broadcast(64), [B_const], [B_cb], par=True)
        dma("sp", rdsel[64:128, l * 4:(l + 1) * 4], rdec_in[:, l * 8 + 4:l * 8 + 8].partition_broadcast(64), [B_const], [B_cb], par=True)
    dma("sp", sink[:], sink_in.partition_broadcast(128), [B_const], [B_cb], par=True)
    for c in range(CTXC):
        dma("sp", xc[:, c, :], ctx_in[c * 128:(c + 1) * 128, :], [B_const], [B_xc[c]])
    B_c2 = Buf("const2")
    P.op("dve", lambda e: e.memset(ones1[:], 1.0), [], [B_c2])
    for rot_ in (gqT, sqT, rT):
        for i in range(len(rot_.tiles)):
            P.op("pool", lambda e, t_=rot_.tiles[i]: e.memset(t_[:], 0.0), [], [rot_.bufs[i]])
    P.op("dve", lambda e: e.memset(V[:], 1.0), [], [B_V])
    P.op("dve", lambda e: e.memset(sVc[:], 1.0), [], [B_sVc])
    for i in range(2):
        P.op("pool", lambda e, i=i: e.memset(vsb.tiles[i][:], 1.0), [], [vsb.bufs[i]])
    act(cT[:], cT[:], AF.Silu, [B_cb], [B_cb])

    def load_w1(l):
        srcs = [(768, 1536, 0), (2048, 2176, 768), (2816, 2944, 896), (2176, 2304, 1024), (2944, 3072, 1152)]
        for k in range(8):
            for (a, b_, d0) in srcs:
                dma("pool", WA[:, k, d0:d0 + (b_ - a)], w_in[l, k * 128:(k + 1) * 128, a:b_], [B_const], [B_WA], par=True)

    def load_w2(l):
        srcs = [(0, 768, 0), (1536, 1792, 768), (2304, 2560, 1024), (3072, 3328, 1280), (1792, 2048, 1536), (2560, 2816, 1792)]
        for k in range(8):
            for (a, b_, d0) in srcs:
                dma("pool", W2[:, k, d0:d0 + (b_ - a)], w_in[l, k * 128:(k + 1) * 128, a:b_], [B_const], [B_W2], par=True)
        dma("pool", mixT[:], mixT_in[l], [B_const], [B_mixT])

    def load_wo(l):
        for k in range(8):
            dma("pool", WA[:, k, 0:1024], w_out[l, k * 128:(k + 1) * 128, :], [B_const], [B_WA], par=True)

    def layer_consts(l):
        rd = [B_cb, B_c2]
        w = [B_lay]
        dma("sp", rnorm[:], rnorm_in[:, l * 256:(l + 1) * 256].partition_broadcast(128), [B_const], [B_rq])
        dma("sp", qkn[:], qkn_in[:, l * 256:(l + 1) * 256].partition_broadcast(128), [B_const], [B_rq], par=True)
        act(lg[:], rdec[:, l * 8:(l + 1) * 8], AF.Exp, rd, w)
        ts("dve", lg[:], lg[:], -1.0, None, ALU.mult, None, w, w)
        act(lgsel[:], rdsel[:, l * 4:(l + 1) * 4], AF.Exp, rd, w)
        ts("dve", lgsel[:], lgsel[:], -1.0, None, ALU.mult, None, w, w)
        for dr in range(2):
            ts("dve", kd[:, dr * 4:(dr + 1) * 4], lg[:, dr * 4:(dr + 1) * 4], pos[:, dr:dr + 1], None, ALU.mult, None, rd + w, w)
            ts("dve", qd[:, dr * 4:(dr + 1) * 4], lg[:, dr * 4:(dr + 1) * 4], pos[:, 2 + dr:3 + dr], None, ALU.mult, None, rd + w, w)
        act(kd[:], kd[:], AF.Exp, w, w)
        ts("dve", kd[:], kd[:], SCALE, None, ALU.mult, None, w, w)
        act(qd[:], qd[:], AF.Exp, w, w)
        for h in range(4):
            ts("dve", DecT[:, h * 128:(h + 1) * 128], rpn[:, 0:128], lg[:, h:h + 1], None, ALU.mult, None, rd + w, w)
            stt(DecT[:, h * 128:(h + 1) * 128], rpn[:, 128:256], lg[:, 4 + h:5 + h], DecT[:, h * 128:(h + 1) * 128],
                ALU.mult, ALU.add, rd + w, w)
        act(DecT[:], DecT[:], AF.Exp, w, w)
        ts("dve", DecT[:], DecT[:], SCALE, None, ALU.mult, None, w, w)
        ts("dve", esink[:], lgsel[:], 128.0, None, ALU.mult, None, w, w)
        act(esink[:], esink[:], AF.Exp, w, w)
        cp("dve", Dt[:].rearrange("p (h e) -> p h e", h=4), esink[:].unsqueeze(2).to_broadcast([128, 4, 64]), w, w)
        for s_ in range(5):
            ts("dve", coef[:, s_, :], lgsel[:], etab[:, s_:s_ + 1], 128.0 * NCH, ALU.mult, ALU.mult, rd + w, w)
        act(coef[:], coef[:], AF.Exp, w, w)
        act(esink[:], sink[:, l * 4:(l + 1) * 4], AF.Exp, rd + w, w)
        P.op("pool", lambda e: e.memset(Aagg[:], 0.0), [], [B_Aagg])
        P.op("pool", lambda e: e.memset(Actx[:], 0.0), [], [B_Actx])
        P.op("pool", lambda e: e.memset(Pw[:], 1.0), [], [B_Pw])
        P.op("pool", lambda e: e.memset(STc[0:64, 1, :], 0.0), [], [B_STc[1]], par=True)
        P.op("pool", lambda e: e.memset(STc[64:128, 2, :], 0.0), [], [B_STc[2]], par=True)

    def mod_compute(l):
        dma("sp", bgate[:], bgate_in[:, l * D:(l + 1) * D], [B_const], [B_bg])
        mp, mb = zps.next()
        for half in range(-1, 2):
            if half < 0:
                for c in range(16):
                    wt, wb = wmst.next()
                    dma("sp", wt[:], w_mod[l, :, c * 128:(c + 1) * 128].rearrange("(k p) c -> p k c", p=128), [B_const], [wb])
                    for k in range(8):
                        mm(mp[:, c * 2:c * 2 + 2], wt[:, k, :], cT[:, 2 * k:2 * k + 2], k == 0, k == 7, [wb, B_cb], [mb])
                continue
            g0, gb0 = sps.next()
            g1, gb1 = sps.next()
            gps = ((g0, gb0), (g1, gb1))
            for q in range(4):
                col = 2048 + half * 512 + q * 128
                wt, wb = wmst.next()
                dma("sp", wt[:], w_mod[l, :, col:col + 128].rearrange("(k p) c -> p k c", p=128), [B_const], [wb])
                for j in range(2):
                    for k in range(8):
                        mm(gps[j][0][0:1, q * 128:(q + 1) * 128], cT[:, 2 * k + j:2 * k + j + 1], wt[:, k, :], k == 0, k == 7, [wb, B_cb], [gps[j][1]])
            for j in range(2):
                tt("dve", grow[:], gps[j][0][0:1, 0:512], bgate[0:1, half * 512:(half + 1) * 512], ALU.add, [gps[j][1], B_bg], [B_grow])
                zp, zb = ops_.next()
                mm(zp[:, 0:512], ones1[0:1, :], grow[0:1, :], True, True, [B_c2, B_grow], [zb])
                cp("dve", gateB[:, j, half * 512:(half + 1) * 512], zp[:, 0:512], [zb], [B_mod], )
        mpv = mp[:, 0:32].rearrange("p (c j) -> p j c", j=2)
        for j in range(2):
            tt("dve", Sm[:, j, :], mpv[:, j, 0:8], bmodT[:, l * 16:l * 16 + 8], ALU.add, [mb, B_cb], [B_mod])
            tt("dve", Gm[:, j, :], mpv[:, j, 8:16], bmodT[:, l * 16 + 8:l * 16 + 16], ALU.add, [mb, B_cb], [B_mod])
            stt(Gm[:, j, :], Gm[:, j, :], 1.0, gainT[:, l * 8:(l + 1) * 8], ALU.add, ALU.mult, [B_mod, B_cb], [B_mod])

    def norm_tile(l, xap, xbuf, j):
        s4, s4b = st4.next()
        xnt, xnb = xn.next()
        P.op("dve", lambda e: e.scalar_tensor_tensor(out=xnt[:], in0=xap, scalar=1.0, in1=xap, op0=ALU.mult, op1=ALU.mult,
                                                      accum_out=s4[:, 0:1]), [xbuf], [xnb, s4b])
        rsqrt_mean(s4[:, 0:1], s4[:, 0:1], D, [s4b], [s4b])
        ts("dve", xnt[:], xap, s4[:, 0:1], None, ALU.mult, None, [xbuf, s4b], [xnb])
        tp, tb = tps.next()
        for k in range(8):
            tr(tp[:, k * 128:(k + 1) * 128], xnt[:, k * 128:(k + 1) * 128], identb[:], [xnb, B_cb], [tb])
        ht, hb = hT.next()
        for k in range(8):
            ts("dve", ht[:, k, :], tp[:, k * 128:(k + 1) * 128], Gm[:, j, k:k + 1], Sm[:, j, k:k + 1], ALU.mult, ALU.add,
               [tb, B_mod], [hb])
        return ht, hb

    def norm_tile_g(l, xap, xbuf, j):
        s4, s4b = st4.next()
        xnt, xnb = xn.next()
        P.op("dve", lambda e: e.scalar_tensor_tensor(out=xnt[:], in0=xap, scalar=1.0, in1=xap, op0=ALU.mult, op1=ALU.mult,
                                                      accum_out=s4[:, 0:1]), [xbuf], [xnb, s4b])
        yield
        yield from rsqrt_mean_g(s4[:, 0:1], s4[:, 0:1], D, [s4b], [s4b])
        ts("dve", xnt[:], xap, s4[:, 0:1], None, ALU.mult, None, [xbuf, s4b], [xnb])
        yield
        tp, tb = tps.next()
        for k in range(8):
            tr(tp[:, k * 128:(k + 1) * 128], xnt[:, k * 128:(k + 1) * 128], identb[:], [xnb, B_cb], [tb])
        yield
        yield
        ht, hb = hT.next()
        for k in range(8):
            ts("dve", ht[:, k, :], tp[:, k * 128:(k + 1) * 128], Gm[:, j, k:k + 1], Sm[:, j, k:k + 1], ALU.mult, ALU.add,
               [tb, B_mod], [hb])
        yield
        yield
        return ht, hb

    def inproj(ht, hb, W, WB, c0, ncols):
        zp, zb = PS["z"].next()
        for k in range(8):
            mm(zp[:, 0:ncols], ht[:, k, :], W[:, k, c0:c0 + ncols], k == 0, k == 7, [hb, WB], [zb])
        return zp, zb

    def qk_norm_rope(src, srcb, nh, gain_ap, csb, rope):
        n = nh * 64
        Bt, Bb = t1.next()
        Ct, Cb = t2.next()
        s4, s4b = st4.next()
        v3 = lambda ap: ap[:, 0:n].rearrange("p (h d) -> p h d", d=64)
        tt("dve", Bt[:, 0:n], src[:, 0:n], src[:, 0:n], ALU.mult, [srcb], [Bb])
        yield
        P.op("dve", lambda e: e.tensor_reduce(out=s4[:, 0:nh], in_=v3(Bt), axis=AX.X, op=ALU.add), [Bb], [s4b])
        yield
        yield from rsqrt_mean_g(s4[:, 0:nh], s4[:, 0:nh], 64, [s4b], [s4b])
        tt("dve", v3(Ct), v3(src), s4[:, 0:nh].unsqueeze(2).to_broadcast([128, nh, 64]), ALU.mult, [srcb, s4b], [Cb])
        tt("dve", Ct[:, 0:n].rearrange("p (a h d) -> p a h d", a=2, d=64), Ct[:, 0:n].rearrange("p (a h d) -> p a h d", a=2, d=64),
           gain_ap.rearrange("p (a d) -> p a d", a=2).unsqueeze(2).to_broadcast([128, 2, nh // 2, 64]), ALU.mult, [Cb, B_rq], [Cb])
        yield
        ot, ob = knb.next()
        if not rope:
            cp("dve", ot[:, 0:n], Ct[:, 0:n], [Cb], [ob])
            return ot, ob
        cst, csbuf = csb
        v4 = lambda ap: ap[:, 0:n].rearrange("p (h a b c) -> p h a b c", a=2, b=2, c=16)
        cosb = cst[:, 0, :].unsqueeze(1).to_broadcast([128, nh, 64])
        sin4 = cst[:, 1, :].rearrange("p (a b c) -> p a b c", a=2, b=2)
        tt("dve", v3(Bt), v3(Ct), cosb, ALU.mult, [Cb, csbuf], [Bb])
        yield
        for b0 in range(2):
            tt("pool", v4(src)[:, :, :, b0, :], v4(Ct)[:, :, :, 1 - b0, :],
               sin4[:, :, b0, :].unsqueeze(1).to_broadcast([128, nh, 2, 16]), ALU.mult, [Cb, csbuf], [srcb], )
        yield
        yield
        tt("dve", ot[:, 0:n], Bt[:, 0:n], src[:, 0:n], ALU.add, [Bb, srcb], [ob])
        yield
        return ot, ob

    def ret_state(is_ctx):
        return (STc, B_STc, Actx, B_Actx) if is_ctx else (ST, B_ST, Aagg, B_Aagg)

    def p1_tile(l, t, xsrc, B_xsrc):
        is_ctx = t < CTXC
        c = t if is_ctx else t - CTXC
        j = 1 if is_ctx else 0
        if is_ctx:
            xap, xbuf = xc[:, c, :], B_xc[c]
            csb = None
        else:
            xt_, xbuf = xt.next()
            dma("sp", xt_[:], xsrc[c * 128:(c + 1) * 128, :], [B_xsrc], [xbuf])
            xap = xt_[:]
            cst, csbuf = cs.next()
            dma("sp", cst[:, 0, :], cos_in[c], [B_const], [csbuf])
            dma("sp", cst[:, 1, :], sin_in[c], [B_const], [csbuf], par=True)
            csb = (cst, csbuf)
        yield
        ht, hb = norm_tile(l, xap, xbuf, j)
        yield
        rq, rqb = rqkv.next()
        kr, krb = kraw.next()
        vs, vsbuf = vsb.next()
        zp, zb = inproj(ht, hb, WA, B_WA, 0, 512)
        cp("act", rq[:, 0:512], zp[:, 0:512], [zb], [rqb])
        yield
        zp, zb = inproj(ht, hb, WA, B_WA, 512, 512)
        cp("act", rq[:, 512:768], zp[:, 0:256], [zb], [rqb], )
        cp("act", kr[:, 0:256], zp[:, 256:512], [zb], [krb])
        yield
        zp, zb = inproj(ht, hb, WA, B_WA, 1024, 256)
        cp("dve", vs[:, 1:129], zp[:, 0:128], [zb], [vsbuf])
        cp("dve", vs[:, 131:259], zp[:, 128:256], [zb], [vsbuf])
        yield
        fbt, fbb = fb.next()
        for w_, dec in ((0, qd), (1, kd)):
            for dr in range(2):
                tt("dve", fbt[:, w_, :, dr, :], rq[:, w_ * 256:(w_ + 1) * 256].rearrange("p (h d) -> p h d", h=4),
                   dec[:, dr * 4:(dr + 1) * 4].unsqueeze(2).to_broadcast([128, 4, 64]), ALU.mult, [rqb, B_lay], [fbb], )
        yield
        tp, tb = tps.next()
        for w_ in range(2):
            for h in range(4):
                tr(tp[:, (w_ * 4 + h) * 128:(w_ * 4 + h + 1) * 128], fbt[:, w_, h, :, :].rearrange("p a d -> p (a d)"), identb[:],
                   [fbb, B_cb], [tb])
        fT, fTb = fbT.next()
        cp("act", fT[:].rearrange("p a h t -> p (a h t)"), tp[:, 0:1024], [tb], [fTb])
        dma("pool", qfb_d[t], fT[:, 0, :, :].rearrange("p h t -> p (h t)"), [fTb], [B_qfb], par=True)
        yield
        tp, tb = tps.next()
        for w_ in range(2):
            for pr in range(2):
                tr(tp[:, (w_ * 2 + pr) * 128:(w_ * 2 + pr + 1) * 128], rq[:, w_ * 256 + pr * 128:w_ * 256 + (pr + 1) * 128], identb[:],
                   [rqb, B_cb], [tb])
        rt, rtb = rT.next()
        cp("act", rt[0:64, 0, :, :], tp[0:64, 0:256].rearrange("p (b t) -> p b t", b=2), [tb], [rtb], )
        cp("act", rt[64:128, 1, :, :], tp[64:128, 0:256].rearrange("p (b t) -> p b t", b=2), [tb], [rtb], )
        cp("act", rt[:, 2, :, :], tp[:, 256:512].rearrange("p (b t) -> p b t", b=2), [tb], [rtb], )
        sp0, sb0 = sps.next()
        sp1, sb1 = sps.next()
        for pr in range(2):
            mm(sp0[:, pr * 128:(pr + 1) * 128], rt[:, 2, pr, :], rt[:, 0, pr, :], True, True, [rtb], [sb0])
            mm(sp1[:, pr * 128:(pr + 1) * 128], rt[:, 2, pr, :], rt[:, 1, pr, :], True, True, [rtb], [sb1])
        pi, pib = pint.next()
        piv = pi[:].rearrange("p (a b i) -> p a b i", a=2, b=2)
        dcv = DecT[:].rearrange("p (a b i) -> p a b i", a=2, b=2)
        for hh, (spx, sbx) in enumerate(((sp0, sb0), (sp1, sb1))):
            tt("dve", piv[:, :, hh, :], spx[:, 0:256].rearrange("p (a i) -> p a i", a=2), dcv[:, :, hh, :], ALU.mult,
               [sbx, B_lay], [pib], )
        mp, mb = PS["z"].next()
        for h in range(4):
            mm(mp[:, h * 64:(h + 1) * 64], pi[:, h * 128:(h + 1) * 128], rq[:, 512 + h * 64:512 + (h + 1) * 64], True, True, [pib, rqb], [mb])
        for h in range(4):
            mm(mp[:, 256 + h * 64:256 + (h + 1) * 64], fbt[:, 1, h, :, :].rearrange("p a d -> p (a d)"),
               rq[:, 512 + h * 64:512 + (h + 1) * 64], True, True, [fbb, rqb], [mb])
        yield
        yt, ytb = yint.next()
        cp("act", yt[:], mp[:, 0:256], [mb], [ytb])
        row0 = t * 128
        dma("pool", yin_d[row0:row0 + 128, :], yt[:], [ytb], [B_yin], par=True)
        Sx, BS, Ax, BA = ret_state(is_ctx)
        cp("act", Sx[0:64, c + 2, :], mp[0:64, 256:512], [mb], [BS[c + 2]], )
        cp("act", Sx[64:128, c, :], mp[64:128, 256:512], [mb], [BS[c]], )
        yield
        tt("pool", Ax[0:64, :], Ax[0:64, :], Dt[0:64, :], ALU.mult, [BA, B_lay], [BA])
        tt("pool", Ax[0:64, :], Ax[0:64, :], Sx[0:64, c + 2, :], ALU.add, [BA, BS[c + 2]], [BA])
        if is_ctx:
            if c == 0:
                tt("pool", Ax[64:128, :], Ax[64:128, :], Sx[64:128, c, :], ALU.add, [BA, BS[c]], [BA])
            else:
                tt("pool", sintmp[64:128, :], Dt[64:128, :], Sx[64:128, c, :], ALU.mult, [B_lay, BS[c]], [B_sinB])
                tt("pool", Ax[64:128, :], Ax[64:128, :], sintmp[64:128, :], ALU.add, [BA, B_sinB], [BA])
        else:
            tt("pool", sintmp[64:128, :], Pw[64:128, :], Sx[64:128, c, :], ALU.mult, [B_Pw, BS[c]], [B_sinB])
            tt("pool", Ax[64:128, :], Ax[64:128, :], sintmp[64:128, :], ALU.add, [BA, B_sinB], [BA])
            tt("pool", Pw[64:128, :], Pw[64:128, :], Dt[64:128, :], ALU.mult, [B_Pw, B_lay], [B_Pw])
        yield
        gk_gain = qkn[:, 128:256]
        kn, knbuf = yield from qk_norm_rope(kr, krb, 4, gk_gain, csb, not is_ctx)
        yield
        tp, tb = tps.next()
        for a in range(2):
            tr(tp[:, a * 128:(a + 1) * 128], kn[:, a * 128:(a + 1) * 128], identb[:], [knbuf, B_cb], [tb])
        if is_ctx:
            cp("act", KT[:, c * 128:(c + 1) * 128], tp[:, 0:128], [tb], [B_KT], )
            cp("act", skTc[:, c * 128:(c + 1) * 128], tp[:, 128:256], [tb], [B_skTc], )
            cp("dve", V[:, c, 1:129], vs[:, 1:129], [vsbuf], [B_V])
            cp("dve", sVc[:, c, 1:129], vs[:, 131:259], [vsbuf], [B_sVc])
        else:
            kt_, ktb = kTs.next()
            cp("act", kt_[:], tp[:, 0:256], [tb], [ktb])
            dma("pool", gk_x[c // GP][:, (c % GP) * 128:(c % GP + 1) * 128], kt_[:, 0:128], [ktb], [B_gkx[c // GP]], par=True)
            dma("pool", sk_d[c], kt_[:, 128:256], [ktb], [B_skd], par=True)
            dma("pool", gv_x[c // GP][:, (c % GP) * 130:(c % GP + 1) * 130], vs[:, 0:130], [vsbuf], [B_gvx[c // GP]], par=True)
            dma("pool", sv_d[c], vs[:, 130:260], [vsbuf], [B_svd], par=True)
            if c == 0:
                dma("pool", bnd_x[:, 0:128], kt_[:, 128:256], [ktb], [B_bndx], par=True)
                dma("pool", bnd_x[:, 256:386], vs[:, 130:260], [vsbuf], [B_bndx], par=True)
            if c == NCH - 1:
                dma("pool", bnd_x[:, 128:256], kt_[:, 128:256], [ktb], [B_bndx], par=True)
                dma("pool", bnd_x[:, 386:516], vs[:, 130:260], [vsbuf], [B_bndx], par=True)

    def allgather(src, bs, dst, bd):
        groups = [[0, 1, 2, 3], [4, 5, 6, 7]]
        P.op("pool", lambda e: e.collective_compute("AllGather", ALU.bypass, replica_groups=groups, ins=[src], outs=[dst]),
             [bs], [bd], kind="cc")

    def gather_piece(i):
        allgather(gk_x[i], B_gkx[i], gk_all[i], B_gkall[i])
        allgather(gv_x[i], B_gvx[i], gv_all[i], B_gvall[i])

    def exchange(l):
        dma("pool", agg_x, Aagg[:], [B_Aagg], [B_aggx])
        allgather(agg_x, B_aggx, agg_all, B_aggall)
        allgather(bnd_x, B_bndx, bnd_all, B_bndall)

    def exchange_b(l):
        dma("sp", aggs[:], agg_all.rearrange("(r p) c -> p r c", p=128), [B_aggall], [B_aggs])
        B_s2 = Buf("s2")
        v3 = lambda ap: ap.rearrange("p (h e) -> p h e", h=4)
        cb = lambda s_: coef[:, s_, :].unsqueeze(2).to_broadcast([128, 4, 64])
        SB2 = [B_sinF, B_sinB]
        tt("dve", v3(sintmp[:]), v3(Actx[:]), cb(4), ALU.mult, [B_Actx, B_lay], SB2)
        for r in range(R):
            tt("dve", v3(aggs[:, r, :]), v3(aggs[:, r, :]), cb(r), ALU.mult, [B_aggs, B_lay], [B_aggs])
            tt("dve", sintmp[:], sintmp[:], aggs[:, r, :], ALU.add, SB2 + [B_aggs], SB2)
        cp("dve", ST[0:64, 1, :], sintmp[0:64, :], [B_sinF], [B_ST[1]], )
        cp("dve", ST[64:128, NCH, :], sintmp[64:128, :], [B_sinB], [B_ST[NCH]], )
        sF, sB_ = sintmp[0:64, :], sintmp[64:128, :]
        for i in range(1, NCH):
            cf = i
            cbk = NCH - 1 - i
            tt("dve", sF, sF, Dt[0:64, :], ALU.mult, [B_sinF, B_lay], [B_sinF])
            tt("dve", sB_, sB_, Dt[64:128, :], ALU.mult, [B_sinB, B_lay], [B_sinB])
            tt("dve", sF, sF, ST[0:64, cf + 1, :], ALU.add, [B_sinF, B_ST[cf + 1]], [B_sinF])
            tt("dve", sB_, sB_, ST[64:128, cbk + 1, :], ALU.add, [B_sinB, B_ST[cbk + 1]], [B_sinB])
            cp("dve", ST[0:64, cf + 1, :], sF, [B_sinF], [B_ST[cf + 1]], )
            cp("dve", ST[64:128, cbk + 1, :], sB_, [B_sinB], [B_ST[cbk + 1]], )

    def small_attn(qT_ap, qTb, blocks, sink_l, mo, mob, gt, gtb, colbase):
        for g in range(2):
            op_, opb = PS["acc"].next()
            pend = []

            def pv(item):
                bi, vap, w_, wb_, bufs = item
                mm(op_[0:65, 0:256], vap[:, g * 65:(g + 1) * 65], w_[:], bi == 0, bi == len(blocks) - 1, [wb_] + bufs, [opb])

            for bi, (kap, vap, mask, bufs) in enumerate(blocks):
                sp_, spb = sps.next()
                mm(sp_[:, 0:256], kap, qT_ap[:, g, :, :].rearrange("p r t -> p (r t)"), True, True,
                   [qTb] + bufs, [spb])
                yield
                w_, wb_ = wp.next()
                act(w_[:], sp_[:, 0:256], AF.Exp, [spb], [wb_], scale=SCALE)
                yield
                if mask is not None:
                    tt("pool", w_[:].rearrange("p (r t) -> p r t", r=2), w_[:].rearrange("p (r t) -> p r t", r=2),
                       mask.unsqueeze(1).to_broadcast([128, 2, 128]), ALU.mult, [wb_, B_cb], [wb_])
                    yield
                pend.append((bi, vap, w_, wb_, bufs))
                if len(pend) > 1:
                    pv(pend.pop(0))
            for item in pend:
                pv(item)
            yield
            wo, wob = woT.next()
            cp("dve", wo[:], op_[0:65, 0:256], [opb], [wob])
            yield
            for r in range(2):
                h = g * 2 + r
                zp, zb = PS["z"].next()
                tr(zp[:, 0:65], wo[:, r * 128:(r + 1) * 128], identf[0:65, 0:65], [wob, B_cb], [zb])
                yield
                finish_head(zp, zb, g, h, sink_l, mo, mob, gt, gtb, colbase)
                yield

    def finish_head(zp, zb, g, h, sink_l, mo, mob, gt, gtb, colbase):
        rd_, rdb = rden.next()
        dcol = 0 if g == 0 else 64
        o0 = 1 if g == 0 else 0
        if sink_l:
            ts("dve", rd_[:, 0:1], zp[:, dcol:dcol + 1], esink[:, h:h + 1], None, ALU.add, None, [zb, B_lay], [rdb])
            P.op("dve", lambda e: e.reciprocal(out=rd_[:, 0:1], in_=rd_[:, 0:1]), [rdb], [rdb])
        else:
            P.op("dve", lambda e: e.reciprocal(out=rd_[:, 0:1], in_=zp[:, dcol:dcol + 1]), [zb], [rdb])
        stt(mo[:, colbase + h * 64:colbase + (h + 1) * 64], zp[:, o0:o0 + 64], rd_[:, 0:1], gt[:, colbase + h * 64:colbase + (h + 1) * 64],
            ALU.mult, ALU.mult, [zb, rdb, gtb], [mob])

    def silu_evac(dst, dstb, zp, zb):
        Ct, Cb = t2.next()
        act(Ct[:, 0:512], zp[:, 0:512], AF.Exp, [zb], [Cb], scale=-1.0)
        yield
        ts("dve", Ct[:, 0:512], Ct[:, 0:512], 1.0, None, ALU.add, None, [Cb], [Cb])
        yield
        P.op("dve", lambda e: e.reciprocal(out=Ct[:, 0:512], in_=Ct[:, 0:512]), [Cb], [Cb])
        yield
        yield
        tt("dve", dst, zp[:, 0:512], Ct[:, 0:512], ALU.mult, [zb, Cb], [dstb])

    def p2_front(l, t, xsrc, B_xsrc):
        is_ctx = t < CTXC
        c = t if is_ctx else t - CTXC
        j = 1 if is_ctx else 0
        if is_ctx:
            xap, xbuf = xc[:, c, :], B_xc[c]
            csb = None
        else:
            xt_, xbuf = xt.next()
            dma("sp", xt_[:], xsrc[c * 128:(c + 1) * 128, :], [B_xsrc], [xbuf])
            xap = xt_[:]
            cst, csbuf = cs.next()
            dma("sp", cst[:, 0, :], cos_in[c], [B_const], [csbuf])
            dma("sp", cst[:, 1, :], sin_in[c], [B_const], [csbuf], par=True)
            csb = (cst, csbuf)
        yl, ylb = yinl.next()
        dma("sp", yl[:], yin_d[t * 128:(t + 1) * 128, :], [B_yin], [ylb])
        ql, qlb = qfbl.next()
        dma("sp", ql[:], qfb_d[t], [B_qfb], [qlb])
        yield
        yield
        ht, hb = yield from norm_tile_g(l, xap, xbuf, j)
        ug, ugb = uvg.next()
        gt, gtb = gates.next()
        mo, mob = mixo.next()
        qr, qrb = kraw.next()
        zp, zb = inproj(ht, hb, W2, B_W2, 0, 512)
        yield
        yield
        act(ug[:], zp[:, 0:512], AF.Gelu, [zb], [ugb])
        yield
        zp, zb = inproj(ht, hb, W2, B_W2, 512, 512)
        yield
        yield
        yield from silu_evac(gt[:, 0:512], gtb, zp, zb)
        yield
        zp, zb = inproj(ht, hb, W2, B_W2, 1024, 512)
        yield
        yield
        yield from silu_evac(gt[:, 512:1024], gtb, zp, zb)
        yield
        zp, zb = inproj(ht, hb, W2, B_W2, 1536, 512)
        yield
        yield
        for blk in range(2):
            cp("dve", qr[:, blk * 256:(blk + 1) * 256].rearrange("p (r g d) -> p r g d", r=2, g=2),
               zp[:, blk * 256:(blk + 1) * 256].rearrange("p (g r d) -> p r g d", r=2, g=2), [zb], [qrb], )
        yield
        yield "RET"
        mp, mb = PS["z"].next()
        for h in range(4):
            mm(mp[:, h * 64:(h + 1) * 64], mixT[:, h * 128:(h + 1) * 128], ug[:, 256 + h * 64:256 + (h + 1) * 64], True, True, [B_mixT, ugb], [mb])
        Sx, BS, _, _ = ret_state(is_ctx)
        for h in range(4):
            mm(mp[:, 256 + h * 64:256 + (h + 1) * 64], ql[:, h * 128:(h + 1) * 128], Sx[:, c + 1, h * 64:(h + 1) * 64], True, True, [qlb, BS[c + 1]], [mb])
        yield
        yield
        ys, ysb = ysum.next()
        v3 = lambda ap: ap.rearrange("p (h d) -> p h d", h=4)
        tt("dve", v3(ys[:]), v3(mp[:, 0:256]), mbias[:, l * 4:(l + 1) * 4].unsqueeze(2).to_broadcast([128, 4, 64]), ALU.add, [mb, B_cb], [ysb])
        tt("dve", ys[:], ys[:], ug[:, 0:256], ALU.mult, [ysb, ugb], [ysb])
        yield
        tt("dve", mo[:, 0:256], ys[:], gt[:, 0:256], ALU.mult, [ysb, gtb], [mob])
        ys, ysb = ysum.next()
        tt("dve", ys[:], mp[:, 256:512], yl[:], ALU.add, [mb, ylb], [ysb])
        yield
        Bt, Bb = t1.next()
        s4, s4b = st4.next()
        tt("dve", Bt[:, 0:256], ys[:], ys[:], ALU.mult, [ysb], [Bb])
        yield
        P.op("dve", lambda e: e.tensor_reduce(out=s4[:, 0:4], in_=v3(Bt[:, 0:256]), axis=AX.X, op=ALU.add), [Bb], [s4b])
        yield
        yield from rsqrt_mean_g(s4[:, 0:4], s4[:, 0:4], 64, [s4b], [s4b])
        tt("dve", v3(ys[:]), v3(ys[:]), s4[:, 0:4].unsqueeze(2).to_broadcast([128, 4, 64]), ALU.mult, [ysb, s4b], [ysb])
        tt("dve", ys[:], ys[:], rnorm[:, 0:256], ALU.mult, [ysb, B_rq], [ysb])
        yield
        tt("dve", mo[:, 256:512], ys[:], gt[:, 256:512], ALU.mult, [ysb, gtb], [mob])
        yield
        qn, qnb = yield from qk_norm_rope(qr, qrb, 8, qkn[:, 0:128], csb, not is_ctx)
        yield
        tp, tb = tps.next()
        for a in range(4):
            tr(tp[:, a * 128:(a + 1) * 128], qn[:, a * 128:(a + 1) * 128], identb[:], [qnb, B_cb], [tb])
        yield
        yield
        return dict(t=t, c=c, is_ctx=is_ctx, gt=gt, gtb=gtb, mo=mo, mob=mob, tp=tp, tb=tb)

    def out_proj(l, st, xsrc, B_xsrc, xdst, B_xdst):
        t, c, is_ctx, mo, mob = st["t"], st["c"], st["is_ctx"], st["mo"], st["mob"]
        j = 1 if is_ctx else 0
        if is_ctx:
            xap, xbuf = xc[:, c, :], B_xc[c]
        else:
            xr_, xbuf = xr.next()
            dma("sp", xr_[:], xsrc[c * 128:(c + 1) * 128, :], [B_xsrc], [xbuf])
            xap = xr_[:]
        tp, tb = tps.next()
        for k in range(8):
            tr(tp[:, k * 128:(k + 1) * 128], mo[:, k * 128:(k + 1) * 128], identb[:], [mob, B_cb], [tb])
        yield
        yield
        mt, mtb = mixTt.next()
        cp("dve", mt[:].rearrange("p k t -> p (k t)"), tp[:, 0:1024], [tb], [mtb])
        yield
        yield
        ot, otb = otmp.next()
        for half in range(2):
            zp, zb = PS["z"].next()
            for k in range(8):
                mm(zp[:, 0:512], mt[:, k, :], WA[:, k, half * 512:(half + 1) * 512], k == 0, k == 7, [mtb, B_WA], [zb])
            yield
            yield
            yield
            tt("dve", ot[:, half * 512:(half + 1) * 512], zp[:, 0:512], gateB[:, j, half * 512:(half + 1) * 512], ALU.mult, [zb, B_mod], [otb], )
            yield
        if is_ctx:
            tt("pool", xc[:, c, :], xc[:, c, :], ot[:], ALU.add, [xbuf, otb], [xbuf])
        else:
            tt("pool", ot[:], ot[:], xap, ALU.add, [otb, xbuf], [otb])
            yield
            yield
            dma("pool", xdst[c * 128:(c + 1) * 128, :], ot[:], [otb], [B_xdst], par=True)
        yield

    def q_evac(st, q_, qb, lo, cols=None):
        for gg in range(2):
            dst = q_[gg * 64:(gg + 1) * 64, gg, :, :] if cols is None else q_[gg * 64:(gg + 1) * 64, gg, :, cols[0]:cols[1]]
            cp("dve", dst, st["tp"][gg * 64:(gg + 1) * 64, lo:lo + 256].rearrange("p (r t) -> p r t", r=2), [st["tb"]], [qb], )

    def p2_ctx(l):
        for c in range(CTXC):
            st = yield from p2_front(l, c, None, None)
            qT_, qTb = sqT.next()
            qT2, qT2b = sqT.next()
            q_evac(st, qT_, qTb, 0)
            q_evac(st, qT2, qT2b, 256)
            yield
            gblocks = [(KT[:, cc * 128:(cc + 1) * 128], V[:, cc, :], None, [B_KT, B_V]) for cc in range(CTXC)]
            yield from small_attn(qT_, qTb, gblocks, False, st["mo"], st["mob"], st["gt"], st["gtb"], 512)
            sblocks = [(skTc[:, cc * 128:(cc + 1) * 128], sVc[:, cc, :], None, [B_skTc, B_sVc]) for cc in range(CTXC)]
            yield from small_attn(qT2, qT2b, sblocks, True, st["mo"], st["mob"], st["gt"], st["gtb"], 768)
            yield from out_proj(l, st, None, None, None, None)

    def swa_tile(l, st, sq_, sqb):
        c = st["c"]
        wk_, wkb = wk.next()
        wv_, wvb = wv.next()
        blocks = []
        bb = [wkb, wvb]
        if c == 0:
            dma("sp", wk_[:, 0:2, :], sk_d[0:2].rearrange("c p k -> p c k"), [B_skd], [wkb])
            dma("sp", wv_[:, 0:2, :], sv_d[0:2].rearrange("c p k -> p c k"), [B_svd], [wvb])
            dma("sp", wk_[:, 2:6, :], bnd_all[:, 128:256].rearrange("(r p) k -> p r k", p=128), [B_bndall], [wkb], par=True)
            dma("sp", wv_[:, 2:6, :], bnd_all[:, 386:516].rearrange("(r p) k -> p r k", p=128), [B_bndall], [wvb], par=True)
            blocks.append((wk_[:, 0, :], wv_[:, 0, :], None, bb))
            blocks.append((wk_[:, 1, :], wv_[:, 1, :], cmask[:, 128:256], bb))
            for r in range(R):
                blocks.append((wk_[:, 2 + r, :], wv_[:, 2 + r, :], hmask[:, r * 128:(r + 1) * 128], bb))
        elif c == NCH - 1:
            dma("sp", wk_[:, 0:2, :], sk_d[c - 1:c + 1].rearrange("c p k -> p c k"), [B_skd], [wkb])
            dma("sp", wv_[:, 0:2, :], sv_d[c - 1:c + 1].rearrange("c p k -> p c k"), [B_svd], [wvb])
            dma("sp", wk_[:, 2:6, :], bnd_all[:, 0:128].rearrange("(r p) k -> p r k", p=128), [B_bndall], [wkb], par=True)
            dma("sp", wv_[:, 2:6, :], bnd_all[:, 256:386].rearrange("(r p) k -> p r k", p=128), [B_bndall], [wvb], par=True)
            blocks.append((wk_[:, 0, :], wv_[:, 0, :], cmask[:, 0:128], bb))
            blocks.append((wk_[:, 1, :], wv_[:, 1, :], None, bb))
            for r in range(R):
                blocks.append((wk_[:, 2 + r, :], wv_[:, 2 + r, :], hmask[:, (4 + r) * 128:(5 + r) * 128], bb))
        else:
            dma("sp", wk_[:, 0:3, :], sk_d[c - 1:c + 2].rearrange("c p k -> p c k"), [B_skd], [wkb])
            dma("sp", wv_[:, 0:3, :], sv_d[c - 1:c + 2].rearrange("c p k -> p c k"), [B_svd], [wvb])
            blocks.append((wk_[:, 0, :], wv_[:, 0, :], cmask[:, 0:128], bb))
            blocks.append((wk_[:, 1, :], wv_[:, 1, :], None, bb))
            blocks.append((wk_[:, 2, :], wv_[:, 2, :], cmask[:, 128:256], bb))
        for cc in range(CTXC):
            blocks.append((skTc[:, cc * 128:(cc + 1) * 128], sVc[:, cc, :], None, [B_skTc, B_sVc]))
        yield
        yield from small_attn(sq_, sqb, blocks, True, st["mo"], st["mob"], st["gt"], st["gtb"], 768)

    def front_group(l, gi, xsrc, B_xsrc, G):
        gq_, gqb = gqT.next()
        G["gq"] = (gq_, gqb)
        G["sts"] = []
        for ti in range(QG):
            st = yield from p2_front(l, CTXC + gi * QG + ti, xsrc, B_xsrc)
            sq_, sqb = sqT.next()
            q_evac(st, gq_, gqb, 0, (ti * 128, (ti + 1) * 128))
            q_evac(st, sq_, sqb, 256)
            yield
            yield from swa_tile(l, st, sq_, sqb)
            G["sts"].append(st)

    def sweep_group(G):
        gq_, gqb = G["gq"]
        accs = [ops_.next() for _ in range(2)]
        pend = []
        NW = 2 * GQ
        pieces = [(None, None)] + [(r, c0) for r in range(R) for c0 in range(0, NCH, PCS)]
        nblk_total = CTXC + R * NCH
        seen = 0

        def pv(item):
            first, last, g0, vap, vb, p0, pb0 = item
            mm(accs[g0][0][0:65, 0:NW], vap[:, g0 * 65:(g0 + 1) * 65], p0[:, 0:NW], first, last, [pb0] + vb, [accs[g0][1]])

        for (r, c0) in pieces:
            if r is None:
                nb = CTXC
                kget = lambda i: KT[:, i * 128:(i + 1) * 128]
                vget = lambda i: V[:, i, :]
                kbufs, vbufs = [B_KT], [B_V]
            else:
                nb = PCS
                kt_, ktb_ = ksl.next()
                vt_, vtb_ = vsl.next()
                gp_, of_ = c0 // GP, c0 % GP
                dma("sp", kt_[:], gk_all[gp_][r * 128:(r + 1) * 128, of_ * 128:(of_ + PCS) * 128], [B_gkall[gp_]], [ktb_])
                dma("sp", vt_[:], gv_all[gp_][r * 128:(r + 1) * 128, of_ * 130:(of_ + PCS) * 130].rearrange("p (c d) -> p c d", d=130), [B_gvall[gp_]], [vtb_])
                kget = lambda i, kt_=kt_: kt_[:, i * 128:(i + 1) * 128]
                vget = lambda i, vt_=vt_: vt_[:, i, :]
                kbufs, vbufs = [ktb_], [vtb_]
            for i in range(nb):
                first, last = seen == 0, seen == nblk_total - 1
                seen += 1
                for g in range(2):
                    sp_, spb = sps.next()
                    mm(sp_[:, 0:NW], kget(i), gq_[:, g, :, :].rearrange("p r t -> p (r t)"), True, True,
                       kbufs + [gqb], [spb])
                    p_, pb = pT.next()
                    act(p_[:, 0:NW], sp_[:, 0:NW], AF.Exp, [spb], [pb], scale=SCALE)
                    pend.append((first, last, g, vget(i), vbufs, p_, pb))
                    if len(pend) > 2:
                        pv(pend.pop(0))
                yield
        for item in pend:
            pv(item)
        G["oT"] = []
        for g in range(2):
            o_, ob_ = oT.next()
            cp("dve", o_[:, 0:NW], accs[g][0][0:65, 0:NW], [accs[g][1]], [ob_])
            G["oT"].append((o_, ob_))

    def tail_group(l, G, xsrc, B_xsrc, xdst, B_xdst):
        sts = G["sts"]
        for g in range(2):
            o_, ob_ = G["oT"][g]
            for r in range(2):
                h = 2 * g + r
                for ti in range(QG):
                    zp, zb = PS["z"].next()
                    tr(zp[:, 0:65], o_[:, r * GQ + ti * 128:r * GQ + (ti + 1) * 128], identf[0:65, 0:65], [ob_, B_cb], [zb])
                    yield
                    yield
                    finish_head(zp, zb, g, h, False, sts[ti]["mo"], sts[ti]["mob"], sts[ti]["gt"], sts[ti]["gtb"], 512)
                    yield
        for st in sts:
            yield from out_proj(l, st, xsrc, B_xsrc, xdst, B_xdst)

    def run(gen):
        try:
            while True:
                next(gen)
        except StopIteration as e:
            return e.value

    def gchain(*gens):
        for g_ in gens:
            yield from g_

    def pass2(l, xsrc, B_xsrc, xdst, B_xdst):
        PS["z"], PS["acc"] = zps1, swacc
        if l < depth - 1:
            run(p2_ctx(l))
        Gs = [dict() for _ in range(NG)]
        g0 = front_group(l, 0, xsrc, B_xsrc, Gs[0])
        while next(g0) != "RET":
            pass
        exchange_b(l)
        run(g0)
        for k in range(NG):
            sides = []
            if k > 0:
                sides.append(tail_group(l, Gs[k - 1], xsrc, B_xsrc, xdst, B_xdst))
            if k + 1 < NG:
                sides.append(front_group(l, k + 1, xsrc, B_xsrc, Gs[k + 1]))
            side = gchain(*sides)
            alive = True
            for _ in sweep_group(Gs[k]):
                for _r in range(SIDE_RATE):
                    if alive:
                        try:
                            next(side)
                        except StopIteration:
                            alive = False
            if alive:
                run(side)
        run(tail_group(l, Gs[NG - 1], xsrc, B_xsrc, xdst, B_xdst))
        PS["z"], PS["acc"] = zpsP1, ops_

    chain = [(x_in, B_xin)]
    inter = [(xsA, B_xsA), (xsB, B_xsB)]
    for l in range(depth):
        chain.append((y_out, B_y) if l == depth - 1 else inter[l % 2])
    for l in range(depth):
        xsrc, B_xsrc = chain[l]
        xdst, B_xdst = chain[l + 1]
        load_w1(l)
        layer_consts(l)
        if stop == "consts":
            break
        mod_compute(l)
        if stop == "mod":
            break
        load_w2(l)
        if stop == "w2":
            break
        gens = [p1_tile(l, t, xsrc, B_xsrc) for t in range(CTXC + NCH)]
        active = []
        nxt = 0
        while nxt < len(gens) or active:
            if nxt < len(gens) and (not active or (len(active) < 2 and active[-1][1] >= P1LAG)):
                active.append([gens[nxt], 0, nxt - CTXC])
                nxt += 1
            for a_ in list(active):
                try:
                    next(a_[0])
                    a_[1] += 1
                except StopIteration:
                    active.remove(a_)
                    if a_[2] >= 0 and a_[2] % GP == GP - 1:
                        gather_piece(a_[2] // GP)
        if stop == "p1":
            break
        exchange(l)
        if stop == "exch":
            break
        load_wo(l)
        pass2(l, xsrc, B_xsrc, xdst, B_xdst)

    P.finalize()
    with nc.Block() as block:
        @block.sync
        def _(e):
            P.emit("sp", e)

        @block.scalar
        def _(e):
            P.emit("act", e)

        @block.vector
        def _(e):
            P.emit("dve", e)

        @block.tensor
        def _(e):
            P.emit("pe", e)

        @block.gpsimd
        def _(e):
            P.emit("pool", e)
            if B_y.sem is not None:
                P.final_wait(e, [B_y])
            else:
                dummy = P.es.enter_context(nc.semaphore("dummy"))
                e.dma_start(out=y_out[0:128, :], in_=xc[:, 0, :]).then_inc(dummy, 16)
                e.wait_ge(dummy, 16)
    es.close()
    return nc


def host_inputs(inputs, NCH, depth=DEPTH, n_cores=8):
    f = np.float32
    x = np.asarray(inputs["x"], f)
    NT = NCH * 128
    bf = ml_dtypes.bfloat16
    c = np.asarray(inputs["c"], f)
    ctx = np.asarray(inputs["ctx"], f)
    c_ctx = np.asarray(inputs["c_ctx"], f)
    ng = np.asarray(inputs["norm_gain"], f)
    b_mod = np.asarray(inputs["b_mod"], f)
    common = {
        "gainT": np.ascontiguousarray(ng.reshape(depth, 8, 128).transpose(2, 0, 1).reshape(128, depth * 8)),
        "w_mod": np.ascontiguousarray(np.asarray(inputs["w_mod"], f)),
        "bmodT": np.ascontiguousarray(b_mod[:, 0:2048].reshape(depth, 16, 128).transpose(2, 0, 1).reshape(128, depth * 16)),
        "bgate": np.ascontiguousarray(b_mod[:, 2048:3072].reshape(1, depth * D)),
        "w_in": np.ascontiguousarray(np.asarray(inputs["w_in"], f)),
        "w_out": np.ascontiguousarray(np.asarray(inputs["w_out"], f)),
        "mixT": np.ascontiguousarray(np.asarray(inputs["mlp_mix"], f).transpose(0, 3, 1, 2).reshape(depth, 128, 512)),
        "mbias": np.ascontiguousarray(np.asarray(inputs["mlp_bias"], f).transpose(2, 0, 1).reshape(128, depth * 4)),
        "rdec": np.ascontiguousarray(np.stack([np.asarray(inputs["ret_decay_fwd"], f), np.asarray(inputs["ret_decay_bwd"], f)], 1).reshape(1, depth * 8)),
        "rnorm": np.ascontiguousarray(np.asarray(inputs["ret_norm"], f).reshape(1, depth * 256)),
        "qkn": np.ascontiguousarray(np.stack([np.asarray(inputs[k], f) for k in ("attn_q_norm", "swa_q_norm", "attn_k_norm", "swa_k_norm")], 1).reshape(1, depth * 256)),
        "sink": np.ascontiguousarray(np.asarray(inputs["swa_sink"], f).reshape(1, depth * 4)),
    }
    jj = np.arange(128, dtype=f)[:, None]
    ii = np.arange(128, dtype=f)[None, :]
    common["identb"] = np.eye(128, dtype=f).astype(bf)
    common["identf"] = np.eye(128, dtype=f)
    common["rpn"] = np.concatenate([np.maximum(ii - jj, 0), np.maximum(jj - ii, 0)], 1).astype(f)
    p = np.arange(128, dtype=f)
    common["pos"] = np.stack([127 - p, p, p + 1, 128 - p], 1).astype(f)
    mprev = (jj >= ii).astype(f)
    mnext = (jj <= ii).astype(f)
    common["cmask"] = np.concatenate([mprev, mnext], 1).astype(bf)
    half = 32
    inv_freq = (1.0 / (10000.0 ** (np.arange(0, half, 2, dtype=f) / f(half)))).astype(f)
    sgn = np.concatenate([-np.ones(16, f), np.ones(16, f), -np.ones(16, f), np.ones(16, f)])
    maps = []
    for core in range(n_cores):
        b, seg = core // R, core % R
        m = dict(common)
        m["x_in"] = np.ascontiguousarray(x[b, seg * NT:(seg + 1) * NT, :])
        m["ctx_in"] = np.ascontiguousarray(ctx[b])
        cT = np.zeros((128, 16), f)
        cT[:, 0::2] = c[b].reshape(8, 128).T
        cT[:, 1::2] = c_ctx.reshape(8, 128).T
        m["cT"] = cT
        tpos = seg * NT + np.arange(NT)
        row = (tpos // 64).astype(f)
        col = (tpos % 64).astype(f)
        ang_r = row[:, None] * inv_freq[None, :]
        ang_c = col[:, None] * inv_freq[None, :]
        ang = np.concatenate([ang_r, ang_r, ang_c, ang_c], -1).astype(f)
        m["cos"] = np.cos(ang).astype(f).reshape(NCH, 128, 64)
        m["sin"] = (np.sin(ang).astype(f) * sgn[None, :]).reshape(NCH, 128, 64)
        et = np.full((128, 5), BIGE, f)
        for r in range(R):
            if r < seg:
                et[0:64, r] = seg - 1 - r
            if r > seg:
                et[64:128, r] = r - seg - 1
        et[0:64, 4] = seg
        et[64:128, 4] = R - 1 - seg
        m["etab"] = et
        hm = np.zeros((128, 8, 128), f)
        if seg - 1 >= 0:
            hm[:, seg - 1, :] = mprev
        if seg + 1 < R:
            hm[:, 4 + seg + 1, :] = mnext
        m["hmask"] = hm.reshape(128, 1024).astype(bf)
        maps.append(m)
    return maps


_NC_CACHE = {}


def kernel(**inputs):
    x = np.asarray(inputs["x"])
    B, L, _ = x.shape
    NCH = L // R // 128
    depth = np.asarray(inputs["w_in"]).shape[0]
    key = (NCH, depth)
    if key not in _NC_CACHE:
        _NC_CACHE[key] = build(NCH, depth)
    nc = _NC_CACHE[key]
    maps = host_inputs(inputs, NCH, depth)
    res = run_bass_kernel_spmd(nc, maps, core_ids=list(range(8)))
    NT = NCH * 128
    out = np.zeros((B, L, D), np.float32)
    for core in range(8):
        b, seg = core // R, core % R
        out[b, seg * NT:(seg + 1) * NT, :] = res.results[core]["y"]
    return out
```

```python
import math
from contextlib import ExitStack
import numpy as np
import ml_dtypes
import concourse.bass as bass
import concourse.mybir as mybir
from concourse.bass_utils import run_bass_kernel_spmd

F32 = mybir.dt.float32
BF16 = mybir.dt.bfloat16
AF = mybir.ActivationFunctionType
ALU = mybir.AluOpType
AX = mybir.AxisListType

D = 1024
DEPTH = 4
CTXC = 2
R = 4
EPS = 1e-6
SCALE = 0.125
BIGE = 1.0e4


class Buf:
    def __init__(self, name):
        self.name = name
        self.last_w = []
        self.readers = []
        self.sem = None
        self.cnt = 0


class Op:
    __slots__ = ("eng", "fn", "deps", "kind", "sig", "val", "sem", "owner")

    def __init__(self, eng, fn, kind):
        self.eng, self.fn, self.kind = eng, fn, kind
        self.deps = []
        self.sig = False
        self.val = 0
        self.sem = None
        self.owner = None


class Prog:
    def __init__(self, nc, es):
        self.nc, self.es = nc, es
        self.ops = {k: [] for k in ("pe", "act", "dve", "pool", "sp")}
        self.esem = {k: es.enter_context(nc.semaphore("e_" + k)) for k in ("pe", "act", "dve", "pool")}
        self.nsem = 4

    def op(self, eng, fn, reads=(), writes=(), kind="c", par=False):
        o = Op(eng, fn, kind)
        deps = []
        for b in reads:
            deps += b.last_w
        for b in writes:
            deps += b.readers
            if not par:
                deps += b.last_w
        seen = set()
        for d in deps:
            if id(d) in seen or d is o:
                continue
            seen.add(id(d))
            if d.kind == "c" and d.eng == "pe" and eng == "pe" and kind == "c":
                continue
            d.sig = True
            o.deps.append((d, d.owner.cnt if d.owner is not None else None))
        for b in reads:
            b.readers.append(o)
        for b in writes:
            if par:
                b.last_w = b.last_w + [o]
            else:
                b.last_w = [o]
            b.readers = []
        if kind in ("d", "cc"):
            b = writes[0]
            if b.sem is None:
                b.sem = self.es.enter_context(self.nc.semaphore("s_" + b.name))
                self.nsem += 1
            b.cnt += 16 if kind == "d" else 1
            o.sem, o.val, o.owner = b.sem, b.cnt, b
        self.ops[eng].append(o)
        return o

    def finalize(self):
        for eng in ("pe", "act", "dve", "pool"):
            c = 0
            for o in self.ops[eng]:
                if o.kind == "c" and o.sig:
                    c += 1
                    o.val = c
                    o.sem = self.esem[eng]

    def emit(self, eng, e):
        waited = {}
        for o in self.ops[eng]:
            need = {}
            for d, fixed in o.deps:
                k = id(d.sem)
                v = d.val if fixed is None else fixed
                if v > waited.get(k, 0) and (k not in need or v > need[k][1]):
                    need[k] = (d.sem, v)
            for k, (sem_, val_) in need.items():
                e.wait_ge(sem_, val_)
                waited[k] = val_
            ins = o.fn(e)
            if o.kind == "d":
                ins.then_inc(o.sem, 16)
            elif o.kind == "cc":
                ins.then_inc(o.sem, 1)
            elif o.sig:
                ins.then_inc(o.sem, 1)

    def final_wait(self, e, bufs):
        for b in bufs:
            e.wait_ge(b.sem, b.cnt)


class Rot:
    def __init__(self, tiles, name, bufs=None):
        self.tiles = tiles
        self.bufs = bufs if bufs is not None else [Buf("%s%d" % (name, i)) for i in range(len(tiles))]
        self.i = -1

    def next(self):
        self.i = (self.i + 1) % len(self.tiles)
        return self.tiles[self.i], self.bufs[self.i]


def build(NCH, depth=DEPTH, QG=2, stop=None, SIDE_RATE=2, P1LAG=9):
    NT = NCH * 128
    NTT = NT + CTXC * 128
    KB = CTXC + R * NCH
    KTOT = KB * 128
    NG = NCH // QG
    GQ = QG * 128
    PCS = min(NCH, 4)
    GP = min(NCH, 8)
    NGP = NCH // GP
    nc = bass.Bass("TRN2", target_bir_lowering=False)
    es = ExitStack()
    P = Prog(nc, es)

    def din(name, shape, dt=F32):
        return nc.dram_tensor(name, shape, dt, kind="ExternalInput").ap()

    def dint(name, shape, dt):
        return nc.dram_tensor(name, shape, dt, kind="Internal").ap()

    x_in = din("x_in", [NT, D])
    ctx_in = din("ctx_in", [CTXC * 128, D])
    cT_in = din("cT", [128, 16])
    gainT_in = din("gainT", [128, depth * 8])
    w_mod = din("w_mod", [depth, D, 3 * D])
    bmodT_in = din("bmodT", [128, depth * 16])
    bgate_in = din("bgate", [1, depth * D])
    w_in = din("w_in", [depth, D, 3328])
    w_out = din("w_out", [depth, D, D])
    mixT_in = din("mixT", [depth, 128, 512])
    mbias_in = din("mbias", [128, depth * 4])
    rdec_in = din("rdec", [1, depth * 8])
    rnorm_in = din("rnorm", [1, depth * 256])
    qkn_in = din("qkn", [1, depth * 256])
    sink_in = din("sink", [1, depth * 4])
    cos_in = din("cos", [NCH, 128, 64])
    sin_in = din("sin", [NCH, 128, 64])
    etab_in = din("etab", [128, 5])
    hmask_in = din("hmask", [128, 8 * 128], BF16)
    cmask_in = din("cmask", [128, 2 * 128], BF16)
    identb_in = din("identb", [128, 128], BF16)
    identf_in = din("identf", [128, 128])
    rpn_in = din("rpn", [128, 256])
    pos_in = din("pos", [128, 4])
    y_out = nc.dram_tensor("y", [NT, D], F32, kind="ExternalOutput").ap()

    xsA = dint("xsA", [NT, D], F32)
    xsB = dint("xsB", [NT, D], F32)
    yin_d = dint("yin_d", [NTT, 256], F32)
    qfb_d = dint("qfb_d", [NCH + CTXC, 128, 512], BF16)
    sk_d = dint("sk_d", [NCH, 128, 128], BF16)
    sv_d = dint("sv_d", [NCH, 128, 130], BF16)
    gk_x = [dint("gk_x%d" % i, [128, GP * 128], BF16) for i in range(NGP)]
    gk_all = [dint("gk_all%d" % i, [R * 128, GP * 128], BF16) for i in range(NGP)]
    gv_x = [dint("gv_x%d" % i, [128, GP * 130], BF16) for i in range(NGP)]
    gv_all = [dint("gv_all%d" % i, [R * 128, GP * 130], BF16) for i in range(NGP)]
    bnd_x = dint("bnd_x", [128, 516], BF16)
    bnd_all = dint("bnd_all", [R * 128, 516], BF16)
    agg_x = dint("agg_x", [128, 256], F32)
    agg_all = dint("agg_all", [R * 128, 256], F32)
    B_xin, B_xsA, B_xsB, B_y = Buf("xin"), Buf("xsA"), Buf("xsB"), Buf("y")
    B_yin, B_qfb, B_skd, B_svd = Buf("yin"), Buf("qfb"), Buf("skd"), Buf("svd")
    B_gkx = [Buf("gkx%d" % i) for i in range(NGP)]
    B_gkall = [Buf("gkall%d" % i) for i in range(NGP)]
    B_gvx = [Buf("gvx%d" % i) for i in range(NGP)]
    B_gvall = [Buf("gvall%d" % i) for i in range(NGP)]
    B_bndx, B_bndall, B_aggx, B_aggall = Buf("bndx"), Buf("bndall"), Buf("aggx"), Buf("aggall")
    B_const = Buf("constin")

    def sb(name, shape, dt=F32):
        return es.enter_context(nc.sbuf_tensor(name, shape, dt))

    def ps(name, shape, dt=F32):
        return es.enter_context(nc.psum_tensor(name, shape, dt))

    KT = sb("KTc", [128, CTXC * 128], BF16);     B_KT = Buf("KT")
    V = sb("Vc", [128, CTXC, 130], BF16);        B_V = Buf("V")
    ksl = Rot([sb("ksl%d" % i, [128, PCS * 128], BF16) for i in range(2)], "ksl")
    vsl = Rot([sb("vsl%d" % i, [128, PCS, 130], BF16) for i in range(2)], "vsl")
    skTc = sb("skTc", [128, CTXC * 128], BF16);  B_skTc = Buf("skTc")
    sVc = sb("sVc", [128, CTXC, 130], BF16);     B_sVc = Buf("sVc")
    ST = sb("ST", [128, NCH + 2, 256], BF16)
    B_ST = [Buf("ST%d" % i) for i in range(NCH + 2)]
    STc = sb("STc", [128, CTXC + 2, 256], BF16)
    B_STc = [Buf("STc%d" % i) for i in range(CTXC + 2)]
    WA = sb("WA", [128, 8, 1280], BF16);         B_WA = Buf("WA")
    W2 = sb("W2s", [128, 8, 2048], BF16);         B_W2 = Buf("W2")
    mixT = sb("mixTs", [128, 512], BF16);        B_mixT = Buf("mixT")
    xc = sb("xc", [128, CTXC, D], F32)
    B_xc = [Buf("xc%d" % i) for i in range(CTXC)]
    identb = sb("identb_s", [128, 128], BF16)
    identf = sb("identf_s", [128, 128], F32)
    rpn = sb("rpn_s", [128, 256], F32)
    pos = sb("pos_s", [128, 4], F32)
    etab = sb("etab_s", [128, 5], F32)
    hmask = sb("hmask_s", [128, 8 * 128], BF16)
    cmask = sb("cmask_s", [128, 256], BF16)
    cT = sb("cT_s", [128, 16], F32)
    gainT = sb("gainT_s", [128, depth * 8], F32)
    bmodT = sb("bmodT_s", [128, depth * 16], F32)
    bgate = sb("bgate_s", [1, D], F32);  B_bg = Buf("bg")
    grow = sb("grow", [1, 512], F32);    B_grow = Buf("grow")
    mbias = sb("mbias_s", [128, depth * 4], F32)
    rdec = sb("rdec_s", [128, depth * 8], F32)
    rdsel = sb("rdsel_s", [128, depth * 4], F32)
    rnorm = sb("rnorm_s", [128, 256], F32);  B_rq = Buf("rqn")
    qkn = sb("qkn_s", [128, 256], F32)
    sink = sb("sink_s", [128, depth * 4], F32)
    ones1 = sb("ones1", [1, 128], F32)
    B_cb = Buf("constsb")
    Gm = sb("Gm", [128, 2, 8], F32)
    Sm = sb("Sm", [128, 2, 8], F32)
    gateB = sb("gateB", [128, 2, D], F32)
    B_mod = Buf("mod")
    lg = sb("lg", [128, 8], F32)
    lgsel = sb("lgsel", [128, 4], F32)
    kd = sb("kd", [128, 8], F32)
    qd = sb("qd", [128, 8], F32)
    DecT = sb("DecT", [128, 512], F32)
    Dt = sb("Dt", [128, 256], F32)
    Pw = sb("Pw", [128, 256], F32)
    Aagg = sb("Aagg", [128, 256], F32)
    Actx = sb("Actx", [128, 256], F32)
    coef = sb("coef", [128, 5, 4], F32)
    esink = sb("esink", [128, 4], F32)
    B_lay = Buf("laysmall")
    B_Aagg, B_Actx, B_Pw = Buf("Aagg"), Buf("Actx"), Buf("Pw")
    aggs = sb("aggs", [128, R, 256], F32);    B_aggs = Buf("aggs")
    sintmp = sb("sintmp", [128, 256], F32);   B_sinF = Buf("sinF");  B_sinB = Buf("sinB")
    xt = Rot([sb("xt%d" % i, [128, D], F32) for i in range(2)], "xt")
    xr = xt
    st4 = Rot([sb("st4_%d" % i, [128, 16], F32) for i in range(4)], "st4")
    xn = Rot([sb("xn%d" % i, [128, D], BF16) for i in range(2)], "xn")
    hT = Rot([sb("hT%d" % i, [128, 8, 128], BF16) for i in range(2)], "hT")
    cs = Rot([sb("cs%d" % i, [128, 2, 64], F32) for i in range(2)], "cs")
    rqkv = Rot([sb("rqkv%d" % i, [128, 768], BF16) for i in range(2)], "rqkv")
    fb = Rot([sb("fb%d" % i, [128, 2, 4, 2, 64], BF16) for i in range(2)], "fb")
    fbT = Rot([sb("fbT%d" % i, [128, 2, 4, 128], BF16) for i in range(1)], "fbT")
    rT = Rot([sb("rT%d" % i, [128, 3, 2, 128], BF16) for i in range(1)], "rT")
    pint = Rot([sb("pint%d" % i, [128, 512], BF16) for i in range(1)], "pint")
    yint = Rot([sb("yint%d" % i, [128, 256], F32) for i in range(1)], "yint")
    kraw = Rot([sb("kraw%d" % i, [128, 512], F32) for i in range(2)], "kraw")
    t1 = Rot([sb("t1_%d" % i, [128, 512], F32) for i in range(1)], "t1")
    t2 = Rot([sb("t2_%d" % i, [128, 512], F32) for i in range(1)], "t2")
    knb = Rot([sb("knb%d" % i, [128, 512], BF16) for i in range(2)], "knb")
    kTs = Rot([sb("kTs%d" % i, [128, 256], BF16) for i in range(2)], "kTs")
    vsb = Rot([sb("vsb%d" % i, [128, 260], BF16) for i in range(2)], "vsb")
    NSL = 2 * QG
    gates = Rot([sb("gates%d" % i, [128, D], BF16) for i in range(NSL)], "gates")
    mixo = Rot([sb("mixo%d" % i, [128, D], BF16) for i in range(NSL)], "mixo")
    uvg = Rot([sb("uvg%d" % i, [128, 512], BF16) for i in range(1)], "uvg")
    gqT = Rot([sb("gqT%d" % i, [128, 2, 2, GQ], BF16) for i in range(2)], "gqT")
    sqT = Rot([sb("sqT%d" % i, [128, 2, 2, 128], BF16) for i in range(2)], "sqT")
    pT = Rot([sb("pT%d" % i, [128, 512], BF16) for i in range(3)], "pT")
    oT = Rot([sb("oT%d" % i, [65, 512], F32) for i in range(2)], "oT")
    rden = Rot([sb("rden%d" % i, [128, 4], F32) for i in range(4)], "rden")
    mixTt = Rot([sb("mixTt%d" % i, [128, 8, 128], BF16) for i in range(1)], "mixTt")
    otmp = Rot([sb("otmp%d" % i, [128, D], F32) for i in range(1)], "otmp")
    wmst = Rot([t_[:].rearrange("p (k c) -> p k c", k=8) for t_ in (otmp.tiles[0], xt.tiles[0], xt.tiles[1])], "wmst",
               bufs=[otmp.bufs[0], xt.bufs[0], xt.bufs[1]])
    yinl = Rot([sb("yinl%d" % i, [128, 256], F32) for i in range(1)], "yinl")
    qfbl = Rot([sb("qfbl%d" % i, [128, 512], BF16) for i in range(1)], "qfbl")
    ysum = Rot([sb("ysum%d" % i, [128, 256], F32) for i in range(2)], "ysum")
    wk = Rot([sb("wk%d" % i, [128, 6, 128], BF16) for i in range(1)], "wk")
    wv = Rot([sb("wv%d" % i, [128, 6, 130], BF16) for i in range(1)], "wv")
    wp = Rot([sb("wp%d" % i, [128, 256], BF16) for i in range(3)], "wp")
    woT = Rot([sb("woT%d" % i, [65, 256], F32) for i in range(1)], "woT")
    zps = Rot([ps("zps%d" % i, [128, 512]) for i in range(2)], "zps")
    tps = Rot([ps("tps%d" % i, [128, 1024], BF16) for i in range(1)], "tps")
    sps = Rot([ps("sps%d" % i, [128, 512]) for i in range(3)], "sps")
    ops_ = Rot([ps("ops%d" % i, [128, 512]) for i in range(2)], "ops")
    zps1 = Rot([zps.tiles[0]], "zps1", bufs=[zps.bufs[0]])
    swacc = Rot([zps.tiles[1]], "swacc", bufs=[zps.bufs[1]])
    zpsP1 = Rot(zps.tiles + ops_.tiles, "zpsP1", bufs=zps.bufs + ops_.bufs)
    PS = {"z": zpsP1, "acc": ops_}

    def dma(q, out, in_, reads, writes, par=False, **kw):
        return P.op(q, lambda e: e.dma_start(out=out, in_=in_, **kw), reads, writes, kind="d", par=par)

    def mm(out, lhsT, rhs, start, stop, reads, writes):
        return P.op("pe", lambda e: e.matmul(out, lhsT=lhsT, rhs=rhs, start=start, stop=stop), reads, writes)

    def tr(out, in_, ident, reads, writes):
        return P.op("pe", lambda e: e.transpose(out, in_, ident), reads, writes)

    def act(out, in_, func, reads, writes, **kw):
        return P.op("act", lambda e: e.activation(out=out, in_=in_, func=func, **kw), reads, writes)

    def tt(eng, out, in0, in1, op, reads, writes):
        return P.op(eng, lambda e: e.tensor_tensor(out=out, in0=in0, in1=in1, op=op), reads, writes)

    def ts(eng, out, in0, s1, s2, op0, op1, reads, writes):
        if op1 is None:
            return P.op(eng, lambda e: e.tensor_scalar(out=out, in0=in0, scalar1=s1, scalar2=None, op0=op0), reads, writes)
        return P.op(eng, lambda e: e.tensor_scalar(out=out, in0=in0, scalar1=s1, scalar2=s2, op0=op0, op1=op1), reads, writes)

    def stt(out, in0, scalar, in1, op0, op1, reads, writes):
        return P.op("dve", lambda e: e.scalar_tensor_tensor(out=out, in0=in0, scalar=scalar, in1=in1, op0=op0, op1=op1), reads, writes)

    def cp(eng, out, in_, reads, writes):
        if eng == "act":
            return P.op("act", lambda e: e.copy(out=out, in_=in_), reads, writes)
        return P.op(eng, lambda e: e.tensor_copy(out=out, in_=in_), reads, writes)

    def rsqrt_mean_g(out, ssum, n, reads, writes):
        ts("dve", out, ssum, 1.0 / n, EPS, ALU.mult, ALU.add, reads, writes)
        yield
        act(out, out, AF.Ln, writes, writes)
        yield
        act(out, out, AF.Exp, writes, writes, scale=-0.5)
        yield

    def rsqrt_mean(out, ssum, n, reads, writes):
        ts("dve", out, ssum, 1.0 / n, EPS, ALU.mult, ALU.add, reads, writes)
        act(out, out, AF.Ln, writes, writes)
        act(out, out, AF.Exp, writes, writes, scale=-0.5)

    def bc(ap, n):
        return ap.partition_broadcast(n)

    for dst, src in ((identb, identb_in), (identf, identf_in), (rpn, rpn_in), (pos, pos_in), (etab, etab_in),
                     (hmask, hmask_in), (cmask, cmask_in), (cT, cT_in), (gainT, gainT_in), (bmodT, bmodT_in),
                     (mbias, mbias_in)):
        dma("sp", dst[:], src, [B_const], [B_cb], par=True)
    dma("sp", rdec[:], rdec_in.partition_broadcast(128), [B_const], [B_cb], par=True)
    for l in range(depth):
        dma("sp", rdsel[0:64, l * 4:(l + 1) * 4], rdec_in[:, l * 8:l * 8 + 4].partition_broadcast(64), [B_const], [B_cb], par=True)
        dma("sp", rdsel[64:128, l * 4:(l + 1) * 4], rdec_in[:, l * 8 + 4:l * 8 + 8].partition_broadcast(64), [B_const], [B_cb], par=True)
    dma("sp", sink[:], sink_in.partition_broadcast(128), [B_const], [B_cb], par=True)
    for c in range(CTXC):
        dma("sp", xc[:, c, :], ctx_in[c * 128:(c + 1) * 128, :], [B_const], [B_xc[c]])
    B_c2 = Buf("const2")
    P.op("dve", lambda e: e.memset(ones1[:], 1.0), [], [B_c2])
    for rot_ in (gqT, sqT, rT):
        for i in range(len(rot_.tiles)):
            P.op("pool", lambda e, t_=rot_.tiles[i]: e.memset(t_[:], 0.0), [], [rot_.bufs[i]])
    P.op("dve", lambda e: e.memset(V[:], 1.0), [], [B_V])
    P.op("dve", lambda e: e.memset(sVc[:], 1.0), [], [B_sVc])
    for i in range(2):
        P.op("pool", lambda e, i=i: e.memset(vsb.tiles[i][:], 1.0), [], [vsb.bufs[i]])
    act(cT[:], cT[:], AF.Silu, [B_cb], [B_cb])

    def load_w1(l):
        srcs = [(768, 1536, 0), (2048, 2176, 768), (2816, 2944, 896), (2176, 2304, 1024), (2944, 3072, 1152)]
        for k in range(8):
            for (a, b_, d0) in srcs:
                dma("pool", WA[:, k, d0:d0 + (b_ - a)], w_in[l, k * 128:(k + 1) * 128, a:b_], [B_const], [B_WA], par=True)

    def load_w2(l):
        srcs = [(0, 768, 0), (1536, 1792, 768), (2304, 2560, 1024), (3072, 3328, 1280), (1792, 2048, 1536), (2560, 2816, 1792)]
        for k in range(8):
            for (a, b_, d0) in srcs:
                dma("pool", W2[:, k, d0:d0 + (b_ - a)], w_in[l, k * 128:(k + 1) * 128, a:b_], [B_const], [B_W2], par=True)
        dma("pool", mixT[:], mixT_in[l], [B_const], [B_mixT])

    def load_wo(l):
        for k in range(8):
            dma("pool", WA[:, k, 0:1024], w_out[l, k * 128:(k + 1) * 128, :], [B_const], [B_WA], par=True)

    def layer_consts(l):
        rd = [B_cb, B_c2]
        w = [B_lay]
        dma("sp", rnorm[:], rnorm_in[:, l * 256:(l + 1) * 256].partition_broadcast(128), [B_const], [B_rq])
        dma("sp", qkn[:], qkn_in[:, l * 256:(l + 1) * 256].partition_broadcast(128), [B_const], [B_rq], par=True)
        act(lg[:], rdec[:, l * 8:(l + 1) * 8], AF.Exp, rd, w)
        ts("dve", lg[:], lg[:], -1.0, None, ALU.mult, None, w, w)
        act(lgsel[:], rdsel[:, l * 4:(l + 1) * 4], AF.Exp, rd, w)
        ts("dve", lgsel[:], lgsel[:], -1.0, None, ALU.mult, None, w, w)
        for dr in range(2):
            ts("dve", kd[:, dr * 4:(dr + 1) * 4], lg[:, dr * 4:(dr + 1) * 4], pos[:, dr:dr + 1], None, ALU.mult, None, rd + w, w)
            ts("dve", qd[:, dr * 4:(dr + 1) * 4], lg[:, dr * 4:(dr + 1) * 4], pos[:, 2 + dr:3 + dr], None, ALU.mult, None, rd + w, w)
        act(kd[:], kd[:], AF.Exp, w, w)
        ts("dve", kd[:], kd[:], SCALE, None, ALU.mult, None, w, w)
        act(qd[:], qd[:], AF.Exp, w, w)
        for h in range(4):
            ts("dve", DecT[:, h * 128:(h + 1) * 128], rpn[:, 0:128], lg[:, h:h + 1], None, ALU.mult, None, rd + w, w)
            stt(DecT[:, h * 128:(h + 1) * 128], rpn[:, 128:256], lg[:, 4 + h:5 + h], DecT[:, h * 128:(h + 1) * 128],
                ALU.mult, ALU.add, rd + w, w)
        act(DecT[:], DecT[:], AF.Exp, w, w)
        ts("dve", DecT[:], DecT[:], SCALE, None, ALU.mult, None, w, w)
        ts("dve", esink[:], lgsel[:], 128.0, None, ALU.mult, None, w, w)
        act(esink[:], esink[:], AF.Exp, w, w)
        cp("dve", Dt[:].rearrange("p (h e) -> p h e", h=4), esink[:].unsqueeze(2).to_broadcast([128, 4, 64]), w, w)
        for s_ in range(5):
            ts("dve", coef[:, s_, :], lgsel[:], etab[:, s_:s_ + 1], 128.0 * NCH, ALU.mult, ALU.mult, rd + w, w)
        act(coef[:], coef[:], AF.Exp, w, w)
        act(esink[:], sink[:, l * 4:(l + 1) * 4], AF.Exp, rd + w, w)
        P.op("pool", lambda e: e.memset(Aagg[:], 0.0), [], [B_Aagg])
        P.op("pool", lambda e: e.memset(Actx[:], 0.0), [], [B_Actx])
        P.op("pool", lambda e: e.memset(Pw[:], 1.0), [], [B_Pw])
        P.op("pool", lambda e: e.memset(STc[0:64, 1, :], 0.0), [], [B_STc[1]], par=True)
        P.op("pool", lambda e: e.memset(STc[64:128, 2, :], 0.0), [], [B_STc[2]], par=True)

    def mod_compute(l):
        dma("sp", bgate[:], bgate_in[:, l * D:(l + 1) * D], [B_const], [B_bg])
        mp, mb = zps.next()
        for half in range(-1, 2):
            if half < 0:
                for c in range(16):
                    wt, wb = wmst.next()
                    dma("sp", wt[:], w_mod[l, :, c * 128:(c + 1) * 128].rearrange("(k p) c -> p k c", p=128), [B_const], [wb])
                    for k in range(8):
                        mm(mp[:, c * 2:c * 2 + 2], wt[:, k, :], cT[:, 2 * k:2 * k + 2], k == 0, k == 7, [wb, B_cb], [mb])
                continue
            g0, gb0 = sps.next()
            g1, gb1 = sps.next()
            gps = ((g0, gb0), (g1, gb1))
            for q in range(4):
                col = 2048 + half * 512 + q * 128
                wt, wb = wmst.next()
                dma("sp", wt[:], w_mod[l, :, col:col + 128].rearrange("(k p) c -> p k c", p=128), [B_const], [wb])
                for j in range(2):
                    for k in range(8):
                        mm(gps[j][0][0:1, q * 128:(q + 1) * 128], cT[:, 2 * k + j:2 * k + j + 1], wt[:, k, :], k == 0, k == 7, [wb, B_cb], [gps[j][1]])
            for j in range(2):
                tt("dve", grow[:], gps[j][0][0:1, 0:512], bgate[0:1, half * 512:(half + 1) * 512], ALU.add, [gps[j][1], B_bg], [B_grow])
                zp, zb = ops_.next()
                mm(zp[:, 0:512], ones1[0:1, :], grow[0:1, :], True, True, [B_c2, B_grow], [zb])
                cp("dve", gateB[:, j, half * 512:(half + 1) * 512], zp[:, 0:512], [zb], [B_mod], )
        mpv = mp[:, 0:32].rearrange("p (c j) -> p j c", j=2)
        for j in range(2):
            tt("dve", Sm[:, j, :], mpv[:, j, 0:8], bmodT[:, l * 16:l * 16 + 8], ALU.add, [mb, B_cb], [B_mod])
            tt("dve", Gm[:, j, :], mpv[:, j, 8:16], bmodT[:, l * 16 + 8:l * 16 + 16], ALU.add, [mb, B_cb], [B_mod])
            stt(Gm[:, j, :], Gm[:, j, :], 1.0, gainT[:, l * 8:(l + 1) * 8], ALU.add, ALU.mult, [B_mod, B_cb], [B_mod])

    def norm_tile(l, xap, xbuf, j):
        s4, s4b = st4.next()
        xnt, xnb = xn.next()
        P.op("dve", lambda e: e.scalar_tensor_tensor(out=xnt[:], in0=xap, scalar=1.0, in1=xap, op0=ALU.mult, op1=ALU.mult,
                                                      accum_out=s4[:, 0:1]), [xbuf], [xnb, s4b])
        rsqrt_mean(s4[:, 0:1], s4[:, 0:1], D, [s4b], [s4b])
        ts("dve", xnt[:], xap, s4[:, 0:1], None, ALU.mult, None, [xbuf, s4b], [xnb])
        tp, tb = tps.next()
        for k in range(8):
            tr(tp[:, k * 128:(k + 1) * 128], xnt[:, k * 128:(k + 1) * 128], identb[:], [xnb, B_cb], [tb])
        ht, hb = hT.next()
        for k in range(8):
            ts("dve", ht[:, k, :], tp[:, k * 128:(k + 1) * 128], Gm[:, j, k:k + 1], Sm[:, j, k:k + 1], ALU.mult, ALU.add,
               [tb, B_mod], [hb])
        return ht, hb

    def norm_tile_g(l, xap, xbuf, j):
        s4, s4b = st4.next()
        xnt, xnb = xn.next()
        P.op("dve", lambda e: e.scalar_tensor_tensor(out=xnt[:], in0=xap, scalar=1.0, in1=xap, op0=ALU.mult, op1=ALU.mult,
                                                      accum_out=s4[:, 0:1]), [xbuf], [xnb, s4b])
        yield
        yield from rsqrt_mean_g(s4[:, 0:1], s4[:, 0:1], D, [s4b], [s4b])
        ts("dve", xnt[:], xap, s4[:, 0:1], None, ALU.mult, None, [xbuf, s4b], [xnb])
        yield
        tp, tb = tps.next()
        for k in range(8):
            tr(tp[:, k * 128:(k + 1) * 128], xnt[:, k * 128:(k + 1) * 128], identb[:], [xnb, B_cb], [tb])
        yield
        yield
        ht, hb = hT.next()
        for k in range(8):
            ts("dve", ht[:, k, :], tp[:, k * 128:(k + 1) * 128], Gm[:, j, k:k + 1], Sm[:, j, k:k + 1], ALU.mult, ALU.add,
               [tb, B_mod], [hb])
        yield
        yield
        return ht, hb

    def inproj(ht, hb, W, WB, c0, ncols):
        zp, zb = PS["z"].next()
        for k in range(8):
            mm(zp[:, 0:ncols], ht[:, k, :], W[:, k, c0:c0 + ncols], k == 0, k == 7, [hb, WB], [zb])
        return zp, zb

    def qk_norm_rope(src, srcb, nh, gain_ap, csb, rope):
        n = nh * 64
        Bt, Bb = t1.next()
        Ct, Cb = t2.next()
        s4, s4b = st4.next()
        v3 = lambda ap: ap[:, 0:n].rearrange("p (h d) -> p h d", d=64)
        tt("dve", Bt[:, 0:n], src[:, 0:n], src[:, 0:n], ALU.mult, [srcb], [Bb])
        yield
        P.op("dve", lambda e: e.tensor_reduce(out=s4[:, 0:nh], in_=v3(Bt), axis=AX.X, op=ALU.add), [Bb], [s4b])
        yield
        yield from rsqrt_mean_g(s4[:, 0:nh], s4[:, 0:nh], 64, [s4b], [s4b])
        tt("dve", v3(Ct), v3(src), s4[:, 0:nh].unsqueeze(2).to_broadcast([128, nh, 64]), ALU.mult, [srcb, s4b], [Cb])
        tt("dve", Ct[:, 0:n].rearrange("p (a h d) -> p a h d", a=2, d=64), Ct[:, 0:n].rearrange("p (a h d) -> p a h d", a=2, d=64),
           gain_ap.rearrange("p (a d) -> p a d", a=2).unsqueeze(2).to_broadcast([128, 2, nh // 2, 64]), ALU.mult, [Cb, B_rq], [Cb])
        yield
        ot, ob = knb.next()
        if not rope:
            cp("dve", ot[:, 0:n], Ct[:, 0:n], [Cb], [ob])
            return ot, ob
        cst, csbuf = csb
        v4 = lambda ap: ap[:, 0:n].rearrange("p (h a b c) -> p h a b c", a=2, b=2, c=16)
        cosb = cst[:, 0, :].unsqueeze(1).to_broadcast([128, nh, 64])
        sin4 = cst[:, 1, :].rearrange("p (a b c) -> p a b c", a=2, b=2)
        tt("dve", v3(Bt), v3(Ct), cosb, ALU.mult, [Cb, csbuf], [Bb])
        yield
        for b0 in range(2):
            tt("pool", v4(src)[:, :, :, b0, :], v4(Ct)[:, :, :, 1 - b0, :],
               sin4[:, :, b0, :].unsqueeze(1).to_broadcast([128, nh, 2, 16]), ALU.mult, [Cb, csbuf], [srcb], )
        yield
        yield
        tt("dve", ot[:, 0:n], Bt[:, 0:n], src[:, 0:n], ALU.add, [Bb, srcb], [ob])
        yield
        return ot, ob

    def ret_state(is_ctx):
        return (STc, B_STc, Actx, B_Actx) if is_ctx else (ST, B_ST, Aagg, B_Aagg)

    def p1_tile(l, t, xsrc, B_xsrc):
        is_ctx = t < CTXC
        c = t if is_ctx else t - CTXC
        j = 1 if is_ctx else 0
        if is_ctx:
            xap, xbuf = xc[:, c, :], B_xc[c]
            csb = None
        else:
            xt_, xbuf = xt.next()
            dma("sp", xt_[:], xsrc[c * 128:(c + 1) * 128, :], [B_xsrc], [xbuf])
            xap = xt_[:]
            cst, csbuf = cs.next()
            dma("sp", cst[:, 0, :], cos_in[c], [B_const], [csbuf])
            dma("sp", cst[:, 1, :], sin_in[c], [B_const], [csbuf], par=True)
            csb = (cst, csbuf)
        yield
        ht, hb = norm_tile(l, xap, xbuf, j)
        yield
        rq, rqb = rqkv.next()
        kr, krb = kraw.next()
        vs, vsbuf = vsb.next()
        zp, zb = inproj(ht, hb, WA, B_WA, 0, 512)
        cp("act", rq[:, 0:512], zp[:, 0:512], [zb], [rqb])
        yield
        zp, zb = inproj(ht, hb, WA, B_WA, 512, 512)
        cp("act", rq[:, 512:768], zp[:, 0:256], [zb], [rqb], )
        cp("act", kr[:, 0:256], zp[:, 256:512], [zb], [krb])
        yield
        zp, zb = inproj(ht, hb, WA, B_WA, 1024, 256)
        cp("dve", vs[:, 1:129], zp[:, 0:128], [zb], [vsbuf])
        cp("dve", vs[:, 131:259], zp[:, 128:256], [zb], [vsbuf])
        yield
        fbt, fbb = fb.next()
        for w_, dec in ((0, qd), (1, kd)):
            for dr in range(2):
                tt("dve", fbt[:, w_, :, dr, :], rq[:, w_ * 256:(w_ + 1) * 256].rearrange("p (h d) -> p h d", h=4),
                   dec[:, dr * 4:(dr + 1) * 4].unsqueeze(2).to_broadcast([128, 4, 64]), ALU.mult, [rqb, B_lay], [fbb], )
        yield
        tp, tb = tps.next()
        for w_ in range(2):
            for h in range(4):
                tr(tp[:, (w_ * 4 + h) * 128:(w_ * 4 + h + 1) * 128], fbt[:, w_, h, :, :].rearrange("p a d -> p (a d)"), identb[:],
                   [fbb, B_cb], [tb])
        fT, fTb = fbT.next()
        cp("act", fT[:].rearrange("p a h t -> p (a h t)"), tp[:, 0:1024], [tb], [fTb])
        dma("pool", qfb_d[t], fT[:, 0, :, :].rearrange("p h t -> p (h t)"), [fTb], [B_qfb], par=True)
        yield
        tp, tb = tps.next()
        for w_ in range(2):
            for pr in range(2):
                tr(tp[:, (w_ * 2 + pr) * 128:(w_ * 2 + pr + 1) * 128], rq[:, w_ * 256 + pr * 128:w_ * 256 + (pr + 1) * 128], identb[:],
                   [rqb, B_cb], [tb])
        rt, rtb = rT.next()
        cp("act", rt[0:64, 0, :, :], tp[0:64, 0:256].rearrange("p (b t) -> p b t", b=2), [tb], [rtb], )
        cp("act", rt[64:128, 1, :, :], tp[64:128, 0:256].rearrange("p (b t) -> p b t", b=2), [tb], [rtb], )
        cp("act", rt[:, 2, :, :], tp[:, 256:512].rearrange("p (b t) -> p b t", b=2), [tb], [rtb], )
        sp0, sb0 = sps.next()
        sp1, sb1 = sps.next()
        for pr in range(2):
            mm(sp0[:, pr * 128:(pr + 1) * 128], rt[:, 2, pr, :], rt[:, 0, pr, :], True, True, [rtb], [sb0])
            mm(sp1[:, pr * 128:(pr + 1) * 128], rt[:, 2, pr, :], rt[:, 1, pr, :], True, True, [rtb], [sb1])
        pi, pib = pint.next()
        piv = pi[:].rearrange("p (a b i) -> p a b i", a=2, b=2)
        dcv = DecT[:].rearrange("p (a b i) -> p a b i", a=2, b=2)
        for hh, (spx, sbx) in enumerate(((sp0, sb0), (sp1, sb1))):
            tt("dve", piv[:, :, hh, :], spx[:, 0:256].rearrange("p (a i) -> p a i", a=2), dcv[:, :, hh, :], ALU.mult,
               [sbx, B_lay], [pib], )
        mp, mb = PS["z"].next()
        for h in range(4):
            mm(mp[:, h * 64:(h + 1) * 64], pi[:, h * 128:(h + 1) * 128], rq[:, 512 + h * 64:512 + (h + 1) * 64], True, True, [pib, rqb], [mb])
        for h in range(4):
            mm(mp[:, 256 + h * 64:256 + (h + 1) * 64], fbt[:, 1, h, :, :].rearrange("p a d -> p (a d)"),
               rq[:, 512 + h * 64:512 + (h + 1) * 64], True, True, [fbb, rqb], [mb])
        yield
        yt, ytb = yint.next()
        cp("act", yt[:], mp[:, 0:256], [mb], [ytb])
        row0 = t * 128
        dma("pool", yin_d[row0:row0 + 128, :], yt[:], [ytb], [B_yin], par=True)
        Sx, BS, Ax, BA = ret_state(is_ctx)
        cp("act", Sx[0:64, c + 2, :], mp[0:64, 256:512], [mb], [BS[c + 2]], )
        cp("act", Sx[64:128, c, :], mp[64:128, 256:512], [mb], [BS[c]], )
        yield
        tt("pool", Ax[0:64, :], Ax[0:64, :], Dt[0:64, :], ALU.mult, [BA, B_lay], [BA])
        tt("pool", Ax[0:64, :], Ax[0:64, :], Sx[0:64, c + 2, :], ALU.add, [BA, BS[c + 2]], [BA])
        if is_ctx:
            if c == 0:
                tt("pool", Ax[64:128, :], Ax[64:128, :], Sx[64:128, c, :], ALU.add, [BA, BS[c]], [BA])
            else:
                tt("pool", sintmp[64:128, :], Dt[64:128, :], Sx[64:128, c, :], ALU.mult, [B_lay, BS[c]], [B_sinB])
                tt("pool", Ax[64:128, :], Ax[64:128, :], sintmp[64:128, :], ALU.add, [BA, B_sinB], [BA])
        else:
            tt("pool", sintmp[64:128, :], Pw[64:128, :], Sx[64:128, c, :], ALU.mult, [B_Pw, BS[c]], [B_sinB])
            tt("pool", Ax[64:128, :], Ax[64:128, :], sintmp[64:128, :], ALU.add, [BA, B_sinB], [BA])
            tt("pool", Pw[64:128, :], Pw[64:128, :], Dt[64:128, :], ALU.mult, [B_Pw, B_lay], [B_Pw])
        yield
        gk_gain = qkn[:, 128:256]
        kn, knbuf = yield from qk_norm_rope(kr, krb, 4, gk_gain, csb, not is_ctx)
        yield
        tp, tb = tps.next()
        for a in range(2):
            tr(tp[:, a * 128:(a + 1) * 128], kn[:, a * 128:(a + 1) * 128], identb[:], [knbuf, B_cb], [tb])
        if is_ctx:
            cp("act", KT[:, c * 128:(c + 1) * 128], tp[:, 0:128], [tb], [B_KT], )
            cp("act", skTc[:, c * 128:(c + 1) * 128], tp[:, 128:256], [tb], [B_skTc], )
            cp("dve", V[:, c, 1:129], vs[:, 1:129], [vsbuf], [B_V])
            cp("dve", sVc[:, c, 1:129], vs[:, 131:259], [vsbuf], [B_sVc])
        else:
            kt_, ktb = kTs.next()
            cp("act", kt_[:], tp[:, 0:256], [tb], [ktb])
            dma("pool", gk_x[c // GP][:, (c % GP) * 128:(c % GP + 1) * 128], kt_[:, 0:128], [ktb], [B_gkx[c // GP]], par=True)
            dma("pool", sk_d[c], kt_[:, 128:256], [ktb], [B_skd], par=True)
            dma("pool", gv_x[c // GP][:, (c % GP) * 130:(c % GP + 1) * 130], vs[:, 0:130], [vsbuf], [B_gvx[c // GP]], par=True)
            dma("pool", sv_d[c], vs[:, 130:260], [vsbuf], [B_svd], par=True)
            if c == 0:
                dma("pool", bnd_x[:, 0:128], kt_[:, 128:256], [ktb], [B_bndx], par=True)
                dma("pool", bnd_x[:, 256:386], vs[:, 130:260], [vsbuf], [B_bndx], par=True)
            if c == NCH - 1:
                dma("pool", bnd_x[:, 128:256], kt_[:, 128:256], [ktb], [B_bndx], par=True)
                dma("pool", bnd_x[:, 386:516], vs[:, 130:260], [vsbuf], [B_bndx], par=True)

    def allgather(src, bs, dst, bd):
        groups = [[0, 1, 2, 3], [4, 5, 6, 7]]
        P.op("pool", lambda e: e.collective_compute("AllGather", ALU.bypass, replica_groups=groups, ins=[src], outs=[dst]),
             [bs], [bd], kind="cc")

    def gather_piece(i):
        allgather(gk_x[i], B_gkx[i], gk_all[i], B_gkall[i])
        allgather(gv_x[i], B_gvx[i], gv_all[i], B_gvall[i])

    def exchange(l):
        dma("pool", agg_x, Aagg[:], [B_Aagg], [B_aggx])
        allgather(agg_x, B_aggx, agg_all, B_aggall)
        allgather(bnd_x, B_bndx, bnd_all, B_bndall)

    def exchange_b(l):
        dma("sp", aggs[:], agg_all.rearrange("(r p) c -> p r c", p=128), [B_aggall], [B_aggs])
        B_s2 = Buf("s2")
        v3 = lambda ap: ap.rearrange("p (h e) -> p h e", h=4)
        cb = lambda s_: coef[:, s_, :].unsqueeze(2).to_broadcast([128, 4, 64])
        SB2 = [B_sinF, B_sinB]
        tt("dve", v3(sintmp[:]), v3(Actx[:]), cb(4), ALU.mult, [B_Actx, B_lay], SB2)
        for r in range(R):
            tt("dve", v3(aggs[:, r, :]), v3(aggs[:, r, :]), cb(r), ALU.mult, [B_aggs, B_lay], [B_aggs])
            tt("dve", sintmp[:], sintmp[:], aggs[:, r, :], ALU.add, SB2 + [B_aggs], SB2)
        cp("dve", ST[0:64, 1, :], sintmp[0:64, :], [B_sinF], [B_ST[1]], )
        cp("dve", ST[64:128, NCH, :], sintmp[64:128, :], [B_sinB], [B_ST[NCH]], )
        sF, sB_ = sintmp[0:64, :], sintmp[64:128, :]
        for i in range(1, NCH):
            cf = i
            cbk = NCH - 1 - i
            tt("dve", sF, sF, Dt[0:64, :], ALU.mult, [B_sinF, B_lay], [B_sinF])
            tt("dve", sB_, sB_, Dt[64:128, :], ALU.mult, [B_sinB, B_lay], [B_sinB])
            tt("dve", sF, sF, ST[0:64, cf + 1, :], ALU.add, [B_sinF, B_ST[cf + 1]], [B_sinF])
            tt("dve", sB_, sB_, ST[64:128, cbk + 1, :], ALU.add, [B_sinB, B_ST[cbk + 1]], [B_sinB])
            cp("dve", ST[0:64, cf + 1, :], sF, [B_sinF], [B_ST[cf + 1]], )
            cp("dve", ST[64:128, cbk + 1, :], sB_, [B_sinB], [B_ST[cbk + 1]], )

    def small_attn(qT_ap, qTb, blocks, sink_l, mo, mob, gt, gtb, colbase):
        for g in range(2):
            op_, opb = PS["acc"].next()
            pend = []

            def pv(item):
                bi, vap, w_, wb_, bufs = item
                mm(op_[0:65, 0:256], vap[:, g * 65:(g + 1) * 65], w_[:], bi == 0, bi == len(blocks) - 1, [wb_] + bufs, [opb])

            for bi, (kap, vap, mask, bufs) in enumerate(blocks):
                sp_, spb = sps.next()
                mm(sp_[:, 0:256], kap, qT_ap[:, g, :, :].rearrange("p r t -> p (r t)"), True, True,
                   [qTb] + bufs, [spb])
                yield
                w_, wb_ = wp.next()
                act(w_[:], sp_[:, 0:256], AF.Exp, [spb], [wb_], scale=SCALE)
                yield
                if mask is not None:
                    tt("pool", w_[:].rearrange("p (r t) -> p r t", r=2), w_[:].rearrange("p (r t) -> p r t", r=2),
                       mask.unsqueeze(1).to_broadcast([128, 2, 128]), ALU.mult, [wb_, B_cb], [wb_])
                    yield
                pend.append((bi, vap, w_, wb_, bufs))
                if len(pend) > 1:
                    pv(pend.pop(0))
            for item in pend:
                pv(item)
            yield
            wo, wob = woT.next()
            cp("dve", wo[:], op_[0:65, 0:256], [opb], [wob])
            yield
            for r in range(2):
                h = g * 2 + r
                zp, zb = PS["z"].next()
                tr(zp[:, 0:65], wo[:, r * 128:(r + 1) * 128], identf[0:65, 0:65], [wob, B_cb], [zb])
                yield
                finish_head(zp, zb, g, h, sink_l, mo, mob, gt, gtb, colbase)
                yield

    def finish_head(zp, zb, g, h, sink_l, mo, mob, gt, gtb, colbase):
        rd_, rdb = rden.next()
        dcol = 0 if g == 0 else 64
        o0 = 1 if g == 0 else 0
        if sink_l:
            ts("dve", rd_[:, 0:1], zp[:, dcol:dcol + 1], esink[:, h:h + 1], None, ALU.add, None, [zb, B_lay], [rdb])
            P.op("dve", lambda e: e.reciprocal(out=rd_[:, 0:1], in_=rd_[:, 0:1]), [rdb], [rdb])
        else:
            P.op("dve", lambda e: e.reciprocal(out=rd_[:, 0:1], in_=zp[:, dcol:dcol + 1]), [zb], [rdb])
        stt(mo[:, colbase + h * 64:colbase + (h + 1) * 64], zp[:, o0:o0 + 64], rd_[:, 0:1], gt[:, colbase + h * 64:colbase + (h + 1) * 64],
            ALU.mult, ALU.mult, [zb, rdb, gtb], [mob])

    def silu_evac(dst, dstb, zp, zb):
        Ct, Cb = t2.next()
        act(Ct[:, 0:512], zp[:, 0:512], AF.Exp, [zb], [Cb], scale=-1.0)
        yield
        ts("dve", Ct[:, 0:512], Ct[:, 0:512], 1.0, None, ALU.add, None, [Cb], [Cb])
        yield
        P.op("dve", lambda e: e.reciprocal(out=Ct[:, 0:512], in_=Ct[:, 0:512]), [Cb], [Cb])
        yield
        yield
        tt("dve", dst, zp[:, 0:512], Ct[:, 0:512], ALU.mult, [zb, Cb], [dstb])

    def p2_front(l, t, xsrc, B_xsrc):
        is_ctx = t < CTXC
        c = t if is_ctx else t - CTXC
        j = 1 if is_ctx else 0
        if is_ctx:
            xap, xbuf = xc[:, c, :], B_xc[c]
            csb = None
        else:
            xt_, xbuf = xt.next()
            dma("sp", xt_[:], xsrc[c * 128:(c + 1) * 128, :], [B_xsrc], [xbuf])
            xap = xt_[:]
            cst, csbuf = cs.next()
            dma("sp", cst[:, 0, :], cos_in[c], [B_const], [csbuf])
            dma("sp", cst[:, 1, :], sin_in[c], [B_const], [csbuf], par=True)
            csb = (cst, csbuf)
        yl, ylb = yinl.next()
        dma("sp", yl[:], yin_d[t * 128:(t + 1) * 128, :], [B_yin], [ylb])
        ql, qlb = qfbl.next()
        dma("sp", ql[:], qfb_d[t], [B_qfb], [qlb])
        yield
        yield
        ht, hb = yield from norm_tile_g(l, xap, xbuf, j)
        ug, ugb = uvg.next()
        gt, gtb = gates.next()
        mo, mob = mixo.next()
        qr, qrb = kraw.next()
        zp, zb = inproj(ht, hb, W2, B_W2, 0, 512)
        yield
        yield
        act(ug[:], zp[:, 0:512], AF.Gelu, [zb], [ugb])
        yield
        zp, zb = inproj(ht, hb, W2, B_W2, 512, 512)
        yield
        yield
        yield from silu_evac(gt[:, 0:512], gtb, zp, zb)
        yield
        zp, zb = inproj(ht, hb, W2, B_W2, 1024, 512)
        yield
        yield
        yield from silu_evac(gt[:, 512:1024], gtb, zp, zb)
        yield
        zp, zb = inproj(ht, hb, W2, B_W2, 1536, 512)
        yield
        yield
        for blk in range(2):
            cp("dve", qr[:, blk * 256:(blk + 1) * 256].rearrange("p (r g d) -> p r g d", r=2, g=2),
               zp[:, blk * 256:(blk + 1) * 256].rearrange("p (g r d) -> p r g d", r=2, g=2), [zb], [qrb], )
        yield
        yield "RET"
        mp, mb = PS["z"].next()
        for h in range(4):
            mm(mp[:, h * 64:(h + 1) * 64], mixT[:, h * 128:(h + 1) * 128], ug[:, 256 + h * 64:256 + (h + 1) * 64], True, True, [B_mixT, ugb], [mb])
        Sx, BS, _, _ = ret_state(is_ctx)
        for h in range(4):
            mm(mp[:, 256 + h * 64:256 + (h + 1) * 64], ql[:, h * 128:(h + 1) * 128], Sx[:, c + 1, h * 64:(h + 1) * 64], True, True, [qlb, BS[c + 1]], [mb])
        yield
        yield
        ys, ysb = ysum.next()
        v3 = lambda ap: ap.rearrange("p (h d) -> p h d", h=4)
        tt("dve", v3(ys[:]), v3(mp[:, 0:256]), mbias[:, l * 4:(l + 1) * 4].unsqueeze(2).to_broadcast([128, 4, 64]), ALU.add, [mb, B_cb], [ysb])
        tt("dve", ys[:], ys[:], ug[:, 0:256], ALU.mult, [ysb, ugb], [ysb])
        yield
        tt("dve", mo[:, 0:256], ys[:], gt[:, 0:256], ALU.mult, [ysb, gtb], [mob])
        ys, ysb = ysum.next()
        tt("dve", ys[:], mp[:, 256:512], yl[:], ALU.add, [mb, ylb], [ysb])
        yield
        Bt, Bb = t1.next()
        s4, s4b = st4.next()
        tt("dve", Bt[:, 0:256], ys[:], ys[:], ALU.mult, [ysb], [Bb])
        yield
        P.op("dve", lambda e: e.tensor_reduce(out=s4[:, 0:4], in_=v3(Bt[:, 0:256]), axis=AX.X, op=ALU.add), [Bb], [s4b])
        yield
        yield from rsqrt_mean_g(s4[:, 0:4], s4[:, 0:4], 64, [s4b], [s4b])
        tt("dve", v3(ys[:]), v3(ys[:]), s4[:, 0:4].unsqueeze(2).to_broadcast([128, 4, 64]), ALU.mult, [ysb, s4b], [ysb])
        tt("dve", ys[:], ys[:], rnorm[:, 0:256], ALU.mult, [ysb, B_rq], [ysb])
        yield
        tt("dve", mo[:, 256:512], ys[:], gt[:, 256:512], ALU.mult, [ysb, gtb], [mob])
        yield
        qn, qnb = yield from qk_norm_rope(qr, qrb, 8, qkn[:, 0:128], csb, not is_ctx)
        yield
        tp, tb = tps.next()
        for a in range(4):
            tr(tp[:, a * 128:(a + 1) * 128], qn[:, a * 128:(a + 1) * 128], identb[:], [qnb, B_cb], [tb])
        yield
        yield
        return dict(t=t, c=c, is_ctx=is_ctx, gt=gt, gtb=gtb, mo=mo, mob=mob, tp=tp, tb=tb)

    def out_proj(l, st, xsrc, B_xsrc, xdst, B_xdst):
        t, c, is_ctx, mo, mob = st["t"], st["c"], st["is_ctx"], st["mo"], st["mob"]
        j = 1 if is_ctx else 0
        if is_ctx:
            xap, xbuf = xc[:, c, :], B_xc[c]
        else:
            xr_, xbuf = xr.next()
            dma("sp", xr_[:], xsrc[c * 128:(c + 1) * 128, :], [B_xsrc], [xbuf])
            xap = xr_[:]
        tp, tb = tps.next()
        for k in range(8):
            tr(tp[:, k * 128:(k + 1) * 128], mo[:, k * 128:(k + 1) * 128], identb[:], [mob, B_cb], [tb])
        yield
        yield
        mt, mtb = mixTt.next()
        cp("dve", mt[:].rearrange("p k t -> p (k t)"), tp[:, 0:1024], [tb], [mtb])
        yield
        yield
        ot, otb = otmp.next()
        for half in range(2):
            zp, zb = PS["z"].next()
            for k in range(8):
                mm(zp[:, 0:512], mt[:, k, :], WA[:, k, half * 512:(half + 1) * 512], k == 0, k == 7, [mtb, B_WA], [zb])
            yield
            yield
            yield
            tt("dve", ot[:, half * 512:(half + 1) * 512], zp[:, 0:512], gateB[:, j, half * 512:(half + 1) * 512], ALU.mult, [zb, B_mod], [otb], )
            yield
        if is_ctx:
            tt("pool", xc[:, c, :], xc[:, c, :], ot[:], ALU.add, [xbuf, otb], [xbuf])
        else:
            tt("pool", ot[:], ot[:], xap, ALU.add, [otb, xbuf], [otb])
            yield
            yield
            dma("pool", xdst[c * 128:(c + 1) * 128, :], ot[:], [otb], [B_xdst], par=True)
        yield

    def q_evac(st, q_, qb, lo, cols=None):
        for gg in range(2):
            dst = q_[gg * 64:(gg + 1) * 64, gg, :, :] if cols is None else q_[gg * 64:(gg + 1) * 64, gg, :, cols[0]:cols[1]]
            cp("dve", dst, st["tp"][gg * 64:(gg + 1) * 64, lo:lo + 256].rearrange("p (r t) -> p r t", r=2), [st["tb"]], [qb], )

    def p2_ctx(l):
        for c in range(CTXC):
            st = yield from p2_front(l, c, None, None)
            qT_, qTb = sqT.next()
            qT2, qT2b = sqT.next()
            q_evac(st, qT_, qTb, 0)
            q_evac(st, qT2, qT2b, 256)
            yield
            gblocks = [(KT[:, cc * 128:(cc + 1) * 128], V[:, cc, :], None, [B_KT, B_V]) for cc in range(CTXC)]
            yield from small_attn(qT_, qTb, gblocks, False, st["mo"], st["mob"], st["gt"], st["gtb"], 512)
            sblocks = [(skTc[:, cc * 128:(cc + 1) * 128], sVc[:, cc, :], None, [B_skTc, B_sVc]) for cc in range(CTXC)]
            yield from small_attn(qT2, qT2b, sblocks, True, st["mo"], st["mob"], st["gt"], st["gtb"], 768)
            yield from out_proj(l, st, None, None, None, None)

    def swa_tile(l, st, sq_, sqb):
        c = st["c"]
        wk_, wkb = wk.next()
        wv_, wvb = wv.next()
        blocks = []
        bb = [wkb, wvb]
        if c == 0:
            dma("sp", wk_[:, 0:2, :], sk_d[0:2].rearrange("c p k -> p c k"), [B_skd], [wkb])
            dma("sp", wv_[:, 0:2, :], sv_d[0:2].rearrange("c p k -> p c k"), [B_svd], [wvb])
            dma("sp", wk_[:, 2:6, :], bnd_all[:, 128:256].rearrange("(r p) k -> p r k", p=128), [B_bndall], [wkb], par=True)
            dma("sp", wv_[:, 2:6, :], bnd_all[:, 386:516].rearrange("(r p) k -> p r k", p=128), [B_bndall], [wvb], par=True)
            blocks.append((wk_[:, 0, :], wv_[:, 0, :], None, bb))
            blocks.append((wk_[:, 1, :], wv_[:, 1, :], cmask[:, 128:256], bb))
            for r in range(R):
                blocks.append((wk_[:, 2 + r, :], wv_[:, 2 + r, :], hmask[:, r * 128:(r + 1) * 128], bb))
        elif c == NCH - 1:
            dma("sp", wk_[:, 0:2, :], sk_d[c - 1:c + 1].rearrange("c p k -> p c k"), [B_skd], [wkb])
            dma("sp", wv_[:, 0:2, :], sv_d[c - 1:c + 1].rearrange("c p k -> p c k"), [B_svd], [wvb])
            dma("sp", wk_[:, 2:6, :], bnd_all[:, 0:128].rearrange("(r p) k -> p r k", p=128), [B_bndall], [wkb], par=True)
            dma("sp", wv_[:, 2:6, :], bnd_all[:, 256:386].rearrange("(r p) k -> p r k", p=128), [B_bndall], [wvb], par=True)
            blocks.append((wk_[:, 0, :], wv_[:, 0, :], cmask[:, 0:128], bb))
            blocks.append((wk_[:, 1, :], wv_[:, 1, :], None, bb))
            for r in range(R):
                blocks.append((wk_[:, 2 + r, :], wv_[:, 2 + r, :], hmask[:, (4 + r) * 128:(5 + r) * 128], bb))
        else:
            dma("sp", wk_[:, 0:3, :], sk_d[c - 1:c + 2].rearrange("c p k -> p c k"), [B_skd], [wkb])
            dma("sp", wv_[:, 0:3, :], sv_d[c - 1:c + 2].rearrange("c p k -> p c k"), [B_svd], [wvb])
            blocks.append((wk_[:, 0, :], wv_[:, 0, :], cmask[:, 0:128], bb))
            blocks.append((wk_[:, 1, :], wv_[:, 1, :], None, bb))
            blocks.append((wk_[:, 2, :], wv_[:, 2, :], cmask[:, 128:256], bb))
        for cc in range(CTXC):
            blocks.append((skTc[:, cc * 128:(cc + 1) * 128], sVc[:, cc, :], None, [B_skTc, B_sVc]))
        yield
        yield from small_attn(sq_, sqb, blocks, True, st["mo"], st["mob"], st["gt"], st["gtb"], 768)

    def front_group(l, gi, xsrc, B_xsrc, G):
        gq_, gqb = gqT.next()
        G["gq"] = (gq_, gqb)
        G["sts"] = []
        for ti in range(QG):
            st = yield from p2_front(l, CTXC + gi * QG + ti, xsrc, B_xsrc)
            sq_, sqb = sqT.next()
            q_evac(st, gq_, gqb, 0, (ti * 128, (ti + 1) * 128))
            q_evac(st, sq_, sqb, 256)
            yield
            yield from swa_tile(l, st, sq_, sqb)
            G["sts"].append(st)

    def sweep_group(G):
        gq_, gqb = G["gq"]
        accs = [ops_.next() for _ in range(2)]
        pend = []
        NW = 2 * GQ
        pieces = [(None, None)] + [(r, c0) for r in range(R) for c0 in range(0, NCH, PCS)]
        nblk_total = CTXC + R * NCH
        seen = 0

        def pv(item):
            first, last, g0, vap, vb, p0, pb0 = item
            mm(accs[g0][0][0:65, 0:NW], vap[:, g0 * 65:(g0 + 1) * 65], p0[:, 0:NW], first, last, [pb0] + vb, [accs[g0][1]])

        for (r, c0) in pieces:
            if r is None:
                nb = CTXC
                kget = lambda i: KT[:, i * 128:(i + 1) * 128]
                vget = lambda i: V[:, i, :]
                kbufs, vbufs = [B_KT], [B_V]
            else:
                nb = PCS
                kt_, ktb_ = ksl.next()
                vt_, vtb_ = vsl.next()
                gp_, of_ = c0 // GP, c0 % GP
                dma("sp", kt_[:], gk_all[gp_][r * 128:(r + 1) * 128, of_ * 128:(of_ + PCS) * 128], [B_gkall[gp_]], [ktb_])
                dma("sp", vt_[:], gv_all[gp_][r * 128:(r + 1) * 128, of_ * 130:(of_ + PCS) * 130].rearrange("p (c d) -> p c d", d=130), [B_gvall[gp_]], [vtb_])
                kget = lambda i, kt_=kt_: kt_[:, i * 128:(i + 1) * 128]
                vget = lambda i, vt_=vt_: vt_[:, i, :]
                kbufs, vbufs = [ktb_], [vtb_]
            for i in range(nb):
                first, last = seen == 0, seen == nblk_total - 1
                seen += 1
                for g in range(2):
                    sp_, spb = sps.next()
                    mm(sp_[:, 0:NW], kget(i), gq_[:, g, :, :].rearrange("p r t -> p (r t)"), True, True,
                       kbufs + [gqb], [spb])
                    p_, pb = pT.next()
                    act(p_[:, 0:NW], sp_[:, 0:NW], AF.Exp, [spb], [pb], scale=SCALE)
                    pend.append((first, last, g, vget(i), vbufs, p_, pb))
                    if len(pend) > 2:
                        pv(pend.pop(0))
                yield
        for item in pend:
            pv(item)
        G["oT"] = []
        for g in range(2):
            o_, ob_ = oT.next()
            cp("dve", o_[:, 0:NW], accs[g][0][0:65, 0:NW], [accs[g][1]], [ob_])
            G["oT"].append((o_, ob_))

    def tail_group(l, G, xsrc, B_xsrc, xdst, B_xdst):
        sts = G["sts"]
        for g in range(2):
            o_, ob_ = G["oT"][g]
            for r in range(2):
                h = 2 * g + r
                for ti in range(QG):
                    zp, zb = PS["z"].next()
                    tr(zp[:, 0:65], o_[:, r * GQ + ti * 128:r * GQ + (ti + 1) * 128], identf[0:65, 0:65], [ob_, B_cb], [zb])
                    yield
                    yield
                    finish_head(zp, zb, g, h, False, sts[ti]["mo"], sts[ti]["mob"], sts[ti]["gt"], sts[ti]["gtb"], 512)
                    yield
        for st in sts:
            yield from out_proj(l, st, xsrc, B_xsrc, xdst, B_xdst)

    def run(gen):
        try:
            while True:
                next(gen)
        except StopIteration as e:
            return e.value

    def gchain(*gens):
        for g_ in gens:
            yield from g_

    def pass2(l, xsrc, B_xsrc, xdst, B_xdst):
        PS["z"], PS["acc"] = zps1, swacc
        if l < depth - 1:
            run(p2_ctx(l))
        Gs = [dict() for _ in range(NG)]
        g0 = front_group(l, 0, xsrc, B_xsrc, Gs[0])
        while next(g0) != "RET":
            pass
        exchange_b(l)
        run(g0)
        for k in range(NG):
            sides = []
            if k > 0:
                sides.append(tail_group(l, Gs[k - 1], xsrc, B_xsrc, xdst, B_xdst))
            if k + 1 < NG:
                sides.append(front_group(l, k + 1, xsrc, B_xsrc, Gs[k + 1]))
            side = gchain(*sides)
            alive = True
            for _ in sweep_group(Gs[k]):
                for _r in range(SIDE_RATE):
                    if alive:
                        try:
                            next(side)
                        except StopIteration:
                            alive = False
            if alive:
                run(side)
        run(tail_group(l, Gs[NG - 1], xsrc, B_xsrc, xdst, B_xdst))
        PS["z"], PS["acc"] = zpsP1, ops_

    chain = [(x_in, B_xin)]
    inter = [(xsA, B_xsA), (xsB, B_xsB)]
    for l in range(depth):
        chain.append((y_out, B_y) if l == depth - 1 else inter[l % 2])
    for l in range(depth):
        xsrc, B_xsrc = chain[l]
        xdst, B_xdst = chain[l + 1]
        load_w1(l)
        layer_consts(l)
        if stop == "consts":
            break
        mod_compute(l)
        if stop == "mod":
            break
        load_w2(l)
        if stop == "w2":
            break
        gens = [p1_tile(l, t, xsrc, B_xsrc) for t in range(CTXC + NCH)]
        active = []
        nxt = 0
        while nxt < len(gens) or active:
            if nxt < len(gens) and (not active or (len(active) < 2 and active[-1][1] >= P1LAG)):
                active.append([gens[nxt], 0, nxt - CTXC])
                nxt += 1
            for a_ in list(active):
                try:
                    next(a_[0])
                    a_[1] += 1
                except StopIteration:
                    active.remove(a_)
                    if a_[2] >= 0 and a_[2] % GP == GP - 1 and a_[2] // GP < NGP - 1:
                        gather_piece(a_[2] // GP)
        if stop == "p1":
            break
        exchange(l)
        gather_piece(NGP - 1)
        if stop == "exch":
            break
        load_wo(l)
        pass2(l, xsrc, B_xsrc, xdst, B_xdst)

    P.finalize()
    with nc.Block() as block:
        @block.sync
        def _(e):
            P.emit("sp", e)

        @block.scalar
        def _(e):
            P.emit("act", e)

        @block.vector
        def _(e):
            P.emit("dve", e)

        @block.tensor
        def _(e):
            P.emit("pe", e)

        @block.gpsimd
        def _(e):
            P.emit("pool", e)
            if B_y.sem is not None:
                P.final_wait(e, [B_y])
            else:
                dummy = P.es.enter_context(nc.semaphore("dummy"))
                e.dma_start(out=y_out[0:128, :], in_=xc[:, 0, :]).then_inc(dummy, 16)
                e.wait_ge(dummy, 16)
    es.close()
    return nc


def host_inputs(inputs, NCH, depth=DEPTH, n_cores=8):
    f = np.float32
    x = np.asarray(inputs["x"], f)
    NT = NCH * 128
    bf = ml_dtypes.bfloat16
    c = np.asarray(inputs["c"], f)
    ctx = np.asarray(inputs["ctx"], f)
    c_ctx = np.asarray(inputs["c_ctx"], f)
    ng = np.asarray(inputs["norm_gain"], f)
    b_mod = np.asarray(inputs["b_mod"], f)
    common = {
        "gainT": np.ascontiguousarray(ng.reshape(depth, 8, 128).transpose(2, 0, 1).reshape(128, depth * 8)),
        "w_mod": np.ascontiguousarray(np.asarray(inputs["w_mod"], f)),
        "bmodT": np.ascontiguousarray(b_mod[:, 0:2048].reshape(depth, 16, 128).transpose(2, 0, 1).reshape(128, depth * 16)),
        "bgate": np.ascontiguousarray(b_mod[:, 2048:3072].reshape(1, depth * D)),
        "w_in": np.ascontiguousarray(np.asarray(inputs["w_in"], f)),
        "w_out": np.ascontiguousarray(np.asarray(inputs["w_out"], f)),
        "mixT": np.ascontiguousarray(np.asarray(inputs["mlp_mix"], f).transpose(0, 3, 1, 2).reshape(depth, 128, 512)),
        "mbias": np.ascontiguousarray(np.asarray(inputs["mlp_bias"], f).transpose(2, 0, 1).reshape(128, depth * 4)),
        "rdec": np.ascontiguousarray(np.stack([np.asarray(inputs["ret_decay_fwd"], f), np.asarray(inputs["ret_decay_bwd"], f)], 1).reshape(1, depth * 8)),
        "rnorm": np.ascontiguousarray(np.asarray(inputs["ret_norm"], f).reshape(1, depth * 256)),
        "qkn": np.ascontiguousarray(np.stack([np.asarray(inputs[k], f) for k in ("attn_q_norm", "swa_q_norm", "attn_k_norm", "swa_k_norm")], 1).reshape(1, depth * 256)),
        "sink": np.ascontiguousarray(np.asarray(inputs["swa_sink"], f).reshape(1, depth * 4)),
    }
    jj = np.arange(128, dtype=f)[:, None]
    ii = np.arange(128, dtype=f)[None, :]
    common["identb"] = np.eye(128, dtype=f).astype(bf)
    common["identf"] = np.eye(128, dtype=f)
    common["rpn"] = np.concatenate([np.maximum(ii - jj, 0), np.maximum(jj - ii, 0)], 1).astype(f)
    p = np.arange(128, dtype=f)
    common["pos"] = np.stack([127 - p, p, p + 1, 128 - p], 1).astype(f)
    mprev = (jj >= ii).astype(f)
    mnext = (jj <= ii).astype(f)
    common["cmask"] = np.concatenate([mprev, mnext], 1).astype(bf)
    half = 32
    inv_freq = (1.0 / (10000.0 ** (np.arange(0, half, 2, dtype=f) / f(half)))).astype(f)
    sgn = np.concatenate([-np.ones(16, f), np.ones(16, f), -np.ones(16, f), np.ones(16, f)])
    maps = []
    for core in range(n_cores):
        b, seg = core // R, core % R
        m = dict(common)
        m["x_in"] = np.ascontiguousarray(x[b, seg * NT:(seg + 1) * NT, :])
        m["ctx_in"] = np.ascontiguousarray(ctx[b])
        cT = np.zeros((128, 16), f)
        cT[:, 0::2] = c[b].reshape(8, 128).T
        cT[:, 1::2] = c_ctx.reshape(8, 128).T
        m["cT"] = cT
        tpos = seg * NT + np.arange(NT)
        row = (tpos // 64).astype(f)
        col = (tpos % 64).astype(f)
        ang_r = row[:, None] * inv_freq[None, :]
        ang_c = col[:, None] * inv_freq[None, :]
        ang = np.concatenate([ang_r, ang_r, ang_c, ang_c], -1).astype(f)
        m["cos"] = np.cos(ang).astype(f).reshape(NCH, 128, 64)
        m["sin"] = (np.sin(ang).astype(f) * sgn[None, :]).reshape(NCH, 128, 64)
        et = np.full((128, 5), BIGE, f)
        for r in range(R):
            if r < seg:
                et[0:64, r] = seg - 1 - r
            if r > seg:
                et[64:128, r] = r - seg - 1
        et[0:64, 4] = seg
        et[64:128, 4] = R - 1 - seg
        m["etab"] = et
        hm = np.zeros((128, 8, 128), f)
        if seg - 1 >= 0:
            hm[:, seg - 1, :] = mprev
        if seg + 1 < R:
            hm[:, 4 + seg + 1, :] = mnext
        m["hmask"] = hm.reshape(128, 1024).astype(bf)
        maps.append(m)
    return maps


_NC_CACHE = {}


def kernel(**inputs):
    x = np.asarray(inputs["x"])
    B, L, _ = x.shape
    NCH = L // R // 128
    depth = np.asarray(inputs["w_in"]).shape[0]
    key = (NCH, depth)
    if key not in _NC_CACHE:
        _NC_CACHE[key] = build(NCH, depth)
    nc = _NC_CACHE[key]
    maps = host_inputs(inputs, NCH, depth)
    res = run_bass_kernel_spmd(nc, maps, core_ids=list(range(8)))
    NT = NCH * 128
    out = np.zeros((B, L, D), np.float32)
    for core in range(8):
        b, seg = core // R, core % R
        out[b, seg * NT:(seg + 1) * NT, :] = res.results[core]["y"]
    return out
```
